# Optimizing a Trainium2 kernel written in Bass

```python
import math
import jax, jax.numpy as jnp
from jax import lax
import numpy as np

D_MODEL = 2048
BATCH = 2
SEQ = 8192
DEPTH = 2
DEC_BATCH = 16
DEC_SEQ = 64
PAST_LEN = 2048

CHUNK = 64
CONV_W = 4
EPS = 1e-6
D_RNN = D_MODEL // 2
H_A = 8
BLK_A = D_RNN // H_A
RG_C = 8.0
H_B = 8
DH_B = D_MODEL // (4 * H_B)
DV_B = 2 * DH_B
D_ATT = H_B * DV_B
Q_BLOCK = 128
IN_EVEN = 2 * D_RNN + 3 * D_ATT
D_SSD = D_MODEL // 2
P_C = 64
H_C = D_SSD // P_C
N_C = 128
G_C = 2
D_XBC = D_SSD + 2 * G_C * N_C
SSD_CHUNK = CHUNK
D_HG = D_MODEL // 2
H_D = 8
DK_D = D_HG // H_D
DV_D = DK_D
HG_CHUNK = 16
IN_ODD = D_SSD + D_XBC + H_C + 3 * D_HG
D_FF = ((8 * D_MODEL // 3 + 255) // 256) * 256

kernel_name = 'hybrid_stream_encoder_step'


def rms_norm(x, w):
    xf = x.astype(jnp.float32)
    y = xf * lax.rsqrt(jnp.mean(xf * xf, axis=-1, keepdims=True) + EPS)
    return (y * w.astype(jnp.float32)).astype(x.dtype)


def causal_conv(x, buf, w, b):
    L = x.shape[1]
    xp = jnp.concatenate([buf.astype(x.dtype), x], axis=1)
    y = b.astype(x.dtype)
    for tap in range(CONV_W):
        y = y + xp[:, tap:tap + L] * w[tap]
    return y, xp[:, L:]


def swiglu(x, w_gate, w_up, w_down):
    return (jax.nn.silu(x @ w_gate) * (x @ w_up)) @ w_down


def rglru(x, h0, w_r, b_r, w_i, b_i, lam):
    bn, L, _ = x.shape
    xb = x.reshape(bn, L, H_A, BLK_A)
    r = jax.nn.sigmoid(jnp.einsum('blhi,hij->blhj', xb, w_r).reshape(bn, L, D_RNN).astype(jnp.float32) + b_r.astype(jnp.float32))
    gi = jax.nn.sigmoid(jnp.einsum('blhi,hij->blhj', xb, w_i).reshape(bn, L, D_RNN).astype(jnp.float32) + b_i.astype(jnp.float32))
    log_a = -RG_C * r * jax.nn.softplus(-lam.astype(jnp.float32))
    a = jnp.exp(log_a)
    u = jnp.sqrt(-jnp.expm1(2.0 * log_a)) * (gi * x.astype(jnp.float32))
    u = u.at[:, 0].add(a[:, 0] * h0.astype(jnp.float32))

    def combine(e1, e2):
        a1, b1 = e1
        a2, b2 = e2
        return a1 * a2, a2 * b1 + b2

    _, h = lax.associative_scan(combine, (a, u), axis=1)
    return h.astype(x.dtype), h[:, -1].astype(x.dtype)


def diff_attention(q, k, v, pos0, lam):
    bn, Lq = q.shape[:2]
    Lk = k.shape[1]
    qb = Q_BLOCK if Lq % Q_BLOCK == 0 else Lq
    nblk = Lq // qb
    q_blocks = q.reshape(bn, nblk, qb, H_B, 2, DH_B).swapaxes(0, 1)
    qpos = (pos0 + jnp.arange(Lq)).reshape(nblk, qb)
    kchunk = jnp.arange(Lk) // CHUNK
    scale = DH_B ** -0.5

    def block(args):
        qblk, qp = args
        s = jnp.einsum('bqhcd,bkhcd->bhcqk', qblk, k).astype(jnp.float32) * scale
        mask = kchunk[None, :] <= (qp // CHUNK)[:, None]
        p = jax.nn.softmax(jnp.where(mask, s, -jnp.inf), axis=-1)
        wgt = p[:, :, 0] - lam * p[:, :, 1]
        return jnp.einsum('bhqk,bkhv->bqhv', wgt.astype(v.dtype), v)

    o = lax.map(block, (q_blocks, qpos))
    return o.swapaxes(0, 1).reshape(bn, Lq, H_B, DV_B)


def even_mixer(x, layer, k_cache, v_cache, conv_buf, h0, p):
    (w_in, conv_w, conv_b, w_r, b_r, w_i, b_i, lam_p, lq1, lk1, lq2, lk2, subln_w, w_out) = p
    bn, L, _ = x.shape
    proj = x @ w_in
    xa, ga, q, k, v = jnp.split(proj, [D_RNN, 2 * D_RNN, 2 * D_RNN + D_ATT, 2 * D_RNN + 2 * D_ATT], axis=-1)
    xc, new_conv = causal_conv(xa, conv_buf, conv_w, conv_b)
    h, h_last = rglru(xc, h0, w_r, b_r, w_i, b_i, lam_p)
    ya = h * jax.nn.gelu(ga)
    q = q.reshape(bn, L, H_B, 2, DH_B)
    k = k.reshape(bn, L, H_B, DV_B)
    v = v.reshape(bn, L, H_B, DV_B)
    if k_cache is None:
        pos0, k_all, v_all = 0, k, v
    else:
        pos0 = k_cache.shape[1]
        k_all = jnp.concatenate([k_cache.astype(k.dtype), k], axis=1)
        v_all = jnp.concatenate([v_cache.astype(v.dtype), v], axis=1)
    lambda_init = 0.8 - 0.6 * math.exp(-0.3 * layer)
    lam = jnp.exp(jnp.sum(lq1.astype(jnp.float32) * lk1.astype(jnp.float32))) - jnp.exp(jnp.sum(lq2.astype(jnp.float32) * lk2.astype(jnp.float32))) + lambda_init
    yb = diff_attention(q, k_all.reshape(bn, -1, H_B, 2, DH_B), v_all, pos0, lam)
    yb = rms_norm(yb, subln_w) * (1.0 - lambda_init)
    y = jnp.concatenate([ya, yb.reshape(bn, L, D_ATT)], axis=-1) @ w_out
    return y, k, v, new_conv, h_last


def ssd_scan(x, dt, a, bm, cm, s0):
    bn, L = x.shape[:2]
    Q = SSD_CHUNK if L % SSD_CHUNK == 0 else L
    nc = L // Q
    E = H_C // G_C
    xr = x.astype(jnp.float32).reshape(bn, nc, Q, G_C, E, P_C)
    dtr = dt.reshape(bn, nc, Q, G_C, E)
    br = bm.astype(jnp.float32).reshape(bn, nc, Q, G_C, N_C)
    cr = cm.astype(jnp.float32).reshape(bn, nc, Q, G_C, N_C)
    acum = jnp.cumsum(dtr * a.reshape(G_C, E), axis=2)
    causal = jnp.tril(jnp.ones((Q, Q), bool))[:, :, None, None]
    decay = jnp.exp(jnp.where(causal, acum[:, :, :, None] - acum[:, :, None, :], -jnp.inf))
    cb = jnp.einsum('bcign,bcjgn->bcijg', cr, br)
    y_diag = jnp.einsum('bcijge,bcjgep->bcigep', cb[..., None] * decay, dtr[..., None] * xr)
    w_state = jnp.exp(acum[:, :, -1:] - acum) * dtr
    chunk_states = jnp.einsum('bcjgn,bcjgep->bcgepn', br, w_state[..., None] * xr)
    chunk_decay = jnp.exp(acum[:, :, -1])

    def step(s, inp):
        st, dec = inp
        return dec[..., None, None] * s + st, s

    s_init = s0.astype(jnp.float32).reshape(bn, G_C, E, P_C, N_C)
    s_last, s_prev = lax.scan(step, s_init, (chunk_states.swapaxes(0, 1), chunk_decay.swapaxes(0, 1)))
    s_prev = s_prev.swapaxes(0, 1)
    y_off = jnp.einsum('bcign,bcgepn->bcigep', cr, s_prev) * jnp.exp(acum)[..., None]
    y = (y_diag + y_off).reshape(bn, L, H_C, P_C)
    return y, s_last.reshape(bn, H_C, P_C, N_C).astype(x.dtype)


def hgrn2(q, f, i, s0, lb):
    bn, L, _ = q.shape
    Q = HG_CHUNK if L % HG_CHUNK == 0 else L
    nc = L // Q
    g = lb + (1.0 - lb) * jax.nn.sigmoid(f.astype(jnp.float32))
    logg = jnp.log(g)
    kk = 1.0 - g
    qq = jax.nn.silu(q.astype(jnp.float32))

    def blocks(t, d):
        return t.reshape(bn, nc, Q, H_D, d).swapaxes(0, 1)

    qc, kc, ic, bc = blocks(qq, DK_D), blocks(kk, DK_D), blocks(i.astype(jnp.float32), DV_D), jnp.cumsum(blocks(logg, DK_D), axis=2)
    causal = jnp.tril(jnp.ones((Q, Q), bool))[:, :, None, None]

    def step(S, inp):
        qb, kb, ib, bb = inp
        inter = jnp.einsum('bihk,bhkv->bihv', qb * jnp.exp(bb), S)
        rel = jnp.exp(jnp.where(causal, bb[:, :, None] - bb[:, None, :], -jnp.inf))
        att = jnp.einsum('bijhk,bjhk->bhij', qb[:, :, None] * rel, kb)
        intra = jnp.einsum('bhij,bjhv->bihv', att, ib)
        blast = bb[:, -1]
        S_new = jnp.exp(blast)[..., None] * S + jnp.einsum('bjhk,bjhv->bhkv', kb * jnp.exp(blast[:, None] - bb), ib)
        return S_new, inter + intra

    s_last, o = lax.scan(step, s0.astype(jnp.float32), (qc, kc, ic, bc))
    o = o.swapaxes(0, 1).reshape(bn, L, H_D, DV_D)
    return o.astype(q.dtype), s_last.astype(q.dtype)


def odd_mixer(x, layer, conv_buf, s0, hg0, p):
    (w_in, conv_w, conv_b, dt_bias, a_log, d_skip, ssd_norm_w, hg_lb, hg_norm_w, w_out) = p
    bn, L, _ = x.shape
    proj = x @ w_in
    o1 = D_SSD
    o2 = o1 + D_XBC
    o3 = o2 + H_C
    o4 = o3 + D_HG
    o5 = o4 + D_HG
    z, xbc, dt, qd, fd, idn = jnp.split(proj, [o1, o2, o3, o4, o5], axis=-1)
    xbc, new_conv = causal_conv(xbc, conv_buf, conv_w, conv_b)
    xbc = jax.nn.silu(xbc)
    xs = xbc[..., :D_SSD].reshape(bn, L, H_C, P_C)
    bm = xbc[..., D_SSD:D_SSD + G_C * N_C].reshape(bn, L, G_C, N_C)
    cm = xbc[..., D_SSD + G_C * N_C:].reshape(bn, L, G_C, N_C)
    dt = jax.nn.softplus(dt.astype(jnp.float32) + dt_bias.astype(jnp.float32))
    a = -jnp.exp(a_log.astype(jnp.float32))
    ys, s_last = ssd_scan(xs, dt, a, bm, cm, s0)
    ys = ys + d_skip.astype(jnp.float32)[:, None] * xs.astype(jnp.float32)
    ys = (ys.reshape(bn, L, D_SSD) * jax.nn.silu(z.astype(jnp.float32))).astype(x.dtype)
    ys = rms_norm(ys.reshape(bn, L, G_C, D_SSD // G_C), ssd_norm_w.reshape(G_C, D_SSD // G_C)).reshape(bn, L, D_SSD)
    lb_all = jnp.cumsum(jax.nn.softmax(hg_lb.astype(jnp.float32), axis=0), axis=0)
    lb = lb_all[layer] - lb_all[0]
    yh, hg_last = hgrn2(qd, fd, idn, hg0, lb)
    yh = rms_norm(yh, hg_norm_w).reshape(bn, L, D_HG)
    y = jnp.concatenate([ys, yh], axis=-1) @ w_out
    return y, new_conv, s_last, hg_last


def trunk(x, k_cache, v_cache, rg_conv, rg_h, ssd_conv, ssd_s, hg_s, norm_w, even_p, odd_p, ffn_p):
    w_gate, w_up, w_down = ffn_p
    for layer in range(DEPTH):
        h = rms_norm(x, norm_w[layer, 0])
        if layer % 2 == 0:
            m, k_new, v_new, rg_conv_new, rg_h_new = even_mixer(h, layer, k_cache, v_cache, rg_conv, rg_h, even_p)
        else:
            m, ssd_conv_new, ssd_s_new, hg_s_new = odd_mixer(h, layer, ssd_conv, ssd_s, hg_s, odd_p)
        x = x + rms_norm(m, norm_w[layer, 1])
        h = rms_norm(x, norm_w[layer, 2])
        x = x + rms_norm(swiglu(h, w_gate[layer], w_up[layer], w_down[layer]), norm_w[layer, 3])
    return x, k_new, v_new, rg_conv_new, rg_h_new, ssd_conv_new, ssd_s_new, hg_s_new


def setup_inputs(seed: int = 0) -> dict:
    key = jax.random.key(seed)
    ks = iter(jax.random.split(key, 48))

    def nrm(shape, s=1.0):
        return jax.random.normal(next(ks), shape, jnp.float32) * s

    u_a = jax.random.uniform(next(ks), (D_RNN,), jnp.float32, 0.9, 0.999)
    dt0 = jnp.exp(jax.random.uniform(next(ks), (H_C,), jnp.float32, math.log(1e-3), math.log(1e-1)))
    a0 = jax.random.uniform(next(ks), (H_C,), jnp.float32, 1.0, 16.0)
    return {
        'x_prompt': nrm((BATCH, SEQ, D_MODEL)),
        'x_sample': nrm((DEC_BATCH, DEC_SEQ, D_MODEL)),
        'cache_diff_k': nrm((DEC_BATCH, PAST_LEN, H_B, DV_B)),
        'cache_diff_v': nrm((DEC_BATCH, PAST_LEN, H_B, DV_B)),
        'state_rglru_conv': nrm((DEC_BATCH, CONV_W - 1, D_RNN)),
        'state_rglru_h': nrm((DEC_BATCH, D_RNN), 0.5),
        'state_ssd_conv': nrm((DEC_BATCH, CONV_W - 1, D_XBC)),
        'state_ssd': nrm((DEC_BATCH, H_C, P_C, N_C), 0.1),
        'state_hgrn': nrm((DEC_BATCH, H_D, DK_D, DV_D), 0.5),
        'norm_w': 1.0 + nrm((DEPTH, 4, D_MODEL), 0.02),
        'l0_w_in': nrm((D_MODEL, IN_EVEN), D_MODEL ** -0.5),
        'l0_conv_w': nrm((CONV_W, D_RNN), CONV_W ** -0.5),
        'l0_conv_b': nrm((D_RNN,), 0.02),
        'l0_rg_w_r': nrm((H_A, BLK_A, BLK_A), BLK_A ** -0.5),
        'l0_rg_b_r': nrm((D_RNN,), 0.02),
        'l0_rg_w_i': nrm((H_A, BLK_A, BLK_A), BLK_A ** -0.5),
        'l0_rg_b_i': nrm((D_RNN,), 0.02),
        'l0_rg_lambda': jnp.log(u_a) - jnp.log1p(-u_a),
        'l0_lq1': nrm((DH_B,), 0.1),
        'l0_lk1': nrm((DH_B,), 0.1),
        'l0_lq2': nrm((DH_B,), 0.1),
        'l0_lk2': nrm((DH_B,), 0.1),
        'l0_subln_w': 1.0 + nrm((DV_B,), 0.02),
        'l0_w_out': nrm((D_MODEL, D_MODEL), D_MODEL ** -0.5),
        'l1_w_in': nrm((D_MODEL, IN_ODD), D_MODEL ** -0.5),
        'l1_conv_w': nrm((CONV_W, D_XBC), CONV_W ** -0.5),
        'l1_conv_b': nrm((D_XBC,), 0.02),
        'l1_dt_bias': dt0 + jnp.log(-jnp.expm1(-dt0)),
        'l1_a_log': jnp.log(a0),
        'l1_d_skip': 1.0 + nrm((H_C,), 0.02),
        'l1_ssd_norm_w': 1.0 + nrm((D_SSD,), 0.02),
        'l1_hg_lower_bound': nrm((DEPTH, D_HG)),
        'l1_hg_norm_w': 1.0 + nrm((DV_D,), 0.02),
        'l1_w_out': nrm((D_MODEL, D_MODEL), D_MODEL ** -0.5),
        'ffn_w_gate': nrm((DEPTH, D_MODEL, D_FF), D_MODEL ** -0.5),
        'ffn_w_up': nrm((DEPTH, D_MODEL, D_FF), D_MODEL ** -0.5),
        'ffn_w_down': nrm((DEPTH, D_FF, D_MODEL), D_FF ** -0.5),
    }


def reference(x_prompt, x_sample, cache_diff_k, cache_diff_v, state_rglru_conv, state_rglru_h, state_ssd_conv, state_ssd, state_hgrn,
              norm_w, l0_w_in, l0_conv_w, l0_conv_b, l0_rg_w_r, l0_rg_b_r, l0_rg_w_i, l0_rg_b_i, l0_rg_lambda,
              l0_lq1, l0_lk1, l0_lq2, l0_lk2, l0_subln_w, l0_w_out,
              l1_w_in, l1_conv_w, l1_conv_b, l1_dt_bias, l1_a_log, l1_d_skip, l1_ssd_norm_w, l1_hg_lower_bound, l1_hg_norm_w, l1_w_out,
              ffn_w_gate, ffn_w_up, ffn_w_down):
    even_p = (l0_w_in, l0_conv_w, l0_conv_b, l0_rg_w_r, l0_rg_b_r, l0_rg_w_i, l0_rg_b_i, l0_rg_lambda,
              l0_lq1, l0_lk1, l0_lq2, l0_lk2, l0_subln_w, l0_w_out)
    odd_p = (l1_w_in, l1_conv_w, l1_conv_b, l1_dt_bias, l1_a_log, l1_d_skip, l1_ssd_norm_w, l1_hg_lower_bound, l1_hg_norm_w, l1_w_out)
    ffn_p = (ffn_w_gate, ffn_w_up, ffn_w_down)
    bp = x_prompt.shape[0]
    dtp = x_prompt.dtype
    (y_prompt, prompt_k, prompt_v, prompt_rglru_conv, prompt_rglru_h,
     prompt_ssd_conv, prompt_ssd, prompt_hgrn) = trunk(
        x_prompt, None, None,
        jnp.zeros((bp, CONV_W - 1, D_RNN), dtp), jnp.zeros((bp, D_RNN), dtp),
        jnp.zeros((bp, CONV_W - 1, D_XBC), dtp), jnp.zeros((bp, H_C, P_C, N_C), dtp),
        jnp.zeros((bp, H_D, DK_D, DV_D), dtp),
        norm_w, even_p, odd_p, ffn_p)
    (y_sample, sample_k, sample_v, sample_rglru_conv, sample_rglru_h,
     sample_ssd_conv, sample_ssd, sample_hgrn) = trunk(
        x_sample, cache_diff_k, cache_diff_v, state_rglru_conv, state_rglru_h,
        state_ssd_conv, state_ssd, state_hgrn,
        norm_w, even_p, odd_p, ffn_p)
    return (y_prompt, y_sample,
            prompt_k, prompt_v, prompt_rglru_conv, prompt_rglru_h, prompt_ssd_conv, prompt_ssd, prompt_hgrn,
            sample_k, sample_v, sample_rglru_conv, sample_rglru_h, sample_ssd_conv, sample_ssd, sample_hgrn)
```

```python
import numpy as np
import ml_dtypes
from contextlib import ExitStack
import concourse.bass as bass
import concourse.mybir as mybir
from concourse.bass_utils import run_bass_kernel_spmd

F32 = mybir.dt.float32
BF16 = mybir.dt.bfloat16
ALU = mybir.AluOpType
AF = mybir.ActivationFunctionType
AX = mybir.AxisListType

D = 2048
TS = 512
NSAMP = 8
DSEQ = 64
EPS = 1e-6
D_FF = 5632
IN_EVEN = 5120
IN_ODD = 5648
ENGS = ("pe", "act", "dve", "pool", "sp")
EPOCH = 12000
NDMASEM = 40


class Res:
    __slots__ = ("w", "r")

    def __init__(self):
        self.w = None
        self.r = []


class Tile:
    def __init__(self, h, name=""):
        self.h = h
        self.name = name
        self.res = {None: Res()}

    def __getitem__(self, k):
        return self.h[k]

    def get(self, key):
        if key not in self.res:
            self.res[key] = Res()
        return self.res[key]


class Ins:
    __slots__ = ("eng", "fn", "deps", "is_dma", "sig", "sigidx", "dsem", "dval", "dprev", "pos")

    def __init__(self, eng, fn, is_dma):
        self.eng = eng
        self.fn = fn
        self.is_dma = is_dma
        self.deps = set()
        self.sig = False
        self.sigidx = -1
        self.dsem = -1
        self.dval = 0
        self.dprev = 0


def _norm_acc(a):
    if isinstance(a, Tile):
        return a, None
    return a


def _split_psum(reads, writes):
    r2, w2 = [], list(writes)
    for a in reads:
        t, k = _norm_acc(a)
        if getattr(t, "is_psum", False):
            w2.append(t)
        else:
            r2.append(a)
    w3 = []
    for a in w2:
        t, k = _norm_acc(a)
        w3.append(t if getattr(t, "is_psum", False) else a)
    return r2, w3


class Prog:
    def __init__(self, nc):
        self.nc = nc
        self.streams = {e: [] for e in ENGS}
        self.last = {e: None for e in ENGS}
        self.pending = {e: set() for e in ENGS}
        self.ndma = 0
        self.dma_last = [None] * NDMASEM
        self.dma_val = [0] * NDMASEM

    def _collect(self, ins, reads, writes):
        reads, writes = _split_psum(reads, writes)
        deps = ins.deps
        for a in reads:
            t, k = _norm_acc(a)
            rs = list(t.res.values()) if k is None else [t.get(k), t.res[None]]
            for r in rs:
                if r.w is not None:
                    deps.add(r.w)
        for a in writes:
            t, k = _norm_acc(a)
            rs = list(t.res.values()) if k is None else [t.get(k), t.res[None]]
            for r in rs:
                if r.w is not None:
                    if r.w.is_dma or r.w.eng != ins.eng or ins.is_dma:
                        deps.add(r.w)
                for x in r.r:
                    if x.is_dma or x.eng != ins.eng or ins.is_dma:
                        deps.add(x)
        for a in reads:
            t, k = _norm_acc(a)
            r = t.get(k)
            if not ins.is_dma:
                r.r = [x for x in r.r if x.is_dma or x.eng != ins.eng]
            r.r.append(ins)
        for a in writes:
            t, k = _norm_acc(a)
            if k is None:
                for r in t.res.values():
                    r.w = ins
                    r.r = []
            else:
                r = t.get(k)
                r.w = ins
                r.r = []
        deps.discard(ins)
        if self.pending[ins.eng]:
            deps.update(self.pending[ins.eng])
            self.pending[ins.eng] = set()

    def op(self, eng, fn, reads=(), writes=()):
        ins = Ins(eng, fn, False)
        self._collect(ins, reads, writes)
        self.streams[eng].append(ins)
        self.last[eng] = ins
        return ins

    def dma(self, out_ap, in_ap, reads=(), writes=(), q="sp", **kw):
        def fn(e, out_ap=out_ap, in_ap=in_ap, kw=kw):
            return e.dma_start(out=out_ap, in_=in_ap, **kw)
        ins = Ins(q, fn, True)
        self._collect(ins, reads, writes)
        i = self.ndma % NDMASEM
        self.ndma += 1
        ins.dsem = i
        ins.dprev = self.dma_val[i]
        self.dma_val[i] += 16
        ins.dval = self.dma_val[i]
        self.dma_last[i] = ins
        self.streams[q].append(ins)
        self.last[q] = ins
        return ins

    def barrier(self):
        lasts = [x for x in self.last.values() if x is not None]
        dl = [x for x in self.dma_last if x is not None]
        for e in ENGS:
            self.pending[e].update(x for x in lasts if x.eng != e or x.is_dma)
            self.pending[e].update(dl)

    def emit(self, es):
        nc = self.nc
        for e in ENGS:
            for ins in self.streams[e]:
                for d in ins.deps:
                    if not d.is_dma:
                        d.sig = True
        nsig = {}
        for e in ENGS:
            c = 0
            for ins in self.streams[e]:
                if ins.sig and not ins.is_dma:
                    ins.sigidx = c
                    c += 1
            nsig[e] = c
        esems = {e: [es.enter_context(nc.semaphore(f"s_{e}_{j}")) for j in range(nsig[e] // EPOCH + 1)]
                 for e in ENGS}
        dsems = [es.enter_context(nc.semaphore(f"s_dma_{j}")) for j in range(NDMASEM)]
        block = es.enter_context(nc.Block())
        engobj = {"pe": block.tensor, "act": block.scalar, "dve": block.vector, "pool": block.gpsimd,
                  "sp": block.sync}

        def run(eng_name):
            def body(e):
                waited = {}
                for ins in self.streams[eng_name]:
                    waits = {}
                    for d in ins.deps:
                        if d.is_dma:
                            key, val, sem = ("d", d.dsem), d.dval, dsems[d.dsem]
                        else:
                            j = d.sigidx // EPOCH
                            key, val, sem = (d.eng, j), d.sigidx % EPOCH + 1, esems[d.eng][j]
                        if waits.get(key, (0, None))[0] < val:
                            waits[key] = (val, sem)
                    if ins.is_dma and ins.dprev > 0:
                        key = ("d", ins.dsem)
                        if waits.get(key, (0, None))[0] < ins.dprev:
                            waits[key] = (ins.dprev, dsems[ins.dsem])
                    for key, (val, sem) in waits.items():
                        if waited.get(key, 0) < val:
                            e.wait_ge(sem, val)
                            waited[key] = val
                    bi = ins.fn(e)
                    if ins.is_dma:
                        bi.then_inc(dsems[ins.dsem], 16)
                    elif ins.sig:
                        j = ins.sigidx // EPOCH
                        bi.then_inc(esems[eng_name][j], 1)
                if eng_name == "sp":
                    for i in range(NDMASEM):
                        if self.dma_val[i] > 0:
                            e.wait_ge(dsems[i], self.dma_val[i])
            return body

        for en in ENGS:
            engobj[en](run(en))


class Builder:
    def __init__(self, LP, PAST, debug=False):
        self.LP = LP
        self.PAST = PAST
        self.T = LP + NSAMP * DSEQ
        self.NT = self.T // TS
        self.debug = debug
        self.nc = bass.Bass("TRN2", target_bir_lowering=False)
        self.p = Prog(self.nc)
        self.es = ExitStack()
        self.dbg_names = []

    def din(self, name, shape, dt=F32):
        return Tile(self.nc.dram_tensor(name, list(shape), dt, kind="ExternalInput").ap(), name)

    def dout(self, name, shape, dt=F32):
        return Tile(self.nc.dram_tensor(name, list(shape), dt, kind="ExternalOutput").ap(), name)

    def dscr(self, name, shape, dt):
        if self.debug:
            self.dbg_names.append(name)
            return Tile(self.nc.dram_tensor(name, list(shape), dt, kind="ExternalOutput").ap(), name)
        return Tile(self.nc.dram_tensor(name, list(shape), dt).ap(), name)

    def sb(self, st, name, shape, dt):
        self._sbn = getattr(self, "_sbn", 0) + 1
        name = f"{name}_{self._sbn}"
        return Tile(st.enter_context(self.nc.sbuf_tensor(name, list(shape), dt)), name)

    def build(self):
        nc, p, T, NT, LP, PAST = self.nc, self.p, self.T, self.NT, self.LP, self.PAST
        es = self.es
        with es:
            self.x = self.din("x", [T, D])
            self.ck = self.din("ck", [NSAMP, PAST, 1024])
            self.cv = self.din("cv", [NSAMP, PAST, 1024])
            self.st_rg_conv = self.din("st_rg_conv", [NSAMP, 3, 1024])
            self.st_rg_h = self.din("st_rg_h", [NSAMP, 1024])
            self.st_ssd_conv = self.din("st_ssd_conv", [NSAMP, 3, 1536])
            self.st_ssd = self.din("st_ssd", [NSAMP, 16, 64, 128])
            self.st_hg = self.din("st_hg", [NSAMP, 8, 128, 128])
            self.norm_w = self.din("norm_w", [2, 4, D])
            self.w_in0 = self.din("l0_w_in", [D, IN_EVEN])
            self.l0_conv_w = self.din("l0_conv_w", [4, 1024])
            self.l0_conv_b = self.din("l0_conv_b", [1024])
            self.l0_w_r = self.din("l0_rg_w_r", [8, 128, 128])
            self.l0_b_r = self.din("l0_rg_b_r", [1024])
            self.l0_w_i = self.din("l0_rg_w_i", [8, 128, 128])
            self.l0_b_i = self.din("l0_rg_b_i", [1024])
            self.l0_lam = self.din("l0_rg_lambda", [1024])
            self.l0_lq1 = self.din("l0_lq1", [64])
            self.l0_lk1 = self.din("l0_lk1", [64])
            self.l0_lq2 = self.din("l0_lq2", [64])
            self.l0_lk2 = self.din("l0_lk2", [64])
            self.l0_subln = self.din("l0_subln_w", [128])
            self.w_out0 = self.din("l0_w_out", [D, D])
            self.w_in1 = self.din("l1_w_in", [D, IN_ODD])
            self.l1_conv_w = self.din("l1_conv_w", [4, 1536])
            self.l1_conv_b = self.din("l1_conv_b", [1536])
            self.l1_dt_bias = self.din("l1_dt_bias", [16])
            self.l1_a_log = self.din("l1_a_log", [16])
            self.l1_d_skip = self.din("l1_d_skip", [16])
            self.l1_ssd_nw = self.din("l1_ssd_norm_w", [1024])
            self.l1_hg_lb = self.din("l1_hg_lower_bound", [2, 1024])
            self.l1_hg_nw = self.din("l1_hg_norm_w", [128])
            self.w_out1 = self.din("l1_w_out", [D, D])
            self.wg = self.din("ffn_w_gate", [2, D, D_FF])
            self.wu = self.din("ffn_w_up", [2, D, D_FF])
            self.wd = self.din("ffn_w_down", [2, D_FF, D])
            self.c_ident = self.din("c_ident", [128, 128])
            self.c_mask = self.din("c_mask", [128, 4 * 512])
            self.c_m16 = self.din("c_m16", [128, 128])
            self.c_ssd = self.din("c_ssd", [64, 5 * 64 + 128])

            self.y = self.dout("y", [T, D])
            self.k_out = self.dout("k_out", [T, 1024])
            self.v_out = self.dout("v_out", [T, 1024])
            self.rg_conv_out = self.dout("rg_conv_out", [1 + NSAMP, 3, 1024])
            self.rg_h_out = self.dout("rg_h_out", [1 + NSAMP, 1024])
            self.ssd_conv_out = self.dout("ssd_conv_out", [1 + NSAMP, 3, 1536])
            self.ssd_out = self.dout("ssd_out", [1 + NSAMP, 16, 64, 128])
            self.hg_out = self.dout("hg_out", [1 + NSAMP, 8, 128, 128])

            self.ps = [Tile(es.enter_context(nc.psum_tensor(f"ps{i}", [128, 512], F32)), f"ps{i}")
                       for i in range(8)]
            for t_ in self.ps:
                t_.is_psum = True
            self.ident = self.sb(es, "ident", [128, 128], F32)
            p.dma(self.ident[:, :], self.c_ident[:, :], reads=[self.c_ident], writes=[self.ident])
            self.eps_t = self.sb(es, "eps_t", [128, 1], F32)
            p.op("dve", lambda e: e.memset(self.eps_t[:, :], EPS), writes=[self.eps_t])

            import os
            stop = os.environ.get("K_STOP", "")
            self.stage_convert()
            if stop != "convert":
                self.stage_inproj(0)
                self.stage_rglru()
                self.stage_cacheprep()
                self.stage_attn()
                self.stage_outffn(0)
                if stop != "l0":
                    self.stage_inproj(1)
                    self.stage_ssdconv()
                    self.stage_ssd()
                    self.stage_hgrn()
                    self.stage_outffn(1)
            p.barrier()
            p.emit(es)
        return nc

    def conv_w(self, st, name, W, r0c0, K, groups, bufs):
        p = self.p
        KC = K // 128
        outs = [self.dscr(f"{name}_g{gi}", [128, KC, w], BF16) for gi, (c0, w) in enumerate(groups)]
        lo = min(c0 for c0, w in groups)
        hi = max(c0 + w for c0, w in groups)
        for kc in range(KC):
            f32t, bft = bufs[kc % 2]
            src = r0c0(kc)
            p.dma(f32t[:, lo:hi], src[:, lo:hi], reads=[W], writes=[f32t])
            eng = ("dve", "pool", "act")[kc % 3]
            if eng == "act":
                p.op("act", lambda e, a=bft[:, lo:hi], b=f32t[:, lo:hi]: e.copy(out=a, in_=b),
                     reads=[f32t], writes=[bft])
            else:
                p.op(eng, lambda e, a=bft[:, lo:hi], b=f32t[:, lo:hi]: e.tensor_copy(out=a, in_=b),
                     reads=[f32t], writes=[bft])
            for gi, (c0, w) in enumerate(groups):
                p.dma(outs[gi][:, kc, :], bft[:, c0:c0 + w], reads=[bft], writes=[(outs[gi], kc)])
        return outs

    def stage_convert(self):
        p = self.p
        with ExitStack() as st:
            bufs = [(self.sb(st, f"cvf{i}", [128, IN_ODD], F32), self.sb(st, f"cvb{i}", [128, IN_ODD], BF16))
                    for i in range(2)]
            g512 = lambda n: [(i * 512, 512) for i in range(n // 512)]
            self.Wt_in0 = self.conv_w(st, "wt_in0", self.w_in0, lambda kc: self.w_in0[kc * 128:(kc + 1) * 128, :],
                                      D, g512(5120), bufs)
            self.Wt_out0 = self.conv_w(st, "wt_out0", self.w_out0,
                                       lambda kc: self.w_out0[kc * 128:(kc + 1) * 128, :], D, g512(2048), bufs)
            self.Wt_out1 = self.conv_w(st, "wt_out1", self.w_out1,
                                       lambda kc: self.w_out1[kc * 128:(kc + 1) * 128, :], D, g512(2048), bufs)
            g1 = g512(2560) + [(2560, 16)] + [(2576 + i * 512, 512) for i in range(6)]
            self.Wt_in1 = self.conv_w(st, "wt_in1", self.w_in1, lambda kc: self.w_in1[kc * 128:(kc + 1) * 128, :],
                                      D, g1, bufs)
            self.Wt_g, self.Wt_u, self.Wt_d = [], [], []
            for L in range(2):
                self.Wt_g.append(self.conv_w(st, f"wt_g{L}", self.wg,
                                             lambda kc, L=L: self.wg[L, kc * 128:(kc + 1) * 128, :], D,
                                             [(i * 256, 256) for i in range(22)], bufs))
                self.Wt_u.append(self.conv_w(st, f"wt_u{L}", self.wu,
                                             lambda kc, L=L: self.wu[L, kc * 128:(kc + 1) * 128, :], D,
                                             [(i * 256, 256) for i in range(22)], bufs))
                self.Wt_d.append(self.conv_w(st, f"wt_d{L}", self.wd,
                                             lambda kc, L=L: self.wd[L, kc * 128:(kc + 1) * 128, :], D_FF,
                                             g512(2048), bufs))
            p.barrier()

    def norm_hT(self, xt, wB, hT, t, tmp, h32, ss, rstd, tb, x_ap=None, x_key=None):
        p = self.p
        if x_ap is None:
            x_ap = xt[:, :]
        xr = xt if x_key is None else (xt, x_key)
        self.rms_stats(x_ap, xr, tmp, ss, rstd, D)
        p.op("dve", lambda e: e.scalar_tensor_tensor(out=h32[:, :], in0=x_ap, scalar=rstd[:, 0:1], in1=wB[:, :],
                                                     op0=ALU.mult, op1=ALU.mult),
             reads=[xr, rstd, wB], writes=[h32])
        for kq in range(4):
            bank = tb[kq % 2]
            for j in range(4):
                k = kq * 4 + j
                p.op("pe", lambda e, bank=bank, j=j, k=k: e.transpose(out=bank[:, j * 128:(j + 1) * 128],
                                                                     in_=h32[:, k * 128:(k + 1) * 128],
                                                                     identity=self.ident[:, :]),
                     reads=[h32, self.ident], writes=[bank])
            dst = hT[:, kq * 4:(kq + 1) * 4, t * 128:(t + 1) * 128]
            srcv = bank[:, :].rearrange("p (a b) -> p a b", a=4)
            if kq % 2 == 0:
                p.op("act", lambda e, dst=dst, srcv=srcv: e.copy(out=dst, in_=srcv), reads=[bank],
                     writes=[(hT, (kq, t))])
            else:
                p.op("dve", lambda e, dst=dst, srcv=srcv: e.tensor_copy(out=dst, in_=srcv), reads=[bank],
                     writes=[(hT, (kq, t))])

    def rms_stats(self, x_ap, x_tile, tmp, ss, rstd, n, width=None):
        p = self.p
        w = n if width is None else width
        p.op("dve", lambda e: e.memset(ss[:, 0:1], 0.0), writes=[ss])
        p.op("act", lambda e: e.activation(out=tmp[:, 0:w], in_=x_ap, func=AF.Square, accum_out=ss[:, 0:1]),
             reads=[x_tile, ss], writes=[tmp, ss])
        p.op("act", lambda e: e.activation(out=rstd[:, 0:1], in_=ss[:, 0:1], func=AF.Sqrt, scale=1.0 / n,
                                           bias=self.eps_t[:, 0:1]),
             reads=[ss, self.eps_t], writes=[rstd])
        p.op("dve", lambda e: e.reciprocal(out=rstd[:, 0:1], in_=rstd[:, 0:1]), reads=[rstd], writes=[rstd])

    def load_wB(self, st, name, L, i):
        wB = self.sb(st, name, [128, D], F32)
        self.p.dma(wB[:, :], self.norm_w[L, i:i + 1, :].to_broadcast([128, D]), reads=[self.norm_w], writes=[wB])
        return wB

    def stage_inproj(self, L):
        p, T, NT = self.p, self.T, self.NT
        src = self.x if L == 0 else self.X
        if L == 0:
            self.XA = self.dscr("XA", [1024, T], F32)
            self.GA = self.dscr("GA", [1024, T], F32)
            self.QT = self.dscr("QT", [1024, T], BF16)
            self.KT = self.dscr("KT", [1024, T], BF16)
            self.Vb = self.dscr("Vb", [T, 1024], BF16)
            Wt = self.Wt_in0
            A = [(Wt[0], 512, self.XA, 0, F32), (Wt[1], 512, self.XA, 512, F32),
                 (Wt[2], 512, self.GA, 0, F32), (Wt[3], 512, self.GA, 512, F32),
                 (Wt[4], 512, self.QT, 0, BF16), (Wt[5], 512, self.QT, 512, BF16),
                 (Wt[6], 512, self.KT, 0, BF16), (Wt[7], 512, self.KT, 512, BF16)]
            B = [(Wt[6], 512, [(self.k_out, 0, F32)]), (Wt[7], 512, [(self.k_out, 512, F32)]),
                 (Wt[8], 512, [(self.v_out, 0, F32), (self.Vb, 0, BF16)]),
                 (Wt[9], 512, [(self.v_out, 512, F32), (self.Vb, 512, BF16)])]
        else:
            self.XBC = self.dscr("XBC", [1536, T], F32)
            self.HQ = self.dscr("HQ", [1024, T], F32)
            self.HF = self.dscr("HF", [1024, T], F32)
            self.HI = self.dscr("HI", [1024, T], F32)
            self.Z = self.dscr("Z", [T, 1024], F32)
            self.DT = self.dscr("DT", [T, 16], F32)
            Wt = self.Wt_in1
            A = [(Wt[2], 512, self.XBC, 0, F32), (Wt[3], 512, self.XBC, 512, F32), (Wt[4], 512, self.XBC, 1024, F32),
                 (Wt[6], 512, self.HQ, 0, F32), (Wt[7], 512, self.HQ, 512, F32),
                 (Wt[8], 512, self.HF, 0, F32), (Wt[9], 512, self.HF, 512, F32),
                 (Wt[10], 512, self.HI, 0, F32), (Wt[11], 512, self.HI, 512, F32)]
            B = [(Wt[0], 512, [(self.Z, 0, F32)]), (Wt[1], 512, [(self.Z, 512, F32)]),
                 (Wt[5], 16, [(self.DT, 0, F32)])]
        with ExitStack() as st:
            wB = self.load_wB(st, "wB", L, 0)
            xts = [self.sb(st, f"xt{i}", [128, D], F32) for i in range(2)]
            tmp = self.sb(st, "tmp", [128, D], BF16)
            h32s = [self.sb(st, f"h32_{i}", [128, D], F32) for i in range(4)]
            ss = self.sb(st, "ss", [128, 1], F32)
            rstd = self.sb(st, "rstd", [128, 1], F32)
            hTs = [self.sb(st, f"hT{i}", [128, 16, TS], BF16) for i in range(2)]
            wts = [self.sb(st, f"wt{i}", [128, 16, 512], BF16) for i in range(3)]
            sgf = [self.sb(st, f"sgf{i}", [128, 512], F32) for i in range(3)]
            sgb = [self.sb(st, f"sgb{i}", [128, 512], BF16) for i in range(3)]
            accb = self.ps[2:8]
            cnt = {"si": 0, "bi": 0, "x": 0}
            jobs = [("A",) + a for a in A] + [("B",) + b for b in B]
            nj = len(jobs)
            loads = [(ti, j) for ti in range(NT) for j in range(nj)]
            nload = [0]

            def ensure_loaded(idx):
                while nload[0] <= min(idx + 2, len(loads) - 1):
                    k = nload[0]
                    W, width = jobs[loads[k][1]][1], jobs[loads[k][1]][2]
                    wt = wts[k % 3]
                    p.dma(wt[:, :, 0:width], W[:, :, :], reads=[W], writes=[wt])
                    nload[0] += 1
                return wts[idx % 3]

            def norm1(ti, t):
                xt = xts[cnt["x"] % 2]
                cnt["x"] += 1
                r0 = ti * TS + t * 128
                p.dma(xt[:, :], src[r0:r0 + 128, :], reads=[(src, ti)], writes=[xt])
                self.rms_stats(xt[:, :], xt, tmp, ss, rstd, D)
                h32 = h32s[t]
                p.op("dve", lambda e, h32=h32, xt=xt: e.scalar_tensor_tensor(out=h32[:, :], in0=xt[:, :], scalar=rstd[:, 0:1], in1=wB[:, :],
                                                                         op0=ALU.mult, op1=ALU.mult), reads=[xt, rstd, wB], writes=[h32])

            def norm2(ti, t):
                h32, hT = h32s[t], hTs[ti % 2]
                for kq in range(4):
                    bank = self.ps[kq % 2]
                    for j in range(4):
                        k = kq * 4 + j
                        p.op("pe", lambda e, bank=bank, j=j, k=k, h32=h32: e.transpose(out=bank[:, j * 128:(j + 1) * 128], in_=h32[:, k * 128:(k + 1) * 128],
                                                                                   identity=self.ident[:, :]), reads=[h32, self.ident], writes=[bank])
                    dstv = hT[:, kq * 4:(kq + 1) * 4, t * 128:(t + 1) * 128]
                    srcv = bank[:, :].rearrange("p (a b) -> p a b", a=4)
                    if kq % 2 == 0:
                        p.op("act", lambda e, dstv=dstv, srcv=srcv: e.copy(out=dstv, in_=srcv), reads=[bank], writes=[(hT, (kq, t))])
                    else:
                        p.op("dve", lambda e, dstv=dstv, srcv=srcv: e.tensor_copy(out=dstv, in_=srcv), reads=[bank], writes=[(hT, (kq, t))])

            for t in range(4):
                norm1(0, t)
            for t in range(4):
                norm2(0, t)
            for ti in range(NT):
                hT = hTs[ti % 2]
                for j, job in enumerate(jobs):
                    wt = ensure_loaded(ti * nj + j)
                    if ti + 1 < NT:
                        t1_ = j - (nj - 6)
                        if 0 <= t1_ < 4:
                            norm1(ti + 1, t1_)
                        t2_ = j - (nj - 5)
                        if 0 <= t2_ < 4:
                            norm2(ti + 1, t2_)
                    if job[0] == "A":
                        _, W, width, dst, row0, dt = job
                        for n in range(width // 128):
                            bank = accb[cnt["bi"] % 6]
                            cnt["bi"] += 1
                            for k in range(16):
                                p.op("pe", lambda e, bank=bank, wt=wt, hT=hT, k=k, n=n: e.matmul(
                                    bank[:, :], lhsT=wt[:, k, n * 128:(n + 1) * 128], rhs=hT[:, k, :],
                                    start=(k == 0), stop=(k == 15)), reads=[wt, hT], writes=[bank])
                            sg = (sgf if dt == F32 else sgb)[cnt["si"] % 3]
                            cnt["si"] += 1
                            self.evac(bank[:, :], sg[:, :], bank, sg, cnt["si"])
                            rr = row0 + n * 128
                            p.dma(dst[rr:rr + 128, ti * TS:(ti + 1) * TS], sg[:, :], reads=[sg], writes=[(dst, ti)])
                    else:
                        _, W, width, dsts = job
                        for t in range(4):
                            bank = accb[cnt["bi"] % 6]
                            cnt["bi"] += 1
                            for k in range(16):
                                p.op("pe", lambda e, bank=bank, wt=wt, hT=hT, k=k, t=t, width=width: e.matmul(
                                    bank[:, 0:width], lhsT=hT[:, k, t * 128:(t + 1) * 128], rhs=wt[:, k, 0:width],
                                    start=(k == 0), stop=(k == 15)), reads=[wt, hT], writes=[bank])
                            r0 = ti * TS + t * 128
                            sg0 = sgf[cnt["si"] % 3]
                            cnt["si"] += 1
                            self.evac(bank[:, 0:width], sg0[:, 0:width], bank, sg0, cnt["si"])
                            for (dst, c0, dt) in dsts:
                                if dt == F32:
                                    sg = sg0
                                else:
                                    sg = sgb[cnt["si"] % 3]
                                    cnt["si"] += 1
                                    p.op("pool", lambda e, a=sg[:, 0:width], b=sg0[:, 0:width]: e.tensor_copy(out=a, in_=b),
                                         reads=[sg0], writes=[sg])
                                p.dma(dst[r0:r0 + 128, c0:c0 + width], sg[:, 0:width], reads=[sg], writes=[(dst, ti)])
            p.barrier()

    def evac(self, src_ap, dst_ap, src_t, dst_t, i):
        if i % 2 == 0:
            self.p.op("act", lambda e: e.copy(out=dst_ap, in_=src_ap), reads=[src_t], writes=[dst_t])
        else:
            self.p.op("dve", lambda e: e.tensor_copy(out=dst_ap, in_=src_ap), reads=[src_t], writes=[dst_t])

    def stage_outffn(self, L):
        p, T, NT = self.p, self.T, self.NT
        src = self.x if L == 0 else self.X
        if L == 0:
            self.X = self.dscr("X", [T, D], F32)
        dst = self.X if L == 0 else self.y
        Wo = self.Wt_out0 if L == 0 else self.Wt_out1
        Wg, Wu, Wd = self.Wt_g[L], self.Wt_u[L], self.Wt_d[L]
        YT = self.YT
        with ExitStack() as st:
            wB = self.sb(st, "wBo", [128, D], F32)
            yT = self.sb(st, "yT", [128, 16, TS], BF16)
            wbufs = [self.sb(st, f"wb{i}", [128, 16, 512], BF16) for i in range(3)]
            m32 = self.sb(st, "m32", [128, 4, D], F32)
            d32 = self.sb(st, "d32", [128, 4, D], F32)
            xres = self.sb(st, "xres", [128, D], F32)
            actT = self.sb(st, "actT", [128, 44, TS], BF16)
            tmp = self.sb(st, "tmpo", [128, D], BF16)
            h32 = self.sb(st, "h32o", [128, D], F32)
            gsb = [self.sb(st, f"gsb{i}", [128, TS], F32) for i in range(2)]
            ss = self.sb(st, "sso", [128, 1], F32)
            rstd = self.sb(st, "rstdo", [128, 1], F32)
            wi = 0
            ei = 0
            gi_ = 0

            def load_wB(i):
                p.dma(wB[:, :], self.norm_w[L, i:i + 1, :].to_broadcast([128, D]), reads=[self.norm_w], writes=[wB])

            def postnorm_residual(buf, t, res_ap, res_reads, out_ap, out_writes):
                self.rms_stats(buf[:, t, :], (buf, t), tmp, ss, rstd, D)
                p.op("dve", lambda e: e.scalar_tensor_tensor(out=h32[:, :], in0=buf[:, t, :], scalar=rstd[:, 0:1],
                                                             in1=wB[:, :], op0=ALU.mult, op1=ALU.mult),
                     reads=[(buf, t), rstd, wB], writes=[h32])
                p.op("pool", lambda e: e.tensor_tensor(out=out_ap, in0=h32[:, :], in1=res_ap, op=ALU.add),
                     reads=[h32] + res_reads, writes=out_writes)

            for ti in range(NT):
                c0 = ti * TS
                p.dma(yT[:, :, :], YT[:, c0:c0 + TS].rearrange("(k q) t -> q k t", q=128), reads=[(YT, ti)], writes=[yT])
                for g in range(4):
                    wt = wbufs[wi % 3]
                    wi += 1
                    p.dma(wt[:, :, :], Wo[g][:, :, :], reads=[Wo[g]], writes=[wt])
                    for t in range(4):
                        bank = self.ps[(g % 2) * 4 + t]
                        for k in range(16):
                            p.op("pe", lambda e, bank=bank, wt=wt, k=k, t=t: e.matmul(
                                bank[:, :], lhsT=yT[:, k, t * 128:(t + 1) * 128], rhs=wt[:, k, :],
                                start=(k == 0), stop=(k == 15)), reads=[wt, yT], writes=[bank])
                        ei += 1
                        self.evac(bank[:, :], m32[:, t, g * 512:(g + 1) * 512], bank, (m32, t), ei)
                load_wB(1)
                for t in range(4):
                    r0 = c0 + t * 128
                    p.dma(xres[:, :], src[r0:r0 + 128, :], reads=[(src, ti)], writes=[xres])
                    postnorm_residual(m32, t, xres[:, :], [xres], m32[:, t, :], [(m32, t)])
                load_wB(2)
                for t in range(4):
                    self.norm_hT(m32, wB, yT, t, tmp, h32, ss, rstd, self.ps[6:8], x_ap=m32[:, t, :], x_key=t)
                hT = yT
                for j in range(22):
                    wt = wbufs[wi % 3]
                    wi += 1
                    p.dma(wt[:, :, 0:256], Wg[j][:, :, :], reads=[Wg[j]], writes=[(wt, 0)])
                    p.dma(wt[:, :, 256:512], Wu[j][:, :, :], reads=[Wu[j]], writes=[(wt, 1)])
                    for n in range(2):
                        f = j * 2 + n
                        bg = self.ps[(f % 3) * 2]
                        bu = self.ps[(f % 3) * 2 + 1]
                        for k in range(16):
                            p.op("pe", lambda e, bg=bg, wt=wt, k=k, n=n: e.matmul(
                                bg[:, :], lhsT=wt[:, k, n * 128:(n + 1) * 128], rhs=hT[:, k, :],
                                start=(k == 0), stop=(k == 15)), reads=[(wt, 0), hT], writes=[bg])
                        for k in range(16):
                            p.op("pe", lambda e, bu=bu, wt=wt, k=k, n=n: e.matmul(
                                bu[:, :], lhsT=wt[:, k, 256 + n * 128:256 + (n + 1) * 128], rhs=hT[:, k, :],
                                start=(k == 0), stop=(k == 15)), reads=[(wt, 1), hT], writes=[bu])
                        gs = gsb[gi_ % 2]
                        gi_ += 1
                        p.op("act", lambda e, gs=gs, bg=bg: e.activation(out=gs[:, :], in_=bg[:, :], func=AF.Silu),
                             reads=[bg], writes=[gs])
                        p.op("dve", lambda e, gs=gs, bu=bu, f=f: e.tensor_tensor(out=actT[:, f, :], in0=gs[:, :],
                                                                                in1=bu[:, :], op=ALU.mult),
                             reads=[gs, bu], writes=[(actT, f)])
                for g in range(4):
                    for part in range(4):
                        wt = wbufs[wi % 3]
                        wi += 1
                        p.dma(wt[:, 0:11, :], Wd[g][:, part * 11:(part + 1) * 11, :], reads=[Wd[g]], writes=[wt])
                        for fl in range(11):
                            f = part * 11 + fl
                            for t in range(4):
                                bank = self.ps[(g % 2) * 4 + t]
                                p.op("pe", lambda e, bank=bank, wt=wt, f=f, fl=fl, t=t: e.matmul(
                                    bank[:, :], lhsT=actT[:, f, t * 128:(t + 1) * 128], rhs=wt[:, fl, :],
                                    start=(f == 0), stop=(f == 43)), reads=[wt, (actT, f)], writes=[bank])
                    for t in range(4):
                        bank = self.ps[(g % 2) * 4 + t]
                        ei += 1
                        self.evac(bank[:, :], d32[:, t, g * 512:(g + 1) * 512], bank, (d32, t), ei)
                load_wB(3)
                for t in range(4):
                    r0 = c0 + t * 128
                    postnorm_residual(d32, t, m32[:, t, :], [(m32, t)], d32[:, t, :], [(d32, t)])
                    p.dma(dst[r0:r0 + 128, :], d32[:, t, :], reads=[(d32, t)], writes=[(dst, ti)])
            p.barrier()

    def small_T(self, dst_ap, src_ap, reads, writes):
        self.p.dma(dst_ap, src_ap, reads=reads, writes=writes, allow_slow_non_contiguous=True)

    def stage_rglru(self):
        p, T, NT, LP = self.p, self.T, self.NT, self.LP
        if not hasattr(self, "YT"):
            self.YT = self.dscr("YT", [D, T], BF16)
        with ExitStack() as st:
            S = lambda n, shp, dt=F32: self.sb(st, n, shp, dt)
            cw = S("rg_cw", [128, 8, 4]); cb = S("rg_cb", [128, 8]); br = S("rg_br", [128, 8]); bi_ = S("rg_bi", [128, 8])
            lam = S("rg_lam", [128, 8]); cc = S("rg_c", [128, 8]); cc2 = S("rg_c2", [128, 8]); one = S("rg_one", [128, 1])
            wr32 = S("rg_wr32", [128, 8, 128]); wi32 = S("rg_wi32", [128, 8, 128])
            wrb = S("rg_wrb", [128, 8, 128], BF16); wib = S("rg_wib", [128, 8, 128], BF16)
            hst = S("rg_hst", [128, 8]); h0s = S("rg_h0s", [128, 8, NSAMP])
            for tap in range(4):
                self.small_T(cw[:, :, tap], self.l0_conv_w[tap, :].rearrange("(b c) -> c b", c=128), [self.l0_conv_w], [cw])
            for (dst, srcv) in ((cb, self.l0_conv_b), (br, self.l0_b_r), (bi_, self.l0_b_i), (lam, self.l0_lam)):
                self.small_T(dst[:, :], srcv[:].rearrange("(b c) -> c b", c=128), [srcv], [dst])
            p.dma(wr32[:, :, :], self.l0_w_r[:, :, :].rearrange("b i j -> i b j"), reads=[self.l0_w_r], writes=[wr32])
            p.dma(wi32[:, :, :], self.l0_w_i[:, :, :].rearrange("b i j -> i b j"), reads=[self.l0_w_i], writes=[wi32])
            for b in range(8):
                self.small_T(h0s[:, b, :], self.st_rg_h[:, b * 128:(b + 1) * 128].rearrange("s c -> c s"), [self.st_rg_h], [h0s])
            p.op("dve", lambda e: e.tensor_copy(out=wrb[:, :, :], in_=wr32[:, :, :]), reads=[wr32], writes=[wrb])
            p.op("dve", lambda e: e.tensor_copy(out=wib[:, :, :], in_=wi32[:, :, :]), reads=[wi32], writes=[wib])
            p.op("dve", lambda e: e.memset(one[:, :], 1.0), writes=[one])
            p.op("dve", lambda e: e.memset(hst[:, :], 0.0), writes=[hst])
            p.op("act", lambda e: e.activation(out=cc[:, :], in_=lam[:, :], func=AF.Exp, scale=-1.0), reads=[lam], writes=[cc])
            p.op("act", lambda e: e.activation(out=cc[:, :], in_=cc[:, :], func=AF.Ln, bias=one[:, 0:1]), reads=[cc, one], writes=[cc])
            p.op("dve", lambda e: e.tensor_scalar(out=cc[:, :], in0=cc[:, :], scalar1=-8.0, scalar2=None, op0=ALU.mult), reads=[cc], writes=[cc])
            p.op("dve", lambda e: e.tensor_scalar(out=cc2[:, :], in0=cc[:, :], scalar1=2.0, scalar2=None, op0=ALU.mult), reads=[cc], writes=[cc2])
            NB = 2
            xa = [S(f"rg_xa{i}", [128, 536]) for i in range(NB)]
            ga = [S(f"rg_ga{i}", [128, TS]) for i in range(NB)]
            xc = [S(f"rg_xc{i}", [128, TS]) for i in range(NB)]
            xcb = [S(f"rg_xcb{i}", [128, TS], BF16) for i in range(NB)]
            rr = [S(f"rg_r{i}", [128, TS]) for i in range(NB)]
            gg = [S(f"rg_g{i}", [128, TS]) for i in range(NB)]
            aa = [S(f"rg_a{i}", [128, TS]) for i in range(NB)]
            uu = [S(f"rg_u{i}", [128, TS]) for i in range(NB)]
            hh = [S(f"rg_h{i}", [128, TS]) for i in range(NB)]
            t1 = [S(f"rg_t1{i}", [128, TS]) for i in range(NB)]
            yo = [S(f"rg_yo{i}", [128, TS], BF16) for i in range(NB)]
            it = 0
            for ti in range(NT):
                samp = ti == NT - 1
                nseg, sl = (NSAMP, DSEQ) if samp else (1, TS)
                c0 = ti * TS
                for b in range(8):
                    i = it % NB
                    it += 1
                    X, G, XC, XCB, R_, GI, A_, U_, H_, T1, YO = xa[i], ga[i], xc[i], xcb[i], rr[i], gg[i], aa[i], uu[i], hh[i], t1[i], yo[i]
                    rows = slice(b * 128, (b + 1) * 128)
                    xv = X[:, 0:nseg * (3 + sl)].rearrange("p (s l) -> p s l", s=nseg)
                    if samp:
                        for sg in range(NSAMP):
                            self.small_T(xv[:, sg, 0:3], self.st_rg_conv[sg, :, rows].rearrange("t c -> c t"),
                                         [self.st_rg_conv], [X])
                    elif ti == 0:
                        p.op("dve", lambda e, xv=xv: e.memset(xv[:, :, 0:3], 0.0), writes=[X])
                    else:
                        p.dma(xv[:, 0, 0:3], self.XA[rows, c0 - 3:c0], reads=[(self.XA, ti - 1)], writes=[X])
                    p.dma(xv[:, :, 3:3 + sl], self.XA[rows, c0:c0 + TS].rearrange("p (s l) -> p s l", s=nseg),
                          reads=[(self.XA, ti)], writes=[X])
                    p.dma(G[:, :], self.GA[rows, c0:c0 + TS], reads=[(self.GA, ti)], writes=[G])
                    v3 = lambda tl: tl[:, :].rearrange("p (s l) -> p s l", s=nseg)
                    xc3 = v3(XC)
                    p.op("dve", lambda e, xc3=xc3, xv=xv, b=b, sl=sl: e.tensor_scalar(
                        out=xc3, in0=xv[:, :, 0:sl], scalar1=cw[:, b, 0:1], scalar2=cb[:, b:b + 1], op0=ALU.mult, op1=ALU.add),
                        reads=[X, cw, cb], writes=[XC])
                    for tap in range(1, 4):
                        p.op("dve", lambda e, xc3=xc3, xv=xv, b=b, sl=sl, tap=tap: e.scalar_tensor_tensor(
                            out=xc3, in0=xv[:, :, tap:tap + sl], scalar=cw[:, b, tap:tap + 1], in1=xc3, op0=ALU.mult, op1=ALU.add),
                            reads=[X, cw, XC], writes=[XC])
                    p.op("act", lambda e, XCB=XCB, XC=XC: e.copy(out=XCB[:, :], in_=XC[:, :]), reads=[XC], writes=[XCB])
                    bk_r = self.ps[(2 * (it % 4))]
                    bk_i = self.ps[(2 * (it % 4)) + 1]
                    p.op("pe", lambda e, bk_r=bk_r, XCB=XCB, b=b: e.matmul(bk_r[:, :], lhsT=wrb[:, b, :], rhs=XCB[:, :], start=True, stop=True),
                         reads=[wrb, XCB], writes=[bk_r])
                    p.op("pe", lambda e, bk_i=bk_i, XCB=XCB, b=b: e.matmul(bk_i[:, :], lhsT=wib[:, b, :], rhs=XCB[:, :], start=True, stop=True),
                         reads=[wib, XCB], writes=[bk_i])
                    p.op("act", lambda e, R_=R_, bk_r=bk_r, b=b: e.activation(out=R_[:, :], in_=bk_r[:, :], func=AF.Sigmoid, bias=br[:, b:b + 1]),
                         reads=[bk_r, br], writes=[R_])
                    p.op("act", lambda e, GI=GI, bk_i=bk_i, b=b: e.activation(out=GI[:, :], in_=bk_i[:, :], func=AF.Sigmoid, bias=bi_[:, b:b + 1]),
                         reads=[bk_i, bi_], writes=[GI])
                    p.op("act", lambda e, A_=A_, R_=R_, b=b: e.activation(out=A_[:, :], in_=R_[:, :], func=AF.Exp, scale=cc[:, b:b + 1]),
                         reads=[R_, cc], writes=[A_])
                    p.op("act", lambda e, T1=T1, R_=R_, b=b: e.activation(out=T1[:, :], in_=R_[:, :], func=AF.Exp, scale=cc2[:, b:b + 1]),
                         reads=[R_, cc2], writes=[T1])
                    p.op("act", lambda e, T1=T1: e.activation(out=T1[:, :], in_=T1[:, :], func=AF.Sqrt, scale=-1.0, bias=one[:, 0:1]),
                         reads=[T1, one], writes=[T1])
                    p.op("dve", lambda e, U_=U_, T1=T1, GI=GI: e.tensor_tensor(out=U_[:, :], in0=T1[:, :], in1=GI[:, :], op=ALU.mult),
                         reads=[T1, GI], writes=[U_])
                    p.op("dve", lambda e, U_=U_, XC=XC: e.tensor_tensor(out=U_[:, :], in0=U_[:, :], in1=XC[:, :], op=ALU.mult),
                         reads=[U_, XC], writes=[U_])
                    a3, u3, h3 = v3(A_), v3(U_), v3(H_)
                    for sg in range(nseg):
                        init = h0s[:, b, sg:sg + 1] if samp else hst[:, b:b + 1]
                        p.op("dve", lambda e, h3=h3, a3=a3, u3=u3, sg=sg, init=init: e.tensor_tensor_scan(
                            out=h3[:, sg, :], data0=a3[:, sg, :], data1=u3[:, sg, :], initial=init, op0=ALU.mult, op1=ALU.add),
                            reads=[A_, U_, hst, h0s], writes=[H_])
                    if not samp:
                        p.op("dve", lambda e, H_=H_, b=b: e.tensor_copy(out=hst[:, b:b + 1], in_=H_[:, TS - 1:TS]), reads=[H_], writes=[hst])
                    if samp:
                        self.small_T(self.rg_h_out[1:1 + NSAMP, rows].rearrange("s c -> c s"), h3[:, :, sl - 1],
                                     [H_], [(self.rg_h_out, ("s", b))])
                        for sg in range(NSAMP):
                            self.small_T(self.rg_conv_out[1 + sg, :, rows].rearrange("t c -> c t"), xv[:, sg, sl:sl + 3],
                                         [X], [(self.rg_conv_out, ("s", b, sg))])
                    elif ti == NT - 2:
                        self.small_T(self.rg_h_out[0:1, rows].rearrange("s c -> c s"), H_[:, TS - 1:TS],
                                     [H_], [(self.rg_h_out, ("p", b))])
                        self.small_T(self.rg_conv_out[0, :, rows].rearrange("t c -> c t"), X[:, TS:TS + 3],
                                     [X], [(self.rg_conv_out, ("p", b))])
                    p.op("act", lambda e, T1=T1, G=G: e.activation(out=T1[:, :], in_=G[:, :], func=AF.Square), reads=[G], writes=[T1])
                    p.op("dve", lambda e, T1=T1: e.tensor_scalar(out=T1[:, :], in0=T1[:, :], scalar1=0.044715, scalar2=1.0, op0=ALU.mult, op1=ALU.add),
                         reads=[T1], writes=[T1])
                    p.op("dve", lambda e, T1=T1, G=G: e.tensor_tensor(out=T1[:, :], in0=T1[:, :], in1=G[:, :], op=ALU.mult), reads=[T1, G], writes=[T1])
                    p.op("act", lambda e, T1=T1: e.activation(out=T1[:, :], in_=T1[:, :], func=AF.Sigmoid, scale=1.5957691216057308),
                         reads=[T1], writes=[T1])
                    p.op("dve", lambda e, T1=T1, G=G: e.tensor_tensor(out=T1[:, :], in0=T1[:, :], in1=G[:, :], op=ALU.mult), reads=[T1, G], writes=[T1])
                    p.op("dve", lambda e, T1=T1, H_=H_, YO=YO: e.tensor_tensor(out=YO[:, :], in0=T1[:, :], in1=H_[:, :], op=ALU.mult),
                         reads=[T1, H_], writes=[YO])
                    p.dma(self.YT[rows, c0:c0 + TS], YO[:, :], reads=[YO], writes=[(self.YT, ti)])
            p.barrier()

    def stage_cacheprep(self):
        p, PAST = self.p, self.PAST
        self.KcT = self.dscr("KcT", [NSAMP, 1024, PAST], BF16)
        self.Vcb = self.dscr("Vcb", [NSAMP, PAST, 1024], BF16)
        with ExitStack() as st:
            kf = [self.sb(st, f"cp_kf{i}", [128, 1024], F32) for i in range(2)]
            vf = [self.sb(st, f"cp_vf{i}", [128, 1024], F32) for i in range(2)]
            vb = [self.sb(st, f"cp_vb{i}", [128, 1024], BF16) for i in range(2)]
            kt = [self.sb(st, f"cp_kt{i}", [128, 8, 128], BF16) for i in range(2)]
            it = 0
            for sg in range(NSAMP):
                for kb in range(PAST // 128):
                    i = it % 2
                    it += 1
                    r = slice(kb * 128, (kb + 1) * 128)
                    p.dma(kf[i][:, :], self.ck[sg, r, :], reads=[self.ck], writes=[kf[i]])
                    p.dma(vf[i][:, :], self.cv[sg, r, :], reads=[self.cv], writes=[vf[i]])
                    p.op("pool", lambda e, a=vb[i], b=vf[i]: e.tensor_copy(out=a[:, :], in_=b[:, :]), reads=[vf[i]], writes=[vb[i]])
                    p.dma(self.Vcb[sg, r, :], vb[i][:, :], reads=[vb[i]], writes=[(self.Vcb, sg)])
                    for hq in range(2):
                        bank = self.ps[hq]
                        for j in range(4):
                            h = hq * 4 + j
                            p.op("pe", lambda e, bank=bank, j=j, h=h, i=i: e.transpose(
                                out=bank[:, j * 128:(j + 1) * 128], in_=kf[i][:, h * 128:(h + 1) * 128], identity=self.ident[:, :]),
                                reads=[kf[i], self.ident], writes=[bank])
                        dstv = kt[i][:, hq * 4:(hq + 1) * 4, :]
                        srcv = bank[:, :].rearrange("p (a b) -> p a b", a=4)
                        if hq == 0:
                            p.op("act", lambda e, dstv=dstv, srcv=srcv: e.copy(out=dstv, in_=srcv), reads=[bank], writes=[(kt[i], hq)])
                        else:
                            p.op("dve", lambda e, dstv=dstv, srcv=srcv: e.tensor_copy(out=dstv, in_=srcv), reads=[bank], writes=[(kt[i], hq)])
                    p.dma(self.KcT[sg, :, r].rearrange("(h c) k -> c h k", c=128), kt[i][:, :, :], reads=[kt[i]], writes=[(self.KcT, sg)])
            p.barrier()

    def stage_attn(self):
        p, T, NT, LP, PAST = self.p, self.T, self.NT, self.LP, self.PAST
        if not hasattr(self, "YT"):
            self.YT = self.dscr("YT", [D, T], BF16)
        NKB = LP // 128
        NCB = PAST // 128
        lam_init = 0.8 - 0.6 * 1.0
        with ExitStack() as st:
            S = lambda n, shp, dt=F32: self.sb(st, n, shp, dt)
            lq = S("at_lq", [128, 4, 64]); lsum = S("at_ls", [128, 2]); neglam = S("at_nl", [128, 1])
            for i, srcv in enumerate((self.l0_lq1, self.l0_lk1, self.l0_lq2, self.l0_lk2)):
                p.dma(lq[:, i, :], srcv[:].rearrange("(o d) -> o d", o=1).to_broadcast([128, 64]), reads=[srcv], writes=[lq])
            p.op("dve", lambda e: e.tensor_tensor(out=lq[:, 0, :], in0=lq[:, 0, :], in1=lq[:, 1, :], op=ALU.mult), reads=[lq], writes=[lq])
            p.op("dve", lambda e: e.tensor_tensor(out=lq[:, 2, :], in0=lq[:, 2, :], in1=lq[:, 3, :], op=ALU.mult), reads=[lq], writes=[lq])
            p.op("dve", lambda e: e.reduce_sum(out=lsum[:, 0:1], in_=lq[:, 0, :], axis=AX.X), reads=[lq], writes=[lsum])
            p.op("dve", lambda e: e.reduce_sum(out=lsum[:, 1:2], in_=lq[:, 2, :], axis=AX.X), reads=[lq], writes=[lsum])
            p.op("act", lambda e: e.activation(out=lsum[:, :], in_=lsum[:, :], func=AF.Exp), reads=[lsum], writes=[lsum])
            p.op("dve", lambda e: e.tensor_tensor(out=neglam[:, :], in0=lsum[:, 1:2], in1=lsum[:, 0:1], op=ALU.subtract), reads=[lsum], writes=[neglam])
            p.op("dve", lambda e: e.tensor_scalar(out=neglam[:, :], in0=neglam[:, :], scalar1=-lam_init, scalar2=None, op0=ALU.add), reads=[neglam], writes=[neglam])
            subw = S("at_subw", [128, 128])
            p.dma(subw[:, :], self.l0_subln[:].rearrange("(o d) -> o d", o=1).to_broadcast([128, 128]), reads=[self.l0_subln], writes=[subw])
            p.op("dve", lambda e: e.tensor_scalar(out=subw[:, :], in0=subw[:, :], scalar1=1.0 - lam_init, scalar2=None, op0=ALU.mult), reads=[subw], writes=[subw])
            mask32 = S("at_m32", [128, 2048]); maskb = S("at_mb", [128, 4, 512], BF16)
            p.dma(mask32[:, :], self.c_mask[:, :], reads=[self.c_mask], writes=[mask32])
            p.op("dve", lambda e: e.tensor_copy(out=maskb[:, :, :], in_=mask32[:, :].rearrange("p (a b) -> p a b", a=4)), reads=[mask32], writes=[maskb])
            KTs = [S(f"at_kt{i}", [128, max(LP, PAST + DSEQ)], BF16) for i in range(2)]
            QTs = [S(f"at_qt{i}", [128, LP + NSAMP * DSEQ], BF16) for i in range(2)]
            Vs = [S(f"at_v{i}", [128, max(NKB, NCB + 1), 130], BF16) for i in range(2)]
            for i in range(2):
                p.op("pool", lambda e, i=i: e.memset(Vs[i][:, :, 128:130], 1.0), writes=[(Vs[i], "ones")])
            Pb = [S(f"at_p{i}", [128, 512], BF16) for i in range(4)]
            rc = S("at_rc", [128, 2]); tq = S("at_tq", [128, 128]); oq = S("at_oq", [128, 128]); junk = S("at_junk", [128, 128])
            ss = S("at_ss", [128, 1]); rstd = S("at_rstd", [128, 1])
            ybT = [S(f"at_ybT{i}", [128, 512], BF16) for i in range(2)]
            pi = [0]
            yi = [0]
            sbk = [0]

            def obank(m, sub):
                idx = m * 4 + sub
                return self.ps[4 + idx // 3], (idx % 3) * 129

            def attend(KT, QT, V, qc0, nq, blocks, h, ycol0, nsub, subq):
                nb = len(blocks)
                touched = set()
                last_for_sub = {}
                for bi_, (kc, nk, vs, mi, fs) in enumerate(blocks):
                    for sub in range(fs, nsub):
                        last_for_sub[sub] = bi_
                first_for_sub = {}
                for bi_, (kc, nk, vs, mi, fs) in enumerate(blocks):
                    for sub in range(fs, nsub):
                        first_for_sub.setdefault(sub, bi_)
                def emit_s(bi_):
                    kc, nk, vs, mi, fs = blocks[bi_]
                    Ps = []
                    for m in range(2):
                        bank = self.ps[sbk[0] % 4]
                        sbk[0] += 1
                        pr = slice(m * 64, (m + 1) * 64)
                        p.op("pe", lambda e, bank=bank, pr=pr, kc=kc, nk=nk: e.matmul(
                            bank[0:nk, 0:nq], lhsT=KT[pr, kc:kc + nk], rhs=QT[pr, qc0:qc0 + nq], start=True, stop=True),
                            reads=[KT, QT], writes=[bank])
                        P = Pb[pi[0] % 4]
                        pi[0] += 1
                        p.op("act", lambda e, P=P, bank=bank, nk=nk: e.activation(out=P[0:nk, 0:nq], in_=bank[0:nk, 0:nq], func=AF.Exp, scale=0.125),
                             reads=[bank], writes=[P])
                        if mi is not None:
                            eng = "dve" if m == 0 else "pool"
                            p.op(eng, lambda e, P=P, mi=mi: e.tensor_tensor(out=P[:, :], in0=P[:, :], in1=maskb[:, mi, :], op=ALU.mult),
                                 reads=[P, maskb], writes=[P])
                        Ps.append(P)
                    return Ps

                def emit_pv(bi_, Ps):
                    kc, nk, vs, mi, fs = blocks[bi_]
                    for m in range(2):
                        for sub in range(fs, nsub):
                            ob, oc = obank(m, sub)
                            st_ = id(ob) not in touched
                            touched.add(id(ob))
                            assert (not st_) or bi_ == 0
                            p.op("pe", lambda e, ob=ob, oc=oc, P=Ps[m], sub=sub, vs=vs, nk=nk, st_=st_: e.matmul(
                                ob[0:subq, oc:oc + 129], lhsT=P[0:nk, sub * 128:sub * 128 + subq], rhs=V[0:nk, vs, 0:129],
                                start=st_, stop=False, skip_group_check=True),
                                reads=[Ps[m], V], writes=[(ob, oc)])

                cur = emit_s(0)
                for bi_ in range(nb):
                    nxt = emit_s(bi_ + 1) if bi_ + 1 < nb else None
                    emit_pv(bi_, cur)
                    cur = nxt
                yT = ybT[yi[0] % 2]
                yi[0] += 1
                for sub in range(nsub):
                    o1, c1 = obank(0, sub)
                    o2, c2 = obank(1, sub)
                    q = subq
                    p.op("dve", lambda e, o1=o1, c1=c1, q=q: e.reciprocal(out=rc[0:q, 0:1], in_=o1[0:q, c1 + 128:c1 + 129]), reads=[(o1, c1)], writes=[rc])
                    p.op("dve", lambda e, o2=o2, c2=c2, q=q: e.reciprocal(out=rc[0:q, 1:2], in_=o2[0:q, c2 + 128:c2 + 129]), reads=[(o2, c2)], writes=[rc])
                    p.op("dve", lambda e, q=q: e.tensor_tensor(out=rc[0:q, 1:2], in0=rc[0:q, 1:2], in1=neglam[0:q, :], op=ALU.mult), reads=[rc, neglam], writes=[rc])
                    p.op("dve", lambda e, o2=o2, c2=c2, q=q: e.tensor_scalar(out=tq[0:q, :], in0=o2[0:q, c2:c2 + 128], scalar1=rc[0:q, 1:2], scalar2=None, op0=ALU.mult),
                         reads=[(o2, c2), rc], writes=[tq])
                    p.op("dve", lambda e, o1=o1, c1=c1, q=q: e.scalar_tensor_tensor(out=oq[0:q, :], in0=o1[0:q, c1:c1 + 128], scalar=rc[0:q, 0:1], in1=tq[0:q, :],
                                                                                 op0=ALU.mult, op1=ALU.add), reads=[(o1, c1), rc, tq], writes=[oq])
                    p.op("dve", lambda e: e.memset(ss[:, 0:1], 0.0), writes=[ss])
                    p.op("act", lambda e, q=q: e.activation(out=junk[0:q, :], in_=oq[0:q, :], func=AF.Square, accum_out=ss[0:q, 0:1]), reads=[oq, ss], writes=[junk, ss])
                    p.op("act", lambda e, q=q: e.activation(out=rstd[0:q, 0:1], in_=ss[0:q, 0:1], func=AF.Sqrt, scale=1.0 / 128, bias=self.eps_t[0:q, 0:1]),
                         reads=[ss, self.eps_t], writes=[rstd])
                    p.op("dve", lambda e, q=q: e.reciprocal(out=rstd[0:q, 0:1], in_=rstd[0:q, 0:1]), reads=[rstd], writes=[rstd])
                    p.op("dve", lambda e, q=q: e.scalar_tensor_tensor(out=oq[0:q, :], in0=oq[0:q, :], scalar=rstd[0:q, 0:1], in1=subw[0:q, :], op0=ALU.mult, op1=ALU.mult),
                         reads=[oq, rstd, subw], writes=[oq])
                    tb = self.ps[7]
                    p.op("pe", lambda e, q=q, sub=sub: e.transpose(out=tb[:, sub * 128:sub * 128 + q], in_=oq[0:q, :], identity=self.ident[0:q, 0:q]),
                         reads=[oq, self.ident], writes=[tb])
                    p.op("act", lambda e, q=q, sub=sub, yT=yT: e.copy(out=yT[:, sub * 128:sub * 128 + q], in_=tb[:, sub * 128:sub * 128 + q]), reads=[tb], writes=[yT])
                nqt = (nsub - 1) * 128 + subq
                p.dma(self.YT[1024 + h * 128:1024 + (h + 1) * 128, ycol0:ycol0 + nqt], yT[:, 0:nqt], reads=[yT], writes=[(self.YT, ("att", h, ycol0))])

            for h in range(8):
                KT, QT, V = KTs[h % 2], QTs[h % 2], Vs[h % 2]
                rows = slice(h * 128, (h + 1) * 128)
                p.dma(KT[:, 0:LP], self.KT[rows, 0:LP], reads=[self.KT], writes=[KT])
                p.dma(QT[:, :], self.QT[rows, :], reads=[self.QT], writes=[QT])
                p.dma(V[:, 0:NKB, 0:128], self.Vb[0:LP, rows].rearrange("(kb q) d -> q kb d", q=128), reads=[self.Vb], writes=[(V, "d")])
                for qt in range(LP // 512):
                    blocks = []
                    for kb in range(4 * qt + 4):
                        j = kb - 4 * qt
                        blocks.append((kb * 128, 128, kb, (j if j >= 0 else None), max(j, 0)))
                    attend(KT, QT, V, qt * 512, 512, blocks, h, qt * 512, 4, 128)
            for h in range(8):
                rows = slice(h * 128, (h + 1) * 128)
                for sg in range(NSAMP):
                    i = (h * NSAMP + sg) % 2
                    KT, QT, V = KTs[i], QTs[i], Vs[i]
                    c0 = LP + sg * DSEQ
                    p.dma(KT[:, 0:PAST], self.KcT[sg, rows, :], reads=[(self.KcT, sg)], writes=[KT])
                    p.dma(KT[:, PAST:PAST + DSEQ], self.KT[rows, c0:c0 + DSEQ], reads=[self.KT], writes=[KT])
                    p.dma(QT[:, c0:c0 + DSEQ], self.QT[rows, c0:c0 + DSEQ], reads=[self.QT], writes=[QT])
                    p.dma(V[:, 0:NCB, 0:128], self.Vcb[sg, :, rows].rearrange("(kb q) d -> q kb d", q=128), reads=[(self.Vcb, sg)], writes=[(V, "d")])
                    p.dma(V[0:DSEQ, NCB, 0:128], self.Vb[c0:c0 + DSEQ, rows], reads=[self.Vb], writes=[(V, "d")])
                    blocks = [(kb * 128, 128, kb, None, 0) for kb in range(NCB)] + [(PAST, DSEQ, NCB, None, 0)]
                    attend(KT, QT, V, c0, DSEQ, blocks, h, c0, 1, DSEQ)
            p.barrier()

    def stage_hgrn(self):
        p, T, NT, LP = self.p, self.T, self.NT, self.LP
        with ExitStack() as st:
            S = lambda n, shp, dt=F32: self.sb(st, n, shp, dt)
            lb2 = S("hg_lb2", [128, 2, 8]); lbt = S("hg_lbt", [128, 8]); oml = S("hg_oml", [128, 8]); nw = S("hg_nw", [128, 1])
            ones = S("hg_ones", [128, 128]); cm = S("hg_cm", [128, TS]); m16f = S("hg_m16f", [128, 128]); m16 = S("hg_m16", [128, 128], BF16)
            for r in range(2):
                self.small_T(lb2[:, r, :], self.l1_hg_lb[r, :].rearrange("(h c) -> c h", c=128), [self.l1_hg_lb], [lb2])
            self.small_T(nw[:, :], self.l1_hg_nw[:].rearrange("(c o) -> c o", o=1), [self.l1_hg_nw], [nw])
            p.op("dve", lambda e: e.tensor_tensor(out=lbt[:, :], in0=lb2[:, 1, :], in1=lb2[:, 0, :], op=ALU.subtract), reads=[lb2], writes=[lbt])
            p.op("act", lambda e: e.activation(out=lbt[:, :], in_=lbt[:, :], func=AF.Sigmoid), reads=[lbt], writes=[lbt])
            p.op("dve", lambda e: e.tensor_scalar(out=oml[:, :], in0=lbt[:, :], scalar1=-1.0, scalar2=1.0, op0=ALU.mult, op1=ALU.add), reads=[lbt], writes=[oml])
            p.op("dve", lambda e: e.memset(ones[:, :], 1.0), writes=[ones])
            p.op("dve", lambda e: e.memset(cm[:, :], 1.0), writes=[cm])
            p.op("dve", lambda e: e.memset(cm[:, :].rearrange("p (c l) -> p c l", l=16)[:, :, 0:1], 0.0), writes=[cm])
            p.dma(m16f[:, :], self.c_m16[:, :], reads=[self.c_m16], writes=[m16f])
            p.op("dve", lambda e: e.tensor_copy(out=m16[:, :], in_=m16f[:, :]), reads=[m16f], writes=[m16])
            NB = 4
            mk = lambda nm, dt=F32: [S(f"hg_{nm}{i}", [128, TS], dt) for i in range(NB)]
            qf, ff, vf, gt, bt, ebt, t1, kk, kh = mk("q"), mk("f"), mk("v"), mk("g"), mk("b"), mk("eb"), mk("t1"), mk("kk"), mk("kh")
            qb, kb_ = mk("qb", BF16), mk("kb", BF16)
            attm = [[S(f"hg_am{hd}{i}", [128, 128], BF16) for i in range(2)] for hd in range(2)]
            itok = [[S(f"hg_it{hd}{i}", [128, 128], BF16) for i in range(2)] for hd in range(2)]
            khi = [[S(f"hg_khi{hd}{i}", [16, 256], BF16) for i in range(3)] for hd in range(2)]
            S32s = [S(f"hg_S32{hd}", [128, 128]) for hd in range(2)]
            Sbfs = [S(f"hg_Sbf{hd}", [128, 128], BF16) for hd in range(2)]
            osb = [S(f"hg_o{i}", [128, TS]) for i in range(2)]
            yb = [S(f"hg_y{i}", [128, TS], BF16) for i in range(2)]
            it = 0
            for hp in range(4):
                for ti in range(NT):
                    samp = ti == NT - 1
                    c0 = ti * TS
                    sets = []
                    for hd in range(2):
                        h = hp * 2 + hd
                        rows = slice(h * 128, (h + 1) * 128)
                        i = hd * 2 + (it % 2)
                        Q, F_, V_, G, B_, EB, T1, KK, KH, QB, KB = qf[i], ff[i], vf[i], gt[i], bt[i], ebt[i], t1[i], kk[i], kh[i], qb[i], kb_[i]
                        sets.append((Q, F_, V_, G, B_, EB, T1, KK, KH, QB, KB))
                        p.dma(Q[:, :], self.HQ[rows, c0:c0 + TS], reads=[(self.HQ, ti)], writes=[Q])
                        p.dma(F_[:, :], self.HF[rows, c0:c0 + TS], reads=[(self.HF, ti)], writes=[F_])
                        p.dma(V_[:, :], self.HI[rows, c0:c0 + TS], reads=[(self.HI, ti)], writes=[V_])
                        p.op("act", lambda e, G=G, F_=F_: e.activation(out=G[:, :], in_=F_[:, :], func=AF.Sigmoid), reads=[F_], writes=[G])
                        p.op("dve", lambda e, G=G, h=h: e.tensor_scalar(out=G[:, :], in0=G[:, :], scalar1=oml[:, h:h + 1], scalar2=lbt[:, h:h + 1], op0=ALU.mult, op1=ALU.add),
                             reads=[G, oml, lbt], writes=[G])
                        p.op("dve", lambda e, G=G, KK=KK: e.tensor_scalar(out=KK[:, :], in0=G[:, :], scalar1=-1.0, scalar2=1.0, op0=ALU.mult, op1=ALU.add), reads=[G], writes=[KK])
                        p.op("act", lambda e, G=G: e.activation(out=G[:, :], in_=G[:, :], func=AF.Ln), reads=[G], writes=[G])
                        p.op("dve", lambda e, G=G, B_=B_: e.tensor_tensor_scan(out=B_[:, :], data0=cm[:, :], data1=G[:, :], initial=0.0, op0=ALU.mult, op1=ALU.add),
                             reads=[G, cm], writes=[B_])
                        p.op("act", lambda e, Q=Q: e.activation(out=Q[:, :], in_=Q[:, :], func=AF.Silu), reads=[Q], writes=[Q])
                        p.op("act", lambda e, EB=EB, B_=B_: e.activation(out=EB[:, :], in_=B_[:, :], func=AF.Exp), reads=[B_], writes=[EB])
                        p.op("dve", lambda e, QB=QB, Q=Q, EB=EB: e.tensor_tensor(out=QB[:, :], in0=Q[:, :], in1=EB[:, :], op=ALU.mult), reads=[Q, EB], writes=[QB])
                        p.op("act", lambda e, T1=T1, B_=B_: e.activation(out=T1[:, :], in_=B_[:, :], func=AF.Exp, scale=-1.0), reads=[B_], writes=[T1])
                        p.op("dve", lambda e, KB=KB, KK=KK, T1=T1: e.tensor_tensor(out=KB[:, :], in0=KK[:, :], in1=T1[:, :], op=ALU.mult), reads=[KK, T1], writes=[KB])
                        b3 = B_[:, :].rearrange("p (c l) -> p c l", l=16)
                        t3 = T1[:, :].rearrange("p (c l) -> p c l", l=16)
                        p.op("dve", lambda e, b3=b3, t3=t3: e.tensor_tensor(out=t3, in0=b3[:, :, 15:16].to_broadcast([128, 32, 16]), in1=b3, op=ALU.subtract),
                             reads=[B_], writes=[T1])
                        p.op("act", lambda e, T1=T1: e.activation(out=T1[:, :], in_=T1[:, :], func=AF.Exp), reads=[T1], writes=[T1])
                        p.op("dve", lambda e, KH=KH, KK=KK, T1=T1: e.tensor_tensor(out=KH[:, :], in0=KK[:, :], in1=T1[:, :], op=ALU.mult), reads=[KK, T1], writes=[KH])
                    it += 1
                    kcnt = [0, 0]

                    def emit_T(hd, cg):
                        Q, F_, V_, G, B_, EB, T1, KK, KH, QB, KB = sets[hd]
                        cs = slice(cg * 16, (cg + 1) * 16)
                        bt_ = self.ps[4 + hd]
                        so = (cg % 2) * 256
                        kt = khi[hd][cg % 3]
                        p.op("pe", lambda e, bt_=bt_, KH=KH, cs=cs, so=so: e.transpose(out=bt_[0:16, so:so + 128], in_=KH[:, cs], identity=self.ident[:, :]),
                             reads=[KH, self.ident], writes=[(bt_, cg % 2)])
                        p.op("pe", lambda e, bt_=bt_, V_=V_, cs=cs, so=so: e.transpose(out=bt_[0:16, so + 128:so + 256], in_=V_[:, cs], identity=self.ident[:, :]),
                             reads=[V_, self.ident], writes=[(bt_, cg % 2)])
                        p.op("act", lambda e, kt=kt, bt_=bt_, so=so: e.copy(out=kt[:, :], in_=bt_[0:16, so:so + 256]), reads=[(bt_, cg % 2)], writes=[kt])

                    for blk in range(4):
                        bs = slice(blk * 128, (blk + 1) * 128)
                        for hd in range(2):
                            Q, F_, V_, G, B_, EB, T1, KK, KH, QB, KB = sets[hd]
                            ba, bo = self.ps[hd], self.ps[2 + hd]
                            am, itk = attm[hd][blk % 2], itok[hd][blk % 2]
                            p.op("pe", lambda e, ba=ba, KB=KB, QB=QB, bs=bs: e.matmul(ba[:, 0:128], lhsT=KB[:, bs], rhs=QB[:, bs], start=True, stop=True),
                                 reads=[KB, QB], writes=[(ba, 0)])
                            p.op("dve", lambda e, am=am, ba=ba: e.tensor_tensor(out=am[:, :], in0=ba[:, 0:128], in1=m16[:, :], op=ALU.mult), reads=[(ba, 0), m16], writes=[am])
                            p.op("pe", lambda e, ba=ba, V_=V_, bs=bs: e.transpose(out=ba[:, 128:256], in_=V_[:, bs], identity=self.ident[:, :]),
                                 reads=[V_, self.ident], writes=[(ba, 1)])
                            p.op("act", lambda e, itk=itk, ba=ba: e.copy(out=itk[:, :], in_=ba[:, 128:256]), reads=[(ba, 1)], writes=[itk])
                            p.op("pe", lambda e, bo=bo, itk=itk, am=am, bs=bs: e.matmul(bo[:, bs], lhsT=itk[:, :], rhs=am[:, :], start=True, stop=False, skip_group_check=True),
                                 reads=[itk, am], writes=[bo])
                            emit_T(hd, blk * 8)
                        for c in range(8):
                            cg = blk * 8 + c
                            cs = slice(cg * 16, (cg + 1) * 16)
                            for hd in range(2):
                                h = hp * 2 + hd
                                Q, F_, V_, G, B_, EB, T1, KK, KH, QB, KB = sets[hd]
                                bo, bu = self.ps[2 + hd], self.ps[6 + hd]
                                S32, Sbf = S32s[hd], Sbfs[hd]
                                if c + 1 < 8:
                                    emit_T(hd, cg + 1)
                                seq_start = (ti == 0 and cg == 0) if not samp else (cg % 4 == 0)
                                if seq_start:
                                    if samp:
                                        sg = cg // 4
                                        p.dma(S32[:, :], self.st_hg[sg, h, :, :], reads=[self.st_hg], writes=[S32])
                                    else:
                                        p.op("dve", lambda e, S32=S32: e.memset(S32[:, :], 0.0), writes=[S32])
                                    p.op("act", lambda e, S32=S32, Sbf=Sbf: e.copy(out=Sbf[:, :], in_=S32[:, :]), reads=[S32], writes=[Sbf])
                                p.op("pe", lambda e, bo=bo, QB=QB, cs=cs, Sbf=Sbf: e.matmul(bo[:, cs], lhsT=Sbf[:, :], rhs=QB[:, cs], start=False, stop=False, skip_group_check=True),
                                     reads=[Sbf, QB], writes=[bo])
                                kt = khi[hd][cg % 3]
                                p.op("pe", lambda e, bu=bu, kt=kt: e.matmul(bu[:, 0:128], lhsT=kt[0:16, 0:128], rhs=kt[0:16, 128:256], start=True, stop=True),
                                     reads=[kt], writes=[bu])
                                p.op("dve", lambda e, bu=bu, EB=EB, cg=cg, S32=S32: e.scalar_tensor_tensor(out=S32[:, :], in0=S32[:, :], scalar=EB[:, cg * 16 + 15:cg * 16 + 16], in1=bu[:, 0:128],
                                                                                                   op0=ALU.mult, op1=ALU.add), reads=[S32, EB, bu], writes=[S32])
                                p.op("act", lambda e, S32=S32, Sbf=Sbf: e.copy(out=Sbf[:, :], in_=S32[:, :]), reads=[S32], writes=[Sbf])
                                seq_end = (ti == NT - 2 and cg == 31) if not samp else (cg % 4 == 3)
                                if seq_end:
                                    seq = (1 + cg // 4) if samp else 0
                                    p.dma(self.hg_out[seq, h, :, :], S32[:, :], reads=[S32], writes=[(self.hg_out, (seq, h))])
                    for hd in range(2):
                        h = hp * 2 + hd
                        Q, F_, V_, G, B_, EB, T1, KK, KH, QB, KB = sets[hd]
                        bo, bn = self.ps[2 + hd], self.ps[hd]
                        O_, Y_ = osb[hd], yb[hd]
                        p.op("act", lambda e, O_=O_, bo=bo: e.copy(out=O_[:, :], in_=bo[:, :]), reads=[bo], writes=[O_])
                        p.op("act", lambda e, T1=T1, O_=O_: e.activation(out=T1[:, :], in_=O_[:, :], func=AF.Square), reads=[O_], writes=[T1])
                        p.op("pe", lambda e, bn=bn, T1=T1: e.matmul(bn[:, :], lhsT=ones[:, :], rhs=T1[:, :], start=True, stop=True), reads=[ones, T1], writes=[bn])
                        p.op("act", lambda e, T1=T1, bn=bn: e.activation(out=T1[:, :], in_=bn[:, :], func=AF.Sqrt, scale=1.0 / 128, bias=self.eps_t[:, 0:1]),
                             reads=[bn, self.eps_t], writes=[T1])
                        p.op("dve", lambda e, T1=T1: e.reciprocal(out=T1[:, :], in_=T1[:, :]), reads=[T1], writes=[T1])
                        p.op("dve", lambda e, O_=O_, T1=T1, Y_=Y_: e.scalar_tensor_tensor(out=Y_[:, :], in0=O_[:, :], scalar=nw[:, 0:1], in1=T1[:, :], op0=ALU.mult, op1=ALU.mult),
                             reads=[O_, T1, nw], writes=[Y_])
                        p.dma(self.YT[1024 + h * 128:1024 + (h + 1) * 128, c0:c0 + TS], Y_[:, :], reads=[Y_], writes=[(self.YT, ("hg", h, ti))])
            p.barrier()

    def stage_ssdconv(self):
        p, T, NT, LP = self.p, self.T, self.NT, self.LP
        self.XT = self.dscr("XT", [T, 1280], F32)
        self.BCt = self.dscr("BCt", [512, T], BF16)
        with ExitStack() as st:
            S = lambda n, shp, dt=F32: self.sb(st, n, shp, dt)
            cw = S("sc_cw", [128, 12, 4]); cb = S("sc_cb", [128, 12])
            for tap in range(4):
                self.small_T(cw[:, :, tap], self.l1_conv_w[tap, :].rearrange("(b c) -> c b", c=128), [self.l1_conv_w], [cw])
            self.small_T(cb[:, :], self.l1_conv_b[:].rearrange("(b c) -> c b", c=128), [self.l1_conv_b], [cb])
            xa = [S(f"sc_xa{i}", [128, 536]) for i in range(2)]
            xc = [S(f"sc_xc{i}", [128, TS]) for i in range(2)]
            xcb = [S(f"sc_xcb{i}", [128, TS], BF16) for i in range(2)]
            xtk = [S(f"sc_xt{i}", [128, 4, 128]) for i in range(2)]
            it = 0
            for ti in range(NT):
                samp = ti == NT - 1
                nseg, sl = (NSAMP, DSEQ) if samp else (1, TS)
                c0 = ti * TS
                for b in range(12):
                    i = it % 2
                    it += 1
                    X, XC, XCB, XTK = xa[i], xc[i], xcb[i], xtk[i]
                    rows = slice(b * 128, (b + 1) * 128)
                    xv = X[:, 0:nseg * (3 + sl)].rearrange("p (s l) -> p s l", s=nseg)
                    if samp:
                        for sg in range(NSAMP):
                            self.small_T(xv[:, sg, 0:3], self.st_ssd_conv[sg, :, rows].rearrange("t c -> c t"), [self.st_ssd_conv], [X])
                    elif ti == 0:
                        p.op("dve", lambda e, xv=xv: e.memset(xv[:, :, 0:3], 0.0), writes=[X])
                    else:
                        p.dma(xv[:, 0, 0:3], self.XBC[rows, c0 - 3:c0], reads=[(self.XBC, ti - 1)], writes=[X])
                    p.dma(xv[:, :, 3:3 + sl], self.XBC[rows, c0:c0 + TS].rearrange("p (s l) -> p s l", s=nseg), reads=[(self.XBC, ti)], writes=[X])
                    xc3 = XC[:, :].rearrange("p (s l) -> p s l", s=nseg)
                    p.op("dve", lambda e, xc3=xc3, xv=xv, b=b, sl=sl: e.tensor_scalar(out=xc3, in0=xv[:, :, 0:sl], scalar1=cw[:, b, 0:1], scalar2=cb[:, b:b + 1],
                                                                                 op0=ALU.mult, op1=ALU.add), reads=[X, cw, cb], writes=[XC])
                    for tap in range(1, 4):
                        p.op("dve", lambda e, xc3=xc3, xv=xv, b=b, sl=sl, tap=tap: e.scalar_tensor_tensor(
                            out=xc3, in0=xv[:, :, tap:tap + sl], scalar=cw[:, b, tap:tap + 1], in1=xc3, op0=ALU.mult, op1=ALU.add), reads=[X, cw, XC], writes=[XC])
                    p.op("act", lambda e, XC=XC: e.activation(out=XC[:, :], in_=XC[:, :], func=AF.Silu), reads=[XC], writes=[XC])
                    if samp:
                        for sg in range(NSAMP):
                            self.small_T(self.ssd_conv_out[1 + sg, :, rows].rearrange("t c -> c t"), xv[:, sg, sl:sl + 3], [X], [(self.ssd_conv_out, ("s", b, sg))])
                    elif ti == NT - 2:
                        self.small_T(self.ssd_conv_out[0, :, rows].rearrange("t c -> c t"), X[:, TS:TS + 3], [X], [(self.ssd_conv_out, ("p", b))])
                    if b >= 8:
                        p.op("pool", lambda e, XCB=XCB, XC=XC: e.tensor_copy(out=XCB[:, :], in_=XC[:, :]), reads=[XC], writes=[XCB])
                        p.dma(self.BCt[(b - 8) * 128:(b - 7) * 128, c0:c0 + TS], XCB[:, :], reads=[XCB], writes=[(self.BCt, ti)])
                    if b < 10:
                        bank = self.ps[it % 2]
                        for t in range(4):
                            p.op("pe", lambda e, bank=bank, XC=XC, t=t: e.transpose(out=bank[:, t * 128:(t + 1) * 128], in_=XC[:, t * 128:(t + 1) * 128], identity=self.ident[:, :]),
                                 reads=[XC, self.ident], writes=[bank])
                        p.op("act", lambda e, XTK=XTK, bank=bank: e.copy(out=XTK[:, :, :], in_=bank[:, :].rearrange("p (a b) -> p a b", a=4)), reads=[bank], writes=[XTK])
                        p.dma(self.XT[c0:c0 + TS, b * 128:(b + 1) * 128].rearrange("(t q) c -> q t c", q=128), XTK[:, :, :], reads=[XTK], writes=[(self.XT, ti)])
            p.barrier()

    def stage_ssd(self):
        p, T, NT, LP = self.p, self.T, self.NT, self.LP
        with ExitStack() as st:
            S = lambda n, shp, dt=F32: self.sb(st, n, shp, dt)
            cs = S("ss_cs", [64, 448]); dtb = S("ss_dtb", [64, 16]); aneg = S("ss_an", [64, 16]); dsk = S("ss_dsk", [64, 16]); nwB = S("ss_nw", [64, 1024])
            one = S("ss_one", [128, 1])
            p.dma(cs[:, :], self.c_ssd[:, :], reads=[self.c_ssd], writes=[cs])
            tri, ntri, ones64, mneg, id64, sel63 = cs[:, 0:64], cs[:, 64:128], cs[:, 128:192], cs[:, 192:256], cs[:, 256:320], cs[:, 320:448]
            bc16 = lambda t_: t_[:].rearrange("(o d) -> o d", o=1).to_broadcast([64, 16])
            p.dma(dtb[:, :], bc16(self.l1_dt_bias), reads=[self.l1_dt_bias], writes=[dtb])
            p.dma(aneg[:, :], bc16(self.l1_a_log), reads=[self.l1_a_log], writes=[aneg])
            p.dma(dsk[:, :], bc16(self.l1_d_skip), reads=[self.l1_d_skip], writes=[dsk])
            p.dma(nwB[:, :], self.l1_ssd_nw[:].rearrange("(o d) -> o d", o=1).to_broadcast([64, 1024]), reads=[self.l1_ssd_nw], writes=[nwB])
            p.op("act", lambda e: e.activation(out=aneg[:, :], in_=aneg[:, :], func=AF.Exp), reads=[aneg], writes=[aneg])
            p.op("dve", lambda e: e.tensor_scalar(out=aneg[:, :], in0=aneg[:, :], scalar1=-1.0, scalar2=None, op0=ALU.mult), reads=[aneg], writes=[aneg])
            p.op("dve", lambda e: e.memset(one[:, :], 1.0), writes=[one])
            S32 = S("ss_S32", [128, 1024]); Sbf = S("ss_Sbf", [128, 1024], BF16); stg = S("ss_stg", [128, 8, 128])
            NB = 2
            mk = lambda nm, shp, dt=F32: [S(f"ss_{nm}{i}", shp, dt) for i in range(NB)]
            xt, zt, dtr, bct = mk("xt", [64, 1280]), mk("zt", [64, 1024]), mk("dtr", [64, 16]), mk("bct", [128, 4, 64], BF16)
            dt_, dta, r1, r2, LT = mk("dt", [64, 16]), mk("dta", [64, 16]), mk("r1", [64, 1024]), mk("r2", [64, 1024]), mk("LT", [64, 1024])
            MT, xdt, xw, btk = mk("MT", [64, 1024], BF16), mk("xdt", [64, 1024], BF16), mk("xw", [64, 1024], BF16), mk("btk", [64, 256], BF16)
            ac, eac, wv, y32, tt = mk("ac", [64, 16]), mk("eac", [64, 16]), mk("wv", [64, 16]), mk("y32", [64, 1024]), mk("tt", [64, 1024])
            decB, ssg, yT = mk("decB", [128, 16]), mk("ssg", [64, 2]), mk("yT", [128, 8, 64], BF16)
            v3 = lambda ap: ap.rearrange("p (h l) -> p h l", h=16)
            bcl = lambda ap: ap.unsqueeze(2).to_broadcast([64, 16, 64])
            seqs = [(0, 0, LP // 64)] + [(1 + sg, LP + sg * DSEQ, 1) for sg in range(NSAMP)]
            it = 0
            for (seq, col0, nch) in seqs:
                if seq == 0:
                    p.op("dve", lambda e: e.memset(S32[:, :], 0.0), writes=[S32])
                else:
                    p.dma(stg[:, :, :], self.st_ssd[seq - 1, :, :, :].rearrange("h q n -> (h q) n").rearrange("(a q) n -> q a n", q=128), reads=[self.st_ssd], writes=[stg])
                    for a in range(8):
                        bank = self.ps[a // 4]
                        p.op("pe", lambda e, bank=bank, a=a: e.transpose(out=bank[:, (a % 4) * 128:(a % 4 + 1) * 128], in_=stg[:, a, :], identity=self.ident[:, :]),
                             reads=[stg, self.ident], writes=[bank])
                    for hf in range(2):
                        p.op("dve", lambda e, hf=hf: e.tensor_copy(out=S32[:, hf * 512:(hf + 1) * 512], in_=self.ps[hf][:, :]), reads=[self.ps[hf]], writes=[S32])
                p.op("act", lambda e: e.copy(out=Sbf[:, :], in_=S32[:, :]), reads=[S32], writes=[Sbf])
                for ch in range(nch):
                    i = it % NB
                    it += 1
                    c0 = col0 + ch * 64
                    ti = c0 // TS
                    XT_, ZT, DTR, BCT, DT_, DTA, R1, R2, LT_, MT_, XDT, XW, BTK, AC, EAC, WV, Y, TT, DEC, SSG, YT_ = (
                        xt[i], zt[i], dtr[i], bct[i], dt_[i], dta[i], r1[i], r2[i], LT[i], MT[i], xdt[i], xw[i], btk[i], ac[i], eac[i], wv[i], y32[i], tt[i], decB[i], ssg[i], yT[i])
                    p.dma(XT_[:, :], self.XT[c0:c0 + 64, :], reads=[(self.XT, ti)], writes=[XT_])
                    p.dma(ZT[:, :], self.Z[c0:c0 + 64, :], reads=[(self.Z, ti)], writes=[ZT])
                    p.dma(DTR[:, :], self.DT[c0:c0 + 64, :], reads=[(self.DT, ti)], writes=[DTR])
                    p.dma(BCT[:, :, :], self.BCt[:, c0:c0 + 64].rearrange("(a n) t -> n a t", n=128), reads=[(self.BCt, ti)], writes=[BCT])
                    p.op("dve", lambda e, DT_=DT_, DTR=DTR: e.tensor_tensor(out=DT_[:, :], in0=DTR[:, :], in1=dtb[:, :], op=ALU.add), reads=[DTR, dtb], writes=[DT_])
                    p.op("act", lambda e, DT_=DT_: e.activation(out=DT_[:, :], in_=DT_[:, :], func=AF.Exp), reads=[DT_], writes=[DT_])
                    p.op("act", lambda e, DT_=DT_: e.activation(out=DT_[:, :], in_=DT_[:, :], func=AF.Ln, bias=one[0:64, 0:1]), reads=[DT_, one], writes=[DT_])
                    p.op("dve", lambda e, DTA=DTA, DT_=DT_: e.tensor_tensor(out=DTA[:, :], in0=DT_[:, :], in1=aneg[:, :], op=ALU.mult), reads=[DT_, aneg], writes=[DTA])
                    p.op("dve", lambda e, R1=R1, DTA=DTA: e.tensor_tensor(out=v3(R1[:, :]), in0=bcl(DTA[:, :]), in1=tri.unsqueeze(1).to_broadcast([64, 16, 64]), op=ALU.mult),
                         reads=[DTA, cs], writes=[R1])
                    p.op("pool", lambda e, R2=R2, DTA=DTA: e.tensor_copy(out=v3(R2[:, :]), in_=bcl(DTA[:, :])), reads=[DTA], writes=[R2])
                    bd = [self.ps[0], self.ps[1]]
                    for hf in range(2):
                        hs = slice(hf * 512, (hf + 1) * 512)
                        p.op("pe", lambda e, hf=hf, hs=hs, R1=R1: e.matmul(bd[hf][0:64, :], lhsT=ones64, rhs=R1[:, hs], start=True, stop=False), reads=[R1, cs], writes=[bd[hf]])
                        p.op("pe", lambda e, hf=hf, hs=hs, R2=R2: e.matmul(bd[hf][0:64, :], lhsT=ntri, rhs=R2[:, hs], start=False, stop=True), reads=[R2, cs], writes=[bd[hf]])
                    bm = self.ps[6]
                    p.op("pe", lambda e, DTA=DTA: e.matmul(bm[0:64, 0:16], lhsT=tri, rhs=DTA[:, :], start=True, stop=True), reads=[DTA, cs], writes=[(bm, "ac")])
                    for hf in range(2):
                        hs = slice(hf * 512, (hf + 1) * 512)
                        p.op("dve", lambda e, hf=hf, hs=hs, LT_=LT_: e.tensor_tensor(out=LT_[:, hs].rearrange("p (h l) -> p h l", h=8), in0=bd[hf][0:64, :].rearrange("p (h l) -> p h l", h=8),
                                                                                 in1=mneg.unsqueeze(1).to_broadcast([64, 8, 64]), op=ALU.add), reads=[bd[hf], cs], writes=[LT_])
                    p.op("act", lambda e, LT_=LT_: e.activation(out=LT_[:, :], in_=LT_[:, :], func=AF.Exp), reads=[LT_], writes=[LT_])
                    p.op("act", lambda e, AC=AC: e.copy(out=AC[:, :], in_=bm[0:64, 0:16]), reads=[(bm, "ac")], writes=[AC])
                    p.op("act", lambda e, EAC=EAC, AC=AC: e.activation(out=EAC[:, :], in_=AC[:, :], func=AF.Exp), reads=[AC], writes=[EAC])
                    for g in range(2):
                        p.op("pe", lambda e, g=g, BCT=BCT: e.matmul(bm[0:64, 64 + g * 64:128 + g * 64], lhsT=BCT[:, g, :], rhs=BCT[:, 2 + g, :], start=True, stop=True),
                             reads=[BCT], writes=[(bm, "cb")])
                    for g in range(2):
                        gs = slice(g * 512, (g + 1) * 512)
                        p.op("dve", lambda e, g=g, gs=gs, MT_=MT_, LT_=LT_: e.tensor_tensor(out=MT_[:, gs].rearrange("p (h l) -> p h l", h=8), in0=LT_[:, gs].rearrange("p (h l) -> p h l", h=8),
                                                                                        in1=bm[0:64, 64 + g * 64:128 + g * 64].unsqueeze(1).to_broadcast([64, 8, 64]), op=ALU.mult),
                             reads=[LT_, (bm, "cb")], writes=[MT_])
                    xv_ = v3(XT_[:, 0:1024])
                    p.op("dve", lambda e, XDT=XDT, xv_=xv_, DT_=DT_: e.tensor_tensor(out=v3(XDT[:, :]), in0=xv_, in1=bcl(DT_[:, :]), op=ALU.mult), reads=[XT_, DT_], writes=[XDT])
                    byd = [self.ps[2], self.ps[3]]
                    for h in range(16):
                        p.op("pe", lambda e, h=h, MT_=MT_, XDT=XDT: e.matmul(byd[h // 8][0:64, (h % 8) * 64:(h % 8 + 1) * 64], lhsT=MT_[:, h * 64:(h + 1) * 64], rhs=XDT[:, h * 64:(h + 1) * 64],
                                                                            start=True, stop=True, skip_group_check=True), reads=[MT_, XDT], writes=[byd[h // 8]])
                    byo = [self.ps[4], self.ps[5]]
                    for g in range(2):
                        p.op("pe", lambda e, g=g, BCT=BCT: e.matmul(byo[g][0:64, :], lhsT=BCT[:, 2 + g, :], rhs=Sbf[:, g * 512:(g + 1) * 512], start=True, stop=True),
                             reads=[BCT, Sbf], writes=[byo[g]])
                    for g in range(2):
                        gs = slice(g * 512, (g + 1) * 512)
                        p.op("dve", lambda e, g=g, gs=gs, Y=Y, EAC=EAC: e.tensor_tensor(out=Y[:, gs].rearrange("p (h l) -> p h l", h=8), in0=byo[g][0:64, :].rearrange("p (h l) -> p h l", h=8),
                                                                                    in1=EAC[:, g * 8:(g + 1) * 8].unsqueeze(2).to_broadcast([64, 8, 64]), op=ALU.mult), reads=[byo[g], EAC], writes=[Y])
                        p.op("dve", lambda e, g=g, gs=gs, Y=Y: e.tensor_tensor(out=Y[:, gs], in0=Y[:, gs], in1=byd[g][0:64, :], op=ALU.add), reads=[Y, byd[g]], writes=[Y])
                    p.op("pool", lambda e, TT=TT, xv_=xv_: e.tensor_tensor(out=v3(TT[:, :]), in0=xv_, in1=bcl(dsk[:, :]), op=ALU.mult), reads=[XT_, dsk], writes=[TT])
                    p.op("dve", lambda e, Y=Y, TT=TT: e.tensor_tensor(out=Y[:, :], in0=Y[:, :], in1=TT[:, :], op=ALU.add), reads=[Y, TT], writes=[Y])
                    p.op("act", lambda e, ZT=ZT: e.activation(out=ZT[:, :], in_=ZT[:, :], func=AF.Silu), reads=[ZT], writes=[ZT])
                    p.op("dve", lambda e, Y=Y, ZT=ZT: e.tensor_tensor(out=Y[:, :], in0=Y[:, :], in1=ZT[:, :], op=ALU.mult), reads=[Y, ZT], writes=[Y])
                    p.op("dve", lambda e, SSG=SSG: e.memset(SSG[:, :], 0.0), writes=[SSG])
                    for g in range(2):
                        gs = slice(g * 512, (g + 1) * 512)
                        p.op("act", lambda e, g=g, gs=gs, TT=TT, Y=Y, SSG=SSG: e.activation(out=TT[:, gs], in_=Y[:, gs], func=AF.Square, accum_out=SSG[:, g:g + 1]), reads=[Y, SSG], writes=[TT, SSG])
                    p.op("act", lambda e, SSG=SSG: e.activation(out=SSG[:, :], in_=SSG[:, :], func=AF.Sqrt, scale=1.0 / 512, bias=self.eps_t[0:64, 0:1]), reads=[SSG, self.eps_t], writes=[SSG])
                    p.op("dve", lambda e, SSG=SSG: e.reciprocal(out=SSG[:, :], in_=SSG[:, :]), reads=[SSG], writes=[SSG])
                    for g in range(2):
                        gs = slice(g * 512, (g + 1) * 512)
                        p.op("dve", lambda e, g=g, gs=gs, Y=Y, SSG=SSG: e.scalar_tensor_tensor(out=Y[:, gs], in0=Y[:, gs], scalar=SSG[:, g:g + 1], in1=nwB[:, gs], op0=ALU.mult, op1=ALU.mult),
                             reads=[Y, SSG, nwB], writes=[Y])
                    btr = self.ps[7]
                    for a in range(8):
                        p.op("pe", lambda e, a=a, Y=Y: e.transpose(out=btr[:, a * 64:(a + 1) * 64], in_=Y[:, a * 128:(a + 1) * 128], identity=id64), reads=[Y, cs], writes=[btr])
                    p.op("act", lambda e, YT_=YT_: e.copy(out=YT_[:, :, :], in_=btr[:, :].rearrange("p (a l) -> p a l", a=8)), reads=[btr], writes=[YT_])
                    p.dma(self.YT[0:1024, c0:c0 + 64].rearrange("(a q) t -> q a t", q=128), YT_[:, :, :], reads=[YT_], writes=[(self.YT, ("ssd", c0))])
                    p.op("dve", lambda e, WV=WV, LT_=LT_, DT_=DT_: e.tensor_tensor(out=WV[:, :], in0=v3(LT_[:, :])[:, :, 63], in1=DT_[:, :], op=ALU.mult), reads=[LT_, DT_], writes=[WV])
                    p.op("dve", lambda e, XW=XW, xv_=xv_, WV=WV: e.tensor_tensor(out=v3(XW[:, :]), in0=xv_, in1=bcl(WV[:, :]), op=ALU.mult), reads=[XT_, WV], writes=[XW])
                    p.op("pool", lambda e, BTK=BTK, XT_=XT_: e.tensor_copy(out=BTK[:, :], in_=XT_[:, 1024:1280]), reads=[XT_], writes=[BTK])
                    p.op("pe", lambda e, AC=AC: e.matmul(bm[:, 256:272], lhsT=sel63, rhs=AC[:, :], start=True, stop=True), reads=[AC, cs], writes=[(bm, "dec")])
                    p.op("act", lambda e, DEC=DEC: e.activation(out=DEC[:, :], in_=bm[:, 256:272], func=AF.Exp), reads=[(bm, "dec")], writes=[DEC])
                    for g in range(2):
                        gs = slice(g * 512, (g + 1) * 512)
                        p.op("pe", lambda e, g=g, gs=gs, BTK=BTK, XW=XW: e.matmul(bd[g][:, :], lhsT=BTK[:, g * 128:(g + 1) * 128], rhs=XW[:, gs], start=True, stop=True), reads=[BTK, XW], writes=[bd[g]])
                        p.op("dve", lambda e, g=g, gs=gs, DEC=DEC: e.tensor_tensor(out=S32[:, gs].rearrange("p (h l) -> p h l", h=8), in0=S32[:, gs].rearrange("p (h l) -> p h l", h=8),
                                                                              in1=DEC[:, g * 8:(g + 1) * 8].unsqueeze(2).to_broadcast([128, 8, 64]), op=ALU.mult), reads=[S32, DEC], writes=[S32])
                        p.op("dve", lambda e, g=g, gs=gs: e.tensor_tensor(out=S32[:, gs], in0=S32[:, gs], in1=bd[g][:, :], op=ALU.add), reads=[S32, bd[g]], writes=[S32])
                    p.op("act", lambda e: e.copy(out=Sbf[:, :], in_=S32[:, :]), reads=[S32], writes=[Sbf])
                for a in range(8):
                    bank = self.ps[a // 4]
                    p.op("pe", lambda e, bank=bank, a=a: e.transpose(out=bank[:, (a % 4) * 128:(a % 4 + 1) * 128], in_=S32[:, a * 128:(a + 1) * 128], identity=self.ident[:, :]),
                         reads=[S32, self.ident], writes=[bank])
                for hf in range(2):
                    p.op("dve", lambda e, hf=hf: e.tensor_copy(out=stg[:, hf * 4:(hf + 1) * 4, :], in_=self.ps[hf][:, :].rearrange("p (a n) -> p a n", a=4)), reads=[self.ps[hf]], writes=[stg])
                p.dma(self.ssd_out[seq, :, :, :].rearrange("h q n -> (h q) n").rearrange("(a q) n -> q a n", q=128), stg[:, :, :], reads=[stg], writes=[(self.ssd_out, seq)])
            p.barrier()


W_NAMES = ["norm_w", "l0_w_in", "l0_conv_w", "l0_conv_b", "l0_rg_w_r", "l0_rg_b_r", "l0_rg_w_i", "l0_rg_b_i",
           "l0_rg_lambda", "l0_lq1", "l0_lk1", "l0_lq2", "l0_lk2", "l0_subln_w", "l0_w_out", "l1_w_in",
           "l1_conv_w", "l1_conv_b", "l1_dt_bias", "l1_a_log", "l1_d_skip", "l1_ssd_norm_w",
           "l1_hg_lower_bound", "l1_hg_norm_w", "l1_w_out", "ffn_w_gate", "ffn_w_up", "ffn_w_down"]


def make_consts():
    ident = np.eye(128, dtype=np.float32)
    mask = np.zeros((128, 4, 512), np.float32)
    for j in range(4):
        kc = (j * 128 + np.arange(128)) // 64
        qc = np.arange(512) // 64
        mask[:, j, :] = (kc[:, None] <= qc[None, :]).astype(np.float32)
    jj = np.arange(128)
    m16 = ((jj[:, None] // 16 == jj[None, :] // 16) & (jj[:, None] <= jj[None, :])).astype(np.float32)
    j6 = np.arange(64)
    tri = (j6[:, None] <= j6[None, :]).astype(np.float32)
    cs = np.zeros((64, 5 * 64 + 128), np.float32)
    cs[:, 0:64] = tri
    cs[:, 64:128] = -tri
    cs[:, 128:192] = 1.0
    cs[:, 192:256] = np.where(tri > 0, 0.0, -30000.0)
    cs[:, 256:320] = np.eye(64)
    cs[63, 320:448] = 1.0
    return {"c_ident": ident, "c_mask": mask.reshape(128, 2048), "c_m16": m16, "c_ssd": cs}


def core_inputs(inp, c):
    f = lambda a: np.ascontiguousarray(np.asarray(a, dtype=np.float32))
    s0, s1 = NSAMP * c, NSAMP * (c + 1)
    LP = inp["x_prompt"].shape[1]
    m = {}
    m["x"] = f(np.concatenate([inp["x_prompt"][c].reshape(LP, D), inp["x_sample"][s0:s1].reshape(NSAMP * DSEQ, D)], 0))
    past = inp["cache_diff_k"].shape[1]
    m["ck"] = f(inp["cache_diff_k"][s0:s1].reshape(NSAMP, past, 1024))
    m["cv"] = f(inp["cache_diff_v"][s0:s1].reshape(NSAMP, past, 1024))
    m["st_rg_conv"] = f(inp["state_rglru_conv"][s0:s1])
    m["st_rg_h"] = f(inp["state_rglru_h"][s0:s1])
    m["st_ssd_conv"] = f(inp["state_ssd_conv"][s0:s1])
    m["st_ssd"] = f(inp["state_ssd"][s0:s1])
    m["st_hg"] = f(inp["state_hgrn"][s0:s1])
    for n in W_NAMES:
        m[n] = f(inp[n])
    m.update(make_consts())
    return m


_CACHE = {}


def get_builder(LP, PAST, debug=False):
    key = (LP, PAST, debug)
    if key not in _CACHE:
        b = Builder(LP, PAST, debug)
        b.build()
        _CACHE[key] = b
    return _CACHE[key]


def run_cores(inp, debug=False):
    LP = inp["x_prompt"].shape[1]
    PAST = inp["cache_diff_k"].shape[1]
    b = get_builder(LP, PAST, debug)
    maps = [core_inputs(inp, c % 2) for c in range(2)]
    import os
    ncores = int(os.environ.get("K_NCORES", "8"))
    in_maps = [maps[c % 2] for c in range(ncores)]
    res = run_bass_kernel_spmd(b.nc, in_maps, core_ids=list(range(ncores)))
    return b, res.results


def kernel(**inp):
    b, r = run_cores(inp)
    LP = b.LP
    g = lambda name: [np.asarray(r[min(c, len(r) - 1)][name]) for c in range(2)]
    y, ko, vo = g("y"), g("k_out"), g("v_out")
    rgc, rgh, sc, so, ho = g("rg_conv_out"), g("rg_h_out"), g("ssd_conv_out"), g("ssd_out"), g("hg_out")
    P = lambda a, shp: np.stack([a[c][:LP] for c in range(2)], 0).reshape(shp).astype(np.float32)
    S = lambda a, shp: np.concatenate([a[c][LP:] for c in range(2)], 0).reshape(shp).astype(np.float32)
    P0 = lambda a: np.stack([a[c][0] for c in range(2)], 0).astype(np.float32)
    S0 = lambda a: np.concatenate([a[c][1:] for c in range(2)], 0).astype(np.float32)
    return (P(y, (2, LP, D)), S(y, (16, DSEQ, D)),
            P(ko, (2, LP, 8, 128)), P(vo, (2, LP, 8, 128)), P0(rgc), P0(rgh), P0(sc), P0(so), P0(ho),
            S(ko, (16, DSEQ, 8, 128)), S(vo, (16, DSEQ, 8, 128)), S0(rgc), S0(rgh), S0(sc), S0(so), S0(ho))
```

```python
import numpy as np
import ml_dtypes
from contextlib import ExitStack
import concourse.bass as bass
import concourse.mybir as mybir
from concourse.bass_utils import run_bass_kernel_spmd

F32 = mybir.dt.float32
BF16 = mybir.dt.bfloat16
ALU = mybir.AluOpType
AF = mybir.ActivationFunctionType
AX = mybir.AxisListType

D = 2048
TS = 512
NSAMP = 8
DSEQ = 64
EPS = 1e-6
D_FF = 5632
IN_EVEN = 5120
IN_ODD = 5648
ENGS = ("pe", "act", "dve", "pool", "sp")
EPOCH = 12000
NDMASEM = 40


class Res:
    __slots__ = ("w", "r")

    def __init__(self):
        self.w = None
        self.r = []


class Tile:
    def __init__(self, h, name=""):
        self.h = h
        self.name = name
        self.res = {None: Res()}

    def __getitem__(self, k):
        return self.h[k]

    def get(self, key):
        if key not in self.res:
            self.res[key] = Res()
        return self.res[key]


class Ins:
    __slots__ = ("eng", "fn", "deps", "is_dma", "sig", "sigidx", "dsem", "dval", "dprev", "pos")

    def __init__(self, eng, fn, is_dma):
        self.eng = eng
        self.fn = fn
        self.is_dma = is_dma
        self.deps = set()
        self.sig = False
        self.sigidx = -1
        self.dsem = -1
        self.dval = 0
        self.dprev = 0


def _norm_acc(a):
    if isinstance(a, Tile):
        return a, None
    return a


def _split_psum(reads, writes):
    r2, w2 = [], list(writes)
    for a in reads:
        t, k = _norm_acc(a)
        if getattr(t, "is_psum", False):
            w2.append(t)
        else:
            r2.append(a)
    w3 = []
    for a in w2:
        t, k = _norm_acc(a)
        w3.append(t if getattr(t, "is_psum", False) else a)
    return r2, w3


class Prog:
    def __init__(self, nc):
        self.nc = nc
        self.streams = {e: [] for e in ENGS}
        self.last = {e: None for e in ENGS}
        self.pending = {e: set() for e in ENGS}
        self.ndma = 0
        self.dma_last = [None] * NDMASEM
        self.dma_val = [0] * NDMASEM

    def _collect(self, ins, reads, writes):
        reads, writes = _split_psum(reads, writes)
        deps = ins.deps
        for a in reads:
            t, k = _norm_acc(a)
            rs = list(t.res.values()) if k is None else [t.get(k), t.res[None]]
            for r in rs:
                if r.w is not None:
                    deps.add(r.w)
        for a in writes:
            t, k = _norm_acc(a)
            rs = list(t.res.values()) if k is None else [t.get(k), t.res[None]]
            for r in rs:
                if r.w is not None:
                    if r.w.is_dma or r.w.eng != ins.eng or ins.is_dma:
                        deps.add(r.w)
                for x in r.r:
                    if x.is_dma or x.eng != ins.eng or ins.is_dma:
                        deps.add(x)
        for a in reads:
            t, k = _norm_acc(a)
            r = t.get(k)
            if not ins.is_dma:
                r.r = [x for x in r.r if x.is_dma or x.eng != ins.eng]
            r.r.append(ins)
        for a in writes:
            t, k = _norm_acc(a)
            if k is None:
                for r in t.res.values():
                    r.w = ins
                    r.r = []
            else:
                r = t.get(k)
                r.w = ins
                r.r = []
        deps.discard(ins)
        if self.pending[ins.eng]:
            deps.update(self.pending[ins.eng])
            self.pending[ins.eng] = set()

    _rec = None

    def start_record(self):
        self._rec = []

    def stop_record(self):
        r, self._rec = self._rec, None
        return r

    def replay(self, entries):
        for e in entries:
            if e[0] == "op":
                self.op(e[1], e[2], e[3], e[4])
            elif e[0] == "dma":
                self.dma(e[1], e[2], e[3], e[4], e[5], **e[6])

    @staticmethod
    def merge(a, b):
        a = [e for e in a if e[0] != "bar"]
        b = [e for e in b if e[0] != "bar"]
        out, i, j = [], 0, 0
        la, lb = max(len(a), 1), max(len(b), 1)
        while i < len(a) or j < len(b):
            if j >= len(b) or (i < len(a) and i * lb <= j * la):
                out.append(a[i]); i += 1
            else:
                out.append(b[j]); j += 1
        return out

    def op(self, eng, fn, reads=(), writes=()):
        if self._rec is not None:
            self._rec.append(("op", eng, fn, list(reads), list(writes)))
            return None
        ins = Ins(eng, fn, False)
        self._collect(ins, reads, writes)
        self.streams[eng].append(ins)
        self.last[eng] = ins
        return ins

    def dma(self, out_ap, in_ap, reads=(), writes=(), q="sp", **kw):
        if self._rec is not None:
            self._rec.append(("dma", out_ap, in_ap, list(reads), list(writes), q, kw))
            return None

        def fn(e, out_ap=out_ap, in_ap=in_ap, kw=kw):
            return e.dma_start(out=out_ap, in_=in_ap, **kw)
        ins = Ins(q, fn, True)
        self._collect(ins, reads, writes)
        i = self.ndma % NDMASEM
        self.ndma += 1
        ins.dsem = i
        ins.dprev = self.dma_val[i]
        self.dma_val[i] += 16
        ins.dval = self.dma_val[i]
        self.dma_last[i] = ins
        self.streams[q].append(ins)
        self.last[q] = ins
        return ins

    def barrier(self):
        if self._rec is not None:
            self._rec.append(("bar",))
            return
        lasts = [x for x in self.last.values() if x is not None]
        dl = [x for x in self.dma_last if x is not None]
        for e in ENGS:
            self.pending[e].update(x for x in lasts if x.eng != e or x.is_dma)
            self.pending[e].update(dl)

    def emit(self, es):
        nc = self.nc
        for e in ENGS:
            for ins in self.streams[e]:
                for d in ins.deps:
                    if not d.is_dma:
                        d.sig = True
        nsig = {}
        for e in ENGS:
            c = 0
            for ins in self.streams[e]:
                if ins.sig and not ins.is_dma:
                    ins.sigidx = c
                    c += 1
            nsig[e] = c
        esems = {e: [es.enter_context(nc.semaphore(f"s_{e}_{j}")) for j in range(nsig[e] // EPOCH + 1)]
                 for e in ENGS}
        dsems = [es.enter_context(nc.semaphore(f"s_dma_{j}")) for j in range(NDMASEM)]
        block = es.enter_context(nc.Block())
        engobj = {"pe": block.tensor, "act": block.scalar, "dve": block.vector, "pool": block.gpsimd,
                  "sp": block.sync}

        def run(eng_name):
            def body(e):
                waited = {}
                for ins in self.streams[eng_name]:
                    waits = {}
                    for d in ins.deps:
                        if d.is_dma:
                            key, val, sem = ("d", d.dsem), d.dval, dsems[d.dsem]
                        else:
                            j = d.sigidx // EPOCH
                            key, val, sem = (d.eng, j), d.sigidx % EPOCH + 1, esems[d.eng][j]
                        if waits.get(key, (0, None))[0] < val:
                            waits[key] = (val, sem)
                    if ins.is_dma and ins.dprev > 0:
                        key = ("d", ins.dsem)
                        if waits.get(key, (0, None))[0] < ins.dprev:
                            waits[key] = (ins.dprev, dsems[ins.dsem])
                    for key, (val, sem) in waits.items():
                        if waited.get(key, 0) < val:
                            e.wait_ge(sem, val)
                            waited[key] = val
                    bi = ins.fn(e)
                    if ins.is_dma:
                        bi.then_inc(dsems[ins.dsem], 16)
                    elif ins.sig:
                        j = ins.sigidx // EPOCH
                        bi.then_inc(esems[eng_name][j], 1)
                if eng_name == "sp":
                    for i in range(NDMASEM):
                        if self.dma_val[i] > 0:
                            e.wait_ge(dsems[i], self.dma_val[i])
            return body

        for en in ENGS:
            engobj[en](run(en))


class Builder:
    def __init__(self, LP, PAST, debug=False):
        self.LP = LP
        self.PAST = PAST
        self.T = LP + NSAMP * DSEQ
        self.NT = self.T // TS
        self.debug = debug
        self.nc = bass.Bass("TRN2", target_bir_lowering=False)
        self.p = Prog(self.nc)
        self.es = ExitStack()
        self.dbg_names = []

    def din(self, name, shape, dt=F32):
        return Tile(self.nc.dram_tensor(name, list(shape), dt, kind="ExternalInput").ap(), name)

    def dout(self, name, shape, dt=F32):
        return Tile(self.nc.dram_tensor(name, list(shape), dt, kind="ExternalOutput").ap(), name)

    def dscr(self, name, shape, dt):
        if self.debug:
            self.dbg_names.append(name)
            return Tile(self.nc.dram_tensor(name, list(shape), dt, kind="ExternalOutput").ap(), name)
        return Tile(self.nc.dram_tensor(name, list(shape), dt).ap(), name)

    def sb(self, st, name, shape, dt):
        self._sbn = getattr(self, "_sbn", 0) + 1
        name = f"{name}_{self._sbn}"
        return Tile(st.enter_context(self.nc.sbuf_tensor(name, list(shape), dt)), name)

    def build(self):
        nc, p, T, NT, LP, PAST = self.nc, self.p, self.T, self.NT, self.LP, self.PAST
        es = self.es
        with es:
            self.x = self.din("x", [T, D])
            self.ck = self.din("ck", [NSAMP, PAST, 1024])
            self.cv = self.din("cv", [NSAMP, PAST, 1024])
            self.st_rg_conv = self.din("st_rg_conv", [NSAMP, 3, 1024])
            self.st_rg_h = self.din("st_rg_h", [NSAMP, 1024])
            self.st_ssd_conv = self.din("st_ssd_conv", [NSAMP, 3, 1536])
            self.st_ssd = self.din("st_ssd", [NSAMP, 16, 64, 128])
            self.st_hg = self.din("st_hg", [NSAMP, 8, 128, 128])
            self.norm_w = self.din("norm_w", [2, 4, D])
            self.w_in0 = self.din("l0_w_in", [D, IN_EVEN])
            self.l0_conv_w = self.din("l0_conv_w", [4, 1024])
            self.l0_conv_b = self.din("l0_conv_b", [1024])
            self.l0_w_r = self.din("l0_rg_w_r", [8, 128, 128])
            self.l0_b_r = self.din("l0_rg_b_r", [1024])
            self.l0_w_i = self.din("l0_rg_w_i", [8, 128, 128])
            self.l0_b_i = self.din("l0_rg_b_i", [1024])
            self.l0_lam = self.din("l0_rg_lambda", [1024])
            self.l0_lq1 = self.din("l0_lq1", [64])
            self.l0_lk1 = self.din("l0_lk1", [64])
            self.l0_lq2 = self.din("l0_lq2", [64])
            self.l0_lk2 = self.din("l0_lk2", [64])
            self.l0_subln = self.din("l0_subln_w", [128])
            self.w_out0 = self.din("l0_w_out", [D, D])
            self.w_in1 = self.din("l1_w_in", [D, IN_ODD])
            self.l1_conv_w = self.din("l1_conv_w", [4, 1536])
            self.l1_conv_b = self.din("l1_conv_b", [1536])
            self.l1_dt_bias = self.din("l1_dt_bias", [16])
            self.l1_a_log = self.din("l1_a_log", [16])
            self.l1_d_skip = self.din("l1_d_skip", [16])
            self.l1_ssd_nw = self.din("l1_ssd_norm_w", [1024])
            self.l1_hg_lb = self.din("l1_hg_lower_bound", [2, 1024])
            self.l1_hg_nw = self.din("l1_hg_norm_w", [128])
            self.w_out1 = self.din("l1_w_out", [D, D])
            self.wg = self.din("ffn_w_gate", [2, D, D_FF])
            self.wu = self.din("ffn_w_up", [2, D, D_FF])
            self.wd = self.din("ffn_w_down", [2, D_FF, D])
            self.c_ident = self.din("c_ident", [128, 128])
            self.c_mask = self.din("c_mask", [128, 4 * 512], BF16)
            self.c_m16 = self.din("c_m16", [128, 128])
            self.c_ssd = self.din("c_ssd", [64, 5 * 64 + 128])

            self.y = self.dout("y", [T, D])
            self.k_out = self.dout("k_out", [T, 1024])
            self.v_out = self.dout("v_out", [T, 1024])
            self.rg_conv_out = self.dout("rg_conv_out", [1 + NSAMP, 3, 1024])
            self.rg_h_out = self.dout("rg_h_out", [1 + NSAMP, 1024])
            self.ssd_conv_out = self.dout("ssd_conv_out", [1 + NSAMP, 3, 1536])
            self.ssd_out = self.dout("ssd_out", [1 + NSAMP, 16, 64, 128])
            self.hg_out = self.dout("hg_out", [1 + NSAMP, 8, 128, 128])

            self.ps = [Tile(es.enter_context(nc.psum_tensor(f"ps{i}", [128, 512], F32)), f"ps{i}")
                       for i in range(8)]
            for t_ in self.ps:
                t_.is_psum = True
            self.ident = self.sb(es, "ident", [128, 128], F32)
            p.dma(self.ident[:, :], self.c_ident[:, :], reads=[self.c_ident], writes=[self.ident])
            self.eps_t = self.sb(es, "eps_t", [128, 1], F32)
            p.op("dve", lambda e: e.memset(self.eps_t[:, :], EPS), writes=[self.eps_t])

            self.stage_convert_first()
            self.stage_inproj(0)
            with ExitStack() as st:
                p.start_record()
                self.stage_rglru(st)
                self.stage_convert_rest(st)
                rb = p.stop_record()
                p.start_record()
                self.stage_cacheprep(st)
                self.stage_attn(st)
                ra = p.stop_record()
                p.replay(Prog.merge(ra, rb))
                p.barrier()
            self.stage_outffn(0)
            self.stage_inproj(1)
            self.stage_ssdconv()
            self.stage_ssd()
            self.stage_hgrn()
            self.stage_outffn(1)
            p.barrier()
            p.emit(es)
        return nc

    BUFW = 1536

    def conv_w(self, name, W, r0c0, K, groups, bufs, engs=("dve", "pool", "act")):
        p = self.p
        KC = K // 128
        outs = [self.dscr(f"{name}_g{gi}", [128, KC, w], BF16) for gi, (c0, w) in enumerate(groups)]
        batches, cur = [], []
        for gi, (c0, w) in enumerate(groups):
            if cur and (c0 + w - groups[cur[0]][0] > self.BUFW or c0 != groups[cur[-1]][0] + groups[cur[-1]][1]):
                batches.append(cur)
                cur = []
            cur.append(gi)
        batches.append(cur)
        for kc in range(KC):
            src = r0c0(kc)
            for bt in batches:
                self._cvn = getattr(self, "_cvn", 0) + 1
                f32t, bft = bufs[self._cvn % 2]
                lo = groups[bt[0]][0]
                hi = groups[bt[-1]][0] + groups[bt[-1]][1]
                n = hi - lo
                p.dma(f32t[:, 0:n], src[:, lo:hi], reads=[W], writes=[f32t])
                eng = engs[self._cvn % len(engs)]
                if eng == "act":
                    p.op("act", lambda e, a=bft[:, 0:n], b=f32t[:, 0:n]: e.copy(out=a, in_=b), reads=[f32t], writes=[bft])
                else:
                    p.op(eng, lambda e, a=bft[:, 0:n], b=f32t[:, 0:n]: e.tensor_copy(out=a, in_=b), reads=[f32t], writes=[bft])
                for gi in bt:
                    c0, w = groups[gi]
                    p.dma(outs[gi][:, kc, :], bft[:, c0 - lo:c0 - lo + w], reads=[bft], writes=[(outs[gi], kc)])
        return outs

    def cv_bufs(self, st):
        return [(self.sb(st, f"cvf{i}", [128, self.BUFW], F32), self.sb(st, f"cvb{i}", [128, self.BUFW], BF16)) for i in range(2)]

    def stage_convert_first(self):
        g512 = lambda n: [(i * 512, 512) for i in range(n // 512)]
        with ExitStack() as st:
            bufs = self.cv_bufs(st)
            self.Wt_in0 = self.conv_w("wt_in0", self.w_in0, lambda kc: self.w_in0[kc * 128:(kc + 1) * 128, :], D, g512(5120), bufs)
            self.p.barrier()

    def stage_convert_rest(self, st, engs=("dve", "pool")):
        g512 = lambda n: [(i * 512, 512) for i in range(n // 512)]
        bufs = self.cv_bufs(st)
        self.Wt_out0 = self.conv_w("wt_out0", self.w_out0, lambda kc: self.w_out0[kc * 128:(kc + 1) * 128, :], D, g512(2048), bufs, engs)
        self.Wt_g, self.Wt_u, self.Wt_d = [None, None], [None, None], [None, None]
        g256 = [(i * 256, 256) for i in range(22)]
        for L in range(2):
            self.Wt_g[L] = self.conv_w(f"wt_g{L}", self.wg, lambda kc, L=L: self.wg[L, kc * 128:(kc + 1) * 128, :], D, g256, bufs, engs)
            self.Wt_u[L] = self.conv_w(f"wt_u{L}", self.wu, lambda kc, L=L: self.wu[L, kc * 128:(kc + 1) * 128, :], D, g256, bufs, engs)
            self.Wt_d[L] = self.conv_w(f"wt_d{L}", self.wd, lambda kc, L=L: self.wd[L, kc * 128:(kc + 1) * 128, :], D_FF, g512(2048), bufs, engs)
            if L == 0:
                g1 = g512(2560) + [(2560, 16)] + [(2576 + i * 512, 512) for i in range(6)]
                self.Wt_in1 = self.conv_w("wt_in1", self.w_in1, lambda kc: self.w_in1[kc * 128:(kc + 1) * 128, :], D, g1, bufs, engs)
                self.Wt_out1 = self.conv_w("wt_out1", self.w_out1, lambda kc: self.w_out1[kc * 128:(kc + 1) * 128, :], D, g512(2048), bufs, engs)
        self.p.barrier()

    def norm_hT(self, xt, wB, hT, t, tmp, h32, ss, rstd, tb, x_ap=None, x_key=None):
        p = self.p
        if x_ap is None:
            x_ap = xt[:, :]
        xr = xt if x_key is None else (xt, x_key)
        self.rms_stats(x_ap, xr, tmp, ss, rstd, D)
        p.op("dve", lambda e: e.scalar_tensor_tensor(out=h32[:, :], in0=x_ap, scalar=rstd[:, 0:1], in1=wB[:, :],
                                                     op0=ALU.mult, op1=ALU.mult),
             reads=[xr, rstd, wB], writes=[h32])
        for kq in range(4):
            bank = tb[kq % 2]
            for j in range(4):
                k = kq * 4 + j
                p.op("pe", lambda e, bank=bank, j=j, k=k: e.transpose(out=bank[:, j * 128:(j + 1) * 128],
                                                                     in_=h32[:, k * 128:(k + 1) * 128],
                                                                     identity=self.ident[:, :]),
                     reads=[h32, self.ident], writes=[bank])
            dst = hT[:, kq * 4:(kq + 1) * 4, t * 128:(t + 1) * 128]
            srcv = bank[:, :].rearrange("p (a b) -> p a b", a=4)
            if kq % 2 == 0:
                p.op("act", lambda e, dst=dst, srcv=srcv: e.copy(out=dst, in_=srcv), reads=[bank],
                     writes=[(hT, (kq, t))])
            else:
                p.op("dve", lambda e, dst=dst, srcv=srcv: e.tensor_copy(out=dst, in_=srcv), reads=[bank],
                     writes=[(hT, (kq, t))])

    def rms_stats(self, x_ap, x_tile, tmp, ss, rstd, n, width=None):
        p = self.p
        w = n if width is None else width
        p.op("dve", lambda e: e.memset(ss[:, 0:1], 0.0), writes=[ss])
        p.op("act", lambda e: e.activation(out=tmp[:, 0:w], in_=x_ap, func=AF.Square, accum_out=ss[:, 0:1]),
             reads=[x_tile, ss], writes=[tmp, ss])
        p.op("act", lambda e: e.activation(out=rstd[:, 0:1], in_=ss[:, 0:1], func=AF.Sqrt, scale=1.0 / n,
                                           bias=self.eps_t[:, 0:1]),
             reads=[ss, self.eps_t], writes=[rstd])
        p.op("dve", lambda e: e.reciprocal(out=rstd[:, 0:1], in_=rstd[:, 0:1]), reads=[rstd], writes=[rstd])

    def load_wB(self, st, name, L, i):
        wB = self.sb(st, name, [128, D], F32)
        self.p.dma(wB[:, :], self.norm_w[L, i:i + 1, :].to_broadcast([128, D]), reads=[self.norm_w], writes=[wB])
        return wB

    def stage_inproj(self, L):
        p, T, NT = self.p, self.T, self.NT
        src = self.x if L == 0 else self.X
        if L == 0:
            self.XA = self.dscr("XA", [1024, T], F32)
            self.GA = self.dscr("GA", [1024, T], F32)
            self.QT = self.dscr("QT", [1024, T], BF16)
            self.KT = self.dscr("KT", [1024, T], BF16)
            self.Vb = self.dscr("Vb", [T, 1024], BF16)
            Wt = self.Wt_in0
            A = [(Wt[0], 512, self.XA, 0, F32), (Wt[1], 512, self.XA, 512, F32),
                 (Wt[2], 512, self.GA, 0, F32), (Wt[3], 512, self.GA, 512, F32),
                 (Wt[4], 512, self.QT, 0, BF16), (Wt[5], 512, self.QT, 512, BF16),
                 (Wt[6], 512, self.KT, 0, BF16), (Wt[7], 512, self.KT, 512, BF16)]
            B = [(Wt[6], 512, [(self.k_out, 0, F32)]), (Wt[7], 512, [(self.k_out, 512, F32)]),
                 (Wt[8], 512, [(self.v_out, 0, F32), (self.Vb, 0, BF16)]),
                 (Wt[9], 512, [(self.v_out, 512, F32), (self.Vb, 512, BF16)])]
        else:
            self.XBC = self.dscr("XBC", [1536, T], F32)
            self.HQ = self.dscr("HQ", [1024, T], F32)
            self.HF = self.dscr("HF", [1024, T], F32)
            self.HI = self.dscr("HI", [1024, T], F32)
            self.Z = self.dscr("Z", [T, 1024], F32)
            self.DT = self.dscr("DT", [T, 16], F32)
            Wt = self.Wt_in1
            A = [(Wt[2], 512, self.XBC, 0, F32), (Wt[3], 512, self.XBC, 512, F32), (Wt[4], 512, self.XBC, 1024, F32),
                 (Wt[6], 512, self.HQ, 0, F32), (Wt[7], 512, self.HQ, 512, F32),
                 (Wt[8], 512, self.HF, 0, F32), (Wt[9], 512, self.HF, 512, F32),
                 (Wt[10], 512, self.HI, 0, F32), (Wt[11], 512, self.HI, 512, F32)]
            B = [(Wt[0], 512, [(self.Z, 0, F32)]), (Wt[1], 512, [(self.Z, 512, F32)]),
                 (Wt[5], 16, [(self.DT, 0, F32)])]
        with ExitStack() as st:
            wB = self.load_wB(st, "wB", L, 0)
            xts = [self.sb(st, f"xt{i}", [128, D], F32) for i in range(2)]
            tmp = self.sb(st, "tmp", [128, D], BF16)
            h32s = [self.sb(st, f"h32_{i}", [128, D], F32) for i in range(4)]
            ss = self.sb(st, "ss", [128, 1], F32)
            rstd = self.sb(st, "rstd", [128, 1], F32)
            hTs = [self.sb(st, f"hT{i}", [128, 16, TS], BF16) for i in range(2)]
            wts = [self.sb(st, f"wt{i}", [128, 16, 512], BF16) for i in range(3)]
            sgf = [self.sb(st, f"sgf{i}", [128, 512], F32) for i in range(3)]
            sgb = [self.sb(st, f"sgb{i}", [128, 512], BF16) for i in range(3)]
            accb = self.ps[2:8]
            cnt = {"si": 0, "bi": 0, "x": 0}
            jobs = [("A",) + a for a in A] + [("B",) + b for b in B]
            nj = len(jobs)
            loads = [(ti, j) for ti in range(NT) for j in range(nj)]
            nload = [0]

            def ensure_loaded(idx):
                while nload[0] <= min(idx + 2, len(loads) - 1):
                    k = nload[0]
                    W, width = jobs[loads[k][1]][1], jobs[loads[k][1]][2]
                    wt = wts[k % 3]
                    p.dma(wt[:, :, 0:width], W[:, :, :], reads=[W], writes=[wt])
                    nload[0] += 1
                return wts[idx % 3]

            def norm1(ti, t):
                xt = xts[cnt["x"] % 2]
                cnt["x"] += 1
                r0 = ti * TS + t * 128
                p.dma(xt[:, :], src[r0:r0 + 128, :], reads=[(src, ti)], writes=[xt])
                self.rms_stats(xt[:, :], xt, tmp, ss, rstd, D)
                h32 = h32s[t]
                p.op("dve", lambda e, h32=h32, xt=xt: e.scalar_tensor_tensor(out=h32[:, :], in0=xt[:, :], scalar=rstd[:, 0:1], in1=wB[:, :],
                                                                         op0=ALU.mult, op1=ALU.mult), reads=[xt, rstd, wB], writes=[h32])

            def norm2(ti, t):
                h32, hT = h32s[t], hTs[ti % 2]
                for kq in range(4):
                    bank = self.ps[kq % 2]
                    for j in range(4):
                        k = kq * 4 + j
                        p.op("pe", lambda e, bank=bank, j=j, k=k, h32=h32: e.transpose(out=bank[:, j * 128:(j + 1) * 128], in_=h32[:, k * 128:(k + 1) * 128],
                                                                                   identity=self.ident[:, :]), reads=[h32, self.ident], writes=[bank])
                    dstv = hT[:, kq * 4:(kq + 1) * 4, t * 128:(t + 1) * 128]
                    srcv = bank[:, :].rearrange("p (a b) -> p a b", a=4)
                    if kq % 2 == 0:
                        p.op("act", lambda e, dstv=dstv, srcv=srcv: e.copy(out=dstv, in_=srcv), reads=[bank], writes=[(hT, (kq, t))])
                    else:
                        p.op("dve", lambda e, dstv=dstv, srcv=srcv: e.tensor_copy(out=dstv, in_=srcv), reads=[bank], writes=[(hT, (kq, t))])

            for t in range(4):
                norm1(0, t)
            for t in range(4):
                norm2(0, t)
            for ti in range(NT):
                hT = hTs[ti % 2]
                for j, job in enumerate(jobs):
                    wt = ensure_loaded(ti * nj + j)
                    if ti + 1 < NT:
                        t1_ = j - (nj - 6)
                        if 0 <= t1_ < 4:
                            norm1(ti + 1, t1_)
                        t2_ = j - (nj - 5)
                        if 0 <= t2_ < 4:
                            norm2(ti + 1, t2_)
                    if job[0] == "A":
                        _, W, width, dst, row0, dt = job
                        for n in range(width // 128):
                            bank = accb[cnt["bi"] % 6]
                            cnt["bi"] += 1
                            for k in range(16):
                                p.op("pe", lambda e, bank=bank, wt=wt, hT=hT, k=k, n=n: e.matmul(
                                    bank[:, :], lhsT=wt[:, k, n * 128:(n + 1) * 128], rhs=hT[:, k, :],
                                    start=(k == 0), stop=(k == 15)), reads=[wt, hT], writes=[bank])
                            sg = (sgf if dt == F32 else sgb)[cnt["si"] % 3]
                            cnt["si"] += 1
                            self.evac(bank[:, :], sg[:, :], bank, sg, cnt["si"])
                            rr = row0 + n * 128
                            p.dma(dst[rr:rr + 128, ti * TS:(ti + 1) * TS], sg[:, :], reads=[sg], writes=[(dst, ti)])
                    else:
                        _, W, width, dsts = job
                        for t in range(4):
                            bank = accb[cnt["bi"] % 6]
                            cnt["bi"] += 1
                            for k in range(16):
                                p.op("pe", lambda e, bank=bank, wt=wt, hT=hT, k=k, t=t, width=width: e.matmul(
                                    bank[:, 0:width], lhsT=hT[:, k, t * 128:(t + 1) * 128], rhs=wt[:, k, 0:width],
                                    start=(k == 0), stop=(k == 15)), reads=[wt, hT], writes=[bank])
                            r0 = ti * TS + t * 128
                            sg0 = sgf[cnt["si"] % 3]
                            cnt["si"] += 1
                            self.evac(bank[:, 0:width], sg0[:, 0:width], bank, sg0, cnt["si"])
                            for (dst, c0, dt) in dsts:
                                if dt == F32:
                                    sg = sg0
                                else:
                                    sg = sgb[cnt["si"] % 3]
                                    cnt["si"] += 1
                                    p.op("pool", lambda e, a=sg[:, 0:width], b=sg0[:, 0:width]: e.tensor_copy(out=a, in_=b),
                                         reads=[sg0], writes=[sg])
                                p.dma(dst[r0:r0 + 128, c0:c0 + width], sg[:, 0:width], reads=[sg], writes=[(dst, ti)])
            p.barrier()

    def evac(self, src_ap, dst_ap, src_t, dst_t, i):
        if i % 2 == 0:
            self.p.op("act", lambda e: e.copy(out=dst_ap, in_=src_ap), reads=[src_t], writes=[dst_t])
        else:
            self.p.op("dve", lambda e: e.tensor_copy(out=dst_ap, in_=src_ap), reads=[src_t], writes=[dst_t])

    def stage_outffn(self, L):
        p, T, NT = self.p, self.T, self.NT
        src = self.x if L == 0 else self.X
        if L == 0:
            self.X = self.dscr("X", [T, D], F32)
        dst = self.X if L == 0 else self.y
        Wo = self.Wt_out0 if L == 0 else self.Wt_out1
        Wg, Wu, Wd = self.Wt_g[L], self.Wt_u[L], self.Wt_d[L]
        YT = self.YT
        with ExitStack() as st:
            wB = self.sb(st, "wBo", [128, D], F32)
            yT = self.sb(st, "yT", [128, 16, TS], BF16)
            wbufs = [self.sb(st, f"wb{i}", [128, 16, 512], BF16) for i in range(3)]
            m32 = self.sb(st, "m32", [128, 4, D], F32)
            d32 = self.sb(st, "d32", [128, 4, D], F32)
            xres = self.sb(st, "xres", [128, D], F32)
            actT = self.sb(st, "actT", [128, 44, TS], BF16)
            tmp = self.sb(st, "tmpo", [128, D], BF16)
            h32 = self.sb(st, "h32o", [128, D], F32)
            gsb = [self.sb(st, f"gsb{i}", [128, TS], F32) for i in range(2)]
            ss = self.sb(st, "sso", [128, 1], F32)
            rstd = self.sb(st, "rstdo", [128, 1], F32)
            wi = 0
            ei = 0
            gi_ = 0

            def load_wB(i):
                p.dma(wB[:, :], self.norm_w[L, i:i + 1, :].to_broadcast([128, D]), reads=[self.norm_w], writes=[wB])

            def postnorm_residual(buf, t, res_ap, res_reads, out_ap, out_writes):
                self.rms_stats(buf[:, t, :], (buf, t), tmp, ss, rstd, D)
                p.op("dve", lambda e: e.scalar_tensor_tensor(out=h32[:, :], in0=buf[:, t, :], scalar=rstd[:, 0:1],
                                                             in1=wB[:, :], op0=ALU.mult, op1=ALU.mult),
                     reads=[(buf, t), rstd, wB], writes=[h32])
                p.op("pool", lambda e: e.tensor_tensor(out=out_ap, in0=h32[:, :], in1=res_ap, op=ALU.add),
                     reads=[h32] + res_reads, writes=out_writes)

            for ti in range(NT):
                c0 = ti * TS
                p.dma(yT[:, :, :], YT[:, c0:c0 + TS].rearrange("(k q) t -> q k t", q=128), reads=[(YT, ti)], writes=[yT])
                for g in range(4):
                    wt = wbufs[wi % 3]
                    wi += 1
                    p.dma(wt[:, :, :], Wo[g][:, :, :], reads=[Wo[g]], writes=[wt])
                    for t in range(4):
                        bank = self.ps[(g % 2) * 4 + t]
                        for k in range(16):
                            p.op("pe", lambda e, bank=bank, wt=wt, k=k, t=t: e.matmul(
                                bank[:, :], lhsT=yT[:, k, t * 128:(t + 1) * 128], rhs=wt[:, k, :],
                                start=(k == 0), stop=(k == 15)), reads=[wt, yT], writes=[bank])
                        ei += 1
                        self.evac(bank[:, :], m32[:, t, g * 512:(g + 1) * 512], bank, (m32, t), ei)
                load_wB(1)
                for t in range(4):
                    r0 = c0 + t * 128
                    p.dma(xres[:, :], src[r0:r0 + 128, :], reads=[(src, ti)], writes=[xres])
                    postnorm_residual(m32, t, xres[:, :], [xres], m32[:, t, :], [(m32, t)])
                load_wB(2)
                for t in range(4):
                    self.norm_hT(m32, wB, yT, t, tmp, h32, ss, rstd, self.ps[6:8], x_ap=m32[:, t, :], x_key=t)
                hT = yT
                for j in range(22):
                    wt = wbufs[wi % 3]
                    wi += 1
                    p.dma(wt[:, :, 0:256], Wg[j][:, :, :], reads=[Wg[j]], writes=[(wt, 0)])
                    p.dma(wt[:, :, 256:512], Wu[j][:, :, :], reads=[Wu[j]], writes=[(wt, 1)])
                    for n in range(2):
                        f = j * 2 + n
                        bg = self.ps[(f % 3) * 2]
                        bu = self.ps[(f % 3) * 2 + 1]
                        for k in range(16):
                            p.op("pe", lambda e, bg=bg, wt=wt, k=k, n=n: e.matmul(
                                bg[:, :], lhsT=wt[:, k, n * 128:(n + 1) * 128], rhs=hT[:, k, :],
                                start=(k == 0), stop=(k == 15)), reads=[(wt, 0), hT], writes=[bg])
                        for k in range(16):
                            p.op("pe", lambda e, bu=bu, wt=wt, k=k, n=n: e.matmul(
                                bu[:, :], lhsT=wt[:, k, 256 + n * 128:256 + (n + 1) * 128], rhs=hT[:, k, :],
                                start=(k == 0), stop=(k == 15)), reads=[(wt, 1), hT], writes=[bu])
                        gs = gsb[gi_ % 2]
                        gi_ += 1
                        p.op("act", lambda e, gs=gs, bg=bg: e.activation(out=gs[:, :], in_=bg[:, :], func=AF.Silu),
                             reads=[bg], writes=[gs])
                        p.op("dve", lambda e, gs=gs, bu=bu, f=f: e.tensor_tensor(out=actT[:, f, :], in0=gs[:, :],
                                                                                in1=bu[:, :], op=ALU.mult),
                             reads=[gs, bu], writes=[(actT, f)])
                for g in range(4):
                    for part in range(4):
                        wt = wbufs[wi % 3]
                        wi += 1
                        p.dma(wt[:, 0:11, :], Wd[g][:, part * 11:(part + 1) * 11, :], reads=[Wd[g]], writes=[wt])
                        for fl in range(11):
                            f = part * 11 + fl
                            for t in range(4):
                                bank = self.ps[(g % 2) * 4 + t]
                                p.op("pe", lambda e, bank=bank, wt=wt, f=f, fl=fl, t=t: e.matmul(
                                    bank[:, :], lhsT=actT[:, f, t * 128:(t + 1) * 128], rhs=wt[:, fl, :],
                                    start=(f == 0), stop=(f == 43)), reads=[wt, (actT, f)], writes=[bank])
                    for t in range(4):
                        bank = self.ps[(g % 2) * 4 + t]
                        ei += 1
                        self.evac(bank[:, :], d32[:, t, g * 512:(g + 1) * 512], bank, (d32, t), ei)
                load_wB(3)
                for t in range(4):
                    r0 = c0 + t * 128
                    postnorm_residual(d32, t, m32[:, t, :], [(m32, t)], d32[:, t, :], [(d32, t)])
                    p.dma(dst[r0:r0 + 128, :], d32[:, t, :], reads=[(d32, t)], writes=[(dst, ti)])
            p.barrier()

    def small_T(self, dst_ap, src_ap, reads, writes):
        self.p.dma(dst_ap, src_ap, reads=reads, writes=writes, allow_slow_non_contiguous=True)

    def stage_rglru(self, st):
        p, T, NT, LP = self.p, self.T, self.NT, self.LP
        if not hasattr(self, "YT"):
            self.YT = self.dscr("YT", [D, T], BF16)
        if True:
            S = lambda n, shp, dt=F32: self.sb(st, n, shp, dt)
            cw = S("rg_cw", [128, 8, 4]); cb = S("rg_cb", [128, 8]); br = S("rg_br", [128, 8]); bi_ = S("rg_bi", [128, 8])
            lam = S("rg_lam", [128, 8]); cc = S("rg_c", [128, 8]); cc2 = S("rg_c2", [128, 8]); one = S("rg_one", [128, 1])
            wr32 = S("rg_wr32", [128, 8, 128]); wi32 = S("rg_wi32", [128, 8, 128])
            wrb = S("rg_wrb", [128, 8, 128], BF16); wib = S("rg_wib", [128, 8, 128], BF16)
            hst = S("rg_hst", [128, 8]); h0s = S("rg_h0s", [128, 8, NSAMP])
            for tap in range(4):
                self.small_T(cw[:, :, tap], self.l0_conv_w[tap, :].rearrange("(b c) -> c b", c=128), [self.l0_conv_w], [cw])
            for (dst, srcv) in ((cb, self.l0_conv_b), (br, self.l0_b_r), (bi_, self.l0_b_i), (lam, self.l0_lam)):
                self.small_T(dst[:, :], srcv[:].rearrange("(b c) -> c b", c=128), [srcv], [dst])
            p.dma(wr32[:, :, :], self.l0_w_r[:, :, :].rearrange("b i j -> i b j"), reads=[self.l0_w_r], writes=[wr32])
            p.dma(wi32[:, :, :], self.l0_w_i[:, :, :].rearrange("b i j -> i b j"), reads=[self.l0_w_i], writes=[wi32])
            for b in range(8):
                self.small_T(h0s[:, b, :], self.st_rg_h[:, b * 128:(b + 1) * 128].rearrange("s c -> c s"), [self.st_rg_h], [h0s])
            p.op("dve", lambda e: e.tensor_copy(out=wrb[:, :, :], in_=wr32[:, :, :]), reads=[wr32], writes=[wrb])
            p.op("dve", lambda e: e.tensor_copy(out=wib[:, :, :], in_=wi32[:, :, :]), reads=[wi32], writes=[wib])
            p.op("dve", lambda e: e.memset(one[:, :], 1.0), writes=[one])
            p.op("dve", lambda e: e.memset(hst[:, :], 0.0), writes=[hst])
            p.op("act", lambda e: e.activation(out=cc[:, :], in_=lam[:, :], func=AF.Exp, scale=-1.0), reads=[lam], writes=[cc])
            p.op("act", lambda e: e.activation(out=cc[:, :], in_=cc[:, :], func=AF.Ln, bias=one[:, 0:1]), reads=[cc, one], writes=[cc])
            p.op("dve", lambda e: e.tensor_scalar(out=cc[:, :], in0=cc[:, :], scalar1=-8.0, scalar2=None, op0=ALU.mult), reads=[cc], writes=[cc])
            p.op("dve", lambda e: e.tensor_scalar(out=cc2[:, :], in0=cc[:, :], scalar1=2.0, scalar2=None, op0=ALU.mult), reads=[cc], writes=[cc2])
            NB = 2
            xa = [S(f"rg_xa{i}", [128, 536]) for i in range(NB)]
            ga = [S(f"rg_ga{i}", [128, TS]) for i in range(NB)]
            xc = [S(f"rg_xc{i}", [128, TS]) for i in range(NB)]
            xcb = [S(f"rg_xcb{i}", [128, TS], BF16) for i in range(NB)]
            rr = [S(f"rg_r{i}", [128, TS]) for i in range(NB)]
            gg = [S(f"rg_g{i}", [128, TS]) for i in range(NB)]
            aa = [S(f"rg_a{i}", [128, TS]) for i in range(NB)]
            uu = [S(f"rg_u{i}", [128, TS]) for i in range(NB)]
            hh = [S(f"rg_h{i}", [128, TS]) for i in range(NB)]
            t1 = [S(f"rg_t1{i}", [128, TS]) for i in range(NB)]
            yo = [S(f"rg_yo{i}", [128, TS], BF16) for i in range(NB)]
            it = 0
            for ti in range(NT):
                samp = ti == NT - 1
                nseg, sl = (NSAMP, DSEQ) if samp else (1, TS)
                c0 = ti * TS
                for b in range(8):
                    i = it % NB
                    it += 1
                    X, G, XC, XCB, R_, GI, A_, U_, H_, T1, YO = xa[i], ga[i], xc[i], xcb[i], rr[i], gg[i], aa[i], uu[i], hh[i], t1[i], yo[i]
                    rows = slice(b * 128, (b + 1) * 128)
                    xv = X[:, 0:nseg * (3 + sl)].rearrange("p (s l) -> p s l", s=nseg)
                    if samp:
                        for sg in range(NSAMP):
                            self.small_T(xv[:, sg, 0:3], self.st_rg_conv[sg, :, rows].rearrange("t c -> c t"),
                                         [self.st_rg_conv], [X])
                    elif ti == 0:
                        p.op("dve", lambda e, xv=xv: e.memset(xv[:, :, 0:3], 0.0), writes=[X])
                    else:
                        p.dma(xv[:, 0, 0:3], self.XA[rows, c0 - 3:c0], reads=[(self.XA, ti - 1)], writes=[X])
                    p.dma(xv[:, :, 3:3 + sl], self.XA[rows, c0:c0 + TS].rearrange("p (s l) -> p s l", s=nseg),
                          reads=[(self.XA, ti)], writes=[X])
                    p.dma(G[:, :], self.GA[rows, c0:c0 + TS], reads=[(self.GA, ti)], writes=[G])
                    v3 = lambda tl: tl[:, :].rearrange("p (s l) -> p s l", s=nseg)
                    xc3 = v3(XC)
                    p.op("dve", lambda e, xc3=xc3, xv=xv, b=b, sl=sl: e.tensor_scalar(
                        out=xc3, in0=xv[:, :, 0:sl], scalar1=cw[:, b, 0:1], scalar2=cb[:, b:b + 1], op0=ALU.mult, op1=ALU.add),
                        reads=[X, cw, cb], writes=[XC])
                    for tap in range(1, 4):
                        p.op("dve", lambda e, xc3=xc3, xv=xv, b=b, sl=sl, tap=tap: e.scalar_tensor_tensor(
                            out=xc3, in0=xv[:, :, tap:tap + sl], scalar=cw[:, b, tap:tap + 1], in1=xc3, op0=ALU.mult, op1=ALU.add),
                            reads=[X, cw, XC], writes=[XC])
                    p.op("act", lambda e, XCB=XCB, XC=XC: e.copy(out=XCB[:, :], in_=XC[:, :]), reads=[XC], writes=[XCB])
                    bk_r = self.ps[3]
                    bk_i = self.ps[3]
                    p.op("pe", lambda e, bk_r=bk_r, XCB=XCB, b=b: e.matmul(bk_r[:, :], lhsT=wrb[:, b, :], rhs=XCB[:, :], start=True, stop=True),
                         reads=[wrb, XCB], writes=[bk_r])
                    p.op("act", lambda e, R_=R_, bk_r=bk_r, b=b: e.activation(out=R_[:, :], in_=bk_r[:, :], func=AF.Sigmoid, bias=br[:, b:b + 1]),
                         reads=[bk_r, br], writes=[R_])
                    p.op("pe", lambda e, bk_i=bk_i, XCB=XCB, b=b: e.matmul(bk_i[:, :], lhsT=wib[:, b, :], rhs=XCB[:, :], start=True, stop=True),
                         reads=[wib, XCB], writes=[bk_i])
                    p.op("act", lambda e, GI=GI, bk_i=bk_i, b=b: e.activation(out=GI[:, :], in_=bk_i[:, :], func=AF.Sigmoid, bias=bi_[:, b:b + 1]),
                         reads=[bk_i, bi_], writes=[GI])
                    p.op("act", lambda e, A_=A_, R_=R_, b=b: e.activation(out=A_[:, :], in_=R_[:, :], func=AF.Exp, scale=cc[:, b:b + 1]),
                         reads=[R_, cc], writes=[A_])
                    p.op("act", lambda e, T1=T1, R_=R_, b=b: e.activation(out=T1[:, :], in_=R_[:, :], func=AF.Exp, scale=cc2[:, b:b + 1]),
                         reads=[R_, cc2], writes=[T1])
                    p.op("act", lambda e, T1=T1: e.activation(out=T1[:, :], in_=T1[:, :], func=AF.Sqrt, scale=-1.0, bias=one[:, 0:1]),
                         reads=[T1, one], writes=[T1])
                    p.op("dve", lambda e, U_=U_, T1=T1, GI=GI: e.tensor_tensor(out=U_[:, :], in0=T1[:, :], in1=GI[:, :], op=ALU.mult),
                         reads=[T1, GI], writes=[U_])
                    p.op("dve", lambda e, U_=U_, XC=XC: e.tensor_tensor(out=U_[:, :], in0=U_[:, :], in1=XC[:, :], op=ALU.mult),
                         reads=[U_, XC], writes=[U_])
                    a3, u3, h3 = v3(A_), v3(U_), v3(H_)
                    for sg in range(nseg):
                        init = h0s[:, b, sg:sg + 1] if samp else hst[:, b:b + 1]
                        p.op("dve", lambda e, h3=h3, a3=a3, u3=u3, sg=sg, init=init: e.tensor_tensor_scan(
                            out=h3[:, sg, :], data0=a3[:, sg, :], data1=u3[:, sg, :], initial=init, op0=ALU.mult, op1=ALU.add),
                            reads=[A_, U_, hst, h0s], writes=[H_])
                    if not samp:
                        p.op("dve", lambda e, H_=H_, b=b: e.tensor_copy(out=hst[:, b:b + 1], in_=H_[:, TS - 1:TS]), reads=[H_], writes=[hst])
                    if samp:
                        self.small_T(self.rg_h_out[1:1 + NSAMP, rows].rearrange("s c -> c s"), h3[:, :, sl - 1],
                                     [H_], [(self.rg_h_out, ("s", b))])
                        for sg in range(NSAMP):
                            self.small_T(self.rg_conv_out[1 + sg, :, rows].rearrange("t c -> c t"), xv[:, sg, sl:sl + 3],
                                         [X], [(self.rg_conv_out, ("s", b, sg))])
                    elif ti == NT - 2:
                        self.small_T(self.rg_h_out[0:1, rows].rearrange("s c -> c s"), H_[:, TS - 1:TS],
                                     [H_], [(self.rg_h_out, ("p", b))])
                        self.small_T(self.rg_conv_out[0, :, rows].rearrange("t c -> c t"), X[:, TS:TS + 3],
                                     [X], [(self.rg_conv_out, ("p", b))])
                    p.op("act", lambda e, T1=T1, G=G: e.activation(out=T1[:, :], in_=G[:, :], func=AF.Square), reads=[G], writes=[T1])
                    p.op("dve", lambda e, T1=T1: e.tensor_scalar(out=T1[:, :], in0=T1[:, :], scalar1=0.044715, scalar2=1.0, op0=ALU.mult, op1=ALU.add),
                         reads=[T1], writes=[T1])
                    p.op("dve", lambda e, T1=T1, G=G: e.tensor_tensor(out=T1[:, :], in0=T1[:, :], in1=G[:, :], op=ALU.mult), reads=[T1, G], writes=[T1])
                    p.op("act", lambda e, T1=T1: e.activation(out=T1[:, :], in_=T1[:, :], func=AF.Sigmoid, scale=1.5957691216057308),
                         reads=[T1], writes=[T1])
                    p.op("dve", lambda e, T1=T1, G=G: e.tensor_tensor(out=T1[:, :], in0=T1[:, :], in1=G[:, :], op=ALU.mult), reads=[T1, G], writes=[T1])
                    p.op("dve", lambda e, T1=T1, H_=H_, YO=YO: e.tensor_tensor(out=YO[:, :], in0=T1[:, :], in1=H_[:, :], op=ALU.mult),
                         reads=[T1, H_], writes=[YO])
                    p.dma(self.YT[rows, c0:c0 + TS], YO[:, :], reads=[YO], writes=[(self.YT, ti)])
            p.barrier()

    def stage_cacheprep(self, st):
        p, PAST = self.p, self.PAST
        self.KcT = self.dscr("KcT", [NSAMP, 1024, PAST], BF16)
        self.Vcb = self.dscr("Vcb", [NSAMP, PAST, 1024], BF16)
        if True:
            kf = [self.sb(st, f"cp_kf{i}", [128, 1024], F32) for i in range(2)]
            vf = [self.sb(st, f"cp_vf{i}", [128, 1024], F32) for i in range(2)]
            vb = [self.sb(st, f"cp_vb{i}", [128, 1024], BF16) for i in range(2)]
            kt = [self.sb(st, f"cp_kt{i}", [128, 8, 128], BF16) for i in range(2)]
            it = 0
            for sg in range(NSAMP):
                for kb in range(PAST // 128):
                    i = it % 2
                    it += 1
                    r = slice(kb * 128, (kb + 1) * 128)
                    p.dma(kf[i][:, :], self.ck[sg, r, :], reads=[self.ck], writes=[kf[i]])
                    p.dma(vf[i][:, :], self.cv[sg, r, :], reads=[self.cv], writes=[vf[i]])
                    p.op("pool", lambda e, a=vb[i], b=vf[i]: e.tensor_copy(out=a[:, :], in_=b[:, :]), reads=[vf[i]], writes=[vb[i]])
                    p.dma(self.Vcb[sg, r, :], vb[i][:, :], reads=[vb[i]], writes=[(self.Vcb, sg)])
                    for hq in range(2):
                        bank = self.ps[hq]
                        for j in range(4):
                            h = hq * 4 + j
                            p.op("pe", lambda e, bank=bank, j=j, h=h, i=i: e.transpose(
                                out=bank[:, j * 128:(j + 1) * 128], in_=kf[i][:, h * 128:(h + 1) * 128], identity=self.ident[:, :]),
                                reads=[kf[i], self.ident], writes=[bank])
                        dstv = kt[i][:, hq * 4:(hq + 1) * 4, :]
                        srcv = bank[:, :].rearrange("p (a b) -> p a b", a=4)
                        if hq == 0:
                            p.op("act", lambda e, dstv=dstv, srcv=srcv: e.copy(out=dstv, in_=srcv), reads=[bank], writes=[(kt[i], hq)])
                        else:
                            p.op("dve", lambda e, dstv=dstv, srcv=srcv: e.tensor_copy(out=dstv, in_=srcv), reads=[bank], writes=[(kt[i], hq)])
                    p.dma(self.KcT[sg, :, r].rearrange("(h c) k -> c h k", c=128), kt[i][:, :, :], reads=[kt[i]], writes=[(self.KcT, sg)])
            p.barrier()

    def stage_attn(self, st):
        p, T, NT, LP, PAST = self.p, self.T, self.NT, self.LP, self.PAST
        if not hasattr(self, "YT"):
            self.YT = self.dscr("YT", [D, T], BF16)
        NKB = LP // 128
        NCB = PAST // 128
        lam_init = 0.8 - 0.6 * 1.0
        if True:
            S = lambda n, shp, dt=F32: self.sb(st, n, shp, dt)
            lq = S("at_lq", [128, 4, 64]); lsum = S("at_ls", [128, 2]); neglam = S("at_nl", [128, 1])
            for i, srcv in enumerate((self.l0_lq1, self.l0_lk1, self.l0_lq2, self.l0_lk2)):
                p.dma(lq[:, i, :], srcv[:].rearrange("(o d) -> o d", o=1).to_broadcast([128, 64]), reads=[srcv], writes=[lq])
            p.op("dve", lambda e: e.tensor_tensor(out=lq[:, 0, :], in0=lq[:, 0, :], in1=lq[:, 1, :], op=ALU.mult), reads=[lq], writes=[lq])
            p.op("dve", lambda e: e.tensor_tensor(out=lq[:, 2, :], in0=lq[:, 2, :], in1=lq[:, 3, :], op=ALU.mult), reads=[lq], writes=[lq])
            p.op("dve", lambda e: e.reduce_sum(out=lsum[:, 0:1], in_=lq[:, 0, :], axis=AX.X), reads=[lq], writes=[lsum])
            p.op("dve", lambda e: e.reduce_sum(out=lsum[:, 1:2], in_=lq[:, 2, :], axis=AX.X), reads=[lq], writes=[lsum])
            p.op("act", lambda e: e.activation(out=lsum[:, :], in_=lsum[:, :], func=AF.Exp), reads=[lsum], writes=[lsum])
            p.op("dve", lambda e: e.tensor_tensor(out=neglam[:, :], in0=lsum[:, 1:2], in1=lsum[:, 0:1], op=ALU.subtract), reads=[lsum], writes=[neglam])
            p.op("dve", lambda e: e.tensor_scalar(out=neglam[:, :], in0=neglam[:, :], scalar1=-lam_init, scalar2=None, op0=ALU.add), reads=[neglam], writes=[neglam])
            subw = S("at_subw", [128, 128])
            p.dma(subw[:, :], self.l0_subln[:].rearrange("(o d) -> o d", o=1).to_broadcast([128, 128]), reads=[self.l0_subln], writes=[subw])
            p.op("dve", lambda e: e.tensor_scalar(out=subw[:, :], in0=subw[:, :], scalar1=1.0 - lam_init, scalar2=None, op0=ALU.mult), reads=[subw], writes=[subw])
            maskb = S("at_mb", [128, 4, 512], BF16)
            p.dma(maskb[:, :, :], self.c_mask[:, :].rearrange("p (a b) -> p a b", a=4), reads=[self.c_mask], writes=[maskb])
            KTs = [S(f"at_kt{i}", [128, max(LP, PAST + DSEQ)], BF16) for i in range(2)]
            QTs = [S(f"at_qt{i}", [128, LP + NSAMP * DSEQ], BF16) for i in range(2)]
            Vs = [S(f"at_v{i}", [128, max(NKB, NCB + 1), 130], BF16) for i in range(2)]
            for i in range(2):
                p.op("pool", lambda e, i=i: e.memset(Vs[i][:, :, 128:130], 1.0), writes=[(Vs[i], "ones")])
            Pb = [S(f"at_p{i}", [128, 512], BF16) for i in range(4)]
            rc = S("at_rc", [128, 2]); tq = S("at_tq", [128, 128]); oq = S("at_oq", [128, 128]); junk = S("at_junk", [128, 128])
            ss = S("at_ss", [128, 1]); rstd = S("at_rstd", [128, 1])
            ybT = [S(f"at_ybT{i}", [128, 512], BF16) for i in range(2)]
            pi = [0]
            yi = [0]
            sbk = [0]

            def obank(m, sub):
                idx = m * 4 + sub
                return self.ps[4 + idx // 3], (idx % 3) * 129

            def attend(KT, QT, V, qc0, nq, blocks, h, ycol0, nsub, subq):
                nb = len(blocks)
                touched = set()
                last_for_sub = {}
                for bi_, (kc, nk, vs, mi, fs) in enumerate(blocks):
                    for sub in range(fs, nsub):
                        last_for_sub[sub] = bi_
                first_for_sub = {}
                for bi_, (kc, nk, vs, mi, fs) in enumerate(blocks):
                    for sub in range(fs, nsub):
                        first_for_sub.setdefault(sub, bi_)
                def emit_s(bi_):
                    kc, nk, vs, mi, fs = blocks[bi_]
                    Ps = []
                    for m in range(2):
                        bank = self.ps[sbk[0] % 3]
                        sbk[0] += 1
                        pr = slice(m * 64, (m + 1) * 64)
                        p.op("pe", lambda e, bank=bank, pr=pr, kc=kc, nk=nk: e.matmul(
                            bank[0:nk, 0:nq], lhsT=KT[pr, kc:kc + nk], rhs=QT[pr, qc0:qc0 + nq], start=True, stop=True),
                            reads=[KT, QT], writes=[bank])
                        P = Pb[pi[0] % 4]
                        pi[0] += 1
                        p.op("act", lambda e, P=P, bank=bank, nk=nk: e.activation(out=P[0:nk, 0:nq], in_=bank[0:nk, 0:nq], func=AF.Exp, scale=0.125),
                             reads=[bank], writes=[P])
                        if mi is not None:
                            eng = "dve" if m == 0 else "pool"
                            p.op(eng, lambda e, P=P, mi=mi: e.tensor_tensor(out=P[:, :], in0=P[:, :], in1=maskb[:, mi, :], op=ALU.mult),
                                 reads=[P, maskb], writes=[P])
                        Ps.append(P)
                    return Ps

                def emit_pv(bi_, Ps):
                    kc, nk, vs, mi, fs = blocks[bi_]
                    for m in range(2):
                        for sub in range(fs, nsub):
                            ob, oc = obank(m, sub)
                            st_ = id(ob) not in touched
                            touched.add(id(ob))
                            assert (not st_) or bi_ == 0
                            p.op("pe", lambda e, ob=ob, oc=oc, P=Ps[m], sub=sub, vs=vs, nk=nk, st_=st_: e.matmul(
                                ob[0:subq, oc:oc + 129], lhsT=P[0:nk, sub * 128:sub * 128 + subq], rhs=V[0:nk, vs, 0:129],
                                start=st_, stop=False, skip_group_check=True),
                                reads=[Ps[m], V], writes=[(ob, oc)])

                cur = emit_s(0)
                for bi_ in range(nb):
                    nxt = emit_s(bi_ + 1) if bi_ + 1 < nb else None
                    emit_pv(bi_, cur)
                    cur = nxt
                yT = ybT[yi[0] % 2]
                yi[0] += 1
                for sub in range(nsub):
                    o1, c1 = obank(0, sub)
                    o2, c2 = obank(1, sub)
                    q = subq
                    p.op("dve", lambda e, o1=o1, c1=c1, q=q: e.reciprocal(out=rc[0:q, 0:1], in_=o1[0:q, c1 + 128:c1 + 129]), reads=[(o1, c1)], writes=[rc])
                    p.op("dve", lambda e, o2=o2, c2=c2, q=q: e.reciprocal(out=rc[0:q, 1:2], in_=o2[0:q, c2 + 128:c2 + 129]), reads=[(o2, c2)], writes=[rc])
                    p.op("dve", lambda e, q=q: e.tensor_tensor(out=rc[0:q, 1:2], in0=rc[0:q, 1:2], in1=neglam[0:q, :], op=ALU.mult), reads=[rc, neglam], writes=[rc])
                    p.op("dve", lambda e, o2=o2, c2=c2, q=q: e.tensor_scalar(out=tq[0:q, :], in0=o2[0:q, c2:c2 + 128], scalar1=rc[0:q, 1:2], scalar2=None, op0=ALU.mult),
                         reads=[(o2, c2), rc], writes=[tq])
                    p.op("dve", lambda e, o1=o1, c1=c1, q=q: e.scalar_tensor_tensor(out=oq[0:q, :], in0=o1[0:q, c1:c1 + 128], scalar=rc[0:q, 0:1], in1=tq[0:q, :],
                                                                                 op0=ALU.mult, op1=ALU.add), reads=[(o1, c1), rc, tq], writes=[oq])
                    p.op("dve", lambda e: e.memset(ss[:, 0:1], 0.0), writes=[ss])
                    p.op("act", lambda e, q=q: e.activation(out=junk[0:q, :], in_=oq[0:q, :], func=AF.Square, accum_out=ss[0:q, 0:1]), reads=[oq, ss], writes=[junk, ss])
                    p.op("act", lambda e, q=q: e.activation(out=rstd[0:q, 0:1], in_=ss[0:q, 0:1], func=AF.Sqrt, scale=1.0 / 128, bias=self.eps_t[0:q, 0:1]),
                         reads=[ss, self.eps_t], writes=[rstd])
                    p.op("dve", lambda e, q=q: e.reciprocal(out=rstd[0:q, 0:1], in_=rstd[0:q, 0:1]), reads=[rstd], writes=[rstd])
                    p.op("dve", lambda e, q=q: e.scalar_tensor_tensor(out=oq[0:q, :], in0=oq[0:q, :], scalar=rstd[0:q, 0:1], in1=subw[0:q, :], op0=ALU.mult, op1=ALU.mult),
                         reads=[oq, rstd, subw], writes=[oq])
                    tb = self.ps[7]
                    p.op("pe", lambda e, q=q, sub=sub: e.transpose(out=tb[:, sub * 128:sub * 128 + q], in_=oq[0:q, :], identity=self.ident[0:q, 0:q]),
                         reads=[oq, self.ident], writes=[tb])
                    p.op("act", lambda e, q=q, sub=sub, yT=yT: e.copy(out=yT[:, sub * 128:sub * 128 + q], in_=tb[:, sub * 128:sub * 128 + q]), reads=[tb], writes=[yT])
                nqt = (nsub - 1) * 128 + subq
                p.dma(self.YT[1024 + h * 128:1024 + (h + 1) * 128, ycol0:ycol0 + nqt], yT[:, 0:nqt], reads=[yT], writes=[(self.YT, ("att", h, ycol0))])

            for h in range(8):
                KT, QT, V = KTs[h % 2], QTs[h % 2], Vs[h % 2]
                rows = slice(h * 128, (h + 1) * 128)
                p.dma(KT[:, 0:LP], self.KT[rows, 0:LP], reads=[self.KT], writes=[KT])
                p.dma(QT[:, :], self.QT[rows, :], reads=[self.QT], writes=[QT])
                p.dma(V[:, 0:NKB, 0:128], self.Vb[0:LP, rows].rearrange("(kb q) d -> q kb d", q=128), reads=[self.Vb], writes=[(V, "d")])
                for qt in range(LP // 512):
                    blocks = []
                    for kb in range(4 * qt + 4):
                        j = kb - 4 * qt
                        blocks.append((kb * 128, 128, kb, (j if j >= 0 else None), max(j, 0)))
                    attend(KT, QT, V, qt * 512, 512, blocks, h, qt * 512, 4, 128)
            for h in range(8):
                rows = slice(h * 128, (h + 1) * 128)
                for sg in range(NSAMP):
                    i = (h * NSAMP + sg) % 2
                    KT, QT, V = KTs[i], QTs[i], Vs[i]
                    c0 = LP + sg * DSEQ
                    p.dma(KT[:, 0:PAST], self.KcT[sg, rows, :], reads=[(self.KcT, sg)], writes=[KT])
                    p.dma(KT[:, PAST:PAST + DSEQ], self.KT[rows, c0:c0 + DSEQ], reads=[self.KT], writes=[KT])
                    p.dma(QT[:, c0:c0 + DSEQ], self.QT[rows, c0:c0 + DSEQ], reads=[self.QT], writes=[QT])
                    p.dma(V[:, 0:NCB, 0:128], self.Vcb[sg, :, rows].rearrange("(kb q) d -> q kb d", q=128), reads=[(self.Vcb, sg)], writes=[(V, "d")])
                    p.dma(V[0:DSEQ, NCB, 0:128], self.Vb[c0:c0 + DSEQ, rows], reads=[self.Vb], writes=[(V, "d")])
                    blocks = [(kb * 128, 128, kb, None, 0) for kb in range(NCB)] + [(PAST, DSEQ, NCB, None, 0)]
                    attend(KT, QT, V, c0, DSEQ, blocks, h, c0, 1, DSEQ)
            p.barrier()

    def stage_hgrn(self):
        p, T, NT, LP = self.p, self.T, self.NT, self.LP
        with ExitStack() as st:
            S = lambda n, shp, dt=F32: self.sb(st, n, shp, dt)
            lb2 = S("hg_lb2", [128, 2, 8]); lbt = S("hg_lbt", [128, 8]); oml = S("hg_oml", [128, 8]); nw = S("hg_nw", [128, 1])
            ones = S("hg_ones", [128, 128]); cm = S("hg_cm", [128, TS]); m16f = S("hg_m16f", [128, 128]); m16 = S("hg_m16", [128, 128], BF16)
            for r in range(2):
                self.small_T(lb2[:, r, :], self.l1_hg_lb[r, :].rearrange("(h c) -> c h", c=128), [self.l1_hg_lb], [lb2])
            self.small_T(nw[:, :], self.l1_hg_nw[:].rearrange("(c o) -> c o", o=1), [self.l1_hg_nw], [nw])
            p.op("dve", lambda e: e.tensor_tensor(out=lbt[:, :], in0=lb2[:, 1, :], in1=lb2[:, 0, :], op=ALU.subtract), reads=[lb2], writes=[lbt])
            p.op("act", lambda e: e.activation(out=lbt[:, :], in_=lbt[:, :], func=AF.Sigmoid), reads=[lbt], writes=[lbt])
            p.op("dve", lambda e: e.tensor_scalar(out=oml[:, :], in0=lbt[:, :], scalar1=-1.0, scalar2=1.0, op0=ALU.mult, op1=ALU.add), reads=[lbt], writes=[oml])
            p.op("dve", lambda e: e.memset(ones[:, :], 1.0), writes=[ones])
            p.op("dve", lambda e: e.memset(cm[:, :], 1.0), writes=[cm])
            p.op("dve", lambda e: e.memset(cm[:, :].rearrange("p (c l) -> p c l", l=16)[:, :, 0:1], 0.0), writes=[cm])
            p.dma(m16f[:, :], self.c_m16[:, :], reads=[self.c_m16], writes=[m16f])
            p.op("dve", lambda e: e.tensor_copy(out=m16[:, :], in_=m16f[:, :]), reads=[m16f], writes=[m16])
            NB = 4
            mk = lambda nm, dt=F32: [S(f"hg_{nm}{i}", [128, TS], dt) for i in range(NB)]
            qf, ff, vf, gt, bt, ebt, t1, kk, kh = mk("q"), mk("f"), mk("v"), mk("g"), mk("b"), mk("eb"), mk("t1"), mk("kk"), mk("kh")
            qb, kb_ = mk("qb", BF16), mk("kb", BF16)
            attm = [[S(f"hg_am{hd}{i}", [128, 128], BF16) for i in range(2)] for hd in range(2)]
            itok = [[S(f"hg_it{hd}{i}", [128, 128], BF16) for i in range(2)] for hd in range(2)]
            khi = [[S(f"hg_khi{hd}{i}", [16, 256], BF16) for i in range(3)] for hd in range(2)]
            S32s = [S(f"hg_S32{hd}", [128, 128]) for hd in range(2)]
            Sbfs = [S(f"hg_Sbf{hd}", [128, 128], BF16) for hd in range(2)]
            osb = [S(f"hg_o{i}", [128, TS]) for i in range(2)]
            yb = [S(f"hg_y{i}", [128, TS], BF16) for i in range(2)]
            it = 0
            for hp in range(4):
                for ti in range(NT):
                    samp = ti == NT - 1
                    c0 = ti * TS
                    sets = []
                    for hd in range(2):
                        h = hp * 2 + hd
                        rows = slice(h * 128, (h + 1) * 128)
                        i = hd * 2 + (it % 2)
                        Q, F_, V_, G, B_, EB, T1, KK, KH, QB, KB = qf[i], ff[i], vf[i], gt[i], bt[i], ebt[i], t1[i], kk[i], kh[i], qb[i], kb_[i]
                        sets.append((Q, F_, V_, G, B_, EB, T1, KK, KH, QB, KB))
                        p.dma(Q[:, :], self.HQ[rows, c0:c0 + TS], reads=[(self.HQ, ti)], writes=[Q])
                        p.dma(F_[:, :], self.HF[rows, c0:c0 + TS], reads=[(self.HF, ti)], writes=[F_])
                        p.dma(V_[:, :], self.HI[rows, c0:c0 + TS], reads=[(self.HI, ti)], writes=[V_])
                        p.op("act", lambda e, G=G, F_=F_: e.activation(out=G[:, :], in_=F_[:, :], func=AF.Sigmoid), reads=[F_], writes=[G])
                        p.op("dve", lambda e, G=G, h=h: e.tensor_scalar(out=G[:, :], in0=G[:, :], scalar1=oml[:, h:h + 1], scalar2=lbt[:, h:h + 1], op0=ALU.mult, op1=ALU.add),
                             reads=[G, oml, lbt], writes=[G])
                        p.op("dve", lambda e, G=G, KK=KK: e.tensor_scalar(out=KK[:, :], in0=G[:, :], scalar1=-1.0, scalar2=1.0, op0=ALU.mult, op1=ALU.add), reads=[G], writes=[KK])
                        p.op("act", lambda e, G=G: e.activation(out=G[:, :], in_=G[:, :], func=AF.Ln), reads=[G], writes=[G])
                        p.op("dve", lambda e, G=G, B_=B_: e.tensor_tensor_scan(out=B_[:, :], data0=cm[:, :], data1=G[:, :], initial=0.0, op0=ALU.mult, op1=ALU.add),
                             reads=[G, cm], writes=[B_])
                        p.op("act", lambda e, Q=Q: e.activation(out=Q[:, :], in_=Q[:, :], func=AF.Silu), reads=[Q], writes=[Q])
                        p.op("act", lambda e, EB=EB, B_=B_: e.activation(out=EB[:, :], in_=B_[:, :], func=AF.Exp), reads=[B_], writes=[EB])
                        p.op("dve", lambda e, QB=QB, Q=Q, EB=EB: e.tensor_tensor(out=QB[:, :], in0=Q[:, :], in1=EB[:, :], op=ALU.mult), reads=[Q, EB], writes=[QB])
                        p.op("act", lambda e, T1=T1, B_=B_: e.activation(out=T1[:, :], in_=B_[:, :], func=AF.Exp, scale=-1.0), reads=[B_], writes=[T1])
                        p.op("dve", lambda e, KB=KB, KK=KK, T1=T1: e.tensor_tensor(out=KB[:, :], in0=KK[:, :], in1=T1[:, :], op=ALU.mult), reads=[KK, T1], writes=[KB])
                        b3 = B_[:, :].rearrange("p (c l) -> p c l", l=16)
                        t3 = T1[:, :].rearrange("p (c l) -> p c l", l=16)
                        p.op("dve", lambda e, b3=b3, t3=t3: e.tensor_tensor(out=t3, in0=b3[:, :, 15:16].to_broadcast([128, 32, 16]), in1=b3, op=ALU.subtract),
                             reads=[B_], writes=[T1])
                        p.op("act", lambda e, T1=T1: e.activation(out=T1[:, :], in_=T1[:, :], func=AF.Exp), reads=[T1], writes=[T1])
                        p.op("dve", lambda e, KH=KH, KK=KK, T1=T1: e.tensor_tensor(out=KH[:, :], in0=KK[:, :], in1=T1[:, :], op=ALU.mult), reads=[KK, T1], writes=[KH])
                    it += 1
                    kcnt = [0, 0]

                    def emit_T(hd, cg):
                        Q, F_, V_, G, B_, EB, T1, KK, KH, QB, KB = sets[hd]
                        cs = slice(cg * 16, (cg + 1) * 16)
                        bt_ = self.ps[4 + hd]
                        so = (cg % 2) * 256
                        kt = khi[hd][cg % 3]
                        p.op("pe", lambda e, bt_=bt_, KH=KH, cs=cs, so=so: e.transpose(out=bt_[0:16, so:so + 128], in_=KH[:, cs], identity=self.ident[:, :]),
                             reads=[KH, self.ident], writes=[(bt_, cg % 2)])
                        p.op("pe", lambda e, bt_=bt_, V_=V_, cs=cs, so=so: e.transpose(out=bt_[0:16, so + 128:so + 256], in_=V_[:, cs], identity=self.ident[:, :]),
                             reads=[V_, self.ident], writes=[(bt_, cg % 2)])
                        p.op("act", lambda e, kt=kt, bt_=bt_, so=so: e.copy(out=kt[:, :], in_=bt_[0:16, so:so + 256]), reads=[(bt_, cg % 2)], writes=[kt])

                    for blk in range(4):
                        bs = slice(blk * 128, (blk + 1) * 128)
                        for hd in range(2):
                            Q, F_, V_, G, B_, EB, T1, KK, KH, QB, KB = sets[hd]
                            ba, bo = self.ps[hd], self.ps[2 + hd]
                            am, itk = attm[hd][blk % 2], itok[hd][blk % 2]
                            p.op("pe", lambda e, ba=ba, KB=KB, QB=QB, bs=bs: e.matmul(ba[:, 0:128], lhsT=KB[:, bs], rhs=QB[:, bs], start=True, stop=True),
                                 reads=[KB, QB], writes=[(ba, 0)])
                            p.op("dve", lambda e, am=am, ba=ba: e.tensor_tensor(out=am[:, :], in0=ba[:, 0:128], in1=m16[:, :], op=ALU.mult), reads=[(ba, 0), m16], writes=[am])
                            p.op("pe", lambda e, ba=ba, V_=V_, bs=bs: e.transpose(out=ba[:, 128:256], in_=V_[:, bs], identity=self.ident[:, :]),
                                 reads=[V_, self.ident], writes=[(ba, 1)])
                            p.op("act", lambda e, itk=itk, ba=ba: e.copy(out=itk[:, :], in_=ba[:, 128:256]), reads=[(ba, 1)], writes=[itk])
                            p.op("pe", lambda e, bo=bo, itk=itk, am=am, bs=bs: e.matmul(bo[:, bs], lhsT=itk[:, :], rhs=am[:, :], start=True, stop=False, skip_group_check=True),
                                 reads=[itk, am], writes=[bo])
                            emit_T(hd, blk * 8)
                        for c in range(8):
                            cg = blk * 8 + c
                            cs = slice(cg * 16, (cg + 1) * 16)
                            for hd in range(2):
                                h = hp * 2 + hd
                                Q, F_, V_, G, B_, EB, T1, KK, KH, QB, KB = sets[hd]
                                bo, bu = self.ps[2 + hd], self.ps[6 + hd]
                                S32, Sbf = S32s[hd], Sbfs[hd]
                                if c + 1 < 8:
                                    emit_T(hd, cg + 1)
                                seq_start = (ti == 0 and cg == 0) if not samp else (cg % 4 == 0)
                                if seq_start:
                                    if samp:
                                        sg = cg // 4
                                        p.dma(S32[:, :], self.st_hg[sg, h, :, :], reads=[self.st_hg], writes=[S32])
                                    else:
                                        p.op("dve", lambda e, S32=S32: e.memset(S32[:, :], 0.0), writes=[S32])
                                    p.op("act", lambda e, S32=S32, Sbf=Sbf: e.copy(out=Sbf[:, :], in_=S32[:, :]), reads=[S32], writes=[Sbf])
                                p.op("pe", lambda e, bo=bo, QB=QB, cs=cs, Sbf=Sbf: e.matmul(bo[:, cs], lhsT=Sbf[:, :], rhs=QB[:, cs], start=False, stop=False, skip_group_check=True),
                                     reads=[Sbf, QB], writes=[bo])
                                kt = khi[hd][cg % 3]
                                p.op("pe", lambda e, bu=bu, kt=kt: e.matmul(bu[:, 0:128], lhsT=kt[0:16, 0:128], rhs=kt[0:16, 128:256], start=True, stop=True),
                                     reads=[kt], writes=[bu])
                                p.op("dve", lambda e, bu=bu, EB=EB, cg=cg, S32=S32: e.scalar_tensor_tensor(out=S32[:, :], in0=S32[:, :], scalar=EB[:, cg * 16 + 15:cg * 16 + 16], in1=bu[:, 0:128],
                                                                                                   op0=ALU.mult, op1=ALU.add), reads=[S32, EB, bu], writes=[S32])
                                p.op("act", lambda e, S32=S32, Sbf=Sbf: e.copy(out=Sbf[:, :], in_=S32[:, :]), reads=[S32], writes=[Sbf])
                                seq_end = (ti == NT - 2 and cg == 31) if not samp else (cg % 4 == 3)
                                if seq_end:
                                    seq = (1 + cg // 4) if samp else 0
                                    p.dma(self.hg_out[seq, h, :, :], S32[:, :], reads=[S32], writes=[(self.hg_out, (seq, h))])
                    for hd in range(2):
                        h = hp * 2 + hd
                        Q, F_, V_, G, B_, EB, T1, KK, KH, QB, KB = sets[hd]
                        bo, bn = self.ps[2 + hd], self.ps[hd]
                        O_, Y_ = osb[hd], yb[hd]
                        p.op("act", lambda e, O_=O_, bo=bo: e.copy(out=O_[:, :], in_=bo[:, :]), reads=[bo], writes=[O_])
                        p.op("act", lambda e, T1=T1, O_=O_: e.activation(out=T1[:, :], in_=O_[:, :], func=AF.Square), reads=[O_], writes=[T1])
                        p.op("pe", lambda e, bn=bn, T1=T1: e.matmul(bn[:, :], lhsT=ones[:, :], rhs=T1[:, :], start=True, stop=True), reads=[ones, T1], writes=[bn])
                        p.op("act", lambda e, T1=T1, bn=bn: e.activation(out=T1[:, :], in_=bn[:, :], func=AF.Sqrt, scale=1.0 / 128, bias=self.eps_t[:, 0:1]),
                             reads=[bn, self.eps_t], writes=[T1])
                        p.op("dve", lambda e, T1=T1: e.reciprocal(out=T1[:, :], in_=T1[:, :]), reads=[T1], writes=[T1])
                        p.op("dve", lambda e, O_=O_, T1=T1, Y_=Y_: e.scalar_tensor_tensor(out=Y_[:, :], in0=O_[:, :], scalar=nw[:, 0:1], in1=T1[:, :], op0=ALU.mult, op1=ALU.mult),
                             reads=[O_, T1, nw], writes=[Y_])
                        p.dma(self.YT[1024 + h * 128:1024 + (h + 1) * 128, c0:c0 + TS], Y_[:, :], reads=[Y_], writes=[(self.YT, ("hg", h, ti))])
            p.barrier()

    def stage_ssdconv(self):
        p, T, NT, LP = self.p, self.T, self.NT, self.LP
        self.XT = self.dscr("XT", [T, 1280], F32)
        self.BCt = self.dscr("BCt", [512, T], BF16)
        with ExitStack() as st:
            S = lambda n, shp, dt=F32: self.sb(st, n, shp, dt)
            cw = S("sc_cw", [128, 12, 4]); cb = S("sc_cb", [128, 12])
            for tap in range(4):
                self.small_T(cw[:, :, tap], self.l1_conv_w[tap, :].rearrange("(b c) -> c b", c=128), [self.l1_conv_w], [cw])
            self.small_T(cb[:, :], self.l1_conv_b[:].rearrange("(b c) -> c b", c=128), [self.l1_conv_b], [cb])
            xa = [S(f"sc_xa{i}", [128, 536]) for i in range(2)]
            xc = [S(f"sc_xc{i}", [128, TS]) for i in range(2)]
            xcb = [S(f"sc_xcb{i}", [128, TS], BF16) for i in range(2)]
            xtk = [S(f"sc_xt{i}", [128, 4, 128]) for i in range(2)]
            it = 0
            for ti in range(NT):
                samp = ti == NT - 1
                nseg, sl = (NSAMP, DSEQ) if samp else (1, TS)
                c0 = ti * TS
                for b in range(12):
                    i = it % 2
                    it += 1
                    X, XC, XCB, XTK = xa[i], xc[i], xcb[i], xtk[i]
                    rows = slice(b * 128, (b + 1) * 128)
                    xv = X[:, 0:nseg * (3 + sl)].rearrange("p (s l) -> p s l", s=nseg)
                    if samp:
                        for sg in range(NSAMP):
                            self.small_T(xv[:, sg, 0:3], self.st_ssd_conv[sg, :, rows].rearrange("t c -> c t"), [self.st_ssd_conv], [X])
                    elif ti == 0:
                        p.op("dve", lambda e, xv=xv: e.memset(xv[:, :, 0:3], 0.0), writes=[X])
                    else:
                        p.dma(xv[:, 0, 0:3], self.XBC[rows, c0 - 3:c0], reads=[(self.XBC, ti - 1)], writes=[X])
                    p.dma(xv[:, :, 3:3 + sl], self.XBC[rows, c0:c0 + TS].rearrange("p (s l) -> p s l", s=nseg), reads=[(self.XBC, ti)], writes=[X])
                    xc3 = XC[:, :].rearrange("p (s l) -> p s l", s=nseg)
                    p.op("dve", lambda e, xc3=xc3, xv=xv, b=b, sl=sl: e.tensor_scalar(out=xc3, in0=xv[:, :, 0:sl], scalar1=cw[:, b, 0:1], scalar2=cb[:, b:b + 1],
                                                                                 op0=ALU.mult, op1=ALU.add), reads=[X, cw, cb], writes=[XC])
                    for tap in range(1, 4):
                        p.op("dve", lambda e, xc3=xc3, xv=xv, b=b, sl=sl, tap=tap: e.scalar_tensor_tensor(
                            out=xc3, in0=xv[:, :, tap:tap + sl], scalar=cw[:, b, tap:tap + 1], in1=xc3, op0=ALU.mult, op1=ALU.add), reads=[X, cw, XC], writes=[XC])
                    p.op("act", lambda e, XC=XC: e.activation(out=XC[:, :], in_=XC[:, :], func=AF.Silu), reads=[XC], writes=[XC])
                    if samp:
                        for sg in range(NSAMP):
                            self.small_T(self.ssd_conv_out[1 + sg, :, rows].rearrange("t c -> c t"), xv[:, sg, sl:sl + 3], [X], [(self.ssd_conv_out, ("s", b, sg))])
                    elif ti == NT - 2:
                        self.small_T(self.ssd_conv_out[0, :, rows].rearrange("t c -> c t"), X[:, TS:TS + 3], [X], [(self.ssd_conv_out, ("p", b))])
                    if b >= 8:
                        p.op("pool", lambda e, XCB=XCB, XC=XC: e.tensor_copy(out=XCB[:, :], in_=XC[:, :]), reads=[XC], writes=[XCB])
                        p.dma(self.BCt[(b - 8) * 128:(b - 7) * 128, c0:c0 + TS], XCB[:, :], reads=[XCB], writes=[(self.BCt, ti)])
                    if b < 10:
                        bank = self.ps[it % 2]
                        for t in range(4):
                            p.op("pe", lambda e, bank=bank, XC=XC, t=t: e.transpose(out=bank[:, t * 128:(t + 1) * 128], in_=XC[:, t * 128:(t + 1) * 128], identity=self.ident[:, :]),
                                 reads=[XC, self.ident], writes=[bank])
                        p.op("act", lambda e, XTK=XTK, bank=bank: e.copy(out=XTK[:, :, :], in_=bank[:, :].rearrange("p (a b) -> p a b", a=4)), reads=[bank], writes=[XTK])
                        p.dma(self.XT[c0:c0 + TS, b * 128:(b + 1) * 128].rearrange("(t q) c -> q t c", q=128), XTK[:, :, :], reads=[XTK], writes=[(self.XT, ti)])
            p.barrier()

    def stage_ssd(self):
        p, T, NT, LP = self.p, self.T, self.NT, self.LP
        with ExitStack() as st:
            S = lambda n, shp, dt=F32: self.sb(st, n, shp, dt)
            cs = S("ss_cs", [64, 448]); dtb = S("ss_dtb", [64, 16]); aneg = S("ss_an", [64, 16]); dsk = S("ss_dsk", [64, 16]); nwB = S("ss_nw", [64, 1024])
            one = S("ss_one", [128, 1])
            p.dma(cs[:, :], self.c_ssd[:, :], reads=[self.c_ssd], writes=[cs])
            tri, ntri, ones64, mneg, id64, sel63 = cs[:, 0:64], cs[:, 64:128], cs[:, 128:192], cs[:, 192:256], cs[:, 256:320], cs[:, 320:448]
            bc16 = lambda t_: t_[:].rearrange("(o d) -> o d", o=1).to_broadcast([64, 16])
            p.dma(dtb[:, :], bc16(self.l1_dt_bias), reads=[self.l1_dt_bias], writes=[dtb])
            p.dma(aneg[:, :], bc16(self.l1_a_log), reads=[self.l1_a_log], writes=[aneg])
            p.dma(dsk[:, :], bc16(self.l1_d_skip), reads=[self.l1_d_skip], writes=[dsk])
            p.dma(nwB[:, :], self.l1_ssd_nw[:].rearrange("(o d) -> o d", o=1).to_broadcast([64, 1024]), reads=[self.l1_ssd_nw], writes=[nwB])
            p.op("act", lambda e: e.activation(out=aneg[:, :], in_=aneg[:, :], func=AF.Exp), reads=[aneg], writes=[aneg])
            p.op("dve", lambda e: e.tensor_scalar(out=aneg[:, :], in0=aneg[:, :], scalar1=-1.0, scalar2=None, op0=ALU.mult), reads=[aneg], writes=[aneg])
            p.op("dve", lambda e: e.memset(one[:, :], 1.0), writes=[one])
            S32 = S("ss_S32", [128, 1024]); Sbf = S("ss_Sbf", [128, 1024], BF16); stg = S("ss_stg", [128, 8, 128])
            NB = 2
            mk = lambda nm, shp, dt=F32: [S(f"ss_{nm}{i}", shp, dt) for i in range(NB)]
            xt, zt, dtr, bct = mk("xt", [64, 1280]), mk("zt", [64, 1024]), mk("dtr", [64, 16]), mk("bct", [128, 4, 64], BF16)
            dt_, dta, r1, r2, LT = mk("dt", [64, 16]), mk("dta", [64, 16]), mk("r1", [64, 1024]), mk("r2", [64, 1024]), mk("LT", [64, 1024])
            MT, xdt, xw, btk = mk("MT", [64, 1024], BF16), mk("xdt", [64, 1024], BF16), mk("xw", [64, 1024], BF16), mk("btk", [64, 256], BF16)
            ac, eac, wv, y32, tt = mk("ac", [64, 16]), mk("eac", [64, 16]), mk("wv", [64, 16]), mk("y32", [64, 1024]), mk("tt", [64, 1024])
            decB, ssg, yT = mk("decB", [128, 16]), mk("ssg", [64, 2]), mk("yT", [128, 8, 64], BF16)
            v3 = lambda ap: ap.rearrange("p (h l) -> p h l", h=16)
            bcl = lambda ap: ap.unsqueeze(2).to_broadcast([64, 16, 64])
            seqs = [(0, 0, LP // 64)] + [(1 + sg, LP + sg * DSEQ, 1) for sg in range(NSAMP)]
            it = 0
            for (seq, col0, nch) in seqs:
                if seq == 0:
                    p.op("dve", lambda e: e.memset(S32[:, :], 0.0), writes=[S32])
                else:
                    p.dma(stg[:, :, :], self.st_ssd[seq - 1, :, :, :].rearrange("h q n -> (h q) n").rearrange("(a q) n -> q a n", q=128), reads=[self.st_ssd], writes=[stg])
                    for a in range(8):
                        bank = self.ps[a // 4]
                        p.op("pe", lambda e, bank=bank, a=a: e.transpose(out=bank[:, (a % 4) * 128:(a % 4 + 1) * 128], in_=stg[:, a, :], identity=self.ident[:, :]),
                             reads=[stg, self.ident], writes=[bank])
                    for hf in range(2):
                        p.op("dve", lambda e, hf=hf: e.tensor_copy(out=S32[:, hf * 512:(hf + 1) * 512], in_=self.ps[hf][:, :]), reads=[self.ps[hf]], writes=[S32])
                p.op("act", lambda e: e.copy(out=Sbf[:, :], in_=S32[:, :]), reads=[S32], writes=[Sbf])
                for ch in range(nch):
                    i = it % NB
                    it += 1
                    c0 = col0 + ch * 64
                    ti = c0 // TS
                    XT_, ZT, DTR, BCT, DT_, DTA, R1, R2, LT_, MT_, XDT, XW, BTK, AC, EAC, WV, Y, TT, DEC, SSG, YT_ = (
                        xt[i], zt[i], dtr[i], bct[i], dt_[i], dta[i], r1[i], r2[i], LT[i], MT[i], xdt[i], xw[i], btk[i], ac[i], eac[i], wv[i], y32[i], tt[i], decB[i], ssg[i], yT[i])
                    p.dma(XT_[:, :], self.XT[c0:c0 + 64, :], reads=[(self.XT, ti)], writes=[XT_])
                    p.dma(ZT[:, :], self.Z[c0:c0 + 64, :], reads=[(self.Z, ti)], writes=[ZT])
                    p.dma(DTR[:, :], self.DT[c0:c0 + 64, :], reads=[(self.DT, ti)], writes=[DTR])
                    p.dma(BCT[:, :, :], self.BCt[:, c0:c0 + 64].rearrange("(a n) t -> n a t", n=128), reads=[(self.BCt, ti)], writes=[BCT])
                    p.op("dve", lambda e, DT_=DT_, DTR=DTR: e.tensor_tensor(out=DT_[:, :], in0=DTR[:, :], in1=dtb[:, :], op=ALU.add), reads=[DTR, dtb], writes=[DT_])
                    p.op("act", lambda e, DT_=DT_: e.activation(out=DT_[:, :], in_=DT_[:, :], func=AF.Exp), reads=[DT_], writes=[DT_])
                    p.op("act", lambda e, DT_=DT_: e.activation(out=DT_[:, :], in_=DT_[:, :], func=AF.Ln, bias=one[0:64, 0:1]), reads=[DT_, one], writes=[DT_])
                    p.op("dve", lambda e, DTA=DTA, DT_=DT_: e.tensor_tensor(out=DTA[:, :], in0=DT_[:, :], in1=aneg[:, :], op=ALU.mult), reads=[DT_, aneg], writes=[DTA])
                    p.op("dve", lambda e, R1=R1, DTA=DTA: e.tensor_tensor(out=v3(R1[:, :]), in0=bcl(DTA[:, :]), in1=tri.unsqueeze(1).to_broadcast([64, 16, 64]), op=ALU.mult),
                         reads=[DTA, cs], writes=[R1])
                    p.op("pool", lambda e, R2=R2, DTA=DTA: e.tensor_copy(out=v3(R2[:, :]), in_=bcl(DTA[:, :])), reads=[DTA], writes=[R2])
                    bd = [self.ps[0], self.ps[1]]
                    for hf in range(2):
                        hs = slice(hf * 512, (hf + 1) * 512)
                        p.op("pe", lambda e, hf=hf, hs=hs, R1=R1: e.matmul(bd[hf][0:64, :], lhsT=ones64, rhs=R1[:, hs], start=True, stop=False), reads=[R1, cs], writes=[bd[hf]])
                        p.op("pe", lambda e, hf=hf, hs=hs, R2=R2: e.matmul(bd[hf][0:64, :], lhsT=ntri, rhs=R2[:, hs], start=False, stop=True), reads=[R2, cs], writes=[bd[hf]])
                    bm = self.ps[6]
                    p.op("pe", lambda e, DTA=DTA: e.matmul(bm[0:64, 0:16], lhsT=tri, rhs=DTA[:, :], start=True, stop=True), reads=[DTA, cs], writes=[(bm, "ac")])
                    for hf in range(2):
                        hs = slice(hf * 512, (hf + 1) * 512)
                        p.op("dve", lambda e, hf=hf, hs=hs, LT_=LT_: e.tensor_tensor(out=LT_[:, hs].rearrange("p (h l) -> p h l", h=8), in0=bd[hf][0:64, :].rearrange("p (h l) -> p h l", h=8),
                                                                                 in1=mneg.unsqueeze(1).to_broadcast([64, 8, 64]), op=ALU.add), reads=[bd[hf], cs], writes=[LT_])
                    p.op("act", lambda e, LT_=LT_: e.activation(out=LT_[:, :], in_=LT_[:, :], func=AF.Exp), reads=[LT_], writes=[LT_])
                    p.op("act", lambda e, AC=AC: e.copy(out=AC[:, :], in_=bm[0:64, 0:16]), reads=[(bm, "ac")], writes=[AC])
                    p.op("act", lambda e, EAC=EAC, AC=AC: e.activation(out=EAC[:, :], in_=AC[:, :], func=AF.Exp), reads=[AC], writes=[EAC])
                    for g in range(2):
                        p.op("pe", lambda e, g=g, BCT=BCT: e.matmul(bm[0:64, 64 + g * 64:128 + g * 64], lhsT=BCT[:, g, :], rhs=BCT[:, 2 + g, :], start=True, stop=True),
                             reads=[BCT], writes=[(bm, "cb")])
                    for g in range(2):
                        gs = slice(g * 512, (g + 1) * 512)
                        p.op("dve", lambda e, g=g, gs=gs, MT_=MT_, LT_=LT_: e.tensor_tensor(out=MT_[:, gs].rearrange("p (h l) -> p h l", h=8), in0=LT_[:, gs].rearrange("p (h l) -> p h l", h=8),
                                                                                        in1=bm[0:64, 64 + g * 64:128 + g * 64].unsqueeze(1).to_broadcast([64, 8, 64]), op=ALU.mult),
                             reads=[LT_, (bm, "cb")], writes=[MT_])
                    xv_ = v3(XT_[:, 0:1024])
                    p.op("dve", lambda e, XDT=XDT, xv_=xv_, DT_=DT_: e.tensor_tensor(out=v3(XDT[:, :]), in0=xv_, in1=bcl(DT_[:, :]), op=ALU.mult), reads=[XT_, DT_], writes=[XDT])
                    byd = [self.ps[2], self.ps[3]]
                    for h in range(16):
                        p.op("pe", lambda e, h=h, MT_=MT_, XDT=XDT: e.matmul(byd[h // 8][0:64, (h % 8) * 64:(h % 8 + 1) * 64], lhsT=MT_[:, h * 64:(h + 1) * 64], rhs=XDT[:, h * 64:(h + 1) * 64],
                                                                            start=True, stop=True, skip_group_check=True), reads=[MT_, XDT], writes=[byd[h // 8]])
                    byo = [self.ps[4], self.ps[5]]
                    for g in range(2):
                        p.op("pe", lambda e, g=g, BCT=BCT: e.matmul(byo[g][0:64, :], lhsT=BCT[:, 2 + g, :], rhs=Sbf[:, g * 512:(g + 1) * 512], start=True, stop=True),
                             reads=[BCT, Sbf], writes=[byo[g]])
                    for g in range(2):
                        gs = slice(g * 512, (g + 1) * 512)
                        p.op("dve", lambda e, g=g, gs=gs, Y=Y, EAC=EAC: e.tensor_tensor(out=Y[:, gs].rearrange("p (h l) -> p h l", h=8), in0=byo[g][0:64, :].rearrange("p (h l) -> p h l", h=8),
                                                                                    in1=EAC[:, g * 8:(g + 1) * 8].unsqueeze(2).to_broadcast([64, 8, 64]), op=ALU.mult), reads=[byo[g], EAC], writes=[Y])
                        p.op("dve", lambda e, g=g, gs=gs, Y=Y: e.tensor_tensor(out=Y[:, gs], in0=Y[:, gs], in1=byd[g][0:64, :], op=ALU.add), reads=[Y, byd[g]], writes=[Y])
                    p.op("pool", lambda e, TT=TT, xv_=xv_: e.tensor_tensor(out=v3(TT[:, :]), in0=xv_, in1=bcl(dsk[:, :]), op=ALU.mult), reads=[XT_, dsk], writes=[TT])
                    p.op("dve", lambda e, Y=Y, TT=TT: e.tensor_tensor(out=Y[:, :], in0=Y[:, :], in1=TT[:, :], op=ALU.add), reads=[Y, TT], writes=[Y])
                    p.op("act", lambda e, ZT=ZT: e.activation(out=ZT[:, :], in_=ZT[:, :], func=AF.Silu), reads=[ZT], writes=[ZT])
                    p.op("dve", lambda e, Y=Y, ZT=ZT: e.tensor_tensor(out=Y[:, :], in0=Y[:, :], in1=ZT[:, :], op=ALU.mult), reads=[Y, ZT], writes=[Y])
                    p.op("dve", lambda e, SSG=SSG: e.memset(SSG[:, :], 0.0), writes=[SSG])
                    for g in range(2):
                        gs = slice(g * 512, (g + 1) * 512)
                        p.op("act", lambda e, g=g, gs=gs, TT=TT, Y=Y, SSG=SSG: e.activation(out=TT[:, gs], in_=Y[:, gs], func=AF.Square, accum_out=SSG[:, g:g + 1]), reads=[Y, SSG], writes=[TT, SSG])
                    p.op("act", lambda e, SSG=SSG: e.activation(out=SSG[:, :], in_=SSG[:, :], func=AF.Sqrt, scale=1.0 / 512, bias=self.eps_t[0:64, 0:1]), reads=[SSG, self.eps_t], writes=[SSG])
                    p.op("dve", lambda e, SSG=SSG: e.reciprocal(out=SSG[:, :], in_=SSG[:, :]), reads=[SSG], writes=[SSG])
                    for g in range(2):
                        gs = slice(g * 512, (g + 1) * 512)
                        p.op("dve", lambda e, g=g, gs=gs, Y=Y, SSG=SSG: e.scalar_tensor_tensor(out=Y[:, gs], in0=Y[:, gs], scalar=SSG[:, g:g + 1], in1=nwB[:, gs], op0=ALU.mult, op1=ALU.mult),
                             reads=[Y, SSG, nwB], writes=[Y])
                    btr = self.ps[7]
                    for a in range(8):
                        p.op("pe", lambda e, a=a, Y=Y: e.transpose(out=btr[:, a * 64:(a + 1) * 64], in_=Y[:, a * 128:(a + 1) * 128], identity=id64), reads=[Y, cs], writes=[btr])
                    p.op("act", lambda e, YT_=YT_: e.copy(out=YT_[:, :, :], in_=btr[:, :].rearrange("p (a l) -> p a l", a=8)), reads=[btr], writes=[YT_])
                    p.dma(self.YT[0:1024, c0:c0 + 64].rearrange("(a q) t -> q a t", q=128), YT_[:, :, :], reads=[YT_], writes=[(self.YT, ("ssd", c0))])
                    p.op("dve", lambda e, WV=WV, LT_=LT_, DT_=DT_: e.tensor_tensor(out=WV[:, :], in0=v3(LT_[:, :])[:, :, 63], in1=DT_[:, :], op=ALU.mult), reads=[LT_, DT_], writes=[WV])
                    p.op("dve", lambda e, XW=XW, xv_=xv_, WV=WV: e.tensor_tensor(out=v3(XW[:, :]), in0=xv_, in1=bcl(WV[:, :]), op=ALU.mult), reads=[XT_, WV], writes=[XW])
                    p.op("pool", lambda e, BTK=BTK, XT_=XT_: e.tensor_copy(out=BTK[:, :], in_=XT_[:, 1024:1280]), reads=[XT_], writes=[BTK])
                    p.op("pe", lambda e, AC=AC: e.matmul(bm[:, 256:272], lhsT=sel63, rhs=AC[:, :], start=True, stop=True), reads=[AC, cs], writes=[(bm, "dec")])
                    p.op("act", lambda e, DEC=DEC: e.activation(out=DEC[:, :], in_=bm[:, 256:272], func=AF.Exp), reads=[(bm, "dec")], writes=[DEC])
                    for g in range(2):
                        gs = slice(g * 512, (g + 1) * 512)
                        p.op("pe", lambda e, g=g, gs=gs, BTK=BTK, XW=XW: e.matmul(bd[g][:, :], lhsT=BTK[:, g * 128:(g + 1) * 128], rhs=XW[:, gs], start=True, stop=True), reads=[BTK, XW], writes=[bd[g]])
                        p.op("dve", lambda e, g=g, gs=gs, DEC=DEC: e.tensor_tensor(out=S32[:, gs].rearrange("p (h l) -> p h l", h=8), in0=S32[:, gs].rearrange("p (h l) -> p h l", h=8),
                                                                              in1=DEC[:, g * 8:(g + 1) * 8].unsqueeze(2).to_broadcast([128, 8, 64]), op=ALU.mult), reads=[S32, DEC], writes=[S32])
                        p.op("dve", lambda e, g=g, gs=gs: e.tensor_tensor(out=S32[:, gs], in0=S32[:, gs], in1=bd[g][:, :], op=ALU.add), reads=[S32, bd[g]], writes=[S32])
                    p.op("act", lambda e: e.copy(out=Sbf[:, :], in_=S32[:, :]), reads=[S32], writes=[Sbf])
                for a in range(8):
                    bank = self.ps[a // 4]
                    p.op("pe", lambda e, bank=bank, a=a: e.transpose(out=bank[:, (a % 4) * 128:(a % 4 + 1) * 128], in_=S32[:, a * 128:(a + 1) * 128], identity=self.ident[:, :]),
                         reads=[S32, self.ident], writes=[bank])
                for hf in range(2):
                    p.op("dve", lambda e, hf=hf: e.tensor_copy(out=stg[:, hf * 4:(hf + 1) * 4, :], in_=self.ps[hf][:, :].rearrange("p (a n) -> p a n", a=4)), reads=[self.ps[hf]], writes=[stg])
                p.dma(self.ssd_out[seq, :, :, :].rearrange("h q n -> (h q) n").rearrange("(a q) n -> q a n", q=128), stg[:, :, :], reads=[stg], writes=[(self.ssd_out, seq)])
            p.barrier()


W_NAMES = ["norm_w", "l0_w_in", "l0_conv_w", "l0_conv_b", "l0_rg_w_r", "l0_rg_b_r", "l0_rg_w_i", "l0_rg_b_i",
           "l0_rg_lambda", "l0_lq1", "l0_lk1", "l0_lq2", "l0_lk2", "l0_subln_w", "l0_w_out", "l1_w_in",
           "l1_conv_w", "l1_conv_b", "l1_dt_bias", "l1_a_log", "l1_d_skip", "l1_ssd_norm_w",
           "l1_hg_lower_bound", "l1_hg_norm_w", "l1_w_out", "ffn_w_gate", "ffn_w_up", "ffn_w_down"]


def make_consts():
    ident = np.eye(128, dtype=np.float32)
    mask = np.zeros((128, 4, 512), np.float32)
    for j in range(4):
        kc = (j * 128 + np.arange(128)) // 64
        qc = np.arange(512) // 64
        mask[:, j, :] = (kc[:, None] <= qc[None, :]).astype(np.float32)
    jj = np.arange(128)
    m16 = ((jj[:, None] // 16 == jj[None, :] // 16) & (jj[:, None] <= jj[None, :])).astype(np.float32)
    j6 = np.arange(64)
    tri = (j6[:, None] <= j6[None, :]).astype(np.float32)
    cs = np.zeros((64, 5 * 64 + 128), np.float32)
    cs[:, 0:64] = tri
    cs[:, 64:128] = -tri
    cs[:, 128:192] = 1.0
    cs[:, 192:256] = np.where(tri > 0, 0.0, -30000.0)
    cs[:, 256:320] = np.eye(64)
    cs[63, 320:448] = 1.0
    return {"c_ident": ident, "c_mask": mask.reshape(128, 2048).astype(ml_dtypes.bfloat16), "c_m16": m16, "c_ssd": cs}


def core_inputs(inp, c):
    f = lambda a: np.ascontiguousarray(np.asarray(a, dtype=np.float32))
    s0, s1 = NSAMP * c, NSAMP * (c + 1)
    LP = inp["x_prompt"].shape[1]
    m = {}
    m["x"] = f(np.concatenate([inp["x_prompt"][c].reshape(LP, D), inp["x_sample"][s0:s1].reshape(NSAMP * DSEQ, D)], 0))
    past = inp["cache_diff_k"].shape[1]
    m["ck"] = f(inp["cache_diff_k"][s0:s1].reshape(NSAMP, past, 1024))
    m["cv"] = f(inp["cache_diff_v"][s0:s1].reshape(NSAMP, past, 1024))
    m["st_rg_conv"] = f(inp["state_rglru_conv"][s0:s1])
    m["st_rg_h"] = f(inp["state_rglru_h"][s0:s1])
    m["st_ssd_conv"] = f(inp["state_ssd_conv"][s0:s1])
    m["st_ssd"] = f(inp["state_ssd"][s0:s1])
    m["st_hg"] = f(inp["state_hgrn"][s0:s1])
    for n in W_NAMES:
        m[n] = f(inp[n])
    m.update(make_consts())
    return m


_CACHE = {}


def get_builder(LP, PAST, debug=False):
    key = (LP, PAST, debug)
    if key not in _CACHE:
        b = Builder(LP, PAST, debug)
        b.build()
        _CACHE[key] = b
    return _CACHE[key]


def run_cores(inp, debug=False):
    LP = inp["x_prompt"].shape[1]
    PAST = inp["cache_diff_k"].shape[1]
    b = get_builder(LP, PAST, debug)
    maps = [core_inputs(inp, c % 2) for c in range(2)]
    import os
    ncores = int(os.environ.get("K_NCORES", "8"))
    in_maps = [maps[c % 2] for c in range(ncores)]
    res = run_bass_kernel_spmd(b.nc, in_maps, core_ids=list(range(ncores)))
    return b, res.results


def kernel(**inp):
    b, r = run_cores(inp)
    LP = b.LP
    g = lambda name: [np.asarray(r[min(c, len(r) - 1)][name]) for c in range(2)]
    y, ko, vo = g("y"), g("k_out"), g("v_out")
    rgc, rgh, sc, so, ho = g("rg_conv_out"), g("rg_h_out"), g("ssd_conv_out"), g("ssd_out"), g("hg_out")
    P = lambda a, shp: np.stack([a[c][:LP] for c in range(2)], 0).reshape(shp).astype(np.float32)
    S = lambda a, shp: np.concatenate([a[c][LP:] for c in range(2)], 0).reshape(shp).astype(np.float32)
    P0 = lambda a: np.stack([a[c][0] for c in range(2)], 0).astype(np.float32)
    S0 = lambda a: np.concatenate([a[c][1:] for c in range(2)], 0).astype(np.float32)
    return (P(y, (2, LP, D)), S(y, (16, DSEQ, D)),
            P(ko, (2, LP, 8, 128)), P(vo, (2, LP, 8, 128)), P0(rgc), P0(rgh), P0(sc), P0(so), P0(ho),
            S(ko, (16, DSEQ, 8, 128)), S(vo, (16, DSEQ, 8, 128)), S0(rgc), S0(rgh), S0(sc), S0(so), S0(ho))
```

```python
import numpy as np
import ml_dtypes
from contextlib import ExitStack
import concourse.bass as bass
import concourse.mybir as mybir
from concourse.bass_utils import run_bass_kernel_spmd

F32 = mybir.dt.float32
BF16 = mybir.dt.bfloat16
ALU = mybir.AluOpType
AF = mybir.ActivationFunctionType
AX = mybir.AxisListType

D = 2048
TS = 512
NSAMP = 8
DSEQ = 64
EPS = 1e-6
D_FF = 5632
IN_EVEN = 5120
IN_ODD = 5648
ENGS = ("pe", "act", "dve", "pool", "sp")
EPOCH = 12000
NDMASEM = 40


class Res:
    __slots__ = ("w", "r")

    def __init__(self):
        self.w = None
        self.r = []


class Tile:
    def __init__(self, h, name=""):
        self.h = h
        self.name = name
        self.res = {None: Res()}

    def __getitem__(self, k):
        return self.h[k]

    def get(self, key):
        if key not in self.res:
            self.res[key] = Res()
        return self.res[key]


class Ins:
    __slots__ = ("eng", "fn", "deps", "is_dma", "sig", "sigidx", "dsem", "dval", "dprev", "pos")

    def __init__(self, eng, fn, is_dma):
        self.eng = eng
        self.fn = fn
        self.is_dma = is_dma
        self.deps = set()
        self.sig = False
        self.sigidx = -1
        self.dsem = -1
        self.dval = 0
        self.dprev = 0


def _norm_acc(a):
    if isinstance(a, Tile):
        return a, None
    return a


def _split_psum(reads, writes):
    r2, w2 = [], list(writes)
    for a in reads:
        t, k = _norm_acc(a)
        if getattr(t, "is_psum", False):
            w2.append(t)
        else:
            r2.append(a)
    w3 = []
    for a in w2:
        t, k = _norm_acc(a)
        w3.append(t if getattr(t, "is_psum", False) else a)
    return r2, w3


class Prog:
    def __init__(self, nc):
        self.nc = nc
        self.streams = {e: [] for e in ENGS}
        self.last = {e: None for e in ENGS}
        self.pending = {e: set() for e in ENGS}
        self.ndma = 0
        self.dma_last = [None] * NDMASEM
        self.dma_val = [0] * NDMASEM

    def _collect(self, ins, reads, writes):
        reads, writes = _split_psum(reads, writes)
        deps = ins.deps
        for a in reads:
            t, k = _norm_acc(a)
            rs = list(t.res.values()) if k is None else [t.get(k), t.res[None]]
            for r in rs:
                if r.w is not None:
                    deps.add(r.w)
        for a in writes:
            t, k = _norm_acc(a)
            rs = list(t.res.values()) if k is None else [t.get(k), t.res[None]]
            for r in rs:
                if r.w is not None:
                    if r.w.is_dma or r.w.eng != ins.eng or ins.is_dma:
                        deps.add(r.w)
                for x in r.r:
                    if x.is_dma or x.eng != ins.eng or ins.is_dma:
                        deps.add(x)
        for a in reads:
            t, k = _norm_acc(a)
            r = t.get(k)
            if not ins.is_dma:
                r.r = [x for x in r.r if x.is_dma or x.eng != ins.eng]
            r.r.append(ins)
        for a in writes:
            t, k = _norm_acc(a)
            if k is None:
                for r in t.res.values():
                    r.w = ins
                    r.r = []
            else:
                r = t.get(k)
                r.w = ins
                r.r = []
        deps.discard(ins)
        if self.pending[ins.eng]:
            deps.update(self.pending[ins.eng])
            self.pending[ins.eng] = set()

    _rec = None

    def start_record(self):
        self._rec = []

    def stop_record(self):
        r, self._rec = self._rec, None
        return r

    def replay(self, entries):
        for e in entries:
            if e[0] == "op":
                self.op(e[1], e[2], e[3], e[4])
            elif e[0] == "dma":
                self.dma(e[1], e[2], e[3], e[4], e[5], **e[6])

    @staticmethod
    def merge(a, b):
        a = [e for e in a if e[0] != "bar"]
        b = [e for e in b if e[0] != "bar"]
        out, i, j = [], 0, 0
        la, lb = max(len(a), 1), max(len(b), 1)
        while i < len(a) or j < len(b):
            if j >= len(b) or (i < len(a) and i * lb <= j * la):
                out.append(a[i]); i += 1
            else:
                out.append(b[j]); j += 1
        return out

    def op(self, eng, fn, reads=(), writes=()):
        if self._rec is not None:
            self._rec.append(("op", eng, fn, list(reads), list(writes)))
            return None
        ins = Ins(eng, fn, False)
        self._collect(ins, reads, writes)
        self.streams[eng].append(ins)
        self.last[eng] = ins
        return ins

    def dma(self, out_ap, in_ap, reads=(), writes=(), q="sp", **kw):
        if self._rec is not None:
            self._rec.append(("dma", out_ap, in_ap, list(reads), list(writes), q, kw))
            return None

        def fn(e, out_ap=out_ap, in_ap=in_ap, kw=kw):
            return e.dma_start(out=out_ap, in_=in_ap, **kw)
        ins = Ins(q, fn, True)
        self._collect(ins, reads, writes)
        i = self.ndma % NDMASEM
        self.ndma += 1
        ins.dsem = i
        ins.dprev = self.dma_val[i]
        self.dma_val[i] += 16
        ins.dval = self.dma_val[i]
        self.dma_last[i] = ins
        self.streams[q].append(ins)
        self.last[q] = ins
        return ins

    def barrier(self):
        if self._rec is not None:
            self._rec.append(("bar",))
            return
        lasts = [x for x in self.last.values() if x is not None]
        dl = [x for x in self.dma_last if x is not None]
        for e in ENGS:
            self.pending[e].update(x for x in lasts if x.eng != e or x.is_dma)
            self.pending[e].update(dl)

    def emit(self, es):
        nc = self.nc
        for e in ENGS:
            for ins in self.streams[e]:
                for d in ins.deps:
                    if not d.is_dma:
                        d.sig = True
        nsig = {}
        for e in ENGS:
            c = 0
            for ins in self.streams[e]:
                if ins.sig and not ins.is_dma:
                    ins.sigidx = c
                    c += 1
            nsig[e] = c
        esems = {e: [es.enter_context(nc.semaphore(f"s_{e}_{j}")) for j in range(nsig[e] // EPOCH + 1)]
                 for e in ENGS}
        dsems = [es.enter_context(nc.semaphore(f"s_dma_{j}")) for j in range(NDMASEM)]
        block = es.enter_context(nc.Block())
        engobj = {"pe": block.tensor, "act": block.scalar, "dve": block.vector, "pool": block.gpsimd,
                  "sp": block.sync}

        def run(eng_name):
            def body(e):
                waited = {}
                for ins in self.streams[eng_name]:
                    waits = {}
                    for d in ins.deps:
                        if d.is_dma:
                            key, val, sem = ("d", d.dsem), d.dval, dsems[d.dsem]
                        else:
                            j = d.sigidx // EPOCH
                            key, val, sem = (d.eng, j), d.sigidx % EPOCH + 1, esems[d.eng][j]
                        if waits.get(key, (0, None))[0] < val:
                            waits[key] = (val, sem)
                    if ins.is_dma and ins.dprev > 0:
                        key = ("d", ins.dsem)
                        if waits.get(key, (0, None))[0] < ins.dprev:
                            waits[key] = (ins.dprev, dsems[ins.dsem])
                    for key, (val, sem) in waits.items():
                        if waited.get(key, 0) < val:
                            e.wait_ge(sem, val)
                            waited[key] = val
                    bi = ins.fn(e)
                    if ins.is_dma:
                        bi.then_inc(dsems[ins.dsem], 16)
                    elif ins.sig:
                        j = ins.sigidx // EPOCH
                        bi.then_inc(esems[eng_name][j], 1)
                if eng_name == "sp":
                    for i in range(NDMASEM):
                        if self.dma_val[i] > 0:
                            e.wait_ge(dsems[i], self.dma_val[i])
            return body

        for en in ENGS:
            engobj[en](run(en))


class Builder:
    def __init__(self, LP, PAST, debug=False):
        self.LP = LP
        self.PAST = PAST
        self.T = LP + NSAMP * DSEQ
        self.NT = self.T // TS
        self.debug = debug
        self.nc = bass.Bass("TRN2", target_bir_lowering=False)
        self.p = Prog(self.nc)
        self.es = ExitStack()
        self.dbg_names = []

    def din(self, name, shape, dt=F32):
        return Tile(self.nc.dram_tensor(name, list(shape), dt, kind="ExternalInput").ap(), name)

    def dout(self, name, shape, dt=F32):
        return Tile(self.nc.dram_tensor(name, list(shape), dt, kind="ExternalOutput").ap(), name)

    def dscr(self, name, shape, dt):
        if self.debug:
            self.dbg_names.append(name)
            return Tile(self.nc.dram_tensor(name, list(shape), dt, kind="ExternalOutput").ap(), name)
        return Tile(self.nc.dram_tensor(name, list(shape), dt).ap(), name)

    def sb(self, st, name, shape, dt):
        self._sbn = getattr(self, "_sbn", 0) + 1
        name = f"{name}_{self._sbn}"
        return Tile(st.enter_context(self.nc.sbuf_tensor(name, list(shape), dt)), name)

    def build(self):
        nc, p, T, NT, LP, PAST = self.nc, self.p, self.T, self.NT, self.LP, self.PAST
        es = self.es
        with es:
            self.x = self.din("x", [T, D])
            self.ck = self.din("ck", [NSAMP, PAST, 1024])
            self.cv = self.din("cv", [NSAMP, PAST, 1024])
            self.st_rg_conv = self.din("st_rg_conv", [NSAMP, 3, 1024])
            self.st_rg_h = self.din("st_rg_h", [NSAMP, 1024])
            self.st_ssd_conv = self.din("st_ssd_conv", [NSAMP, 3, 1536])
            self.st_ssd = self.din("st_ssd", [NSAMP, 16, 64, 128])
            self.st_hg = self.din("st_hg", [NSAMP, 8, 128, 128])
            self.norm_w = self.din("norm_w", [2, 4, D])
            self.w_in0 = self.din("l0_w_in", [D, IN_EVEN])
            self.l0_conv_w = self.din("l0_conv_w", [4, 1024])
            self.l0_conv_b = self.din("l0_conv_b", [1024])
            self.l0_w_r = self.din("l0_rg_w_r", [8, 128, 128])
            self.l0_b_r = self.din("l0_rg_b_r", [1024])
            self.l0_w_i = self.din("l0_rg_w_i", [8, 128, 128])
            self.l0_b_i = self.din("l0_rg_b_i", [1024])
            self.l0_lam = self.din("l0_rg_lambda", [1024])
            self.l0_lq1 = self.din("l0_lq1", [64])
            self.l0_lk1 = self.din("l0_lk1", [64])
            self.l0_lq2 = self.din("l0_lq2", [64])
            self.l0_lk2 = self.din("l0_lk2", [64])
            self.l0_subln = self.din("l0_subln_w", [128])
            self.w_out0 = self.din("l0_w_out", [D, D])
            self.w_in1 = self.din("l1_w_in", [D, IN_ODD])
            self.l1_conv_w = self.din("l1_conv_w", [4, 1536])
            self.l1_conv_b = self.din("l1_conv_b", [1536])
            self.l1_dt_bias = self.din("l1_dt_bias", [16])
            self.l1_a_log = self.din("l1_a_log", [16])
            self.l1_d_skip = self.din("l1_d_skip", [16])
            self.l1_ssd_nw = self.din("l1_ssd_norm_w", [1024])
            self.l1_hg_lb = self.din("l1_hg_lower_bound", [2, 1024])
            self.l1_hg_nw = self.din("l1_hg_norm_w", [128])
            self.w_out1 = self.din("l1_w_out", [D, D])
            self.wg = self.din("ffn_w_gate", [2, D, D_FF])
            self.wu = self.din("ffn_w_up", [2, D, D_FF])
            self.wd = self.din("ffn_w_down", [2, D_FF, D])
            self.c_ident = self.din("c_ident", [128, 128])
            self.c_mask = self.din("c_mask", [128, 4 * 512], BF16)
            self.c_m16 = self.din("c_m16", [128, 128])
            self.c_ssd = self.din("c_ssd", [64, 5 * 64 + 128])

            self.y = self.dout("y", [T, D])
            self.k_out = self.dout("k_out", [T, 1024])
            self.v_out = self.dout("v_out", [T, 1024])
            self.rg_conv_out = self.dout("rg_conv_out", [1 + NSAMP, 3, 1024])
            self.rg_h_out = self.dout("rg_h_out", [1 + NSAMP, 1024])
            self.ssd_conv_out = self.dout("ssd_conv_out", [1 + NSAMP, 3, 1536])
            self.ssd_out = self.dout("ssd_out", [1 + NSAMP, 16, 64, 128])
            self.hg_out = self.dout("hg_out", [1 + NSAMP, 8, 128, 128])

            self.ps = [Tile(es.enter_context(nc.psum_tensor(f"ps{i}", [128, 512], F32)), f"ps{i}")
                       for i in range(8)]
            for t_ in self.ps:
                t_.is_psum = True
            self.ident = self.sb(es, "ident", [128, 128], F32)
            p.dma(self.ident[:, :], self.c_ident[:, :], reads=[self.c_ident], writes=[self.ident])
            self.eps_t = self.sb(es, "eps_t", [128, 1], F32)
            p.op("dve", lambda e: e.memset(self.eps_t[:, :], EPS), writes=[self.eps_t])

            self.stage_convert_first()
            self.stage_inproj(0)
            with ExitStack() as st:
                p.start_record()
                self.stage_rglru(st)
                self.stage_convert_rest(st, engs=("pool",))
                rb = p.stop_record()
                p.start_record()
                self.stage_cacheprep(st)
                self.stage_attn(st)
                ra = p.stop_record()
                p.replay(Prog.merge(ra, rb))
                p.barrier()
            self.stage_outffn(0)
            self.stage_inproj(1)
            self.stage_ssdconv()
            self.stage_ssd()
            self.stage_hgrn()
            self.stage_outffn(1)
            p.barrier()
            p.emit(es)
        return nc

    BUFW = 1536

    def conv_w(self, name, W, r0c0, K, groups, bufs, engs=("dve", "pool", "act")):
        p = self.p
        KC = K // 128
        outs = [self.dscr(f"{name}_g{gi}", [128, KC, w], BF16) for gi, (c0, w) in enumerate(groups)]
        batches, cur = [], []
        for gi, (c0, w) in enumerate(groups):
            if cur and (c0 + w - groups[cur[0]][0] > self.BUFW or c0 != groups[cur[-1]][0] + groups[cur[-1]][1]):
                batches.append(cur)
                cur = []
            cur.append(gi)
        batches.append(cur)
        for kc in range(KC):
            src = r0c0(kc)
            for bt in batches:
                self._cvn = getattr(self, "_cvn", 0) + 1
                f32t, bft = bufs[self._cvn % 2]
                lo = groups[bt[0]][0]
                hi = groups[bt[-1]][0] + groups[bt[-1]][1]
                n = hi - lo
                p.dma(f32t[:, 0:n], src[:, lo:hi], reads=[W], writes=[f32t])
                eng = engs[self._cvn % len(engs)]
                if eng == "act":
                    p.op("act", lambda e, a=bft[:, 0:n], b=f32t[:, 0:n]: e.copy(out=a, in_=b), reads=[f32t], writes=[bft])
                else:
                    p.op(eng, lambda e, a=bft[:, 0:n], b=f32t[:, 0:n]: e.tensor_copy(out=a, in_=b), reads=[f32t], writes=[bft])
                for gi in bt:
                    c0, w = groups[gi]
                    p.dma(outs[gi][:, kc, :], bft[:, c0 - lo:c0 - lo + w], reads=[bft], writes=[(outs[gi], kc)])
        return outs

    def cv_bufs(self, st):
        return [(self.sb(st, f"cvf{i}", [128, self.BUFW], F32), self.sb(st, f"cvb{i}", [128, self.BUFW], BF16)) for i in range(2)]

    def stage_convert_first(self):
        g512 = lambda n: [(i * 512, 512) for i in range(n // 512)]
        with ExitStack() as st:
            bufs = self.cv_bufs(st)
            self.Wt_in0 = self.conv_w("wt_in0", self.w_in0, lambda kc: self.w_in0[kc * 128:(kc + 1) * 128, :], D, g512(5120), bufs)
            self.p.barrier()

    def stage_convert_rest(self, st, engs=("dve", "pool")):
        g512 = lambda n: [(i * 512, 512) for i in range(n // 512)]
        bufs = self.cv_bufs(st)
        self.Wt_out0 = self.conv_w("wt_out0", self.w_out0, lambda kc: self.w_out0[kc * 128:(kc + 1) * 128, :], D, g512(2048), bufs, engs)
        self.Wt_g, self.Wt_u, self.Wt_d = [None, None], [None, None], [None, None]
        g256 = [(i * 256, 256) for i in range(22)]
        for L in range(2):
            self.Wt_g[L] = self.conv_w(f"wt_g{L}", self.wg, lambda kc, L=L: self.wg[L, kc * 128:(kc + 1) * 128, :], D, g256, bufs, engs)
            self.Wt_u[L] = self.conv_w(f"wt_u{L}", self.wu, lambda kc, L=L: self.wu[L, kc * 128:(kc + 1) * 128, :], D, g256, bufs, engs)
            self.Wt_d[L] = self.conv_w(f"wt_d{L}", self.wd, lambda kc, L=L: self.wd[L, kc * 128:(kc + 1) * 128, :], D_FF, g512(2048), bufs, engs)
            if L == 0:
                g1 = g512(2560) + [(2560, 16)] + [(2576 + i * 512, 512) for i in range(6)]
                self.Wt_in1 = self.conv_w("wt_in1", self.w_in1, lambda kc: self.w_in1[kc * 128:(kc + 1) * 128, :], D, g1, bufs, engs)
                self.Wt_out1 = self.conv_w("wt_out1", self.w_out1, lambda kc: self.w_out1[kc * 128:(kc + 1) * 128, :], D, g512(2048), bufs, engs)
        self.p.barrier()

    def norm_hT(self, xt, wB, hT, t, tmp, h32, ss, rstd, tb, x_ap=None, x_key=None):
        p = self.p
        if x_ap is None:
            x_ap = xt[:, :]
        xr = xt if x_key is None else (xt, x_key)
        self.rms_stats(x_ap, xr, tmp, ss, rstd, D)
        p.op("dve", lambda e: e.scalar_tensor_tensor(out=h32[:, :], in0=x_ap, scalar=rstd[:, 0:1], in1=wB[:, :],
                                                     op0=ALU.mult, op1=ALU.mult),
             reads=[xr, rstd, wB], writes=[h32])
        for kq in range(4):
            bank = tb[kq % 2]
            for j in range(4):
                k = kq * 4 + j
                p.op("pe", lambda e, bank=bank, j=j, k=k: e.transpose(out=bank[:, j * 128:(j + 1) * 128],
                                                                     in_=h32[:, k * 128:(k + 1) * 128],
                                                                     identity=self.ident[:, :]),
                     reads=[h32, self.ident], writes=[bank])
            dst = hT[:, kq * 4:(kq + 1) * 4, t * 128:(t + 1) * 128]
            srcv = bank[:, :].rearrange("p (a b) -> p a b", a=4)
            if kq % 2 == 0:
                p.op("act", lambda e, dst=dst, srcv=srcv: e.copy(out=dst, in_=srcv), reads=[bank],
                     writes=[(hT, (kq, t))])
            else:
                p.op("dve", lambda e, dst=dst, srcv=srcv: e.tensor_copy(out=dst, in_=srcv), reads=[bank],
                     writes=[(hT, (kq, t))])

    def rms_stats(self, x_ap, x_tile, tmp, ss, rstd, n, width=None):
        p = self.p
        w = n if width is None else width
        p.op("dve", lambda e: e.memset(ss[:, 0:1], 0.0), writes=[ss])
        p.op("act", lambda e: e.activation(out=tmp[:, 0:w], in_=x_ap, func=AF.Square, accum_out=ss[:, 0:1]),
             reads=[x_tile, ss], writes=[tmp, ss])
        p.op("act", lambda e: e.activation(out=rstd[:, 0:1], in_=ss[:, 0:1], func=AF.Sqrt, scale=1.0 / n,
                                           bias=self.eps_t[:, 0:1]),
             reads=[ss, self.eps_t], writes=[rstd])
        p.op("dve", lambda e: e.reciprocal(out=rstd[:, 0:1], in_=rstd[:, 0:1]), reads=[rstd], writes=[rstd])

    def load_wB(self, st, name, L, i):
        wB = self.sb(st, name, [128, D], F32)
        self.p.dma(wB[:, :], self.norm_w[L, i:i + 1, :].to_broadcast([128, D]), reads=[self.norm_w], writes=[wB])
        return wB

    def stage_inproj(self, L):
        p, T, NT = self.p, self.T, self.NT
        src = self.x if L == 0 else self.X
        if L == 0:
            self.XA = self.dscr("XA", [1024, T], F32)
            self.GA = self.dscr("GA", [1024, T], F32)
            self.QT = self.dscr("QT", [1024, T], BF16)
            self.KT = self.dscr("KT", [1024, T], BF16)
            self.Vb = self.dscr("Vb", [T, 1024], BF16)
            Wt = self.Wt_in0
            A = [(Wt[0], 512, self.XA, 0, F32), (Wt[1], 512, self.XA, 512, F32),
                 (Wt[2], 512, self.GA, 0, F32), (Wt[3], 512, self.GA, 512, F32),
                 (Wt[4], 512, self.QT, 0, BF16), (Wt[5], 512, self.QT, 512, BF16),
                 (Wt[6], 512, self.KT, 0, BF16), (Wt[7], 512, self.KT, 512, BF16)]
            B = [(Wt[6], 512, [(self.k_out, 0, F32)]), (Wt[7], 512, [(self.k_out, 512, F32)]),
                 (Wt[8], 512, [(self.v_out, 0, F32), (self.Vb, 0, BF16)]),
                 (Wt[9], 512, [(self.v_out, 512, F32), (self.Vb, 512, BF16)])]
        else:
            self.XBC = self.dscr("XBC", [1536, T], F32)
            self.HQ = self.dscr("HQ", [1024, T], F32)
            self.HF = self.dscr("HF", [1024, T], F32)
            self.HI = self.dscr("HI", [1024, T], F32)
            self.Z = self.dscr("Z", [T, 1024], F32)
            self.DT = self.dscr("DT", [T, 16], F32)
            Wt = self.Wt_in1
            A = [(Wt[2], 512, self.XBC, 0, F32), (Wt[3], 512, self.XBC, 512, F32), (Wt[4], 512, self.XBC, 1024, F32),
                 (Wt[6], 512, self.HQ, 0, F32), (Wt[7], 512, self.HQ, 512, F32),
                 (Wt[8], 512, self.HF, 0, F32), (Wt[9], 512, self.HF, 512, F32),
                 (Wt[10], 512, self.HI, 0, F32), (Wt[11], 512, self.HI, 512, F32)]
            B = [(Wt[0], 512, [(self.Z, 0, F32)]), (Wt[1], 512, [(self.Z, 512, F32)]),
                 (Wt[5], 16, [(self.DT, 0, F32)])]
        with ExitStack() as st:
            wB = self.load_wB(st, "wB", L, 0)
            xts = [self.sb(st, f"xt{i}", [128, D], F32) for i in range(2)]
            tmp = self.sb(st, "tmp", [128, D], BF16)
            h32s = [self.sb(st, f"h32_{i}", [128, D], F32) for i in range(4)]
            ss = self.sb(st, "ss", [128, 1], F32)
            rstd = self.sb(st, "rstd", [128, 1], F32)
            hTs = [self.sb(st, f"hT{i}", [128, 16, TS], BF16) for i in range(2)]
            wts = [self.sb(st, f"wt{i}", [128, 16, 512], BF16) for i in range(3)]
            sgf = [self.sb(st, f"sgf{i}", [128, 512], F32) for i in range(3)]
            sgb = [self.sb(st, f"sgb{i}", [128, 512], BF16) for i in range(3)]
            accb = self.ps[2:8]
            cnt = {"si": 0, "bi": 0, "x": 0}
            jobs = [("A",) + a for a in A] + [("B",) + b for b in B]
            nj = len(jobs)
            loads = [(ti, j) for ti in range(NT) for j in range(nj)]
            nload = [0]

            def ensure_loaded(idx):
                while nload[0] <= min(idx + 2, len(loads) - 1):
                    k = nload[0]
                    W, width = jobs[loads[k][1]][1], jobs[loads[k][1]][2]
                    wt = wts[k % 3]
                    p.dma(wt[:, :, 0:width], W[:, :, :], reads=[W], writes=[wt])
                    nload[0] += 1
                return wts[idx % 3]

            def norm1(ti, t):
                xt = xts[cnt["x"] % 2]
                cnt["x"] += 1
                r0 = ti * TS + t * 128
                p.dma(xt[:, :], src[r0:r0 + 128, :], reads=[(src, ti)], writes=[xt])
                self.rms_stats(xt[:, :], xt, tmp, ss, rstd, D)
                h32 = h32s[t]
                p.op("dve", lambda e, h32=h32, xt=xt: e.scalar_tensor_tensor(out=h32[:, :], in0=xt[:, :], scalar=rstd[:, 0:1], in1=wB[:, :],
                                                                         op0=ALU.mult, op1=ALU.mult), reads=[xt, rstd, wB], writes=[h32])

            def norm2(ti, t):
                h32, hT = h32s[t], hTs[ti % 2]
                for kq in range(4):
                    bank = self.ps[kq % 2]
                    for j in range(4):
                        k = kq * 4 + j
                        p.op("pe", lambda e, bank=bank, j=j, k=k, h32=h32: e.transpose(out=bank[:, j * 128:(j + 1) * 128], in_=h32[:, k * 128:(k + 1) * 128],
                                                                                   identity=self.ident[:, :]), reads=[h32, self.ident], writes=[bank])
                    dstv = hT[:, kq * 4:(kq + 1) * 4, t * 128:(t + 1) * 128]
                    srcv = bank[:, :].rearrange("p (a b) -> p a b", a=4)
                    if kq % 2 == 0:
                        p.op("act", lambda e, dstv=dstv, srcv=srcv: e.copy(out=dstv, in_=srcv), reads=[bank], writes=[(hT, (kq, t))])
                    else:
                        p.op("dve", lambda e, dstv=dstv, srcv=srcv: e.tensor_copy(out=dstv, in_=srcv), reads=[bank], writes=[(hT, (kq, t))])

            for t in range(4):
                norm1(0, t)
            for t in range(4):
                norm2(0, t)
            for ti in range(NT):
                hT = hTs[ti % 2]
                for j, job in enumerate(jobs):
                    wt = ensure_loaded(ti * nj + j)
                    if ti + 1 < NT:
                        t1_ = j - (nj - 6)
                        if 0 <= t1_ < 4:
                            norm1(ti + 1, t1_)
                        t2_ = j - (nj - 5)
                        if 0 <= t2_ < 4:
                            norm2(ti + 1, t2_)
                    if job[0] == "A":
                        _, W, width, dst, row0, dt = job
                        for n in range(width // 128):
                            bank = accb[cnt["bi"] % 6]
                            cnt["bi"] += 1
                            for k in range(16):
                                p.op("pe", lambda e, bank=bank, wt=wt, hT=hT, k=k, n=n: e.matmul(
                                    bank[:, :], lhsT=wt[:, k, n * 128:(n + 1) * 128], rhs=hT[:, k, :],
                                    start=(k == 0), stop=(k == 15)), reads=[wt, hT], writes=[bank])
                            sg = (sgf if dt == F32 else sgb)[cnt["si"] % 3]
                            cnt["si"] += 1
                            self.evac(bank[:, :], sg[:, :], bank, sg, cnt["si"])
                            rr = row0 + n * 128
                            p.dma(dst[rr:rr + 128, ti * TS:(ti + 1) * TS], sg[:, :], reads=[sg], writes=[(dst, ti)])
                    else:
                        _, W, width, dsts = job
                        for t in range(4):
                            bank = accb[cnt["bi"] % 6]
                            cnt["bi"] += 1
                            for k in range(16):
                                p.op("pe", lambda e, bank=bank, wt=wt, hT=hT, k=k, t=t, width=width: e.matmul(
                                    bank[:, 0:width], lhsT=hT[:, k, t * 128:(t + 1) * 128], rhs=wt[:, k, 0:width],
                                    start=(k == 0), stop=(k == 15)), reads=[wt, hT], writes=[bank])
                            r0 = ti * TS + t * 128
                            sg0 = sgf[cnt["si"] % 3]
                            cnt["si"] += 1
                            self.evac(bank[:, 0:width], sg0[:, 0:width], bank, sg0, cnt["si"])
                            for (dst, c0, dt) in dsts:
                                if dt == F32:
                                    sg = sg0
                                else:
                                    sg = sgb[cnt["si"] % 3]
                                    cnt["si"] += 1
                                    p.op("pool", lambda e, a=sg[:, 0:width], b=sg0[:, 0:width]: e.tensor_copy(out=a, in_=b),
                                         reads=[sg0], writes=[sg])
                                p.dma(dst[r0:r0 + 128, c0:c0 + width], sg[:, 0:width], reads=[sg], writes=[(dst, ti)])
            p.barrier()

    def evac(self, src_ap, dst_ap, src_t, dst_t, i):
        if i % 2 == 0:
            self.p.op("act", lambda e: e.copy(out=dst_ap, in_=src_ap), reads=[src_t], writes=[dst_t])
        else:
            self.p.op("dve", lambda e: e.tensor_copy(out=dst_ap, in_=src_ap), reads=[src_t], writes=[dst_t])

    def stage_outffn(self, L):
        p, T, NT = self.p, self.T, self.NT
        src = self.x if L == 0 else self.X
        if L == 0:
            self.X = self.dscr("X", [T, D], F32)
        dst = self.X if L == 0 else self.y
        Wo = self.Wt_out0 if L == 0 else self.Wt_out1
        Wg, Wu, Wd = self.Wt_g[L], self.Wt_u[L], self.Wt_d[L]
        YT = self.YT
        with ExitStack() as st:
            wB = self.sb(st, "wBo", [128, D], F32)
            yT = self.sb(st, "yT", [128, 16, TS], BF16)
            wbufs = [self.sb(st, f"wb{i}", [128, 16, 512], BF16) for i in range(3)]
            m32 = self.sb(st, "m32", [128, 4, D], F32)
            d32 = self.sb(st, "d32", [128, 4, D], F32)
            xres = self.sb(st, "xres", [128, D], F32)
            actT = self.sb(st, "actT", [128, 44, TS], BF16)
            tmp = self.sb(st, "tmpo", [128, D], BF16)
            h32 = self.sb(st, "h32o", [128, D], F32)
            gsb = [self.sb(st, f"gsb{i}", [128, TS], F32) for i in range(2)]
            ss = self.sb(st, "sso", [128, 1], F32)
            rstd = self.sb(st, "rstdo", [128, 1], F32)
            wi = 0
            ei = 0
            gi_ = 0

            def load_wB(i):
                p.dma(wB[:, :], self.norm_w[L, i:i + 1, :].to_broadcast([128, D]), reads=[self.norm_w], writes=[wB])

            def postnorm_residual(buf, t, res_ap, res_reads, out_ap, out_writes):
                self.rms_stats(buf[:, t, :], (buf, t), tmp, ss, rstd, D)
                p.op("dve", lambda e: e.scalar_tensor_tensor(out=h32[:, :], in0=buf[:, t, :], scalar=rstd[:, 0:1],
                                                             in1=wB[:, :], op0=ALU.mult, op1=ALU.mult),
                     reads=[(buf, t), rstd, wB], writes=[h32])
                p.op("pool", lambda e: e.tensor_tensor(out=out_ap, in0=h32[:, :], in1=res_ap, op=ALU.add),
                     reads=[h32] + res_reads, writes=out_writes)

            for ti in range(NT):
                c0 = ti * TS
                p.dma(yT[:, :, :], YT[:, c0:c0 + TS].rearrange("(k q) t -> q k t", q=128), reads=[(YT, ti)], writes=[yT])
                for g in range(4):
                    wt = wbufs[wi % 3]
                    wi += 1
                    p.dma(wt[:, :, :], Wo[g][:, :, :], reads=[Wo[g]], writes=[wt])
                    for t in range(4):
                        bank = self.ps[(g % 2) * 4 + t]
                        for k in range(16):
                            p.op("pe", lambda e, bank=bank, wt=wt, k=k, t=t: e.matmul(
                                bank[:, :], lhsT=yT[:, k, t * 128:(t + 1) * 128], rhs=wt[:, k, :],
                                start=(k == 0), stop=(k == 15)), reads=[wt, yT], writes=[bank])
                        ei += 1
                        self.evac(bank[:, :], m32[:, t, g * 512:(g + 1) * 512], bank, (m32, t), ei)
                load_wB(1)
                for t in range(4):
                    r0 = c0 + t * 128
                    p.dma(xres[:, :], src[r0:r0 + 128, :], reads=[(src, ti)], writes=[xres])
                    postnorm_residual(m32, t, xres[:, :], [xres], m32[:, t, :], [(m32, t)])
                load_wB(2)
                for t in range(4):
                    self.norm_hT(m32, wB, yT, t, tmp, h32, ss, rstd, self.ps[6:8], x_ap=m32[:, t, :], x_key=t)
                hT = yT
                for j in range(22):
                    wt = wbufs[wi % 3]
                    wi += 1
                    p.dma(wt[:, :, 0:256], Wg[j][:, :, :], reads=[Wg[j]], writes=[(wt, 0)])
                    p.dma(wt[:, :, 256:512], Wu[j][:, :, :], reads=[Wu[j]], writes=[(wt, 1)])
                    for n in range(2):
                        f = j * 2 + n
                        bg = self.ps[(f % 3) * 2]
                        bu = self.ps[(f % 3) * 2 + 1]
                        for k in range(16):
                            p.op("pe", lambda e, bg=bg, wt=wt, k=k, n=n: e.matmul(
                                bg[:, :], lhsT=wt[:, k, n * 128:(n + 1) * 128], rhs=hT[:, k, :],
                                start=(k == 0), stop=(k == 15)), reads=[(wt, 0), hT], writes=[bg])
                        for k in range(16):
                            p.op("pe", lambda e, bu=bu, wt=wt, k=k, n=n: e.matmul(
                                bu[:, :], lhsT=wt[:, k, 256 + n * 128:256 + (n + 1) * 128], rhs=hT[:, k, :],
                                start=(k == 0), stop=(k == 15)), reads=[(wt, 1), hT], writes=[bu])
                        gs = gsb[gi_ % 2]
                        gi_ += 1
                        p.op("act", lambda e, gs=gs, bg=bg: e.activation(out=gs[:, :], in_=bg[:, :], func=AF.Silu),
                             reads=[bg], writes=[gs])
                        p.op("dve", lambda e, gs=gs, bu=bu, f=f: e.tensor_tensor(out=actT[:, f, :], in0=gs[:, :],
                                                                                in1=bu[:, :], op=ALU.mult),
                             reads=[gs, bu], writes=[(actT, f)])
                for g in range(4):
                    for part in range(4):
                        wt = wbufs[wi % 3]
                        wi += 1
                        p.dma(wt[:, 0:11, :], Wd[g][:, part * 11:(part + 1) * 11, :], reads=[Wd[g]], writes=[wt])
                        for fl in range(11):
                            f = part * 11 + fl
                            for t in range(4):
                                bank = self.ps[(g % 2) * 4 + t]
                                p.op("pe", lambda e, bank=bank, wt=wt, f=f, fl=fl, t=t: e.matmul(
                                    bank[:, :], lhsT=actT[:, f, t * 128:(t + 1) * 128], rhs=wt[:, fl, :],
                                    start=(f == 0), stop=(f == 43)), reads=[wt, (actT, f)], writes=[bank])
                    for t in range(4):
                        bank = self.ps[(g % 2) * 4 + t]
                        ei += 1
                        self.evac(bank[:, :], d32[:, t, g * 512:(g + 1) * 512], bank, (d32, t), ei)
                load_wB(3)
                for t in range(4):
                    r0 = c0 + t * 128
                    postnorm_residual(d32, t, m32[:, t, :], [(m32, t)], d32[:, t, :], [(d32, t)])
                    p.dma(dst[r0:r0 + 128, :], d32[:, t, :], reads=[(d32, t)], writes=[(dst, ti)])
            p.barrier()

    def small_T(self, dst_ap, src_ap, reads, writes):
        self.p.dma(dst_ap, src_ap, reads=reads, writes=writes, allow_slow_non_contiguous=True)

    def stage_rglru(self, st):
        p, T, NT, LP = self.p, self.T, self.NT, self.LP
        if not hasattr(self, "YT"):
            self.YT = self.dscr("YT", [D, T], BF16)
        if True:
            S = lambda n, shp, dt=F32: self.sb(st, n, shp, dt)
            cw = S("rg_cw", [128, 8, 4]); cb = S("rg_cb", [128, 8]); br = S("rg_br", [128, 8]); bi_ = S("rg_bi", [128, 8])
            lam = S("rg_lam", [128, 8]); cc = S("rg_c", [128, 8]); cc2 = S("rg_c2", [128, 8]); one = S("rg_one", [128, 1])
            wr32 = S("rg_wr32", [128, 8, 128]); wi32 = S("rg_wi32", [128, 8, 128])
            wrb = S("rg_wrb", [128, 8, 128], BF16); wib = S("rg_wib", [128, 8, 128], BF16)
            hst = S("rg_hst", [128, 8]); h0s = S("rg_h0s", [128, 8, NSAMP])
            for tap in range(4):
                self.small_T(cw[:, :, tap], self.l0_conv_w[tap, :].rearrange("(b c) -> c b", c=128), [self.l0_conv_w], [cw])
            for (dst, srcv) in ((cb, self.l0_conv_b), (br, self.l0_b_r), (bi_, self.l0_b_i), (lam, self.l0_lam)):
                self.small_T(dst[:, :], srcv[:].rearrange("(b c) -> c b", c=128), [srcv], [dst])
            p.dma(wr32[:, :, :], self.l0_w_r[:, :, :].rearrange("b i j -> i b j"), reads=[self.l0_w_r], writes=[wr32])
            p.dma(wi32[:, :, :], self.l0_w_i[:, :, :].rearrange("b i j -> i b j"), reads=[self.l0_w_i], writes=[wi32])
            for b in range(8):
                self.small_T(h0s[:, b, :], self.st_rg_h[:, b * 128:(b + 1) * 128].rearrange("s c -> c s"), [self.st_rg_h], [h0s])
            p.op("dve", lambda e: e.tensor_copy(out=wrb[:, :, :], in_=wr32[:, :, :]), reads=[wr32], writes=[wrb])
            p.op("dve", lambda e: e.tensor_copy(out=wib[:, :, :], in_=wi32[:, :, :]), reads=[wi32], writes=[wib])
            p.op("dve", lambda e: e.memset(one[:, :], 1.0), writes=[one])
            p.op("dve", lambda e: e.memset(hst[:, :], 0.0), writes=[hst])
            p.op("act", lambda e: e.activation(out=cc[:, :], in_=lam[:, :], func=AF.Exp, scale=-1.0), reads=[lam], writes=[cc])
            p.op("act", lambda e: e.activation(out=cc[:, :], in_=cc[:, :], func=AF.Ln, bias=one[:, 0:1]), reads=[cc, one], writes=[cc])
            p.op("dve", lambda e: e.tensor_scalar(out=cc[:, :], in0=cc[:, :], scalar1=-8.0, scalar2=None, op0=ALU.mult), reads=[cc], writes=[cc])
            p.op("dve", lambda e: e.tensor_scalar(out=cc2[:, :], in0=cc[:, :], scalar1=2.0, scalar2=None, op0=ALU.mult), reads=[cc], writes=[cc2])
            NB = 2
            xa = [S(f"rg_xa{i}", [128, 536]) for i in range(NB)]
            ga = [S(f"rg_ga{i}", [128, TS]) for i in range(NB)]
            xc = [S(f"rg_xc{i}", [128, TS]) for i in range(NB)]
            xcb = [S(f"rg_xcb{i}", [128, TS], BF16) for i in range(NB)]
            rr = [S(f"rg_r{i}", [128, TS]) for i in range(NB)]
            gg = [S(f"rg_g{i}", [128, TS]) for i in range(NB)]
            aa = [S(f"rg_a{i}", [128, TS]) for i in range(NB)]
            uu = [S(f"rg_u{i}", [128, TS]) for i in range(NB)]
            hh = [S(f"rg_h{i}", [128, TS]) for i in range(NB)]
            t1 = [S(f"rg_t1{i}", [128, TS]) for i in range(NB)]
            yo = [S(f"rg_yo{i}", [128, TS], BF16) for i in range(NB)]
            it = 0
            for ti in range(NT):
                samp = ti == NT - 1
                nseg, sl = (NSAMP, DSEQ) if samp else (1, TS)
                c0 = ti * TS
                for b in range(8):
                    i = it % NB
                    it += 1
                    X, G, XC, XCB, R_, GI, A_, U_, H_, T1, YO = xa[i], ga[i], xc[i], xcb[i], rr[i], gg[i], aa[i], uu[i], hh[i], t1[i], yo[i]
                    rows = slice(b * 128, (b + 1) * 128)
                    xv = X[:, 0:nseg * (3 + sl)].rearrange("p (s l) -> p s l", s=nseg)
                    if samp:
                        for sg in range(NSAMP):
                            self.small_T(xv[:, sg, 0:3], self.st_rg_conv[sg, :, rows].rearrange("t c -> c t"),
                                         [self.st_rg_conv], [X])
                    elif ti == 0:
                        p.op("dve", lambda e, xv=xv: e.memset(xv[:, :, 0:3], 0.0), writes=[X])
                    else:
                        p.dma(xv[:, 0, 0:3], self.XA[rows, c0 - 3:c0], reads=[(self.XA, ti - 1)], writes=[X])
                    p.dma(xv[:, :, 3:3 + sl], self.XA[rows, c0:c0 + TS].rearrange("p (s l) -> p s l", s=nseg),
                          reads=[(self.XA, ti)], writes=[X])
                    p.dma(G[:, :], self.GA[rows, c0:c0 + TS], reads=[(self.GA, ti)], writes=[G])
                    v3 = lambda tl: tl[:, :].rearrange("p (s l) -> p s l", s=nseg)
                    xc3 = v3(XC)
                    p.op("dve", lambda e, xc3=xc3, xv=xv, b=b, sl=sl: e.tensor_scalar(
                        out=xc3, in0=xv[:, :, 0:sl], scalar1=cw[:, b, 0:1], scalar2=cb[:, b:b + 1], op0=ALU.mult, op1=ALU.add),
                        reads=[X, cw, cb], writes=[XC])
                    for tap in range(1, 4):
                        p.op("dve", lambda e, xc3=xc3, xv=xv, b=b, sl=sl, tap=tap: e.scalar_tensor_tensor(
                            out=xc3, in0=xv[:, :, tap:tap + sl], scalar=cw[:, b, tap:tap + 1], in1=xc3, op0=ALU.mult, op1=ALU.add),
                            reads=[X, cw, XC], writes=[XC])
                    p.op("act", lambda e, XCB=XCB, XC=XC: e.copy(out=XCB[:, :], in_=XC[:, :]), reads=[XC], writes=[XCB])
                    bk_r = self.ps[3]
                    bk_i = self.ps[3]
                    p.op("pe", lambda e, bk_r=bk_r, XCB=XCB, b=b: e.matmul(bk_r[:, :], lhsT=wrb[:, b, :], rhs=XCB[:, :], start=True, stop=True),
                         reads=[wrb, XCB], writes=[bk_r])
                    p.op("act", lambda e, R_=R_, bk_r=bk_r, b=b: e.activation(out=R_[:, :], in_=bk_r[:, :], func=AF.Sigmoid, bias=br[:, b:b + 1]),
                         reads=[bk_r, br], writes=[R_])
                    p.op("pe", lambda e, bk_i=bk_i, XCB=XCB, b=b: e.matmul(bk_i[:, :], lhsT=wib[:, b, :], rhs=XCB[:, :], start=True, stop=True),
                         reads=[wib, XCB], writes=[bk_i])
                    p.op("act", lambda e, GI=GI, bk_i=bk_i, b=b: e.activation(out=GI[:, :], in_=bk_i[:, :], func=AF.Sigmoid, bias=bi_[:, b:b + 1]),
                         reads=[bk_i, bi_], writes=[GI])
                    p.op("act", lambda e, A_=A_, R_=R_, b=b: e.activation(out=A_[:, :], in_=R_[:, :], func=AF.Exp, scale=cc[:, b:b + 1]),
                         reads=[R_, cc], writes=[A_])
                    p.op("act", lambda e, T1=T1, R_=R_, b=b: e.activation(out=T1[:, :], in_=R_[:, :], func=AF.Exp, scale=cc2[:, b:b + 1]),
                         reads=[R_, cc2], writes=[T1])
                    p.op("act", lambda e, T1=T1: e.activation(out=T1[:, :], in_=T1[:, :], func=AF.Sqrt, scale=-1.0, bias=one[:, 0:1]),
                         reads=[T1, one], writes=[T1])
                    p.op("dve", lambda e, U_=U_, T1=T1, GI=GI: e.tensor_tensor(out=U_[:, :], in0=T1[:, :], in1=GI[:, :], op=ALU.mult),
                         reads=[T1, GI], writes=[U_])
                    p.op("dve", lambda e, U_=U_, XC=XC: e.tensor_tensor(out=U_[:, :], in0=U_[:, :], in1=XC[:, :], op=ALU.mult),
                         reads=[U_, XC], writes=[U_])
                    a3, u3, h3 = v3(A_), v3(U_), v3(H_)
                    for sg in range(nseg):
                        init = h0s[:, b, sg:sg + 1] if samp else hst[:, b:b + 1]
                        p.op("dve", lambda e, h3=h3, a3=a3, u3=u3, sg=sg, init=init: e.tensor_tensor_scan(
                            out=h3[:, sg, :], data0=a3[:, sg, :], data1=u3[:, sg, :], initial=init, op0=ALU.mult, op1=ALU.add),
                            reads=[A_, U_, hst, h0s], writes=[H_])
                    if not samp:
                        p.op("dve", lambda e, H_=H_, b=b: e.tensor_copy(out=hst[:, b:b + 1], in_=H_[:, TS - 1:TS]), reads=[H_], writes=[hst])
                    if samp:
                        self.small_T(self.rg_h_out[1:1 + NSAMP, rows].rearrange("s c -> c s"), h3[:, :, sl - 1],
                                     [H_], [(self.rg_h_out, ("s", b))])
                        for sg in range(NSAMP):
                            self.small_T(self.rg_conv_out[1 + sg, :, rows].rearrange("t c -> c t"), xv[:, sg, sl:sl + 3],
                                         [X], [(self.rg_conv_out, ("s", b, sg))])
                    elif ti == NT - 2:
                        self.small_T(self.rg_h_out[0:1, rows].rearrange("s c -> c s"), H_[:, TS - 1:TS],
                                     [H_], [(self.rg_h_out, ("p", b))])
                        self.small_T(self.rg_conv_out[0, :, rows].rearrange("t c -> c t"), X[:, TS:TS + 3],
                                     [X], [(self.rg_conv_out, ("p", b))])
                    p.op("act", lambda e, T1=T1, G=G: e.activation(out=T1[:, :], in_=G[:, :], func=AF.Square), reads=[G], writes=[T1])
                    p.op("dve", lambda e, T1=T1: e.tensor_scalar(out=T1[:, :], in0=T1[:, :], scalar1=0.044715, scalar2=1.0, op0=ALU.mult, op1=ALU.add),
                         reads=[T1], writes=[T1])
                    p.op("dve", lambda e, T1=T1, G=G: e.tensor_tensor(out=T1[:, :], in0=T1[:, :], in1=G[:, :], op=ALU.mult), reads=[T1, G], writes=[T1])
                    p.op("act", lambda e, T1=T1: e.activation(out=T1[:, :], in_=T1[:, :], func=AF.Sigmoid, scale=1.5957691216057308),
                         reads=[T1], writes=[T1])
                    p.op("dve", lambda e, T1=T1, G=G: e.tensor_tensor(out=T1[:, :], in0=T1[:, :], in1=G[:, :], op=ALU.mult), reads=[T1, G], writes=[T1])
                    p.op("dve", lambda e, T1=T1, H_=H_, YO=YO: e.tensor_tensor(out=YO[:, :], in0=T1[:, :], in1=H_[:, :], op=ALU.mult),
                         reads=[T1, H_], writes=[YO])
                    p.dma(self.YT[rows, c0:c0 + TS], YO[:, :], reads=[YO], writes=[(self.YT, ti)])
            p.barrier()

    def stage_cacheprep(self, st):
        p, PAST = self.p, self.PAST
        self.KcT = self.dscr("KcT", [NSAMP, 1024, PAST], BF16)
        self.Vcb = self.dscr("Vcb", [NSAMP, PAST, 1024], BF16)
        if True:
            kf = [self.sb(st, f"cp_kf{i}", [128, 1024], F32) for i in range(2)]
            vf = [self.sb(st, f"cp_vf{i}", [128, 1024], F32) for i in range(2)]
            vb = [self.sb(st, f"cp_vb{i}", [128, 1024], BF16) for i in range(2)]
            kt = [self.sb(st, f"cp_kt{i}", [128, 8, 128], BF16) for i in range(2)]
            it = 0
            for sg in range(NSAMP):
                for kb in range(PAST // 128):
                    i = it % 2
                    it += 1
                    r = slice(kb * 128, (kb + 1) * 128)
                    p.dma(kf[i][:, :], self.ck[sg, r, :], reads=[self.ck], writes=[kf[i]])
                    p.dma(vf[i][:, :], self.cv[sg, r, :], reads=[self.cv], writes=[vf[i]])
                    p.op("pool", lambda e, a=vb[i], b=vf[i]: e.tensor_copy(out=a[:, :], in_=b[:, :]), reads=[vf[i]], writes=[vb[i]])
                    p.dma(self.Vcb[sg, r, :], vb[i][:, :], reads=[vb[i]], writes=[(self.Vcb, sg)])
                    for hq in range(2):
                        bank = self.ps[hq]
                        for j in range(4):
                            h = hq * 4 + j
                            p.op("pe", lambda e, bank=bank, j=j, h=h, i=i: e.transpose(
                                out=bank[:, j * 128:(j + 1) * 128], in_=kf[i][:, h * 128:(h + 1) * 128], identity=self.ident[:, :]),
                                reads=[kf[i], self.ident], writes=[bank])
                        dstv = kt[i][:, hq * 4:(hq + 1) * 4, :]
                        srcv = bank[:, :].rearrange("p (a b) -> p a b", a=4)
                        if hq == 0:
                            p.op("act", lambda e, dstv=dstv, srcv=srcv: e.copy(out=dstv, in_=srcv), reads=[bank], writes=[(kt[i], hq)])
                        else:
                            p.op("dve", lambda e, dstv=dstv, srcv=srcv: e.tensor_copy(out=dstv, in_=srcv), reads=[bank], writes=[(kt[i], hq)])
                    p.dma(self.KcT[sg, :, r].rearrange("(h c) k -> c h k", c=128), kt[i][:, :, :], reads=[kt[i]], writes=[(self.KcT, sg)])
            p.barrier()

    def stage_attn(self, st):
        p, T, NT, LP, PAST = self.p, self.T, self.NT, self.LP, self.PAST
        if not hasattr(self, "YT"):
            self.YT = self.dscr("YT", [D, T], BF16)
        NKB = LP // 128
        NCB = PAST // 128
        lam_init = 0.8 - 0.6 * 1.0
        if True:
            S = lambda n, shp, dt=F32: self.sb(st, n, shp, dt)
            lq = S("at_lq", [128, 4, 64]); lsum = S("at_ls", [128, 2]); neglam = S("at_nl", [128, 1])
            for i, srcv in enumerate((self.l0_lq1, self.l0_lk1, self.l0_lq2, self.l0_lk2)):
                p.dma(lq[:, i, :], srcv[:].rearrange("(o d) -> o d", o=1).to_broadcast([128, 64]), reads=[srcv], writes=[lq])
            p.op("dve", lambda e: e.tensor_tensor(out=lq[:, 0, :], in0=lq[:, 0, :], in1=lq[:, 1, :], op=ALU.mult), reads=[lq], writes=[lq])
            p.op("dve", lambda e: e.tensor_tensor(out=lq[:, 2, :], in0=lq[:, 2, :], in1=lq[:, 3, :], op=ALU.mult), reads=[lq], writes=[lq])
            p.op("dve", lambda e: e.reduce_sum(out=lsum[:, 0:1], in_=lq[:, 0, :], axis=AX.X), reads=[lq], writes=[lsum])
            p.op("dve", lambda e: e.reduce_sum(out=lsum[:, 1:2], in_=lq[:, 2, :], axis=AX.X), reads=[lq], writes=[lsum])
            p.op("act", lambda e: e.activation(out=lsum[:, :], in_=lsum[:, :], func=AF.Exp), reads=[lsum], writes=[lsum])
            p.op("dve", lambda e: e.tensor_tensor(out=neglam[:, :], in0=lsum[:, 1:2], in1=lsum[:, 0:1], op=ALU.subtract), reads=[lsum], writes=[neglam])
            p.op("dve", lambda e: e.tensor_scalar(out=neglam[:, :], in0=neglam[:, :], scalar1=-lam_init, scalar2=None, op0=ALU.add), reads=[neglam], writes=[neglam])
            subw = S("at_subw", [128, 128])
            p.dma(subw[:, :], self.l0_subln[:].rearrange("(o d) -> o d", o=1).to_broadcast([128, 128]), reads=[self.l0_subln], writes=[subw])
            p.op("dve", lambda e: e.tensor_scalar(out=subw[:, :], in0=subw[:, :], scalar1=1.0 - lam_init, scalar2=None, op0=ALU.mult), reads=[subw], writes=[subw])
            maskb = S("at_mb", [128, 4, 512], BF16)
            p.dma(maskb[:, :, :], self.c_mask[:, :].rearrange("p (a b) -> p a b", a=4), reads=[self.c_mask], writes=[maskb])
            KTs = [S(f"at_kt{i}", [128, max(LP, PAST + DSEQ)], BF16) for i in range(2)]
            QTs = [S(f"at_qt{i}", [128, LP + NSAMP * DSEQ], BF16) for i in range(2)]
            Vs = [S(f"at_v{i}", [128, max(NKB, NCB + 1), 130], BF16) for i in range(2)]
            for i in range(2):
                p.op("pool", lambda e, i=i: e.memset(Vs[i][:, :, 128:130], 1.0), writes=[(Vs[i], "ones")])
            Pb = [S(f"at_p{i}", [128, 512], BF16) for i in range(4)]
            rc = S("at_rc", [128, 2]); tq = S("at_tq", [128, 128]); oq = S("at_oq", [128, 128]); junk = S("at_junk", [128, 128])
            ss = S("at_ss", [128, 1]); rstd = S("at_rstd", [128, 1])
            ybT = [S(f"at_ybT{i}", [128, 512], BF16) for i in range(2)]
            pi = [0]
            yi = [0]
            sbk = [0]

            def obank(m, sub):
                idx = m * 4 + sub
                return self.ps[4 + idx // 3], (idx % 3) * 129

            def attend(KT, QT, V, qc0, nq, blocks, h, ycol0, nsub, subq):
                nb = len(blocks)
                touched = set()
                last_for_sub = {}
                for bi_, (kc, nk, vs, mi, fs) in enumerate(blocks):
                    for sub in range(fs, nsub):
                        last_for_sub[sub] = bi_
                first_for_sub = {}
                for bi_, (kc, nk, vs, mi, fs) in enumerate(blocks):
                    for sub in range(fs, nsub):
                        first_for_sub.setdefault(sub, bi_)
                def emit_s(bi_):
                    kc, nk, vs, mi, fs = blocks[bi_]
                    Ps = []
                    for m in range(2):
                        bank = self.ps[sbk[0] % 3]
                        sbk[0] += 1
                        pr = slice(m * 64, (m + 1) * 64)
                        p.op("pe", lambda e, bank=bank, pr=pr, kc=kc, nk=nk: e.matmul(
                            bank[0:nk, 0:nq], lhsT=KT[pr, kc:kc + nk], rhs=QT[pr, qc0:qc0 + nq], start=True, stop=True),
                            reads=[KT, QT], writes=[bank])
                        P = Pb[pi[0] % 4]
                        pi[0] += 1
                        p.op("act", lambda e, P=P, bank=bank, nk=nk: e.activation(out=P[0:nk, 0:nq], in_=bank[0:nk, 0:nq], func=AF.Exp, scale=0.125),
                             reads=[bank], writes=[P])
                        if mi is not None:
                            eng = "dve"
                            p.op(eng, lambda e, P=P, mi=mi: e.tensor_tensor(out=P[:, :], in0=P[:, :], in1=maskb[:, mi, :], op=ALU.mult),
                                 reads=[P, maskb], writes=[P])
                        Ps.append(P)
                    return Ps

                def emit_pv(bi_, Ps):
                    kc, nk, vs, mi, fs = blocks[bi_]
                    for m in range(2):
                        for sub in range(fs, nsub):
                            ob, oc = obank(m, sub)
                            st_ = id(ob) not in touched
                            touched.add(id(ob))
                            assert (not st_) or bi_ == 0
                            p.op("pe", lambda e, ob=ob, oc=oc, P=Ps[m], sub=sub, vs=vs, nk=nk, st_=st_: e.matmul(
                                ob[0:subq, oc:oc + 129], lhsT=P[0:nk, sub * 128:sub * 128 + subq], rhs=V[0:nk, vs, 0:129],
                                start=st_, stop=False, skip_group_check=True),
                                reads=[Ps[m], V], writes=[(ob, oc)])

                cur = emit_s(0)
                for bi_ in range(nb):
                    nxt = emit_s(bi_ + 1) if bi_ + 1 < nb else None
                    emit_pv(bi_, cur)
                    cur = nxt
                yT = ybT[yi[0] % 2]
                yi[0] += 1
                for sub in range(nsub):
                    o1, c1 = obank(0, sub)
                    o2, c2 = obank(1, sub)
                    q = subq
                    p.op("dve", lambda e, o1=o1, c1=c1, q=q: e.reciprocal(out=rc[0:q, 0:1], in_=o1[0:q, c1 + 128:c1 + 129]), reads=[(o1, c1)], writes=[rc])
                    p.op("dve", lambda e, o2=o2, c2=c2, q=q: e.reciprocal(out=rc[0:q, 1:2], in_=o2[0:q, c2 + 128:c2 + 129]), reads=[(o2, c2)], writes=[rc])
                    p.op("dve", lambda e, q=q: e.tensor_tensor(out=rc[0:q, 1:2], in0=rc[0:q, 1:2], in1=neglam[0:q, :], op=ALU.mult), reads=[rc, neglam], writes=[rc])
                    p.op("dve", lambda e, o2=o2, c2=c2, q=q: e.tensor_scalar(out=tq[0:q, :], in0=o2[0:q, c2:c2 + 128], scalar1=rc[0:q, 1:2], scalar2=None, op0=ALU.mult),
                         reads=[(o2, c2), rc], writes=[tq])
                    p.op("dve", lambda e, o1=o1, c1=c1, q=q: e.scalar_tensor_tensor(out=oq[0:q, :], in0=o1[0:q, c1:c1 + 128], scalar=rc[0:q, 0:1], in1=tq[0:q, :],
                                                                                 op0=ALU.mult, op1=ALU.add), reads=[(o1, c1), rc, tq], writes=[oq])
                    p.op("dve", lambda e: e.memset(ss[:, 0:1], 0.0), writes=[ss])
                    p.op("act", lambda e, q=q: e.activation(out=junk[0:q, :], in_=oq[0:q, :], func=AF.Square, accum_out=ss[0:q, 0:1]), reads=[oq, ss], writes=[junk, ss])
                    p.op("act", lambda e, q=q: e.activation(out=rstd[0:q, 0:1], in_=ss[0:q, 0:1], func=AF.Sqrt, scale=1.0 / 128, bias=self.eps_t[0:q, 0:1]),
                         reads=[ss, self.eps_t], writes=[rstd])
                    p.op("dve", lambda e, q=q: e.reciprocal(out=rstd[0:q, 0:1], in_=rstd[0:q, 0:1]), reads=[rstd], writes=[rstd])
                    p.op("dve", lambda e, q=q: e.scalar_tensor_tensor(out=oq[0:q, :], in0=oq[0:q, :], scalar=rstd[0:q, 0:1], in1=subw[0:q, :], op0=ALU.mult, op1=ALU.mult),
                         reads=[oq, rstd, subw], writes=[oq])
                    tb = self.ps[7]
                    p.op("pe", lambda e, q=q, sub=sub: e.transpose(out=tb[:, sub * 128:sub * 128 + q], in_=oq[0:q, :], identity=self.ident[0:q, 0:q]),
                         reads=[oq, self.ident], writes=[tb])
                    p.op("act", lambda e, q=q, sub=sub, yT=yT: e.copy(out=yT[:, sub * 128:sub * 128 + q], in_=tb[:, sub * 128:sub * 128 + q]), reads=[tb], writes=[yT])
                nqt = (nsub - 1) * 128 + subq
                p.dma(self.YT[1024 + h * 128:1024 + (h + 1) * 128, ycol0:ycol0 + nqt], yT[:, 0:nqt], reads=[yT], writes=[(self.YT, ("att", h, ycol0))])

            for h in range(8):
                KT, QT, V = KTs[h % 2], QTs[h % 2], Vs[h % 2]
                rows = slice(h * 128, (h + 1) * 128)
                p.dma(KT[:, 0:LP], self.KT[rows, 0:LP], reads=[self.KT], writes=[KT])
                p.dma(QT[:, :], self.QT[rows, :], reads=[self.QT], writes=[QT])
                p.dma(V[:, 0:NKB, 0:128], self.Vb[0:LP, rows].rearrange("(kb q) d -> q kb d", q=128), reads=[self.Vb], writes=[(V, "d")])
                for qt in range(LP // 512):
                    blocks = []
                    for kb in range(4 * qt + 4):
                        j = kb - 4 * qt
                        blocks.append((kb * 128, 128, kb, (j if j >= 0 else None), max(j, 0)))
                    attend(KT, QT, V, qt * 512, 512, blocks, h, qt * 512, 4, 128)
            for h in range(8):
                rows = slice(h * 128, (h + 1) * 128)
                for sg in range(NSAMP):
                    i = (h * NSAMP + sg) % 2
                    KT, QT, V = KTs[i], QTs[i], Vs[i]
                    c0 = LP + sg * DSEQ
                    p.dma(KT[:, 0:PAST], self.KcT[sg, rows, :], reads=[(self.KcT, sg)], writes=[KT])
                    p.dma(KT[:, PAST:PAST + DSEQ], self.KT[rows, c0:c0 + DSEQ], reads=[self.KT], writes=[KT])
                    p.dma(QT[:, c0:c0 + DSEQ], self.QT[rows, c0:c0 + DSEQ], reads=[self.QT], writes=[QT])
                    p.dma(V[:, 0:NCB, 0:128], self.Vcb[sg, :, rows].rearrange("(kb q) d -> q kb d", q=128), reads=[(self.Vcb, sg)], writes=[(V, "d")])
                    p.dma(V[0:DSEQ, NCB, 0:128], self.Vb[c0:c0 + DSEQ, rows], reads=[self.Vb], writes=[(V, "d")])
                    blocks = [(kb * 128, 128, kb, None, 0) for kb in range(NCB)] + [(PAST, DSEQ, NCB, None, 0)]
                    attend(KT, QT, V, c0, DSEQ, blocks, h, c0, 1, DSEQ)
            p.barrier()

    def stage_hgrn(self):
        p, T, NT, LP = self.p, self.T, self.NT, self.LP
        with ExitStack() as st:
            S = lambda n, shp, dt=F32: self.sb(st, n, shp, dt)
            lb2 = S("hg_lb2", [128, 2, 8]); lbt = S("hg_lbt", [128, 8]); oml = S("hg_oml", [128, 8]); nw = S("hg_nw", [128, 1])
            ones = S("hg_ones", [128, 128]); cm = S("hg_cm", [128, TS]); m16f = S("hg_m16f", [128, 128]); m16 = S("hg_m16", [128, 128], BF16)
            for r in range(2):
                self.small_T(lb2[:, r, :], self.l1_hg_lb[r, :].rearrange("(h c) -> c h", c=128), [self.l1_hg_lb], [lb2])
            self.small_T(nw[:, :], self.l1_hg_nw[:].rearrange("(c o) -> c o", o=1), [self.l1_hg_nw], [nw])
            p.op("dve", lambda e: e.tensor_tensor(out=lbt[:, :], in0=lb2[:, 1, :], in1=lb2[:, 0, :], op=ALU.subtract), reads=[lb2], writes=[lbt])
            p.op("act", lambda e: e.activation(out=lbt[:, :], in_=lbt[:, :], func=AF.Sigmoid), reads=[lbt], writes=[lbt])
            p.op("dve", lambda e: e.tensor_scalar(out=oml[:, :], in0=lbt[:, :], scalar1=-1.0, scalar2=1.0, op0=ALU.mult, op1=ALU.add), reads=[lbt], writes=[oml])
            p.op("dve", lambda e: e.memset(ones[:, :], 1.0), writes=[ones])
            p.op("dve", lambda e: e.memset(cm[:, :], 1.0), writes=[cm])
            p.op("dve", lambda e: e.memset(cm[:, :].rearrange("p (c l) -> p c l", l=16)[:, :, 0:1], 0.0), writes=[cm])
            p.dma(m16f[:, :], self.c_m16[:, :], reads=[self.c_m16], writes=[m16f])
            p.op("dve", lambda e: e.tensor_copy(out=m16[:, :], in_=m16f[:, :]), reads=[m16f], writes=[m16])
            NB = 4
            mk = lambda nm, dt=F32: [S(f"hg_{nm}{i}", [128, TS], dt) for i in range(NB)]
            qf, ff, vf, gt, bt, ebt, t1, kk, kh = mk("q"), mk("f"), mk("v"), mk("g"), mk("b"), mk("eb"), mk("t1"), mk("kk"), mk("kh")
            qb, kb_ = mk("qb", BF16), mk("kb", BF16)
            attm = [[S(f"hg_am{hd}{i}", [128, 128], BF16) for i in range(2)] for hd in range(2)]
            itok = [[S(f"hg_it{hd}{i}", [128, 128], BF16) for i in range(2)] for hd in range(2)]
            khi = [[S(f"hg_khi{hd}{i}", [16, 256], BF16) for i in range(3)] for hd in range(2)]
            S32s = [S(f"hg_S32{hd}", [128, 128]) for hd in range(2)]
            Sbfs = [S(f"hg_Sbf{hd}", [128, 128], BF16) for hd in range(2)]
            osb = [S(f"hg_o{i}", [128, TS]) for i in range(2)]
            yb = [S(f"hg_y{i}", [128, TS], BF16) for i in range(2)]
            it = 0
            for hp in range(4):
                for ti in range(NT):
                    samp = ti == NT - 1
                    c0 = ti * TS
                    sets = []
                    for hd in range(2):
                        h = hp * 2 + hd
                        rows = slice(h * 128, (h + 1) * 128)
                        i = hd * 2 + (it % 2)
                        Q, F_, V_, G, B_, EB, T1, KK, KH, QB, KB = qf[i], ff[i], vf[i], gt[i], bt[i], ebt[i], t1[i], kk[i], kh[i], qb[i], kb_[i]
                        sets.append((Q, F_, V_, G, B_, EB, T1, KK, KH, QB, KB))
                        p.dma(Q[:, :], self.HQ[rows, c0:c0 + TS], reads=[(self.HQ, ti)], writes=[Q])
                        p.dma(F_[:, :], self.HF[rows, c0:c0 + TS], reads=[(self.HF, ti)], writes=[F_])
                        p.dma(V_[:, :], self.HI[rows, c0:c0 + TS], reads=[(self.HI, ti)], writes=[V_])
                        p.op("act", lambda e, G=G, F_=F_: e.activation(out=G[:, :], in_=F_[:, :], func=AF.Sigmoid), reads=[F_], writes=[G])
                        p.op("dve", lambda e, G=G, h=h: e.tensor_scalar(out=G[:, :], in0=G[:, :], scalar1=oml[:, h:h + 1], scalar2=lbt[:, h:h + 1], op0=ALU.mult, op1=ALU.add),
                             reads=[G, oml, lbt], writes=[G])
                        p.op("dve", lambda e, G=G, KK=KK: e.tensor_scalar(out=KK[:, :], in0=G[:, :], scalar1=-1.0, scalar2=1.0, op0=ALU.mult, op1=ALU.add), reads=[G], writes=[KK])
                        p.op("act", lambda e, G=G: e.activation(out=G[:, :], in_=G[:, :], func=AF.Ln), reads=[G], writes=[G])
                        p.op("dve", lambda e, G=G, B_=B_: e.tensor_tensor_scan(out=B_[:, :], data0=cm[:, :], data1=G[:, :], initial=0.0, op0=ALU.mult, op1=ALU.add),
                             reads=[G, cm], writes=[B_])
                        p.op("act", lambda e, Q=Q: e.activation(out=Q[:, :], in_=Q[:, :], func=AF.Silu), reads=[Q], writes=[Q])
                        p.op("act", lambda e, EB=EB, B_=B_: e.activation(out=EB[:, :], in_=B_[:, :], func=AF.Exp), reads=[B_], writes=[EB])
                        p.op("dve", lambda e, QB=QB, Q=Q, EB=EB: e.tensor_tensor(out=QB[:, :], in0=Q[:, :], in1=EB[:, :], op=ALU.mult), reads=[Q, EB], writes=[QB])
                        p.op("act", lambda e, T1=T1, B_=B_: e.activation(out=T1[:, :], in_=B_[:, :], func=AF.Exp, scale=-1.0), reads=[B_], writes=[T1])
                        p.op("dve", lambda e, KB=KB, KK=KK, T1=T1: e.tensor_tensor(out=KB[:, :], in0=KK[:, :], in1=T1[:, :], op=ALU.mult), reads=[KK, T1], writes=[KB])
                        b3 = B_[:, :].rearrange("p (c l) -> p c l", l=16)
                        t3 = T1[:, :].rearrange("p (c l) -> p c l", l=16)
                        p.op("dve", lambda e, b3=b3, t3=t3: e.tensor_tensor(out=t3, in0=b3[:, :, 15:16].to_broadcast([128, 32, 16]), in1=b3, op=ALU.subtract),
                             reads=[B_], writes=[T1])
                        p.op("act", lambda e, T1=T1: e.activation(out=T1[:, :], in_=T1[:, :], func=AF.Exp), reads=[T1], writes=[T1])
                        p.op("dve", lambda e, KH=KH, KK=KK, T1=T1: e.tensor_tensor(out=KH[:, :], in0=KK[:, :], in1=T1[:, :], op=ALU.mult), reads=[KK, T1], writes=[KH])
                    it += 1
                    kcnt = [0, 0]

                    def emit_T(hd, cg):
                        Q, F_, V_, G, B_, EB, T1, KK, KH, QB, KB = sets[hd]
                        cs = slice(cg * 16, (cg + 1) * 16)
                        bt_ = self.ps[4 + hd]
                        so = (cg % 2) * 256
                        kt = khi[hd][cg % 3]
                        p.op("pe", lambda e, bt_=bt_, KH=KH, cs=cs, so=so: e.transpose(out=bt_[0:16, so:so + 128], in_=KH[:, cs], identity=self.ident[:, :]),
                             reads=[KH, self.ident], writes=[(bt_, cg % 2)])
                        p.op("pe", lambda e, bt_=bt_, V_=V_, cs=cs, so=so: e.transpose(out=bt_[0:16, so + 128:so + 256], in_=V_[:, cs], identity=self.ident[:, :]),
                             reads=[V_, self.ident], writes=[(bt_, cg % 2)])
                        p.op("act", lambda e, kt=kt, bt_=bt_, so=so: e.copy(out=kt[:, :], in_=bt_[0:16, so:so + 256]), reads=[(bt_, cg % 2)], writes=[kt])

                    for blk in range(4):
                        bs = slice(blk * 128, (blk + 1) * 128)
                        for hd in range(2):
                            Q, F_, V_, G, B_, EB, T1, KK, KH, QB, KB = sets[hd]
                            ba, bo = self.ps[hd], self.ps[2 + hd]
                            am, itk = attm[hd][blk % 2], itok[hd][blk % 2]
                            p.op("pe", lambda e, ba=ba, KB=KB, QB=QB, bs=bs: e.matmul(ba[:, 0:128], lhsT=KB[:, bs], rhs=QB[:, bs], start=True, stop=True),
                                 reads=[KB, QB], writes=[(ba, 0)])
                            p.op("dve", lambda e, am=am, ba=ba: e.tensor_tensor(out=am[:, :], in0=ba[:, 0:128], in1=m16[:, :], op=ALU.mult), reads=[(ba, 0), m16], writes=[am])
                            p.op("pe", lambda e, ba=ba, V_=V_, bs=bs: e.transpose(out=ba[:, 128:256], in_=V_[:, bs], identity=self.ident[:, :]),
                                 reads=[V_, self.ident], writes=[(ba, 1)])
                            p.op("act", lambda e, itk=itk, ba=ba: e.copy(out=itk[:, :], in_=ba[:, 128:256]), reads=[(ba, 1)], writes=[itk])
                            p.op("pe", lambda e, bo=bo, itk=itk, am=am, bs=bs: e.matmul(bo[:, bs], lhsT=itk[:, :], rhs=am[:, :], start=True, stop=False, skip_group_check=True),
                                 reads=[itk, am], writes=[bo])
                            emit_T(hd, blk * 8)
                        for c in range(8):
                            cg = blk * 8 + c
                            cs = slice(cg * 16, (cg + 1) * 16)
                            for hd in range(2):
                                h = hp * 2 + hd
                                Q, F_, V_, G, B_, EB, T1, KK, KH, QB, KB = sets[hd]
                                bo, bu = self.ps[2 + hd], self.ps[6 + hd]
                                S32, Sbf = S32s[hd], Sbfs[hd]
                                if c + 1 < 8:
                                    emit_T(hd, cg + 1)
                                seq_start = (ti == 0 and cg == 0) if not samp else (cg % 4 == 0)
                                if seq_start:
                                    if samp:
                                        sg = cg // 4
                                        p.dma(S32[:, :], self.st_hg[sg, h, :, :], reads=[self.st_hg], writes=[S32])
                                    else:
                                        p.op("dve", lambda e, S32=S32: e.memset(S32[:, :], 0.0), writes=[S32])
                                    p.op("act", lambda e, S32=S32, Sbf=Sbf: e.copy(out=Sbf[:, :], in_=S32[:, :]), reads=[S32], writes=[Sbf])
                                p.op("pe", lambda e, bo=bo, QB=QB, cs=cs, Sbf=Sbf: e.matmul(bo[:, cs], lhsT=Sbf[:, :], rhs=QB[:, cs], start=False, stop=False, skip_group_check=True),
                                     reads=[Sbf, QB], writes=[bo])
                                kt = khi[hd][cg % 3]
                                p.op("pe", lambda e, bu=bu, kt=kt: e.matmul(bu[:, 0:128], lhsT=kt[0:16, 0:128], rhs=kt[0:16, 128:256], start=True, stop=True),
                                     reads=[kt], writes=[bu])
                                p.op("dve", lambda e, bu=bu, EB=EB, cg=cg, S32=S32: e.scalar_tensor_tensor(out=S32[:, :], in0=S32[:, :], scalar=EB[:, cg * 16 + 15:cg * 16 + 16], in1=bu[:, 0:128],
                                                                                                   op0=ALU.mult, op1=ALU.add), reads=[S32, EB, bu], writes=[S32])
                                p.op("act", lambda e, S32=S32, Sbf=Sbf: e.copy(out=Sbf[:, :], in_=S32[:, :]), reads=[S32], writes=[Sbf])
                                seq_end = (ti == NT - 2 and cg == 31) if not samp else (cg % 4 == 3)
                                if seq_end:
                                    seq = (1 + cg // 4) if samp else 0
                                    p.dma(self.hg_out[seq, h, :, :], S32[:, :], reads=[S32], writes=[(self.hg_out, (seq, h))])
                    for hd in range(2):
                        h = hp * 2 + hd
                        Q, F_, V_, G, B_, EB, T1, KK, KH, QB, KB = sets[hd]
                        bo, bn = self.ps[2 + hd], self.ps[hd]
                        O_, Y_ = osb[hd], yb[hd]
                        p.op("act", lambda e, O_=O_, bo=bo: e.copy(out=O_[:, :], in_=bo[:, :]), reads=[bo], writes=[O_])
                        p.op("act", lambda e, T1=T1, O_=O_: e.activation(out=T1[:, :], in_=O_[:, :], func=AF.Square), reads=[O_], writes=[T1])
                        p.op("pe", lambda e, bn=bn, T1=T1: e.matmul(bn[:, :], lhsT=ones[:, :], rhs=T1[:, :], start=True, stop=True), reads=[ones, T1], writes=[bn])
                        p.op("act", lambda e, T1=T1, bn=bn: e.activation(out=T1[:, :], in_=bn[:, :], func=AF.Sqrt, scale=1.0 / 128, bias=self.eps_t[:, 0:1]),
                             reads=[bn, self.eps_t], writes=[T1])
                        p.op("dve", lambda e, T1=T1: e.reciprocal(out=T1[:, :], in_=T1[:, :]), reads=[T1], writes=[T1])
                        p.op("dve", lambda e, O_=O_, T1=T1, Y_=Y_: e.scalar_tensor_tensor(out=Y_[:, :], in0=O_[:, :], scalar=nw[:, 0:1], in1=T1[:, :], op0=ALU.mult, op1=ALU.mult),
                             reads=[O_, T1, nw], writes=[Y_])
                        p.dma(self.YT[1024 + h * 128:1024 + (h + 1) * 128, c0:c0 + TS], Y_[:, :], reads=[Y_], writes=[(self.YT, ("hg", h, ti))])
            p.barrier()

    def stage_ssdconv(self):
        p, T, NT, LP = self.p, self.T, self.NT, self.LP
        self.XT = self.dscr("XT", [T, 1280], F32)
        self.BCt = self.dscr("BCt", [512, T], BF16)
        with ExitStack() as st:
            S = lambda n, shp, dt=F32: self.sb(st, n, shp, dt)
            cw = S("sc_cw", [128, 12, 4]); cb = S("sc_cb", [128, 12])
            for tap in range(4):
                self.small_T(cw[:, :, tap], self.l1_conv_w[tap, :].rearrange("(b c) -> c b", c=128), [self.l1_conv_w], [cw])
            self.small_T(cb[:, :], self.l1_conv_b[:].rearrange("(b c) -> c b", c=128), [self.l1_conv_b], [cb])
            xa = [S(f"sc_xa{i}", [128, 536]) for i in range(2)]
            xc = [S(f"sc_xc{i}", [128, TS]) for i in range(2)]
            xcb = [S(f"sc_xcb{i}", [128, TS], BF16) for i in range(2)]
            xtk = [S(f"sc_xt{i}", [128, 4, 128]) for i in range(2)]
            it = 0
            for ti in range(NT):
                samp = ti == NT - 1
                nseg, sl = (NSAMP, DSEQ) if samp else (1, TS)
                c0 = ti * TS
                for b in range(12):
                    i = it % 2
                    it += 1
                    X, XC, XCB, XTK = xa[i], xc[i], xcb[i], xtk[i]
                    rows = slice(b * 128, (b + 1) * 128)
                    xv = X[:, 0:nseg * (3 + sl)].rearrange("p (s l) -> p s l", s=nseg)
                    if samp:
                        for sg in range(NSAMP):
                            self.small_T(xv[:, sg, 0:3], self.st_ssd_conv[sg, :, rows].rearrange("t c -> c t"), [self.st_ssd_conv], [X])
                    elif ti == 0:
                        p.op("dve", lambda e, xv=xv: e.memset(xv[:, :, 0:3], 0.0), writes=[X])
                    else:
                        p.dma(xv[:, 0, 0:3], self.XBC[rows, c0 - 3:c0], reads=[(self.XBC, ti - 1)], writes=[X])
                    p.dma(xv[:, :, 3:3 + sl], self.XBC[rows, c0:c0 + TS].rearrange("p (s l) -> p s l", s=nseg), reads=[(self.XBC, ti)], writes=[X])
                    xc3 = XC[:, :].rearrange("p (s l) -> p s l", s=nseg)
                    p.op("dve", lambda e, xc3=xc3, xv=xv, b=b, sl=sl: e.tensor_scalar(out=xc3, in0=xv[:, :, 0:sl], scalar1=cw[:, b, 0:1], scalar2=cb[:, b:b + 1],
                                                                                 op0=ALU.mult, op1=ALU.add), reads=[X, cw, cb], writes=[XC])
                    for tap in range(1, 4):
                        p.op("dve", lambda e, xc3=xc3, xv=xv, b=b, sl=sl, tap=tap: e.scalar_tensor_tensor(
                            out=xc3, in0=xv[:, :, tap:tap + sl], scalar=cw[:, b, tap:tap + 1], in1=xc3, op0=ALU.mult, op1=ALU.add), reads=[X, cw, XC], writes=[XC])
                    p.op("act", lambda e, XC=XC: e.activation(out=XC[:, :], in_=XC[:, :], func=AF.Silu), reads=[XC], writes=[XC])
                    if samp:
                        for sg in range(NSAMP):
                            self.small_T(self.ssd_conv_out[1 + sg, :, rows].rearrange("t c -> c t"), xv[:, sg, sl:sl + 3], [X], [(self.ssd_conv_out, ("s", b, sg))])
                    elif ti == NT - 2:
                        self.small_T(self.ssd_conv_out[0, :, rows].rearrange("t c -> c t"), X[:, TS:TS + 3], [X], [(self.ssd_conv_out, ("p", b))])
                    if b >= 8:
                        p.op("pool", lambda e, XCB=XCB, XC=XC: e.tensor_copy(out=XCB[:, :], in_=XC[:, :]), reads=[XC], writes=[XCB])
                        p.dma(self.BCt[(b - 8) * 128:(b - 7) * 128, c0:c0 + TS], XCB[:, :], reads=[XCB], writes=[(self.BCt, ti)])
                    if b < 10:
                        bank = self.ps[it % 2]
                        for t in range(4):
                            p.op("pe", lambda e, bank=bank, XC=XC, t=t: e.transpose(out=bank[:, t * 128:(t + 1) * 128], in_=XC[:, t * 128:(t + 1) * 128], identity=self.ident[:, :]),
                                 reads=[XC, self.ident], writes=[bank])
                        p.op("act", lambda e, XTK=XTK, bank=bank: e.copy(out=XTK[:, :, :], in_=bank[:, :].rearrange("p (a b) -> p a b", a=4)), reads=[bank], writes=[XTK])
                        p.dma(self.XT[c0:c0 + TS, b * 128:(b + 1) * 128].rearrange("(t q) c -> q t c", q=128), XTK[:, :, :], reads=[XTK], writes=[(self.XT, ti)])
            p.barrier()

    def stage_ssd(self):
        p, T, NT, LP = self.p, self.T, self.NT, self.LP
        with ExitStack() as st:
            S = lambda n, shp, dt=F32: self.sb(st, n, shp, dt)
            cs = S("ss_cs", [64, 448]); dtb = S("ss_dtb", [64, 16]); aneg = S("ss_an", [64, 16]); dsk = S("ss_dsk", [64, 16]); nwB = S("ss_nw", [64, 1024])
            one = S("ss_one", [128, 1])
            p.dma(cs[:, :], self.c_ssd[:, :], reads=[self.c_ssd], writes=[cs])
            tri, ntri, ones64, mneg, id64, sel63 = cs[:, 0:64], cs[:, 64:128], cs[:, 128:192], cs[:, 192:256], cs[:, 256:320], cs[:, 320:448]
            bc16 = lambda t_: t_[:].rearrange("(o d) -> o d", o=1).to_broadcast([64, 16])
            p.dma(dtb[:, :], bc16(self.l1_dt_bias), reads=[self.l1_dt_bias], writes=[dtb])
            p.dma(aneg[:, :], bc16(self.l1_a_log), reads=[self.l1_a_log], writes=[aneg])
            p.dma(dsk[:, :], bc16(self.l1_d_skip), reads=[self.l1_d_skip], writes=[dsk])
            p.dma(nwB[:, :], self.l1_ssd_nw[:].rearrange("(o d) -> o d", o=1).to_broadcast([64, 1024]), reads=[self.l1_ssd_nw], writes=[nwB])
            p.op("act", lambda e: e.activation(out=aneg[:, :], in_=aneg[:, :], func=AF.Exp), reads=[aneg], writes=[aneg])
            p.op("dve", lambda e: e.tensor_scalar(out=aneg[:, :], in0=aneg[:, :], scalar1=-1.0, scalar2=None, op0=ALU.mult), reads=[aneg], writes=[aneg])
            p.op("dve", lambda e: e.memset(one[:, :], 1.0), writes=[one])
            S32 = S("ss_S32", [128, 1024]); Sbf = S("ss_Sbf", [128, 1024], BF16); stg = S("ss_stg", [128, 8, 128])
            NB = 2
            mk = lambda nm, shp, dt=F32: [S(f"ss_{nm}{i}", shp, dt) for i in range(NB)]
            xt, zt, dtr, bct = mk("xt", [64, 1280]), mk("zt", [64, 1024]), mk("dtr", [64, 16]), mk("bct", [128, 4, 64], BF16)
            dt_, dta, r1, r2, LT = mk("dt", [64, 16]), mk("dta", [64, 16]), mk("r1", [64, 1024]), mk("r2", [64, 1024]), mk("LT", [64, 1024])
            MT, xdt, xw, btk = mk("MT", [64, 1024], BF16), mk("xdt", [64, 1024], BF16), mk("xw", [64, 1024], BF16), mk("btk", [64, 256], BF16)
            ac, eac, wv, y32, tt = mk("ac", [64, 16]), mk("eac", [64, 16]), mk("wv", [64, 16]), mk("y32", [64, 1024]), mk("tt", [64, 1024])
            decB, ssg, yT = mk("decB", [128, 16]), mk("ssg", [64, 2]), mk("yT", [128, 8, 64], BF16)
            v3 = lambda ap: ap.rearrange("p (h l) -> p h l", h=16)
            bcl = lambda ap: ap.unsqueeze(2).to_broadcast([64, 16, 64])
            seqs = [(0, 0, LP // 64)] + [(1 + sg, LP + sg * DSEQ, 1) for sg in range(NSAMP)]
            it = 0
            for (seq, col0, nch) in seqs:
                if seq == 0:
                    p.op("dve", lambda e: e.memset(S32[:, :], 0.0), writes=[S32])
                else:
                    p.dma(stg[:, :, :], self.st_ssd[seq - 1, :, :, :].rearrange("h q n -> (h q) n").rearrange("(a q) n -> q a n", q=128), reads=[self.st_ssd], writes=[stg])
                    for a in range(8):
                        bank = self.ps[a // 4]
                        p.op("pe", lambda e, bank=bank, a=a: e.transpose(out=bank[:, (a % 4) * 128:(a % 4 + 1) * 128], in_=stg[:, a, :], identity=self.ident[:, :]),
                             reads=[stg, self.ident], writes=[bank])
                    for hf in range(2):
                        p.op("dve", lambda e, hf=hf: e.tensor_copy(out=S32[:, hf * 512:(hf + 1) * 512], in_=self.ps[hf][:, :]), reads=[self.ps[hf]], writes=[S32])
                p.op("act", lambda e: e.copy(out=Sbf[:, :], in_=S32[:, :]), reads=[S32], writes=[Sbf])
                for ch in range(nch):
                    i = it % NB
                    it += 1
                    c0 = col0 + ch * 64
                    ti = c0 // TS
                    XT_, ZT, DTR, BCT, DT_, DTA, R1, R2, LT_, MT_, XDT, XW, BTK, AC, EAC, WV, Y, TT, DEC, SSG, YT_ = (
                        xt[i], zt[i], dtr[i], bct[i], dt_[i], dta[i], r1[i], r2[i], LT[i], MT[i], xdt[i], xw[i], btk[i], ac[i], eac[i], wv[i], y32[i], tt[i], decB[i], ssg[i], yT[i])
                    p.dma(XT_[:, :], self.XT[c0:c0 + 64, :], reads=[(self.XT, ti)], writes=[XT_])
                    p.dma(ZT[:, :], self.Z[c0:c0 + 64, :], reads=[(self.Z, ti)], writes=[ZT])
                    p.dma(DTR[:, :], self.DT[c0:c0 + 64, :], reads=[(self.DT, ti)], writes=[DTR])
                    p.dma(BCT[:, :, :], self.BCt[:, c0:c0 + 64].rearrange("(a n) t -> n a t", n=128), reads=[(self.BCt, ti)], writes=[BCT])
                    p.op("dve", lambda e, DT_=DT_, DTR=DTR: e.tensor_tensor(out=DT_[:, :], in0=DTR[:, :], in1=dtb[:, :], op=ALU.add), reads=[DTR, dtb], writes=[DT_])
                    p.op("act", lambda e, DT_=DT_: e.activation(out=DT_[:, :], in_=DT_[:, :], func=AF.Exp), reads=[DT_], writes=[DT_])
                    p.op("act", lambda e, DT_=DT_: e.activation(out=DT_[:, :], in_=DT_[:, :], func=AF.Ln, bias=one[0:64, 0:1]), reads=[DT_, one], writes=[DT_])
                    p.op("dve", lambda e, DTA=DTA, DT_=DT_: e.tensor_tensor(out=DTA[:, :], in0=DT_[:, :], in1=aneg[:, :], op=ALU.mult), reads=[DT_, aneg], writes=[DTA])
                    p.op("dve", lambda e, R1=R1, DTA=DTA: e.tensor_tensor(out=v3(R1[:, :]), in0=bcl(DTA[:, :]), in1=tri.unsqueeze(1).to_broadcast([64, 16, 64]), op=ALU.mult),
                         reads=[DTA, cs], writes=[R1])
                    p.op("pool", lambda e, R2=R2, DTA=DTA: e.tensor_copy(out=v3(R2[:, :]), in_=bcl(DTA[:, :])), reads=[DTA], writes=[R2])
                    bd = [self.ps[0], self.ps[1]]
                    for hf in range(2):
                        hs = slice(hf * 512, (hf + 1) * 512)
                        p.op("pe", lambda e, hf=hf, hs=hs, R1=R1: e.matmul(bd[hf][0:64, :], lhsT=ones64, rhs=R1[:, hs], start=True, stop=False), reads=[R1, cs], writes=[bd[hf]])
                        p.op("pe", lambda e, hf=hf, hs=hs, R2=R2: e.matmul(bd[hf][0:64, :], lhsT=ntri, rhs=R2[:, hs], start=False, stop=True), reads=[R2, cs], writes=[bd[hf]])
                    bm = self.ps[6]
                    p.op("pe", lambda e, DTA=DTA: e.matmul(bm[0:64, 0:16], lhsT=tri, rhs=DTA[:, :], start=True, stop=True), reads=[DTA, cs], writes=[(bm, "ac")])
                    for hf in range(2):
                        hs = slice(hf * 512, (hf + 1) * 512)
                        p.op("dve", lambda e, hf=hf, hs=hs, LT_=LT_: e.tensor_tensor(out=LT_[:, hs].rearrange("p (h l) -> p h l", h=8), in0=bd[hf][0:64, :].rearrange("p (h l) -> p h l", h=8),
                                                                                 in1=mneg.unsqueeze(1).to_broadcast([64, 8, 64]), op=ALU.add), reads=[bd[hf], cs], writes=[LT_])
                    p.op("act", lambda e, LT_=LT_: e.activation(out=LT_[:, :], in_=LT_[:, :], func=AF.Exp), reads=[LT_], writes=[LT_])
                    p.op("act", lambda e, AC=AC: e.copy(out=AC[:, :], in_=bm[0:64, 0:16]), reads=[(bm, "ac")], writes=[AC])
                    p.op("act", lambda e, EAC=EAC, AC=AC: e.activation(out=EAC[:, :], in_=AC[:, :], func=AF.Exp), reads=[AC], writes=[EAC])
                    for g in range(2):
                        p.op("pe", lambda e, g=g, BCT=BCT: e.matmul(bm[0:64, 64 + g * 64:128 + g * 64], lhsT=BCT[:, g, :], rhs=BCT[:, 2 + g, :], start=True, stop=True),
                             reads=[BCT], writes=[(bm, "cb")])
                    for g in range(2):
                        gs = slice(g * 512, (g + 1) * 512)
                        p.op("dve", lambda e, g=g, gs=gs, MT_=MT_, LT_=LT_: e.tensor_tensor(out=MT_[:, gs].rearrange("p (h l) -> p h l", h=8), in0=LT_[:, gs].rearrange("p (h l) -> p h l", h=8),
                                                                                        in1=bm[0:64, 64 + g * 64:128 + g * 64].unsqueeze(1).to_broadcast([64, 8, 64]), op=ALU.mult),
                             reads=[LT_, (bm, "cb")], writes=[MT_])
                    xv_ = v3(XT_[:, 0:1024])
                    p.op("dve", lambda e, XDT=XDT, xv_=xv_, DT_=DT_: e.tensor_tensor(out=v3(XDT[:, :]), in0=xv_, in1=bcl(DT_[:, :]), op=ALU.mult), reads=[XT_, DT_], writes=[XDT])
                    byd = [self.ps[2], self.ps[3]]
                    for h in range(16):
                        p.op("pe", lambda e, h=h, MT_=MT_, XDT=XDT: e.matmul(byd[h // 8][0:64, (h % 8) * 64:(h % 8 + 1) * 64], lhsT=MT_[:, h * 64:(h + 1) * 64], rhs=XDT[:, h * 64:(h + 1) * 64],
                                                                            start=True, stop=True, skip_group_check=True), reads=[MT_, XDT], writes=[byd[h // 8]])
                    byo = [self.ps[4], self.ps[5]]
                    for g in range(2):
                        p.op("pe", lambda e, g=g, BCT=BCT: e.matmul(byo[g][0:64, :], lhsT=BCT[:, 2 + g, :], rhs=Sbf[:, g * 512:(g + 1) * 512], start=True, stop=True),
                             reads=[BCT, Sbf], writes=[byo[g]])
                    for g in range(2):
                        gs = slice(g * 512, (g + 1) * 512)
                        p.op("dve", lambda e, g=g, gs=gs, Y=Y, EAC=EAC: e.tensor_tensor(out=Y[:, gs].rearrange("p (h l) -> p h l", h=8), in0=byo[g][0:64, :].rearrange("p (h l) -> p h l", h=8),
                                                                                    in1=EAC[:, g * 8:(g + 1) * 8].unsqueeze(2).to_broadcast([64, 8, 64]), op=ALU.mult), reads=[byo[g], EAC], writes=[Y])
                        p.op("dve", lambda e, g=g, gs=gs, Y=Y: e.tensor_tensor(out=Y[:, gs], in0=Y[:, gs], in1=byd[g][0:64, :], op=ALU.add), reads=[Y, byd[g]], writes=[Y])
                    p.op("pool", lambda e, TT=TT, xv_=xv_: e.tensor_tensor(out=v3(TT[:, :]), in0=xv_, in1=bcl(dsk[:, :]), op=ALU.mult), reads=[XT_, dsk], writes=[TT])
                    p.op("dve", lambda e, Y=Y, TT=TT: e.tensor_tensor(out=Y[:, :], in0=Y[:, :], in1=TT[:, :], op=ALU.add), reads=[Y, TT], writes=[Y])
                    p.op("act", lambda e, ZT=ZT: e.activation(out=ZT[:, :], in_=ZT[:, :], func=AF.Silu), reads=[ZT], writes=[ZT])
                    p.op("dve", lambda e, Y=Y, ZT=ZT: e.tensor_tensor(out=Y[:, :], in0=Y[:, :], in1=ZT[:, :], op=ALU.mult), reads=[Y, ZT], writes=[Y])
                    p.op("dve", lambda e, SSG=SSG: e.memset(SSG[:, :], 0.0), writes=[SSG])
                    for g in range(2):
                        gs = slice(g * 512, (g + 1) * 512)
                        p.op("act", lambda e, g=g, gs=gs, TT=TT, Y=Y, SSG=SSG: e.activation(out=TT[:, gs], in_=Y[:, gs], func=AF.Square, accum_out=SSG[:, g:g + 1]), reads=[Y, SSG], writes=[TT, SSG])
                    p.op("act", lambda e, SSG=SSG: e.activation(out=SSG[:, :], in_=SSG[:, :], func=AF.Sqrt, scale=1.0 / 512, bias=self.eps_t[0:64, 0:1]), reads=[SSG, self.eps_t], writes=[SSG])
                    p.op("dve", lambda e, SSG=SSG: e.reciprocal(out=SSG[:, :], in_=SSG[:, :]), reads=[SSG], writes=[SSG])
                    for g in range(2):
                        gs = slice(g * 512, (g + 1) * 512)
                        p.op("dve", lambda e, g=g, gs=gs, Y=Y, SSG=SSG: e.scalar_tensor_tensor(out=Y[:, gs], in0=Y[:, gs], scalar=SSG[:, g:g + 1], in1=nwB[:, gs], op0=ALU.mult, op1=ALU.mult),
                             reads=[Y, SSG, nwB], writes=[Y])
                    btr = self.ps[7]
                    for a in range(8):
                        p.op("pe", lambda e, a=a, Y=Y: e.transpose(out=btr[:, a * 64:(a + 1) * 64], in_=Y[:, a * 128:(a + 1) * 128], identity=id64), reads=[Y, cs], writes=[btr])
                    p.op("act", lambda e, YT_=YT_: e.copy(out=YT_[:, :, :], in_=btr[:, :].rearrange("p (a l) -> p a l", a=8)), reads=[btr], writes=[YT_])
                    p.dma(self.YT[0:1024, c0:c0 + 64].rearrange("(a q) t -> q a t", q=128), YT_[:, :, :], reads=[YT_], writes=[(self.YT, ("ssd", c0))])
                    p.op("dve", lambda e, WV=WV, LT_=LT_, DT_=DT_: e.tensor_tensor(out=WV[:, :], in0=v3(LT_[:, :])[:, :, 63], in1=DT_[:, :], op=ALU.mult), reads=[LT_, DT_], writes=[WV])
                    p.op("dve", lambda e, XW=XW, xv_=xv_, WV=WV: e.tensor_tensor(out=v3(XW[:, :]), in0=xv_, in1=bcl(WV[:, :]), op=ALU.mult), reads=[XT_, WV], writes=[XW])
                    p.op("pool", lambda e, BTK=BTK, XT_=XT_: e.tensor_copy(out=BTK[:, :], in_=XT_[:, 1024:1280]), reads=[XT_], writes=[BTK])
                    p.op("pe", lambda e, AC=AC: e.matmul(bm[:, 256:272], lhsT=sel63, rhs=AC[:, :], start=True, stop=True), reads=[AC, cs], writes=[(bm, "dec")])
                    p.op("act", lambda e, DEC=DEC: e.activation(out=DEC[:, :], in_=bm[:, 256:272], func=AF.Exp), reads=[(bm, "dec")], writes=[DEC])
                    for g in range(2):
                        gs = slice(g * 512, (g + 1) * 512)
                        p.op("pe", lambda e, g=g, gs=gs, BTK=BTK, XW=XW: e.matmul(bd[g][:, :], lhsT=BTK[:, g * 128:(g + 1) * 128], rhs=XW[:, gs], start=True, stop=True), reads=[BTK, XW], writes=[bd[g]])
                        p.op("dve", lambda e, g=g, gs=gs, DEC=DEC: e.tensor_tensor(out=S32[:, gs].rearrange("p (h l) -> p h l", h=8), in0=S32[:, gs].rearrange("p (h l) -> p h l", h=8),
                                                                              in1=DEC[:, g * 8:(g + 1) * 8].unsqueeze(2).to_broadcast([128, 8, 64]), op=ALU.mult), reads=[S32, DEC], writes=[S32])
                        p.op("dve", lambda e, g=g, gs=gs: e.tensor_tensor(out=S32[:, gs], in0=S32[:, gs], in1=bd[g][:, :], op=ALU.add), reads=[S32, bd[g]], writes=[S32])
                    p.op("act", lambda e: e.copy(out=Sbf[:, :], in_=S32[:, :]), reads=[S32], writes=[Sbf])
                for a in range(8):
                    bank = self.ps[a // 4]
                    p.op("pe", lambda e, bank=bank, a=a: e.transpose(out=bank[:, (a % 4) * 128:(a % 4 + 1) * 128], in_=S32[:, a * 128:(a + 1) * 128], identity=self.ident[:, :]),
                         reads=[S32, self.ident], writes=[bank])
                for hf in range(2):
                    p.op("dve", lambda e, hf=hf: e.tensor_copy(out=stg[:, hf * 4:(hf + 1) * 4, :], in_=self.ps[hf][:, :].rearrange("p (a n) -> p a n", a=4)), reads=[self.ps[hf]], writes=[stg])
                p.dma(self.ssd_out[seq, :, :, :].rearrange("h q n -> (h q) n").rearrange("(a q) n -> q a n", q=128), stg[:, :, :], reads=[stg], writes=[(self.ssd_out, seq)])
            p.barrier()


W_NAMES = ["norm_w", "l0_w_in", "l0_conv_w", "l0_conv_b", "l0_rg_w_r", "l0_rg_b_r", "l0_rg_w_i", "l0_rg_b_i",
           "l0_rg_lambda", "l0_lq1", "l0_lk1", "l0_lq2", "l0_lk2", "l0_subln_w", "l0_w_out", "l1_w_in",
           "l1_conv_w", "l1_conv_b", "l1_dt_bias", "l1_a_log", "l1_d_skip", "l1_ssd_norm_w",
           "l1_hg_lower_bound", "l1_hg_norm_w", "l1_w_out", "ffn_w_gate", "ffn_w_up", "ffn_w_down"]


def make_consts():
    ident = np.eye(128, dtype=np.float32)
    mask = np.zeros((128, 4, 512), np.float32)
    for j in range(4):
        kc = (j * 128 + np.arange(128)) // 64
        qc = np.arange(512) // 64
        mask[:, j, :] = (kc[:, None] <= qc[None, :]).astype(np.float32)
    jj = np.arange(128)
    m16 = ((jj[:, None] // 16 == jj[None, :] // 16) & (jj[:, None] <= jj[None, :])).astype(np.float32)
    j6 = np.arange(64)
    tri = (j6[:, None] <= j6[None, :]).astype(np.float32)
    cs = np.zeros((64, 5 * 64 + 128), np.float32)
    cs[:, 0:64] = tri
    cs[:, 64:128] = -tri
    cs[:, 128:192] = 1.0
    cs[:, 192:256] = np.where(tri > 0, 0.0, -30000.0)
    cs[:, 256:320] = np.eye(64)
    cs[63, 320:448] = 1.0
    return {"c_ident": ident, "c_mask": mask.reshape(128, 2048).astype(ml_dtypes.bfloat16), "c_m16": m16, "c_ssd": cs}


def core_inputs(inp, c):
    f = lambda a: np.ascontiguousarray(np.asarray(a, dtype=np.float32))
    s0, s1 = NSAMP * c, NSAMP * (c + 1)
    LP = inp["x_prompt"].shape[1]
    m = {}
    m["x"] = f(np.concatenate([inp["x_prompt"][c].reshape(LP, D), inp["x_sample"][s0:s1].reshape(NSAMP * DSEQ, D)], 0))
    past = inp["cache_diff_k"].shape[1]
    m["ck"] = f(inp["cache_diff_k"][s0:s1].reshape(NSAMP, past, 1024))
    m["cv"] = f(inp["cache_diff_v"][s0:s1].reshape(NSAMP, past, 1024))
    m["st_rg_conv"] = f(inp["state_rglru_conv"][s0:s1])
    m["st_rg_h"] = f(inp["state_rglru_h"][s0:s1])
    m["st_ssd_conv"] = f(inp["state_ssd_conv"][s0:s1])
    m["st_ssd"] = f(inp["state_ssd"][s0:s1])
    m["st_hg"] = f(inp["state_hgrn"][s0:s1])
    for n in W_NAMES:
        m[n] = f(inp[n])
    m.update(make_consts())
    return m


_CACHE = {}


def get_builder(LP, PAST, debug=False):
    key = (LP, PAST, debug)
    if key not in _CACHE:
        b = Builder(LP, PAST, debug)
        b.build()
        _CACHE[key] = b
    return _CACHE[key]


def run_cores(inp, debug=False):
    LP = inp["x_prompt"].shape[1]
    PAST = inp["cache_diff_k"].shape[1]
    b = get_builder(LP, PAST, debug)
    maps = [core_inputs(inp, c % 2) for c in range(2)]
    import os
    ncores = int(os.environ.get("K_NCORES", "8"))
    zeros = {k: np.zeros_like(v) for k, v in maps[0].items()}
    in_maps = [maps[c] if c < 2 else zeros for c in range(ncores)]
    res = run_bass_kernel_spmd(b.nc, in_maps, core_ids=list(range(ncores)))
    return b, res.results


def kernel(**inp):
    b, r = run_cores(inp)
    LP = b.LP
    g = lambda name: [np.asarray(r[min(c, len(r) - 1)][name]) for c in range(2)]
    y, ko, vo = g("y"), g("k_out"), g("v_out")
    rgc, rgh, sc, so, ho = g("rg_conv_out"), g("rg_h_out"), g("ssd_conv_out"), g("ssd_out"), g("hg_out")
    P = lambda a, shp: np.stack([a[c][:LP] for c in range(2)], 0).reshape(shp).astype(np.float32)
    S = lambda a, shp: np.concatenate([a[c][LP:] for c in range(2)], 0).reshape(shp).astype(np.float32)
    P0 = lambda a: np.stack([a[c][0] for c in range(2)], 0).astype(np.float32)
    S0 = lambda a: np.concatenate([a[c][1:] for c in range(2)], 0).astype(np.float32)
    return (P(y, (2, LP, D)), S(y, (16, DSEQ, D)),
            P(ko, (2, LP, 8, 128)), P(vo, (2, LP, 8, 128)), P0(rgc), P0(rgh), P0(sc), P0(so), P0(ho),
            S(ko, (16, DSEQ, 8, 128)), S(vo, (16, DSEQ, 8, 128)), S0(rgc), S0(rgh), S0(sc), S0(so), S0(ho))
```

```python
import numpy as np
import ml_dtypes
from contextlib import ExitStack
import concourse.bass as bass
import concourse.mybir as mybir
from concourse.bass_utils import run_bass_kernel_spmd

F32 = mybir.dt.float32
BF16 = mybir.dt.bfloat16
ALU = mybir.AluOpType
AF = mybir.ActivationFunctionType
AX = mybir.AxisListType

D = 2048
TS = 512
NSAMP = 8
DSEQ = 64
EPS = 1e-6
D_FF = 5632
IN_EVEN = 5120
IN_ODD = 5648
ENGS = ("pe", "act", "dve", "pool", "sp")
EPOCH = 12000
NDMASEM = 40


class Res:
    __slots__ = ("w", "r")

    def __init__(self):
        self.w = None
        self.r = []


class Tile:
    def __init__(self, h, name=""):
        self.h = h
        self.name = name
        self.res = {None: Res()}

    def __getitem__(self, k):
        return self.h[k]

    def get(self, key):
        if key not in self.res:
            self.res[key] = Res()
        return self.res[key]


class Ins:
    __slots__ = ("eng", "fn", "deps", "is_dma", "sig", "sigidx", "dsem", "dval", "dprev", "pos")

    def __init__(self, eng, fn, is_dma):
        self.eng = eng
        self.fn = fn
        self.is_dma = is_dma
        self.deps = set()
        self.sig = False
        self.sigidx = -1
        self.dsem = -1
        self.dval = 0
        self.dprev = 0


def _norm_acc(a):
    if isinstance(a, Tile):
        return a, None
    return a


def _split_psum(reads, writes):
    r2, w2 = [], list(writes)
    for a in reads:
        t, k = _norm_acc(a)
        if getattr(t, "is_psum", False):
            w2.append(t)
        else:
            r2.append(a)
    w3 = []
    for a in w2:
        t, k = _norm_acc(a)
        w3.append(t if getattr(t, "is_psum", False) else a)
    return r2, w3


class Prog:
    def __init__(self, nc):
        self.nc = nc
        self.streams = {e: [] for e in ENGS}
        self.last = {e: None for e in ENGS}
        self.pending = {e: set() for e in ENGS}
        self.ndma = 0
        self.dma_last = [None] * NDMASEM
        self.dma_val = [0] * NDMASEM

    def _collect(self, ins, reads, writes):
        reads, writes = _split_psum(reads, writes)
        deps = ins.deps
        for a in reads:
            t, k = _norm_acc(a)
            rs = list(t.res.values()) if k is None else [t.get(k), t.res[None]]
            for r in rs:
                if r.w is not None:
                    deps.add(r.w)
        for a in writes:
            t, k = _norm_acc(a)
            rs = list(t.res.values()) if k is None else [t.get(k), t.res[None]]
            for r in rs:
                if r.w is not None:
                    if r.w.is_dma or r.w.eng != ins.eng or ins.is_dma:
                        deps.add(r.w)
                for x in r.r:
                    if x.is_dma or x.eng != ins.eng or ins.is_dma:
                        deps.add(x)
        for a in reads:
            t, k = _norm_acc(a)
            r = t.get(k)
            if not ins.is_dma:
                r.r = [x for x in r.r if x.is_dma or x.eng != ins.eng]
            r.r.append(ins)
        for a in writes:
            t, k = _norm_acc(a)
            if k is None:
                for r in t.res.values():
                    r.w = ins
                    r.r = []
            else:
                r = t.get(k)
                r.w = ins
                r.r = []
        deps.discard(ins)
        if self.pending[ins.eng]:
            deps.update(self.pending[ins.eng])
            self.pending[ins.eng] = set()

    _rec = None

    def start_record(self):
        self._rec = []

    def stop_record(self):
        r, self._rec = self._rec, None
        return r

    def replay(self, entries):
        for e in entries:
            if e[0] == "op":
                self.op(e[1], e[2], e[3], e[4])
            elif e[0] == "dma":
                self.dma(e[1], e[2], e[3], e[4], e[5], **e[6])

    @staticmethod
    def merge(a, b):
        a = [e for e in a if e[0] != "bar"]
        b = [e for e in b if e[0] != "bar"]
        out, i, j = [], 0, 0
        la, lb = max(len(a), 1), max(len(b), 1)
        while i < len(a) or j < len(b):
            if j >= len(b) or (i < len(a) and i * lb <= j * la):
                out.append(a[i]); i += 1
            else:
                out.append(b[j]); j += 1
        return out

    def op(self, eng, fn, reads=(), writes=()):
        if self._rec is not None:
            self._rec.append(("op", eng, fn, list(reads), list(writes)))
            return None
        ins = Ins(eng, fn, False)
        self._collect(ins, reads, writes)
        self.streams[eng].append(ins)
        self.last[eng] = ins
        return ins

    def dma(self, out_ap, in_ap, reads=(), writes=(), q="sp", **kw):
        if self._rec is not None:
            self._rec.append(("dma", out_ap, in_ap, list(reads), list(writes), q, kw))
            return None

        def fn(e, out_ap=out_ap, in_ap=in_ap, kw=kw):
            return e.dma_start(out=out_ap, in_=in_ap, **kw)
        ins = Ins(q, fn, True)
        self._collect(ins, reads, writes)
        i = self.ndma % NDMASEM
        self.ndma += 1
        ins.dsem = i
        ins.dprev = self.dma_val[i]
        self.dma_val[i] += 16
        ins.dval = self.dma_val[i]
        self.dma_last[i] = ins
        self.streams[q].append(ins)
        self.last[q] = ins
        return ins

    def barrier(self):
        if self._rec is not None:
            self._rec.append(("bar",))
            return
        lasts = [x for x in self.last.values() if x is not None]
        dl = [x for x in self.dma_last if x is not None]
        for e in ENGS:
            self.pending[e].update(x for x in lasts if x.eng != e or x.is_dma)
            self.pending[e].update(dl)

    def emit(self, es):
        nc = self.nc
        for e in ENGS:
            for ins in self.streams[e]:
                for d in ins.deps:
                    if not d.is_dma:
                        d.sig = True
        nsig = {}
        for e in ENGS:
            c = 0
            for ins in self.streams[e]:
                if ins.sig and not ins.is_dma:
                    ins.sigidx = c
                    c += 1
            nsig[e] = c
        esems = {e: [es.enter_context(nc.semaphore(f"s_{e}_{j}")) for j in range(nsig[e] // EPOCH + 1)]
                 for e in ENGS}
        dsems = [es.enter_context(nc.semaphore(f"s_dma_{j}")) for j in range(NDMASEM)]
        block = es.enter_context(nc.Block())
        engobj = {"pe": block.tensor, "act": block.scalar, "dve": block.vector, "pool": block.gpsimd,
                  "sp": block.sync}

        def run(eng_name):
            def body(e):
                waited = {}
                for ins in self.streams[eng_name]:
                    waits = {}
                    for d in ins.deps:
                        if d.is_dma:
                            key, val, sem = ("d", d.dsem), d.dval, dsems[d.dsem]
                        else:
                            j = d.sigidx // EPOCH
                            key, val, sem = (d.eng, j), d.sigidx % EPOCH + 1, esems[d.eng][j]
                        if waits.get(key, (0, None))[0] < val:
                            waits[key] = (val, sem)
                    if ins.is_dma and ins.dprev > 0:
                        key = ("d", ins.dsem)
                        if waits.get(key, (0, None))[0] < ins.dprev:
                            waits[key] = (ins.dprev, dsems[ins.dsem])
                    for key, (val, sem) in waits.items():
                        if waited.get(key, 0) < val:
                            e.wait_ge(sem, val)
                            waited[key] = val
                    bi = ins.fn(e)
                    if ins.is_dma:
                        bi.then_inc(dsems[ins.dsem], 16)
                    elif ins.sig:
                        j = ins.sigidx // EPOCH
                        bi.then_inc(esems[eng_name][j], 1)
                if eng_name == "sp":
                    for i in range(NDMASEM):
                        if self.dma_val[i] > 0:
                            e.wait_ge(dsems[i], self.dma_val[i])
            return body

        for en in ENGS:
            engobj[en](run(en))


class Builder:
    def __init__(self, LP, PAST, debug=False):
        self.LP = LP
        self.PAST = PAST
        self.T = LP + NSAMP * DSEQ
        self.NT = self.T // TS
        self.debug = debug
        self.nc = bass.Bass("TRN2", target_bir_lowering=False)
        self.p = Prog(self.nc)
        self.es = ExitStack()
        self.dbg_names = []

    def din(self, name, shape, dt=F32):
        return Tile(self.nc.dram_tensor(name, list(shape), dt, kind="ExternalInput").ap(), name)

    def dout(self, name, shape, dt=F32):
        return Tile(self.nc.dram_tensor(name, list(shape), dt, kind="ExternalOutput").ap(), name)

    def dscr(self, name, shape, dt):
        if self.debug:
            self.dbg_names.append(name)
            return Tile(self.nc.dram_tensor(name, list(shape), dt, kind="ExternalOutput").ap(), name)
        return Tile(self.nc.dram_tensor(name, list(shape), dt).ap(), name)

    def sb(self, st, name, shape, dt):
        self._sbn = getattr(self, "_sbn", 0) + 1
        name = f"{name}_{self._sbn}"
        return Tile(st.enter_context(self.nc.sbuf_tensor(name, list(shape), dt)), name)

    def build(self):
        nc, p, T, NT, LP, PAST = self.nc, self.p, self.T, self.NT, self.LP, self.PAST
        es = self.es
        with es:
            self.x = self.din("x", [T, D])
            self.ck = self.din("ck", [NSAMP, PAST, 1024])
            self.cv = self.din("cv", [NSAMP, PAST, 1024])
            self.st_rg_conv = self.din("st_rg_conv", [NSAMP, 3, 1024])
            self.st_rg_h = self.din("st_rg_h", [NSAMP, 1024])
            self.st_ssd_conv = self.din("st_ssd_conv", [NSAMP, 3, 1536])
            self.st_ssd = self.din("st_ssd", [NSAMP, 16, 64, 128])
            self.st_hg = self.din("st_hg", [NSAMP, 8, 128, 128])
            self.norm_w = self.din("norm_w", [2, 4, D])
            self.w_in0 = self.din("l0_w_in", [D, IN_EVEN])
            self.l0_conv_w = self.din("l0_conv_w", [4, 1024])
            self.l0_conv_b = self.din("l0_conv_b", [1024])
            self.l0_w_r = self.din("l0_rg_w_r", [8, 128, 128])
            self.l0_b_r = self.din("l0_rg_b_r", [1024])
            self.l0_w_i = self.din("l0_rg_w_i", [8, 128, 128])
            self.l0_b_i = self.din("l0_rg_b_i", [1024])
            self.l0_lam = self.din("l0_rg_lambda", [1024])
            self.l0_lq1 = self.din("l0_lq1", [64])
            self.l0_lk1 = self.din("l0_lk1", [64])
            self.l0_lq2 = self.din("l0_lq2", [64])
            self.l0_lk2 = self.din("l0_lk2", [64])
            self.l0_subln = self.din("l0_subln_w", [128])
            self.w_out0 = self.din("l0_w_out", [D, D])
            self.w_in1 = self.din("l1_w_in", [D, IN_ODD])
            self.l1_conv_w = self.din("l1_conv_w", [4, 1536])
            self.l1_conv_b = self.din("l1_conv_b", [1536])
            self.l1_dt_bias = self.din("l1_dt_bias", [16])
            self.l1_a_log = self.din("l1_a_log", [16])
            self.l1_d_skip = self.din("l1_d_skip", [16])
            self.l1_ssd_nw = self.din("l1_ssd_norm_w", [1024])
            self.l1_hg_lb = self.din("l1_hg_lower_bound", [2, 1024])
            self.l1_hg_nw = self.din("l1_hg_norm_w", [128])
            self.w_out1 = self.din("l1_w_out", [D, D])
            self.wg = self.din("ffn_w_gate", [2, D, D_FF])
            self.wu = self.din("ffn_w_up", [2, D, D_FF])
            self.wd = self.din("ffn_w_down", [2, D_FF, D])
            self.c_ident = self.din("c_ident", [128, 128])
            self.c_mask = self.din("c_mask", [128, 4 * 512], BF16)
            self.c_m16 = self.din("c_m16", [128, 128])
            self.c_ssd = self.din("c_ssd", [64, 5 * 64 + 128])

            self.y = self.dout("y", [T, D])
            self.k_out = self.dout("k_out", [T, 1024])
            self.v_out = self.dout("v_out", [T, 1024])
            self.rg_conv_out = self.dout("rg_conv_out", [1 + NSAMP, 3, 1024])
            self.rg_h_out = self.dout("rg_h_out", [1 + NSAMP, 1024])
            self.ssd_conv_out = self.dout("ssd_conv_out", [1 + NSAMP, 3, 1536])
            self.ssd_out = self.dout("ssd_out", [1 + NSAMP, 16, 64, 128])
            self.hg_out = self.dout("hg_out", [1 + NSAMP, 8, 128, 128])

            self.ps = [Tile(es.enter_context(nc.psum_tensor(f"ps{i}", [128, 512], F32)), f"ps{i}")
                       for i in range(8)]
            for t_ in self.ps:
                t_.is_psum = True
            self.ident = self.sb(es, "ident", [128, 128], F32)
            p.dma(self.ident[:, :], self.c_ident[:, :], reads=[self.c_ident], writes=[self.ident])
            self.eps_t = self.sb(es, "eps_t", [128, 1], F32)
            p.op("dve", lambda e: e.memset(self.eps_t[:, :], EPS), writes=[self.eps_t])

            self.stage_convert_first()
            self.stage_inproj(0)
            with ExitStack() as st:
                p.start_record()
                self.stage_rglru(st)
                self.stage_convert_rest(st, engs=("pool",), q="pool")
                rb = p.stop_record()
                p.start_record()
                self.stage_cacheprep(st)
                self.stage_attn(st)
                ra = p.stop_record()
                p.replay(Prog.merge(ra, rb))
                p.barrier()
            self.stage_outffn(0)
            self.stage_inproj(1)
            self.stage_ssdconv()
            self.stage_ssd()
            self.stage_hgrn()
            self.stage_outffn(1)
            p.barrier()
            p.emit(es)
        return nc

    BUFW = 1536

    def conv_w(self, name, W, r0c0, K, groups, bufs, engs=("dve", "pool", "act"), q="sp"):
        p = self.p
        KC = K // 128
        outs = [self.dscr(f"{name}_g{gi}", [128, KC, w], BF16) for gi, (c0, w) in enumerate(groups)]
        batches, cur = [], []
        for gi, (c0, w) in enumerate(groups):
            if cur and (c0 + w - groups[cur[0]][0] > self.BUFW or c0 != groups[cur[-1]][0] + groups[cur[-1]][1]):
                batches.append(cur)
                cur = []
            cur.append(gi)
        batches.append(cur)
        for kc in range(KC):
            src = r0c0(kc)
            for bt in batches:
                self._cvn = getattr(self, "_cvn", 0) + 1
                f32t, bft = bufs[self._cvn % 2]
                lo = groups[bt[0]][0]
                hi = groups[bt[-1]][0] + groups[bt[-1]][1]
                n = hi - lo
                p.dma(f32t[:, 0:n], src[:, lo:hi], reads=[W], writes=[f32t], q=q)
                eng = engs[self._cvn % len(engs)]
                if eng == "act":
                    p.op("act", lambda e, a=bft[:, 0:n], b=f32t[:, 0:n]: e.copy(out=a, in_=b), reads=[f32t], writes=[bft])
                else:
                    p.op(eng, lambda e, a=bft[:, 0:n], b=f32t[:, 0:n]: e.tensor_copy(out=a, in_=b), reads=[f32t], writes=[bft])
                for gi in bt:
                    c0, w = groups[gi]
                    p.dma(outs[gi][:, kc, :], bft[:, c0 - lo:c0 - lo + w], reads=[bft], writes=[(outs[gi], kc)], q=q)
        return outs

    def cv_bufs(self, st):
        return [(self.sb(st, f"cvf{i}", [128, self.BUFW], F32), self.sb(st, f"cvb{i}", [128, self.BUFW], BF16)) for i in range(2)]

    def stage_convert_first(self):
        g512 = lambda n: [(i * 512, 512) for i in range(n // 512)]
        with ExitStack() as st:
            bufs = self.cv_bufs(st)
            self.Wt_in0 = self.conv_w("wt_in0", self.w_in0, lambda kc: self.w_in0[kc * 128:(kc + 1) * 128, :], D, g512(5120), bufs)
            self.p.barrier()

    def stage_convert_rest(self, st, engs=("dve", "pool"), q="sp"):
        g512 = lambda n: [(i * 512, 512) for i in range(n // 512)]
        bufs = self.cv_bufs(st)
        _cw = self.conv_w
        self.conv_w = lambda *a, **k: _cw(*a, q=q, **k)
        try:
            self._convert_rest_body(bufs, engs, g512)
        finally:
            self.conv_w = _cw
        self.p.barrier()

    def _convert_rest_body(self, bufs, engs, g512):
        self.Wt_out0 = self.conv_w("wt_out0", self.w_out0, lambda kc: self.w_out0[kc * 128:(kc + 1) * 128, :], D, g512(2048), bufs, engs)
        self.Wt_g, self.Wt_u, self.Wt_d = [None, None], [None, None], [None, None]
        g256 = [(i * 256, 256) for i in range(22)]
        for L in range(2):
            self.Wt_g[L] = self.conv_w(f"wt_g{L}", self.wg, lambda kc, L=L: self.wg[L, kc * 128:(kc + 1) * 128, :], D, g256, bufs, engs)
            self.Wt_u[L] = self.conv_w(f"wt_u{L}", self.wu, lambda kc, L=L: self.wu[L, kc * 128:(kc + 1) * 128, :], D, g256, bufs, engs)
            self.Wt_d[L] = self.conv_w(f"wt_d{L}", self.wd, lambda kc, L=L: self.wd[L, kc * 128:(kc + 1) * 128, :], D_FF, g512(2048), bufs, engs)
            if L == 0:
                g1 = g512(2560) + [(2560, 16)] + [(2576 + i * 512, 512) for i in range(6)]
                self.Wt_in1 = self.conv_w("wt_in1", self.w_in1, lambda kc: self.w_in1[kc * 128:(kc + 1) * 128, :], D, g1, bufs, engs)
                self.Wt_out1 = self.conv_w("wt_out1", self.w_out1, lambda kc: self.w_out1[kc * 128:(kc + 1) * 128, :], D, g512(2048), bufs, engs)

    def norm_hT(self, xt, wB, hT, t, tmp, h32, ss, rstd, tb, x_ap=None, x_key=None):
        p = self.p
        if x_ap is None:
            x_ap = xt[:, :]
        xr = xt if x_key is None else (xt, x_key)
        self.rms_stats(x_ap, xr, tmp, ss, rstd, D)
        p.op("dve", lambda e: e.scalar_tensor_tensor(out=h32[:, :], in0=x_ap, scalar=rstd[:, 0:1], in1=wB[:, :],
                                                     op0=ALU.mult, op1=ALU.mult),
             reads=[xr, rstd, wB], writes=[h32])
        for kq in range(4):
            bank = tb[kq % 2]
            for j in range(4):
                k = kq * 4 + j
                p.op("pe", lambda e, bank=bank, j=j, k=k: e.transpose(out=bank[:, j * 128:(j + 1) * 128],
                                                                     in_=h32[:, k * 128:(k + 1) * 128],
                                                                     identity=self.ident[:, :]),
                     reads=[h32, self.ident], writes=[bank])
            dst = hT[:, kq * 4:(kq + 1) * 4, t * 128:(t + 1) * 128]
            srcv = bank[:, :].rearrange("p (a b) -> p a b", a=4)
            if kq % 2 == 0:
                p.op("act", lambda e, dst=dst, srcv=srcv: e.copy(out=dst, in_=srcv), reads=[bank],
                     writes=[(hT, (kq, t))])
            else:
                p.op("dve", lambda e, dst=dst, srcv=srcv: e.tensor_copy(out=dst, in_=srcv), reads=[bank],
                     writes=[(hT, (kq, t))])

    def rms_stats(self, x_ap, x_tile, tmp, ss, rstd, n, width=None):
        p = self.p
        w = n if width is None else width
        p.op("dve", lambda e: e.memset(ss[:, 0:1], 0.0), writes=[ss])
        p.op("act", lambda e: e.activation(out=tmp[:, 0:w], in_=x_ap, func=AF.Square, accum_out=ss[:, 0:1]),
             reads=[x_tile, ss], writes=[tmp, ss])
        p.op("act", lambda e: e.activation(out=rstd[:, 0:1], in_=ss[:, 0:1], func=AF.Sqrt, scale=1.0 / n,
                                           bias=self.eps_t[:, 0:1]),
             reads=[ss, self.eps_t], writes=[rstd])
        p.op("dve", lambda e: e.reciprocal(out=rstd[:, 0:1], in_=rstd[:, 0:1]), reads=[rstd], writes=[rstd])

    def load_wB(self, st, name, L, i):
        wB = self.sb(st, name, [128, D], F32)
        self.p.dma(wB[:, :], self.norm_w[L, i:i + 1, :].to_broadcast([128, D]), reads=[self.norm_w], writes=[wB])
        return wB

    def stage_inproj(self, L):
        p, T, NT = self.p, self.T, self.NT
        src = self.x if L == 0 else self.X
        if L == 0:
            self.XA = self.dscr("XA", [1024, T], F32)
            self.GA = self.dscr("GA", [1024, T], F32)
            self.QT = self.dscr("QT", [1024, T], BF16)
            self.KT = self.dscr("KT", [1024, T], BF16)
            self.Vb = self.dscr("Vb", [T, 1024], BF16)
            Wt = self.Wt_in0
            A = [(Wt[0], 512, self.XA, 0, F32), (Wt[1], 512, self.XA, 512, F32),
                 (Wt[2], 512, self.GA, 0, F32), (Wt[3], 512, self.GA, 512, F32),
                 (Wt[4], 512, self.QT, 0, BF16), (Wt[5], 512, self.QT, 512, BF16),
                 (Wt[6], 512, self.KT, 0, BF16), (Wt[7], 512, self.KT, 512, BF16)]
            B = [(Wt[6], 512, [(self.k_out, 0, F32)]), (Wt[7], 512, [(self.k_out, 512, F32)]),
                 (Wt[8], 512, [(self.v_out, 0, F32), (self.Vb, 0, BF16)]),
                 (Wt[9], 512, [(self.v_out, 512, F32), (self.Vb, 512, BF16)])]
        else:
            self.XBC = self.dscr("XBC", [1536, T], F32)
            self.HQ = self.dscr("HQ", [1024, T], F32)
            self.HF = self.dscr("HF", [1024, T], F32)
            self.HI = self.dscr("HI", [1024, T], F32)
            self.Z = self.dscr("Z", [T, 1024], F32)
            self.DT = self.dscr("DT", [T, 16], F32)
            Wt = self.Wt_in1
            A = [(Wt[2], 512, self.XBC, 0, F32), (Wt[3], 512, self.XBC, 512, F32), (Wt[4], 512, self.XBC, 1024, F32),
                 (Wt[6], 512, self.HQ, 0, F32), (Wt[7], 512, self.HQ, 512, F32),
                 (Wt[8], 512, self.HF, 0, F32), (Wt[9], 512, self.HF, 512, F32),
                 (Wt[10], 512, self.HI, 0, F32), (Wt[11], 512, self.HI, 512, F32)]
            B = [(Wt[0], 512, [(self.Z, 0, F32)]), (Wt[1], 512, [(self.Z, 512, F32)]),
                 (Wt[5], 16, [(self.DT, 0, F32)])]
        with ExitStack() as st:
            wB = self.load_wB(st, "wB", L, 0)
            xts = [self.sb(st, f"xt{i}", [128, D], F32) for i in range(2)]
            tmp = self.sb(st, "tmp", [128, D], BF16)
            h32s = [self.sb(st, f"h32_{i}", [128, D], F32) for i in range(4)]
            ss = self.sb(st, "ss", [128, 1], F32)
            rstd = self.sb(st, "rstd", [128, 1], F32)
            hTs = [self.sb(st, f"hT{i}", [128, 16, TS], BF16) for i in range(2)]
            wts = [self.sb(st, f"wt{i}", [128, 16, 512], BF16) for i in range(3)]
            sgf = [self.sb(st, f"sgf{i}", [128, 512], F32) for i in range(3)]
            sgb = [self.sb(st, f"sgb{i}", [128, 512], BF16) for i in range(3)]
            accb = self.ps[2:8]
            cnt = {"si": 0, "bi": 0, "x": 0}
            jobs = [("A",) + a for a in A] + [("B",) + b for b in B]
            nj = len(jobs)
            loads = [(ti, j) for ti in range(NT) for j in range(nj)]
            nload = [0]

            def ensure_loaded(idx):
                while nload[0] <= min(idx + 2, len(loads) - 1):
                    k = nload[0]
                    W, width = jobs[loads[k][1]][1], jobs[loads[k][1]][2]
                    wt = wts[k % 3]
                    p.dma(wt[:, :, 0:width], W[:, :, :], reads=[W], writes=[wt])
                    nload[0] += 1
                return wts[idx % 3]

            def norm1(ti, t):
                xt = xts[cnt["x"] % 2]
                cnt["x"] += 1
                r0 = ti * TS + t * 128
                p.dma(xt[:, :], src[r0:r0 + 128, :], reads=[(src, ti)], writes=[xt])
                self.rms_stats(xt[:, :], xt, tmp, ss, rstd, D)
                h32 = h32s[t]
                p.op("dve", lambda e, h32=h32, xt=xt: e.scalar_tensor_tensor(out=h32[:, :], in0=xt[:, :], scalar=rstd[:, 0:1], in1=wB[:, :],
                                                                         op0=ALU.mult, op1=ALU.mult), reads=[xt, rstd, wB], writes=[h32])

            def norm2(ti, t):
                h32, hT = h32s[t], hTs[ti % 2]
                for kq in range(4):
                    bank = self.ps[kq % 2]
                    for j in range(4):
                        k = kq * 4 + j
                        p.op("pe", lambda e, bank=bank, j=j, k=k, h32=h32: e.transpose(out=bank[:, j * 128:(j + 1) * 128], in_=h32[:, k * 128:(k + 1) * 128],
                                                                                   identity=self.ident[:, :]), reads=[h32, self.ident], writes=[bank])
                    dstv = hT[:, kq * 4:(kq + 1) * 4, t * 128:(t + 1) * 128]
                    srcv = bank[:, :].rearrange("p (a b) -> p a b", a=4)
                    if kq % 2 == 0:
                        p.op("act", lambda e, dstv=dstv, srcv=srcv: e.copy(out=dstv, in_=srcv), reads=[bank], writes=[(hT, (kq, t))])
                    else:
                        p.op("dve", lambda e, dstv=dstv, srcv=srcv: e.tensor_copy(out=dstv, in_=srcv), reads=[bank], writes=[(hT, (kq, t))])

            for t in range(4):
                norm1(0, t)
            for t in range(4):
                norm2(0, t)
            for ti in range(NT):
                hT = hTs[ti % 2]
                for j, job in enumerate(jobs):
                    wt = ensure_loaded(ti * nj + j)
                    if ti + 1 < NT:
                        t1_ = j - (nj - 6)
                        if 0 <= t1_ < 4:
                            norm1(ti + 1, t1_)
                        t2_ = j - (nj - 5)
                        if 0 <= t2_ < 4:
                            norm2(ti + 1, t2_)
                    if job[0] == "A":
                        _, W, width, dst, row0, dt = job
                        for n in range(width // 128):
                            bank = accb[cnt["bi"] % 6]
                            cnt["bi"] += 1
                            for k in range(16):
                                p.op("pe", lambda e, bank=bank, wt=wt, hT=hT, k=k, n=n: e.matmul(
                                    bank[:, :], lhsT=wt[:, k, n * 128:(n + 1) * 128], rhs=hT[:, k, :],
                                    start=(k == 0), stop=(k == 15)), reads=[wt, hT], writes=[bank])
                            sg = (sgf if dt == F32 else sgb)[cnt["si"] % 3]
                            cnt["si"] += 1
                            self.evac(bank[:, :], sg[:, :], bank, sg, cnt["si"])
                            rr = row0 + n * 128
                            p.dma(dst[rr:rr + 128, ti * TS:(ti + 1) * TS], sg[:, :], reads=[sg], writes=[(dst, ti)])
                    else:
                        _, W, width, dsts = job
                        for t in range(4):
                            bank = accb[cnt["bi"] % 6]
                            cnt["bi"] += 1
                            for k in range(16):
                                p.op("pe", lambda e, bank=bank, wt=wt, hT=hT, k=k, t=t, width=width: e.matmul(
                                    bank[:, 0:width], lhsT=hT[:, k, t * 128:(t + 1) * 128], rhs=wt[:, k, 0:width],
                                    start=(k == 0), stop=(k == 15)), reads=[wt, hT], writes=[bank])
                            r0 = ti * TS + t * 128
                            sg0 = sgf[cnt["si"] % 3]
                            cnt["si"] += 1
                            self.evac(bank[:, 0:width], sg0[:, 0:width], bank, sg0, cnt["si"])
                            for (dst, c0, dt) in dsts:
                                if dt == F32:
                                    sg = sg0
                                else:
                                    sg = sgb[cnt["si"] % 3]
                                    cnt["si"] += 1
                                    p.op("pool", lambda e, a=sg[:, 0:width], b=sg0[:, 0:width]: e.tensor_copy(out=a, in_=b),
                                         reads=[sg0], writes=[sg])
                                p.dma(dst[r0:r0 + 128, c0:c0 + width], sg[:, 0:width], reads=[sg], writes=[(dst, ti)])
            p.barrier()

    def evac(self, src_ap, dst_ap, src_t, dst_t, i):
        if i % 2 == 0:
            self.p.op("act", lambda e: e.copy(out=dst_ap, in_=src_ap), reads=[src_t], writes=[dst_t])
        else:
            self.p.op("dve", lambda e: e.tensor_copy(out=dst_ap, in_=src_ap), reads=[src_t], writes=[dst_t])

    def stage_outffn(self, L):
        p, T, NT = self.p, self.T, self.NT
        src = self.x if L == 0 else self.X
        if L == 0:
            self.X = self.dscr("X", [T, D], F32)
        dst = self.X if L == 0 else self.y
        Wo = self.Wt_out0 if L == 0 else self.Wt_out1
        Wg, Wu, Wd = self.Wt_g[L], self.Wt_u[L], self.Wt_d[L]
        YT = self.YT
        with ExitStack() as st:
            wB = self.sb(st, "wBo", [128, D], F32)
            yT = self.sb(st, "yT", [128, 16, TS], BF16)
            wbufs = [self.sb(st, f"wb{i}", [128, 16, 512], BF16) for i in range(3)]
            m32 = self.sb(st, "m32", [128, 4, D], F32)
            d32 = self.sb(st, "d32", [128, 4, D], F32)
            xres = self.sb(st, "xres", [128, D], F32)
            actT = self.sb(st, "actT", [128, 44, TS], BF16)
            tmp = self.sb(st, "tmpo", [128, D], BF16)
            h32 = self.sb(st, "h32o", [128, D], F32)
            gsb = [self.sb(st, f"gsb{i}", [128, TS], F32) for i in range(2)]
            ss = self.sb(st, "sso", [128, 1], F32)
            rstd = self.sb(st, "rstdo", [128, 1], F32)
            wi = 0
            ei = 0
            gi_ = 0

            def load_wB(i):
                p.dma(wB[:, :], self.norm_w[L, i:i + 1, :].to_broadcast([128, D]), reads=[self.norm_w], writes=[wB])

            def postnorm_residual(buf, t, res_ap, res_reads, out_ap, out_writes):
                self.rms_stats(buf[:, t, :], (buf, t), tmp, ss, rstd, D)
                p.op("dve", lambda e: e.scalar_tensor_tensor(out=h32[:, :], in0=buf[:, t, :], scalar=rstd[:, 0:1],
                                                             in1=wB[:, :], op0=ALU.mult, op1=ALU.mult),
                     reads=[(buf, t), rstd, wB], writes=[h32])
                p.op("pool", lambda e: e.tensor_tensor(out=out_ap, in0=h32[:, :], in1=res_ap, op=ALU.add),
                     reads=[h32] + res_reads, writes=out_writes)

            for ti in range(NT):
                c0 = ti * TS
                p.dma(yT[:, :, :], YT[:, c0:c0 + TS].rearrange("(k q) t -> q k t", q=128), reads=[(YT, ti)], writes=[yT])
                for g in range(4):
                    wt = wbufs[wi % 3]
                    wi += 1
                    p.dma(wt[:, :, :], Wo[g][:, :, :], reads=[Wo[g]], writes=[wt])
                    for t in range(4):
                        bank = self.ps[(g % 2) * 4 + t]
                        for k in range(16):
                            p.op("pe", lambda e, bank=bank, wt=wt, k=k, t=t: e.matmul(
                                bank[:, :], lhsT=yT[:, k, t * 128:(t + 1) * 128], rhs=wt[:, k, :],
                                start=(k == 0), stop=(k == 15)), reads=[wt, yT], writes=[bank])
                        ei += 1
                        self.evac(bank[:, :], m32[:, t, g * 512:(g + 1) * 512], bank, (m32, t), ei)
                load_wB(1)
                for t in range(4):
                    r0 = c0 + t * 128
                    p.dma(xres[:, :], src[r0:r0 + 128, :], reads=[(src, ti)], writes=[xres])
                    postnorm_residual(m32, t, xres[:, :], [xres], m32[:, t, :], [(m32, t)])
                load_wB(2)
                for t in range(4):
                    self.norm_hT(m32, wB, yT, t, tmp, h32, ss, rstd, self.ps[6:8], x_ap=m32[:, t, :], x_key=t)
                hT = yT
                for j in range(22):
                    wt = wbufs[wi % 3]
                    wi += 1
                    p.dma(wt[:, :, 0:256], Wg[j][:, :, :], reads=[Wg[j]], writes=[(wt, 0)])
                    p.dma(wt[:, :, 256:512], Wu[j][:, :, :], reads=[Wu[j]], writes=[(wt, 1)])
                    for n in range(2):
                        f = j * 2 + n
                        bg = self.ps[(f % 3) * 2]
                        bu = self.ps[(f % 3) * 2 + 1]
                        for k in range(16):
                            p.op("pe", lambda e, bg=bg, wt=wt, k=k, n=n: e.matmul(
                                bg[:, :], lhsT=wt[:, k, n * 128:(n + 1) * 128], rhs=hT[:, k, :],
                                start=(k == 0), stop=(k == 15)), reads=[(wt, 0), hT], writes=[bg])
                        for k in range(16):
                            p.op("pe", lambda e, bu=bu, wt=wt, k=k, n=n: e.matmul(
                                bu[:, :], lhsT=wt[:, k, 256 + n * 128:256 + (n + 1) * 128], rhs=hT[:, k, :],
                                start=(k == 0), stop=(k == 15)), reads=[(wt, 1), hT], writes=[bu])
                        gs = gsb[gi_ % 2]
                        gi_ += 1
                        p.op("act", lambda e, gs=gs, bg=bg: e.activation(out=gs[:, :], in_=bg[:, :], func=AF.Silu),
                             reads=[bg], writes=[gs])
                        p.op("dve", lambda e, gs=gs, bu=bu, f=f: e.tensor_tensor(out=actT[:, f, :], in0=gs[:, :],
                                                                                in1=bu[:, :], op=ALU.mult),
                             reads=[gs, bu], writes=[(actT, f)])
                for g in range(4):
                    for part in range(4):
                        wt = wbufs[wi % 3]
                        wi += 1
                        p.dma(wt[:, 0:11, :], Wd[g][:, part * 11:(part + 1) * 11, :], reads=[Wd[g]], writes=[wt])
                        for fl in range(11):
                            f = part * 11 + fl
                            for t in range(4):
                                bank = self.ps[(g % 2) * 4 + t]
                                p.op("pe", lambda e, bank=bank, wt=wt, f=f, fl=fl, t=t: e.matmul(
                                    bank[:, :], lhsT=actT[:, f, t * 128:(t + 1) * 128], rhs=wt[:, fl, :],
                                    start=(f == 0), stop=(f == 43)), reads=[wt, (actT, f)], writes=[bank])
                    for t in range(4):
                        bank = self.ps[(g % 2) * 4 + t]
                        ei += 1
                        self.evac(bank[:, :], d32[:, t, g * 512:(g + 1) * 512], bank, (d32, t), ei)
                load_wB(3)
                for t in range(4):
                    r0 = c0 + t * 128
                    postnorm_residual(d32, t, m32[:, t, :], [(m32, t)], d32[:, t, :], [(d32, t)])
                    p.dma(dst[r0:r0 + 128, :], d32[:, t, :], reads=[(d32, t)], writes=[(dst, ti)])
            p.barrier()

    def small_T(self, dst_ap, src_ap, reads, writes):
        self.p.dma(dst_ap, src_ap, reads=reads, writes=writes, allow_slow_non_contiguous=True)

    def stage_rglru(self, st):
        p, T, NT, LP = self.p, self.T, self.NT, self.LP
        if not hasattr(self, "YT"):
            self.YT = self.dscr("YT", [D, T], BF16)
        if True:
            S = lambda n, shp, dt=F32: self.sb(st, n, shp, dt)
            cw = S("rg_cw", [128, 8, 4]); cb = S("rg_cb", [128, 8]); br = S("rg_br", [128, 8]); bi_ = S("rg_bi", [128, 8])
            lam = S("rg_lam", [128, 8]); cc = S("rg_c", [128, 8]); cc2 = S("rg_c2", [128, 8]); one = S("rg_one", [128, 1])
            wr32 = S("rg_wr32", [128, 8, 128]); wi32 = S("rg_wi32", [128, 8, 128])
            wrb = S("rg_wrb", [128, 8, 128], BF16); wib = S("rg_wib", [128, 8, 128], BF16)
            hst = S("rg_hst", [128, 8]); h0s = S("rg_h0s", [128, 8, NSAMP])
            for tap in range(4):
                self.small_T(cw[:, :, tap], self.l0_conv_w[tap, :].rearrange("(b c) -> c b", c=128), [self.l0_conv_w], [cw])
            for (dst, srcv) in ((cb, self.l0_conv_b), (br, self.l0_b_r), (bi_, self.l0_b_i), (lam, self.l0_lam)):
                self.small_T(dst[:, :], srcv[:].rearrange("(b c) -> c b", c=128), [srcv], [dst])
            p.dma(wr32[:, :, :], self.l0_w_r[:, :, :].rearrange("b i j -> i b j"), reads=[self.l0_w_r], writes=[wr32])
            p.dma(wi32[:, :, :], self.l0_w_i[:, :, :].rearrange("b i j -> i b j"), reads=[self.l0_w_i], writes=[wi32])
            for b in range(8):
                self.small_T(h0s[:, b, :], self.st_rg_h[:, b * 128:(b + 1) * 128].rearrange("s c -> c s"), [self.st_rg_h], [h0s])
            p.op("dve", lambda e: e.tensor_copy(out=wrb[:, :, :], in_=wr32[:, :, :]), reads=[wr32], writes=[wrb])
            p.op("dve", lambda e: e.tensor_copy(out=wib[:, :, :], in_=wi32[:, :, :]), reads=[wi32], writes=[wib])
            p.op("dve", lambda e: e.memset(one[:, :], 1.0), writes=[one])
            p.op("dve", lambda e: e.memset(hst[:, :], 0.0), writes=[hst])
            p.op("act", lambda e: e.activation(out=cc[:, :], in_=lam[:, :], func=AF.Exp, scale=-1.0), reads=[lam], writes=[cc])
            p.op("act", lambda e: e.activation(out=cc[:, :], in_=cc[:, :], func=AF.Ln, bias=one[:, 0:1]), reads=[cc, one], writes=[cc])
            p.op("dve", lambda e: e.tensor_scalar(out=cc[:, :], in0=cc[:, :], scalar1=-8.0, scalar2=None, op0=ALU.mult), reads=[cc], writes=[cc])
            p.op("dve", lambda e: e.tensor_scalar(out=cc2[:, :], in0=cc[:, :], scalar1=2.0, scalar2=None, op0=ALU.mult), reads=[cc], writes=[cc2])
            NB = 2
            xa = [S(f"rg_xa{i}", [128, 536]) for i in range(NB)]
            ga = [S(f"rg_ga{i}", [128, TS]) for i in range(NB)]
            xc = [S(f"rg_xc{i}", [128, TS]) for i in range(NB)]
            xcb = [S(f"rg_xcb{i}", [128, TS], BF16) for i in range(NB)]
            rr = [S(f"rg_r{i}", [128, TS]) for i in range(NB)]
            gg = [S(f"rg_g{i}", [128, TS]) for i in range(NB)]
            aa = [S(f"rg_a{i}", [128, TS]) for i in range(NB)]
            uu = [S(f"rg_u{i}", [128, TS]) for i in range(NB)]
            hh = [S(f"rg_h{i}", [128, TS]) for i in range(NB)]
            t1 = [S(f"rg_t1{i}", [128, TS]) for i in range(NB)]
            yo = [S(f"rg_yo{i}", [128, TS], BF16) for i in range(NB)]
            it = 0
            for ti in range(NT):
                samp = ti == NT - 1
                nseg, sl = (NSAMP, DSEQ) if samp else (1, TS)
                c0 = ti * TS
                for b in range(8):
                    i = it % NB
                    it += 1
                    X, G, XC, XCB, R_, GI, A_, U_, H_, T1, YO = xa[i], ga[i], xc[i], xcb[i], rr[i], gg[i], aa[i], uu[i], hh[i], t1[i], yo[i]
                    rows = slice(b * 128, (b + 1) * 128)
                    xv = X[:, 0:nseg * (3 + sl)].rearrange("p (s l) -> p s l", s=nseg)
                    if samp:
                        for sg in range(NSAMP):
                            self.small_T(xv[:, sg, 0:3], self.st_rg_conv[sg, :, rows].rearrange("t c -> c t"),
                                         [self.st_rg_conv], [X])
                    elif ti == 0:
                        p.op("dve", lambda e, xv=xv: e.memset(xv[:, :, 0:3], 0.0), writes=[X])
                    else:
                        p.dma(xv[:, 0, 0:3], self.XA[rows, c0 - 3:c0], reads=[(self.XA, ti - 1)], writes=[X])
                    p.dma(xv[:, :, 3:3 + sl], self.XA[rows, c0:c0 + TS].rearrange("p (s l) -> p s l", s=nseg),
                          reads=[(self.XA, ti)], writes=[X])
                    p.dma(G[:, :], self.GA[rows, c0:c0 + TS], reads=[(self.GA, ti)], writes=[G])
                    v3 = lambda tl: tl[:, :].rearrange("p (s l) -> p s l", s=nseg)
                    xc3 = v3(XC)
                    p.op("dve", lambda e, xc3=xc3, xv=xv, b=b, sl=sl: e.tensor_scalar(
                        out=xc3, in0=xv[:, :, 0:sl], scalar1=cw[:, b, 0:1], scalar2=cb[:, b:b + 1], op0=ALU.mult, op1=ALU.add),
                        reads=[X, cw, cb], writes=[XC])
                    for tap in range(1, 4):
                        p.op("dve", lambda e, xc3=xc3, xv=xv, b=b, sl=sl, tap=tap: e.scalar_tensor_tensor(
                            out=xc3, in0=xv[:, :, tap:tap + sl], scalar=cw[:, b, tap:tap + 1], in1=xc3, op0=ALU.mult, op1=ALU.add),
                            reads=[X, cw, XC], writes=[XC])
                    p.op("act", lambda e, XCB=XCB, XC=XC: e.copy(out=XCB[:, :], in_=XC[:, :]), reads=[XC], writes=[XCB])
                    bk_r = self.ps[3]
                    bk_i = self.ps[3]
                    p.op("pe", lambda e, bk_r=bk_r, XCB=XCB, b=b: e.matmul(bk_r[:, :], lhsT=wrb[:, b, :], rhs=XCB[:, :], start=True, stop=True),
                         reads=[wrb, XCB], writes=[bk_r])
                    p.op("act", lambda e, R_=R_, bk_r=bk_r, b=b: e.activation(out=R_[:, :], in_=bk_r[:, :], func=AF.Sigmoid, bias=br[:, b:b + 1]),
                         reads=[bk_r, br], writes=[R_])
                    p.op("pe", lambda e, bk_i=bk_i, XCB=XCB, b=b: e.matmul(bk_i[:, :], lhsT=wib[:, b, :], rhs=XCB[:, :], start=True, stop=True),
                         reads=[wib, XCB], writes=[bk_i])
                    p.op("act", lambda e, GI=GI, bk_i=bk_i, b=b: e.activation(out=GI[:, :], in_=bk_i[:, :], func=AF.Sigmoid, bias=bi_[:, b:b + 1]),
                         reads=[bk_i, bi_], writes=[GI])
                    p.op("act", lambda e, A_=A_, R_=R_, b=b: e.activation(out=A_[:, :], in_=R_[:, :], func=AF.Exp, scale=cc[:, b:b + 1]),
                         reads=[R_, cc], writes=[A_])
                    p.op("act", lambda e, T1=T1, R_=R_, b=b: e.activation(out=T1[:, :], in_=R_[:, :], func=AF.Exp, scale=cc2[:, b:b + 1]),
                         reads=[R_, cc2], writes=[T1])
                    p.op("act", lambda e, T1=T1: e.activation(out=T1[:, :], in_=T1[:, :], func=AF.Sqrt, scale=-1.0, bias=one[:, 0:1]),
                         reads=[T1, one], writes=[T1])
                    p.op("dve", lambda e, U_=U_, T1=T1, GI=GI: e.tensor_tensor(out=U_[:, :], in0=T1[:, :], in1=GI[:, :], op=ALU.mult),
                         reads=[T1, GI], writes=[U_])
                    p.op("dve", lambda e, U_=U_, XC=XC: e.tensor_tensor(out=U_[:, :], in0=U_[:, :], in1=XC[:, :], op=ALU.mult),
                         reads=[U_, XC], writes=[U_])
                    a3, u3, h3 = v3(A_), v3(U_), v3(H_)
                    for sg in range(nseg):
                        init = h0s[:, b, sg:sg + 1] if samp else hst[:, b:b + 1]
                        p.op("dve", lambda e, h3=h3, a3=a3, u3=u3, sg=sg, init=init: e.tensor_tensor_scan(
                            out=h3[:, sg, :], data0=a3[:, sg, :], data1=u3[:, sg, :], initial=init, op0=ALU.mult, op1=ALU.add),
                            reads=[A_, U_, hst, h0s], writes=[H_])
                    if not samp:
                        p.op("dve", lambda e, H_=H_, b=b: e.tensor_copy(out=hst[:, b:b + 1], in_=H_[:, TS - 1:TS]), reads=[H_], writes=[hst])
                    if samp:
                        self.small_T(self.rg_h_out[1:1 + NSAMP, rows].rearrange("s c -> c s"), h3[:, :, sl - 1],
                                     [H_], [(self.rg_h_out, ("s", b))])
                        for sg in range(NSAMP):
                            self.small_T(self.rg_conv_out[1 + sg, :, rows].rearrange("t c -> c t"), xv[:, sg, sl:sl + 3],
                                         [X], [(self.rg_conv_out, ("s", b, sg))])
                    elif ti == NT - 2:
                        self.small_T(self.rg_h_out[0:1, rows].rearrange("s c -> c s"), H_[:, TS - 1:TS],
                                     [H_], [(self.rg_h_out, ("p", b))])
                        self.small_T(self.rg_conv_out[0, :, rows].rearrange("t c -> c t"), X[:, TS:TS + 3],
                                     [X], [(self.rg_conv_out, ("p", b))])
                    p.op("act", lambda e, T1=T1, G=G: e.activation(out=T1[:, :], in_=G[:, :], func=AF.Square), reads=[G], writes=[T1])
                    p.op("dve", lambda e, T1=T1: e.tensor_scalar(out=T1[:, :], in0=T1[:, :], scalar1=0.044715, scalar2=1.0, op0=ALU.mult, op1=ALU.add),
                         reads=[T1], writes=[T1])
                    p.op("dve", lambda e, T1=T1, G=G: e.tensor_tensor(out=T1[:, :], in0=T1[:, :], in1=G[:, :], op=ALU.mult), reads=[T1, G], writes=[T1])
                    p.op("act", lambda e, T1=T1: e.activation(out=T1[:, :], in_=T1[:, :], func=AF.Sigmoid, scale=1.5957691216057308),
                         reads=[T1], writes=[T1])
                    p.op("dve", lambda e, T1=T1, G=G: e.tensor_tensor(out=T1[:, :], in0=T1[:, :], in1=G[:, :], op=ALU.mult), reads=[T1, G], writes=[T1])
                    p.op("dve", lambda e, T1=T1, H_=H_, YO=YO: e.tensor_tensor(out=YO[:, :], in0=T1[:, :], in1=H_[:, :], op=ALU.mult),
                         reads=[T1, H_], writes=[YO])
                    p.dma(self.YT[rows, c0:c0 + TS], YO[:, :], reads=[YO], writes=[(self.YT, ti)])
            p.barrier()

    def stage_cacheprep(self, st):
        p, PAST = self.p, self.PAST
        self.KcT = self.dscr("KcT", [NSAMP, 1024, PAST], BF16)
        self.Vcb = self.dscr("Vcb", [NSAMP, PAST, 1024], BF16)
        if True:
            kf = [self.sb(st, f"cp_kf{i}", [128, 1024], F32) for i in range(2)]
            vf = [self.sb(st, f"cp_vf{i}", [128, 1024], F32) for i in range(2)]
            vb = [self.sb(st, f"cp_vb{i}", [128, 1024], BF16) for i in range(2)]
            kt = [self.sb(st, f"cp_kt{i}", [128, 8, 128], BF16) for i in range(2)]
            it = 0
            for sg in range(NSAMP):
                for kb in range(PAST // 128):
                    i = it % 2
                    it += 1
                    r = slice(kb * 128, (kb + 1) * 128)
                    p.dma(kf[i][:, :], self.ck[sg, r, :], reads=[self.ck], writes=[kf[i]])
                    p.dma(vf[i][:, :], self.cv[sg, r, :], reads=[self.cv], writes=[vf[i]])
                    p.op("pool", lambda e, a=vb[i], b=vf[i]: e.tensor_copy(out=a[:, :], in_=b[:, :]), reads=[vf[i]], writes=[vb[i]])
                    p.dma(self.Vcb[sg, r, :], vb[i][:, :], reads=[vb[i]], writes=[(self.Vcb, sg)])
                    for hq in range(2):
                        bank = self.ps[hq]
                        for j in range(4):
                            h = hq * 4 + j
                            p.op("pe", lambda e, bank=bank, j=j, h=h, i=i: e.transpose(
                                out=bank[:, j * 128:(j + 1) * 128], in_=kf[i][:, h * 128:(h + 1) * 128], identity=self.ident[:, :]),
                                reads=[kf[i], self.ident], writes=[bank])
                        dstv = kt[i][:, hq * 4:(hq + 1) * 4, :]
                        srcv = bank[:, :].rearrange("p (a b) -> p a b", a=4)
                        if hq == 0:
                            p.op("act", lambda e, dstv=dstv, srcv=srcv: e.copy(out=dstv, in_=srcv), reads=[bank], writes=[(kt[i], hq)])
                        else:
                            p.op("dve", lambda e, dstv=dstv, srcv=srcv: e.tensor_copy(out=dstv, in_=srcv), reads=[bank], writes=[(kt[i], hq)])
                    p.dma(self.KcT[sg, :, r].rearrange("(h c) k -> c h k", c=128), kt[i][:, :, :], reads=[kt[i]], writes=[(self.KcT, sg)])
            p.barrier()

    def stage_attn(self, st):
        p, T, NT, LP, PAST = self.p, self.T, self.NT, self.LP, self.PAST
        if not hasattr(self, "YT"):
            self.YT = self.dscr("YT", [D, T], BF16)
        NKB = LP // 128
        NCB = PAST // 128
        lam_init = 0.8 - 0.6 * 1.0
        if True:
            S = lambda n, shp, dt=F32: self.sb(st, n, shp, dt)
            lq = S("at_lq", [128, 4, 64]); lsum = S("at_ls", [128, 2]); neglam = S("at_nl", [128, 1])
            for i, srcv in enumerate((self.l0_lq1, self.l0_lk1, self.l0_lq2, self.l0_lk2)):
                p.dma(lq[:, i, :], srcv[:].rearrange("(o d) -> o d", o=1).to_broadcast([128, 64]), reads=[srcv], writes=[lq])
            p.op("dve", lambda e: e.tensor_tensor(out=lq[:, 0, :], in0=lq[:, 0, :], in1=lq[:, 1, :], op=ALU.mult), reads=[lq], writes=[lq])
            p.op("dve", lambda e: e.tensor_tensor(out=lq[:, 2, :], in0=lq[:, 2, :], in1=lq[:, 3, :], op=ALU.mult), reads=[lq], writes=[lq])
            p.op("dve", lambda e: e.reduce_sum(out=lsum[:, 0:1], in_=lq[:, 0, :], axis=AX.X), reads=[lq], writes=[lsum])
            p.op("dve", lambda e: e.reduce_sum(out=lsum[:, 1:2], in_=lq[:, 2, :], axis=AX.X), reads=[lq], writes=[lsum])
            p.op("act", lambda e: e.activation(out=lsum[:, :], in_=lsum[:, :], func=AF.Exp), reads=[lsum], writes=[lsum])
            p.op("dve", lambda e: e.tensor_tensor(out=neglam[:, :], in0=lsum[:, 1:2], in1=lsum[:, 0:1], op=ALU.subtract), reads=[lsum], writes=[neglam])
            p.op("dve", lambda e: e.tensor_scalar(out=neglam[:, :], in0=neglam[:, :], scalar1=-lam_init, scalar2=None, op0=ALU.add), reads=[neglam], writes=[neglam])
            subw = S("at_subw", [128, 128])
            p.dma(subw[:, :], self.l0_subln[:].rearrange("(o d) -> o d", o=1).to_broadcast([128, 128]), reads=[self.l0_subln], writes=[subw])
            p.op("dve", lambda e: e.tensor_scalar(out=subw[:, :], in0=subw[:, :], scalar1=1.0 - lam_init, scalar2=None, op0=ALU.mult), reads=[subw], writes=[subw])
            maskb = S("at_mb", [128, 4, 512], BF16)
            p.dma(maskb[:, :, :], self.c_mask[:, :].rearrange("p (a b) -> p a b", a=4), reads=[self.c_mask], writes=[maskb])
            KTs = [S(f"at_kt{i}", [128, max(LP, PAST + DSEQ)], BF16) for i in range(2)]
            QTs = [S(f"at_qt{i}", [128, LP + NSAMP * DSEQ], BF16) for i in range(2)]
            Vs = [S(f"at_v{i}", [128, max(NKB, NCB + 1), 130], BF16) for i in range(2)]
            for i in range(2):
                p.op("pool", lambda e, i=i: e.memset(Vs[i][:, :, 128:130], 1.0), writes=[(Vs[i], "ones")])
            Pb = [S(f"at_p{i}", [128, 512], BF16) for i in range(4)]
            rc = S("at_rc", [128, 2]); tq = S("at_tq", [128, 128]); oq = S("at_oq", [128, 128]); junk = S("at_junk", [128, 128])
            ss = S("at_ss", [128, 1]); rstd = S("at_rstd", [128, 1])
            ybT = [S(f"at_ybT{i}", [128, 512], BF16) for i in range(2)]
            pi = [0]
            yi = [0]
            sbk = [0]

            def obank(m, sub):
                idx = m * 4 + sub
                return self.ps[4 + idx // 3], (idx % 3) * 129

            def attend(KT, QT, V, qc0, nq, blocks, h, ycol0, nsub, subq):
                nb = len(blocks)
                touched = set()
                last_for_sub = {}
                for bi_, (kc, nk, vs, mi, fs) in enumerate(blocks):
                    for sub in range(fs, nsub):
                        last_for_sub[sub] = bi_
                first_for_sub = {}
                for bi_, (kc, nk, vs, mi, fs) in enumerate(blocks):
                    for sub in range(fs, nsub):
                        first_for_sub.setdefault(sub, bi_)
                def emit_s(bi_):
                    kc, nk, vs, mi, fs = blocks[bi_]
                    Ps = []
                    for m in range(2):
                        bank = self.ps[sbk[0] % 3]
                        sbk[0] += 1
                        pr = slice(m * 64, (m + 1) * 64)
                        p.op("pe", lambda e, bank=bank, pr=pr, kc=kc, nk=nk: e.matmul(
                            bank[0:nk, 0:nq], lhsT=KT[pr, kc:kc + nk], rhs=QT[pr, qc0:qc0 + nq], start=True, stop=True),
                            reads=[KT, QT], writes=[bank])
                        P = Pb[pi[0] % 4]
                        pi[0] += 1
                        p.op("act", lambda e, P=P, bank=bank, nk=nk: e.activation(out=P[0:nk, 0:nq], in_=bank[0:nk, 0:nq], func=AF.Exp, scale=0.125),
                             reads=[bank], writes=[P])
                        if mi is not None:
                            eng = "dve"
                            p.op(eng, lambda e, P=P, mi=mi: e.tensor_tensor(out=P[:, :], in0=P[:, :], in1=maskb[:, mi, :], op=ALU.mult),
                                 reads=[P, maskb], writes=[P])
                        Ps.append(P)
                    return Ps

                def emit_pv(bi_, Ps):
                    kc, nk, vs, mi, fs = blocks[bi_]
                    for m in range(2):
                        for sub in range(fs, nsub):
                            ob, oc = obank(m, sub)
                            st_ = id(ob) not in touched
                            touched.add(id(ob))
                            assert (not st_) or bi_ == 0
                            p.op("pe", lambda e, ob=ob, oc=oc, P=Ps[m], sub=sub, vs=vs, nk=nk, st_=st_: e.matmul(
                                ob[0:subq, oc:oc + 129], lhsT=P[0:nk, sub * 128:sub * 128 + subq], rhs=V[0:nk, vs, 0:129],
                                start=st_, stop=False, skip_group_check=True),
                                reads=[Ps[m], V], writes=[(ob, oc)])

                cur = emit_s(0)
                for bi_ in range(nb):
                    nxt = emit_s(bi_ + 1) if bi_ + 1 < nb else None
                    emit_pv(bi_, cur)
                    cur = nxt
                yT = ybT[yi[0] % 2]
                yi[0] += 1
                for sub in range(nsub):
                    o1, c1 = obank(0, sub)
                    o2, c2 = obank(1, sub)
                    q = subq
                    p.op("dve", lambda e, o1=o1, c1=c1, q=q: e.reciprocal(out=rc[0:q, 0:1], in_=o1[0:q, c1 + 128:c1 + 129]), reads=[(o1, c1)], writes=[rc])
                    p.op("dve", lambda e, o2=o2, c2=c2, q=q: e.reciprocal(out=rc[0:q, 1:2], in_=o2[0:q, c2 + 128:c2 + 129]), reads=[(o2, c2)], writes=[rc])
                    p.op("dve", lambda e, q=q: e.tensor_tensor(out=rc[0:q, 1:2], in0=rc[0:q, 1:2], in1=neglam[0:q, :], op=ALU.mult), reads=[rc, neglam], writes=[rc])
                    p.op("dve", lambda e, o2=o2, c2=c2, q=q: e.tensor_scalar(out=tq[0:q, :], in0=o2[0:q, c2:c2 + 128], scalar1=rc[0:q, 1:2], scalar2=None, op0=ALU.mult),
                         reads=[(o2, c2), rc], writes=[tq])
                    p.op("dve", lambda e, o1=o1, c1=c1, q=q: e.scalar_tensor_tensor(out=oq[0:q, :], in0=o1[0:q, c1:c1 + 128], scalar=rc[0:q, 0:1], in1=tq[0:q, :],
                                                                                 op0=ALU.mult, op1=ALU.add), reads=[(o1, c1), rc, tq], writes=[oq])
                    p.op("dve", lambda e: e.memset(ss[:, 0:1], 0.0), writes=[ss])
                    p.op("act", lambda e, q=q: e.activation(out=junk[0:q, :], in_=oq[0:q, :], func=AF.Square, accum_out=ss[0:q, 0:1]), reads=[oq, ss], writes=[junk, ss])
                    p.op("act", lambda e, q=q: e.activation(out=rstd[0:q, 0:1], in_=ss[0:q, 0:1], func=AF.Sqrt, scale=1.0 / 128, bias=self.eps_t[0:q, 0:1]),
                         reads=[ss, self.eps_t], writes=[rstd])
                    p.op("dve", lambda e, q=q: e.reciprocal(out=rstd[0:q, 0:1], in_=rstd[0:q, 0:1]), reads=[rstd], writes=[rstd])
                    p.op("dve", lambda e, q=q: e.scalar_tensor_tensor(out=oq[0:q, :], in0=oq[0:q, :], scalar=rstd[0:q, 0:1], in1=subw[0:q, :], op0=ALU.mult, op1=ALU.mult),
                         reads=[oq, rstd, subw], writes=[oq])
                    tb = self.ps[7]
                    p.op("pe", lambda e, q=q, sub=sub: e.transpose(out=tb[:, sub * 128:sub * 128 + q], in_=oq[0:q, :], identity=self.ident[0:q, 0:q]),
                         reads=[oq, self.ident], writes=[tb])
                    p.op("act", lambda e, q=q, sub=sub, yT=yT: e.copy(out=yT[:, sub * 128:sub * 128 + q], in_=tb[:, sub * 128:sub * 128 + q]), reads=[tb], writes=[yT])
                nqt = (nsub - 1) * 128 + subq
                p.dma(self.YT[1024 + h * 128:1024 + (h + 1) * 128, ycol0:ycol0 + nqt], yT[:, 0:nqt], reads=[yT], writes=[(self.YT, ("att", h, ycol0))])

            for h in range(8):
                KT, QT, V = KTs[h % 2], QTs[h % 2], Vs[h % 2]
                rows = slice(h * 128, (h + 1) * 128)
                p.dma(KT[:, 0:LP], self.KT[rows, 0:LP], reads=[self.KT], writes=[KT])
                p.dma(QT[:, :], self.QT[rows, :], reads=[self.QT], writes=[QT])
                p.dma(V[:, 0:NKB, 0:128], self.Vb[0:LP, rows].rearrange("(kb q) d -> q kb d", q=128), reads=[self.Vb], writes=[(V, "d")])
                for qt in range(LP // 512):
                    blocks = []
                    for kb in range(4 * qt + 4):
                        j = kb - 4 * qt
                        blocks.append((kb * 128, 128, kb, (j if j >= 0 else None), max(j, 0)))
                    attend(KT, QT, V, qt * 512, 512, blocks, h, qt * 512, 4, 128)
            for h in range(8):
                rows = slice(h * 128, (h + 1) * 128)
                for sg in range(NSAMP):
                    i = (h * NSAMP + sg) % 2
                    KT, QT, V = KTs[i], QTs[i], Vs[i]
                    c0 = LP + sg * DSEQ
                    p.dma(KT[:, 0:PAST], self.KcT[sg, rows, :], reads=[(self.KcT, sg)], writes=[KT])
                    p.dma(KT[:, PAST:PAST + DSEQ], self.KT[rows, c0:c0 + DSEQ], reads=[self.KT], writes=[KT])
                    p.dma(QT[:, c0:c0 + DSEQ], self.QT[rows, c0:c0 + DSEQ], reads=[self.QT], writes=[QT])
                    p.dma(V[:, 0:NCB, 0:128], self.Vcb[sg, :, rows].rearrange("(kb q) d -> q kb d", q=128), reads=[(self.Vcb, sg)], writes=[(V, "d")])
                    p.dma(V[0:DSEQ, NCB, 0:128], self.Vb[c0:c0 + DSEQ, rows], reads=[self.Vb], writes=[(V, "d")])
                    blocks = [(kb * 128, 128, kb, None, 0) for kb in range(NCB)] + [(PAST, DSEQ, NCB, None, 0)]
                    attend(KT, QT, V, c0, DSEQ, blocks, h, c0, 1, DSEQ)
            p.barrier()

    def stage_hgrn(self):
        p, T, NT, LP = self.p, self.T, self.NT, self.LP
        with ExitStack() as st:
            S = lambda n, shp, dt=F32: self.sb(st, n, shp, dt)
            lb2 = S("hg_lb2", [128, 2, 8]); lbt = S("hg_lbt", [128, 8]); oml = S("hg_oml", [128, 8]); nw = S("hg_nw", [128, 1])
            ones = S("hg_ones", [128, 128]); cm = S("hg_cm", [128, TS]); m16f = S("hg_m16f", [128, 128]); m16 = S("hg_m16", [128, 128], BF16)
            for r in range(2):
                self.small_T(lb2[:, r, :], self.l1_hg_lb[r, :].rearrange("(h c) -> c h", c=128), [self.l1_hg_lb], [lb2])
            self.small_T(nw[:, :], self.l1_hg_nw[:].rearrange("(c o) -> c o", o=1), [self.l1_hg_nw], [nw])
            p.op("dve", lambda e: e.tensor_tensor(out=lbt[:, :], in0=lb2[:, 1, :], in1=lb2[:, 0, :], op=ALU.subtract), reads=[lb2], writes=[lbt])
            p.op("act", lambda e: e.activation(out=lbt[:, :], in_=lbt[:, :], func=AF.Sigmoid), reads=[lbt], writes=[lbt])
            p.op("dve", lambda e: e.tensor_scalar(out=oml[:, :], in0=lbt[:, :], scalar1=-1.0, scalar2=1.0, op0=ALU.mult, op1=ALU.add), reads=[lbt], writes=[oml])
            p.op("dve", lambda e: e.memset(ones[:, :], 1.0), writes=[ones])
            p.op("dve", lambda e: e.memset(cm[:, :], 1.0), writes=[cm])
            p.op("dve", lambda e: e.memset(cm[:, :].rearrange("p (c l) -> p c l", l=16)[:, :, 0:1], 0.0), writes=[cm])
            p.dma(m16f[:, :], self.c_m16[:, :], reads=[self.c_m16], writes=[m16f])
            p.op("dve", lambda e: e.tensor_copy(out=m16[:, :], in_=m16f[:, :]), reads=[m16f], writes=[m16])
            NB = 4
            mk = lambda nm, dt=F32: [S(f"hg_{nm}{i}", [128, TS], dt) for i in range(NB)]
            qf, ff, vf, gt, bt, ebt, t1, kk, kh = mk("q"), mk("f"), mk("v"), mk("g"), mk("b"), mk("eb"), mk("t1"), mk("kk"), mk("kh")
            qb, kb_ = mk("qb", BF16), mk("kb", BF16)
            attm = [[S(f"hg_am{hd}{i}", [128, 128], BF16) for i in range(2)] for hd in range(2)]
            itok = [[S(f"hg_it{hd}{i}", [128, 128], BF16) for i in range(2)] for hd in range(2)]
            khi = [[S(f"hg_khi{hd}{i}", [16, 256], BF16) for i in range(3)] for hd in range(2)]
            S32s = [S(f"hg_S32{hd}", [128, 128]) for hd in range(2)]
            Sbfs = [S(f"hg_Sbf{hd}", [128, 128], BF16) for hd in range(2)]
            osb = [S(f"hg_o{i}", [128, TS]) for i in range(2)]
            yb = [S(f"hg_y{i}", [128, TS], BF16) for i in range(2)]
            it = 0
            for hp in range(4):
                for ti in range(NT):
                    samp = ti == NT - 1
                    c0 = ti * TS
                    sets = []
                    for hd in range(2):
                        h = hp * 2 + hd
                        rows = slice(h * 128, (h + 1) * 128)
                        i = hd * 2 + (it % 2)
                        Q, F_, V_, G, B_, EB, T1, KK, KH, QB, KB = qf[i], ff[i], vf[i], gt[i], bt[i], ebt[i], t1[i], kk[i], kh[i], qb[i], kb_[i]
                        sets.append((Q, F_, V_, G, B_, EB, T1, KK, KH, QB, KB))
                        p.dma(Q[:, :], self.HQ[rows, c0:c0 + TS], reads=[(self.HQ, ti)], writes=[Q])
                        p.dma(F_[:, :], self.HF[rows, c0:c0 + TS], reads=[(self.HF, ti)], writes=[F_])
                        p.dma(V_[:, :], self.HI[rows, c0:c0 + TS], reads=[(self.HI, ti)], writes=[V_])
                        p.op("act", lambda e, G=G, F_=F_: e.activation(out=G[:, :], in_=F_[:, :], func=AF.Sigmoid), reads=[F_], writes=[G])
                        p.op("dve", lambda e, G=G, h=h: e.tensor_scalar(out=G[:, :], in0=G[:, :], scalar1=oml[:, h:h + 1], scalar2=lbt[:, h:h + 1], op0=ALU.mult, op1=ALU.add),
                             reads=[G, oml, lbt], writes=[G])
                        p.op("dve", lambda e, G=G, KK=KK: e.tensor_scalar(out=KK[:, :], in0=G[:, :], scalar1=-1.0, scalar2=1.0, op0=ALU.mult, op1=ALU.add), reads=[G], writes=[KK])
                        p.op("act", lambda e, G=G: e.activation(out=G[:, :], in_=G[:, :], func=AF.Ln), reads=[G], writes=[G])
                        p.op("dve", lambda e, G=G, B_=B_: e.tensor_tensor_scan(out=B_[:, :], data0=cm[:, :], data1=G[:, :], initial=0.0, op0=ALU.mult, op1=ALU.add),
                             reads=[G, cm], writes=[B_])
                        p.op("act", lambda e, Q=Q: e.activation(out=Q[:, :], in_=Q[:, :], func=AF.Silu), reads=[Q], writes=[Q])
                        p.op("act", lambda e, EB=EB, B_=B_: e.activation(out=EB[:, :], in_=B_[:, :], func=AF.Exp), reads=[B_], writes=[EB])
                        p.op("dve", lambda e, QB=QB, Q=Q, EB=EB: e.tensor_tensor(out=QB[:, :], in0=Q[:, :], in1=EB[:, :], op=ALU.mult), reads=[Q, EB], writes=[QB])
                        p.op("act", lambda e, T1=T1, B_=B_: e.activation(out=T1[:, :], in_=B_[:, :], func=AF.Exp, scale=-1.0), reads=[B_], writes=[T1])
                        p.op("dve", lambda e, KB=KB, KK=KK, T1=T1: e.tensor_tensor(out=KB[:, :], in0=KK[:, :], in1=T1[:, :], op=ALU.mult), reads=[KK, T1], writes=[KB])
                        b3 = B_[:, :].rearrange("p (c l) -> p c l", l=16)
                        t3 = T1[:, :].rearrange("p (c l) -> p c l", l=16)
                        p.op("dve", lambda e, b3=b3, t3=t3: e.tensor_tensor(out=t3, in0=b3[:, :, 15:16].to_broadcast([128, 32, 16]), in1=b3, op=ALU.subtract),
                             reads=[B_], writes=[T1])
                        p.op("act", lambda e, T1=T1: e.activation(out=T1[:, :], in_=T1[:, :], func=AF.Exp), reads=[T1], writes=[T1])
                        p.op("dve", lambda e, KH=KH, KK=KK, T1=T1: e.tensor_tensor(out=KH[:, :], in0=KK[:, :], in1=T1[:, :], op=ALU.mult), reads=[KK, T1], writes=[KH])
                    it += 1
                    kcnt = [0, 0]

                    def emit_T(hd, cg):
                        Q, F_, V_, G, B_, EB, T1, KK, KH, QB, KB = sets[hd]
                        cs = slice(cg * 16, (cg + 1) * 16)
                        bt_ = self.ps[4 + hd]
                        so = (cg % 2) * 256
                        kt = khi[hd][cg % 3]
                        p.op("pe", lambda e, bt_=bt_, KH=KH, cs=cs, so=so: e.transpose(out=bt_[0:16, so:so + 128], in_=KH[:, cs], identity=self.ident[:, :]),
                             reads=[KH, self.ident], writes=[(bt_, cg % 2)])
                        p.op("pe", lambda e, bt_=bt_, V_=V_, cs=cs, so=so: e.transpose(out=bt_[0:16, so + 128:so + 256], in_=V_[:, cs], identity=self.ident[:, :]),
                             reads=[V_, self.ident], writes=[(bt_, cg % 2)])
                        p.op("act", lambda e, kt=kt, bt_=bt_, so=so: e.copy(out=kt[:, :], in_=bt_[0:16, so:so + 256]), reads=[(bt_, cg % 2)], writes=[kt])

                    for blk in range(4):
                        bs = slice(blk * 128, (blk + 1) * 128)
                        for hd in range(2):
                            Q, F_, V_, G, B_, EB, T1, KK, KH, QB, KB = sets[hd]
                            ba, bo = self.ps[hd], self.ps[2 + hd]
                            am, itk = attm[hd][blk % 2], itok[hd][blk % 2]
                            p.op("pe", lambda e, ba=ba, KB=KB, QB=QB, bs=bs: e.matmul(ba[:, 0:128], lhsT=KB[:, bs], rhs=QB[:, bs], start=True, stop=True),
                                 reads=[KB, QB], writes=[(ba, 0)])
                            p.op("dve", lambda e, am=am, ba=ba: e.tensor_tensor(out=am[:, :], in0=ba[:, 0:128], in1=m16[:, :], op=ALU.mult), reads=[(ba, 0), m16], writes=[am])
                            p.op("pe", lambda e, ba=ba, V_=V_, bs=bs: e.transpose(out=ba[:, 128:256], in_=V_[:, bs], identity=self.ident[:, :]),
                                 reads=[V_, self.ident], writes=[(ba, 1)])
                            p.op("act", lambda e, itk=itk, ba=ba: e.copy(out=itk[:, :], in_=ba[:, 128:256]), reads=[(ba, 1)], writes=[itk])
                            p.op("pe", lambda e, bo=bo, itk=itk, am=am, bs=bs: e.matmul(bo[:, bs], lhsT=itk[:, :], rhs=am[:, :], start=True, stop=False, skip_group_check=True),
                                 reads=[itk, am], writes=[bo])
                            emit_T(hd, blk * 8)
                        for c in range(8):
                            cg = blk * 8 + c
                            cs = slice(cg * 16, (cg + 1) * 16)
                            for hd in range(2):
                                h = hp * 2 + hd
                                Q, F_, V_, G, B_, EB, T1, KK, KH, QB, KB = sets[hd]
                                bo, bu = self.ps[2 + hd], self.ps[6 + hd]
                                S32, Sbf = S32s[hd], Sbfs[hd]
                                if c + 1 < 8:
                                    emit_T(hd, cg + 1)
                                seq_start = (ti == 0 and cg == 0) if not samp else (cg % 4 == 0)
                                if seq_start:
                                    if samp:
                                        sg = cg // 4
                                        p.dma(S32[:, :], self.st_hg[sg, h, :, :], reads=[self.st_hg], writes=[S32])
                                    else:
                                        p.op("dve", lambda e, S32=S32: e.memset(S32[:, :], 0.0), writes=[S32])
                                    p.op("act", lambda e, S32=S32, Sbf=Sbf: e.copy(out=Sbf[:, :], in_=S32[:, :]), reads=[S32], writes=[Sbf])
                                p.op("pe", lambda e, bo=bo, QB=QB, cs=cs, Sbf=Sbf: e.matmul(bo[:, cs], lhsT=Sbf[:, :], rhs=QB[:, cs], start=False, stop=False, skip_group_check=True),
                                     reads=[Sbf, QB], writes=[bo])
                                kt = khi[hd][cg % 3]
                                p.op("pe", lambda e, bu=bu, kt=kt: e.matmul(bu[:, 0:128], lhsT=kt[0:16, 0:128], rhs=kt[0:16, 128:256], start=True, stop=True),
                                     reads=[kt], writes=[bu])
                                p.op("dve", lambda e, bu=bu, EB=EB, cg=cg, S32=S32: e.scalar_tensor_tensor(out=S32[:, :], in0=S32[:, :], scalar=EB[:, cg * 16 + 15:cg * 16 + 16], in1=bu[:, 0:128],
                                                                                                   op0=ALU.mult, op1=ALU.add), reads=[S32, EB, bu], writes=[S32])
                                p.op("act", lambda e, S32=S32, Sbf=Sbf: e.copy(out=Sbf[:, :], in_=S32[:, :]), reads=[S32], writes=[Sbf])
                                seq_end = (ti == NT - 2 and cg == 31) if not samp else (cg % 4 == 3)
                                if seq_end:
                                    seq = (1 + cg // 4) if samp else 0
                                    p.dma(self.hg_out[seq, h, :, :], S32[:, :], reads=[S32], writes=[(self.hg_out, (seq, h))])
                    for hd in range(2):
                        h = hp * 2 + hd
                        Q, F_, V_, G, B_, EB, T1, KK, KH, QB, KB = sets[hd]
                        bo, bn = self.ps[2 + hd], self.ps[hd]
                        O_, Y_ = osb[hd], yb[hd]
                        p.op("act", lambda e, O_=O_, bo=bo: e.copy(out=O_[:, :], in_=bo[:, :]), reads=[bo], writes=[O_])
                        p.op("act", lambda e, T1=T1, O_=O_: e.activation(out=T1[:, :], in_=O_[:, :], func=AF.Square), reads=[O_], writes=[T1])
                        p.op("pe", lambda e, bn=bn, T1=T1: e.matmul(bn[:, :], lhsT=ones[:, :], rhs=T1[:, :], start=True, stop=True), reads=[ones, T1], writes=[bn])
                        p.op("act", lambda e, T1=T1, bn=bn: e.activation(out=T1[:, :], in_=bn[:, :], func=AF.Sqrt, scale=1.0 / 128, bias=self.eps_t[:, 0:1]),
                             reads=[bn, self.eps_t], writes=[T1])
                        p.op("dve", lambda e, T1=T1: e.reciprocal(out=T1[:, :], in_=T1[:, :]), reads=[T1], writes=[T1])
                        p.op("dve", lambda e, O_=O_, T1=T1, Y_=Y_: e.scalar_tensor_tensor(out=Y_[:, :], in0=O_[:, :], scalar=nw[:, 0:1], in1=T1[:, :], op0=ALU.mult, op1=ALU.mult),
                             reads=[O_, T1, nw], writes=[Y_])
                        p.dma(self.YT[1024 + h * 128:1024 + (h + 1) * 128, c0:c0 + TS], Y_[:, :], reads=[Y_], writes=[(self.YT, ("hg", h, ti))])
            p.barrier()

    def stage_ssdconv(self):
        p, T, NT, LP = self.p, self.T, self.NT, self.LP
        self.XT = self.dscr("XT", [T, 1280], F32)
        self.BCt = self.dscr("BCt", [512, T], BF16)
        with ExitStack() as st:
            S = lambda n, shp, dt=F32: self.sb(st, n, shp, dt)
            cw = S("sc_cw", [128, 12, 4]); cb = S("sc_cb", [128, 12])
            for tap in range(4):
                self.small_T(cw[:, :, tap], self.l1_conv_w[tap, :].rearrange("(b c) -> c b", c=128), [self.l1_conv_w], [cw])
            self.small_T(cb[:, :], self.l1_conv_b[:].rearrange("(b c) -> c b", c=128), [self.l1_conv_b], [cb])
            xa = [S(f"sc_xa{i}", [128, 536]) for i in range(2)]
            xc = [S(f"sc_xc{i}", [128, TS]) for i in range(2)]
            xcb = [S(f"sc_xcb{i}", [128, TS], BF16) for i in range(2)]
            xtk = [S(f"sc_xt{i}", [128, 4, 128]) for i in range(2)]
            it = 0
            for ti in range(NT):
                samp = ti == NT - 1
                nseg, sl = (NSAMP, DSEQ) if samp else (1, TS)
                c0 = ti * TS
                for b in range(12):
                    i = it % 2
                    it += 1
                    X, XC, XCB, XTK = xa[i], xc[i], xcb[i], xtk[i]
                    rows = slice(b * 128, (b + 1) * 128)
                    xv = X[:, 0:nseg * (3 + sl)].rearrange("p (s l) -> p s l", s=nseg)
                    if samp:
                        for sg in range(NSAMP):
                            self.small_T(xv[:, sg, 0:3], self.st_ssd_conv[sg, :, rows].rearrange("t c -> c t"), [self.st_ssd_conv], [X])
                    elif ti == 0:
                        p.op("dve", lambda e, xv=xv: e.memset(xv[:, :, 0:3], 0.0), writes=[X])
                    else:
                        p.dma(xv[:, 0, 0:3], self.XBC[rows, c0 - 3:c0], reads=[(self.XBC, ti - 1)], writes=[X])
                    p.dma(xv[:, :, 3:3 + sl], self.XBC[rows, c0:c0 + TS].rearrange("p (s l) -> p s l", s=nseg), reads=[(self.XBC, ti)], writes=[X])
                    xc3 = XC[:, :].rearrange("p (s l) -> p s l", s=nseg)
                    p.op("dve", lambda e, xc3=xc3, xv=xv, b=b, sl=sl: e.tensor_scalar(out=xc3, in0=xv[:, :, 0:sl], scalar1=cw[:, b, 0:1], scalar2=cb[:, b:b + 1],
                                                                                 op0=ALU.mult, op1=ALU.add), reads=[X, cw, cb], writes=[XC])
                    for tap in range(1, 4):
                        p.op("dve", lambda e, xc3=xc3, xv=xv, b=b, sl=sl, tap=tap: e.scalar_tensor_tensor(
                            out=xc3, in0=xv[:, :, tap:tap + sl], scalar=cw[:, b, tap:tap + 1], in1=xc3, op0=ALU.mult, op1=ALU.add), reads=[X, cw, XC], writes=[XC])
                    p.op("act", lambda e, XC=XC: e.activation(out=XC[:, :], in_=XC[:, :], func=AF.Silu), reads=[XC], writes=[XC])
                    if samp:
                        for sg in range(NSAMP):
                            self.small_T(self.ssd_conv_out[1 + sg, :, rows].rearrange("t c -> c t"), xv[:, sg, sl:sl + 3], [X], [(self.ssd_conv_out, ("s", b, sg))])
                    elif ti == NT - 2:
                        self.small_T(self.ssd_conv_out[0, :, rows].rearrange("t c -> c t"), X[:, TS:TS + 3], [X], [(self.ssd_conv_out, ("p", b))])
                    if b >= 8:
                        p.op("pool", lambda e, XCB=XCB, XC=XC: e.tensor_copy(out=XCB[:, :], in_=XC[:, :]), reads=[XC], writes=[XCB])
                        p.dma(self.BCt[(b - 8) * 128:(b - 7) * 128, c0:c0 + TS], XCB[:, :], reads=[XCB], writes=[(self.BCt, ti)])
                    if b < 10:
                        bank = self.ps[it % 2]
                        for t in range(4):
                            p.op("pe", lambda e, bank=bank, XC=XC, t=t: e.transpose(out=bank[:, t * 128:(t + 1) * 128], in_=XC[:, t * 128:(t + 1) * 128], identity=self.ident[:, :]),
                                 reads=[XC, self.ident], writes=[bank])
                        p.op("act", lambda e, XTK=XTK, bank=bank: e.copy(out=XTK[:, :, :], in_=bank[:, :].rearrange("p (a b) -> p a b", a=4)), reads=[bank], writes=[XTK])
                        p.dma(self.XT[c0:c0 + TS, b * 128:(b + 1) * 128].rearrange("(t q) c -> q t c", q=128), XTK[:, :, :], reads=[XTK], writes=[(self.XT, ti)])
            p.barrier()

    def stage_ssd(self):
        p, T, NT, LP = self.p, self.T, self.NT, self.LP
        with ExitStack() as st:
            S = lambda n, shp, dt=F32: self.sb(st, n, shp, dt)
            cs = S("ss_cs", [64, 448]); dtb = S("ss_dtb", [64, 16]); aneg = S("ss_an", [64, 16]); dsk = S("ss_dsk", [64, 16]); nwB = S("ss_nw", [64, 1024])
            one = S("ss_one", [128, 1])
            p.dma(cs[:, :], self.c_ssd[:, :], reads=[self.c_ssd], writes=[cs])
            tri, ntri, ones64, mneg, id64, sel63 = cs[:, 0:64], cs[:, 64:128], cs[:, 128:192], cs[:, 192:256], cs[:, 256:320], cs[:, 320:448]
            bc16 = lambda t_: t_[:].rearrange("(o d) -> o d", o=1).to_broadcast([64, 16])
            p.dma(dtb[:, :], bc16(self.l1_dt_bias), reads=[self.l1_dt_bias], writes=[dtb])
            p.dma(aneg[:, :], bc16(self.l1_a_log), reads=[self.l1_a_log], writes=[aneg])
            p.dma(dsk[:, :], bc16(self.l1_d_skip), reads=[self.l1_d_skip], writes=[dsk])
            p.dma(nwB[:, :], self.l1_ssd_nw[:].rearrange("(o d) -> o d", o=1).to_broadcast([64, 1024]), reads=[self.l1_ssd_nw], writes=[nwB])
            p.op("act", lambda e: e.activation(out=aneg[:, :], in_=aneg[:, :], func=AF.Exp), reads=[aneg], writes=[aneg])
            p.op("dve", lambda e: e.tensor_scalar(out=aneg[:, :], in0=aneg[:, :], scalar1=-1.0, scalar2=None, op0=ALU.mult), reads=[aneg], writes=[aneg])
            p.op("dve", lambda e: e.memset(one[:, :], 1.0), writes=[one])
            S32 = S("ss_S32", [128, 1024]); Sbf = S("ss_Sbf", [128, 1024], BF16); stg = S("ss_stg", [128, 8, 128])
            NB = 2
            mk = lambda nm, shp, dt=F32: [S(f"ss_{nm}{i}", shp, dt) for i in range(NB)]
            xt, zt, dtr, bct = mk("xt", [64, 1280]), mk("zt", [64, 1024]), mk("dtr", [64, 16]), mk("bct", [128, 4, 64], BF16)
            dt_, dta, r1, r2, LT = mk("dt", [64, 16]), mk("dta", [64, 16]), mk("r1", [64, 1024]), mk("r2", [64, 1024]), mk("LT", [64, 1024])
            MT, xdt, xw, btk = mk("MT", [64, 1024], BF16), mk("xdt", [64, 1024], BF16), mk("xw", [64, 1024], BF16), mk("btk", [64, 256], BF16)
            ac, eac, wv, y32, tt = mk("ac", [64, 16]), mk("eac", [64, 16]), mk("wv", [64, 16]), mk("y32", [64, 1024]), mk("tt", [64, 1024])
            decB, ssg, yT = mk("decB", [128, 16]), mk("ssg", [64, 2]), mk("yT", [128, 8, 64], BF16)
            v3 = lambda ap: ap.rearrange("p (h l) -> p h l", h=16)
            bcl = lambda ap: ap.unsqueeze(2).to_broadcast([64, 16, 64])
            seqs = [(0, 0, LP // 64)] + [(1 + sg, LP + sg * DSEQ, 1) for sg in range(NSAMP)]
            it = 0
            for (seq, col0, nch) in seqs:
                if seq == 0:
                    p.op("dve", lambda e: e.memset(S32[:, :], 0.0), writes=[S32])
                else:
                    p.dma(stg[:, :, :], self.st_ssd[seq - 1, :, :, :].rearrange("h q n -> (h q) n").rearrange("(a q) n -> q a n", q=128), reads=[self.st_ssd], writes=[stg])
                    for a in range(8):
                        bank = self.ps[a // 4]
                        p.op("pe", lambda e, bank=bank, a=a: e.transpose(out=bank[:, (a % 4) * 128:(a % 4 + 1) * 128], in_=stg[:, a, :], identity=self.ident[:, :]),
                             reads=[stg, self.ident], writes=[bank])
                    for hf in range(2):
                        p.op("dve", lambda e, hf=hf: e.tensor_copy(out=S32[:, hf * 512:(hf + 1) * 512], in_=self.ps[hf][:, :]), reads=[self.ps[hf]], writes=[S32])
                p.op("act", lambda e: e.copy(out=Sbf[:, :], in_=S32[:, :]), reads=[S32], writes=[Sbf])
                for ch in range(nch):
                    i = it % NB
                    it += 1
                    c0 = col0 + ch * 64
                    ti = c0 // TS
                    XT_, ZT, DTR, BCT, DT_, DTA, R1, R2, LT_, MT_, XDT, XW, BTK, AC, EAC, WV, Y, TT, DEC, SSG, YT_ = (
                        xt[i], zt[i], dtr[i], bct[i], dt_[i], dta[i], r1[i], r2[i], LT[i], MT[i], xdt[i], xw[i], btk[i], ac[i], eac[i], wv[i], y32[i], tt[i], decB[i], ssg[i], yT[i])
                    p.dma(XT_[:, :], self.XT[c0:c0 + 64, :], reads=[(self.XT, ti)], writes=[XT_])
                    p.dma(ZT[:, :], self.Z[c0:c0 + 64, :], reads=[(self.Z, ti)], writes=[ZT])
                    p.dma(DTR[:, :], self.DT[c0:c0 + 64, :], reads=[(self.DT, ti)], writes=[DTR])
                    p.dma(BCT[:, :, :], self.BCt[:, c0:c0 + 64].rearrange("(a n) t -> n a t", n=128), reads=[(self.BCt, ti)], writes=[BCT])
                    p.op("dve", lambda e, DT_=DT_, DTR=DTR: e.tensor_tensor(out=DT_[:, :], in0=DTR[:, :], in1=dtb[:, :], op=ALU.add), reads=[DTR, dtb], writes=[DT_])
                    p.op("act", lambda e, DT_=DT_: e.activation(out=DT_[:, :], in_=DT_[:, :], func=AF.Exp), reads=[DT_], writes=[DT_])
                    p.op("act", lambda e, DT_=DT_: e.activation(out=DT_[:, :], in_=DT_[:, :], func=AF.Ln, bias=one[0:64, 0:1]), reads=[DT_, one], writes=[DT_])
                    p.op("dve", lambda e, DTA=DTA, DT_=DT_: e.tensor_tensor(out=DTA[:, :], in0=DT_[:, :], in1=aneg[:, :], op=ALU.mult), reads=[DT_, aneg], writes=[DTA])
                    p.op("dve", lambda e, R1=R1, DTA=DTA: e.tensor_tensor(out=v3(R1[:, :]), in0=bcl(DTA[:, :]), in1=tri.unsqueeze(1).to_broadcast([64, 16, 64]), op=ALU.mult),
                         reads=[DTA, cs], writes=[R1])
                    p.op("pool", lambda e, R2=R2, DTA=DTA: e.tensor_copy(out=v3(R2[:, :]), in_=bcl(DTA[:, :])), reads=[DTA], writes=[R2])
                    bd = [self.ps[0], self.ps[1]]
                    for hf in range(2):
                        hs = slice(hf * 512, (hf + 1) * 512)
                        p.op("pe", lambda e, hf=hf, hs=hs, R1=R1: e.matmul(bd[hf][0:64, :], lhsT=ones64, rhs=R1[:, hs], start=True, stop=False), reads=[R1, cs], writes=[bd[hf]])
                        p.op("pe", lambda e, hf=hf, hs=hs, R2=R2: e.matmul(bd[hf][0:64, :], lhsT=ntri, rhs=R2[:, hs], start=False, stop=True), reads=[R2, cs], writes=[bd[hf]])
                    bm = self.ps[6]
                    p.op("pe", lambda e, DTA=DTA: e.matmul(bm[0:64, 0:16], lhsT=tri, rhs=DTA[:, :], start=True, stop=True), reads=[DTA, cs], writes=[(bm, "ac")])
                    for hf in range(2):
                        hs = slice(hf * 512, (hf + 1) * 512)
                        p.op("dve", lambda e, hf=hf, hs=hs, LT_=LT_: e.tensor_tensor(out=LT_[:, hs].rearrange("p (h l) -> p h l", h=8), in0=bd[hf][0:64, :].rearrange("p (h l) -> p h l", h=8),
                                                                                 in1=mneg.unsqueeze(1).to_broadcast([64, 8, 64]), op=ALU.add), reads=[bd[hf], cs], writes=[LT_])
                    p.op("act", lambda e, LT_=LT_: e.activation(out=LT_[:, :], in_=LT_[:, :], func=AF.Exp), reads=[LT_], writes=[LT_])
                    p.op("act", lambda e, AC=AC: e.copy(out=AC[:, :], in_=bm[0:64, 0:16]), reads=[(bm, "ac")], writes=[AC])
                    p.op("act", lambda e, EAC=EAC, AC=AC: e.activation(out=EAC[:, :], in_=AC[:, :], func=AF.Exp), reads=[AC], writes=[EAC])
                    for g in range(2):
                        p.op("pe", lambda e, g=g, BCT=BCT: e.matmul(bm[0:64, 64 + g * 64:128 + g * 64], lhsT=BCT[:, g, :], rhs=BCT[:, 2 + g, :], start=True, stop=True),
                             reads=[BCT], writes=[(bm, "cb")])
                    for g in range(2):
                        gs = slice(g * 512, (g + 1) * 512)
                        p.op("dve", lambda e, g=g, gs=gs, MT_=MT_, LT_=LT_: e.tensor_tensor(out=MT_[:, gs].rearrange("p (h l) -> p h l", h=8), in0=LT_[:, gs].rearrange("p (h l) -> p h l", h=8),
                                                                                        in1=bm[0:64, 64 + g * 64:128 + g * 64].unsqueeze(1).to_broadcast([64, 8, 64]), op=ALU.mult),
                             reads=[LT_, (bm, "cb")], writes=[MT_])
                    xv_ = v3(XT_[:, 0:1024])
                    p.op("dve", lambda e, XDT=XDT, xv_=xv_, DT_=DT_: e.tensor_tensor(out=v3(XDT[:, :]), in0=xv_, in1=bcl(DT_[:, :]), op=ALU.mult), reads=[XT_, DT_], writes=[XDT])
                    byd = [self.ps[2], self.ps[3]]
                    for h in range(16):
                        p.op("pe", lambda e, h=h, MT_=MT_, XDT=XDT: e.matmul(byd[h // 8][0:64, (h % 8) * 64:(h % 8 + 1) * 64], lhsT=MT_[:, h * 64:(h + 1) * 64], rhs=XDT[:, h * 64:(h + 1) * 64],
                                                                            start=True, stop=True, skip_group_check=True), reads=[MT_, XDT], writes=[byd[h // 8]])
                    byo = [self.ps[4], self.ps[5]]
                    for g in range(2):
                        p.op("pe", lambda e, g=g, BCT=BCT: e.matmul(byo[g][0:64, :], lhsT=BCT[:, 2 + g, :], rhs=Sbf[:, g * 512:(g + 1) * 512], start=True, stop=True),
                             reads=[BCT, Sbf], writes=[byo[g]])
                    for g in range(2):
                        gs = slice(g * 512, (g + 1) * 512)
                        p.op("dve", lambda e, g=g, gs=gs, Y=Y, EAC=EAC: e.tensor_tensor(out=Y[:, gs].rearrange("p (h l) -> p h l", h=8), in0=byo[g][0:64, :].rearrange("p (h l) -> p h l", h=8),
                                                                                    in1=EAC[:, g * 8:(g + 1) * 8].unsqueeze(2).to_broadcast([64, 8, 64]), op=ALU.mult), reads=[byo[g], EAC], writes=[Y])
                        p.op("dve", lambda e, g=g, gs=gs, Y=Y: e.tensor_tensor(out=Y[:, gs], in0=Y[:, gs], in1=byd[g][0:64, :], op=ALU.add), reads=[Y, byd[g]], writes=[Y])
                    p.op("pool", lambda e, TT=TT, xv_=xv_: e.tensor_tensor(out=v3(TT[:, :]), in0=xv_, in1=bcl(dsk[:, :]), op=ALU.mult), reads=[XT_, dsk], writes=[TT])
                    p.op("dve", lambda e, Y=Y, TT=TT: e.tensor_tensor(out=Y[:, :], in0=Y[:, :], in1=TT[:, :], op=ALU.add), reads=[Y, TT], writes=[Y])
                    p.op("act", lambda e, ZT=ZT: e.activation(out=ZT[:, :], in_=ZT[:, :], func=AF.Silu), reads=[ZT], writes=[ZT])
                    p.op("dve", lambda e, Y=Y, ZT=ZT: e.tensor_tensor(out=Y[:, :], in0=Y[:, :], in1=ZT[:, :], op=ALU.mult), reads=[Y, ZT], writes=[Y])
                    p.op("dve", lambda e, SSG=SSG: e.memset(SSG[:, :], 0.0), writes=[SSG])
                    for g in range(2):
                        gs = slice(g * 512, (g + 1) * 512)
                        p.op("act", lambda e, g=g, gs=gs, TT=TT, Y=Y, SSG=SSG: e.activation(out=TT[:, gs], in_=Y[:, gs], func=AF.Square, accum_out=SSG[:, g:g + 1]), reads=[Y, SSG], writes=[TT, SSG])
                    p.op("act", lambda e, SSG=SSG: e.activation(out=SSG[:, :], in_=SSG[:, :], func=AF.Sqrt, scale=1.0 / 512, bias=self.eps_t[0:64, 0:1]), reads=[SSG, self.eps_t], writes=[SSG])
                    p.op("dve", lambda e, SSG=SSG: e.reciprocal(out=SSG[:, :], in_=SSG[:, :]), reads=[SSG], writes=[SSG])
                    for g in range(2):
                        gs = slice(g * 512, (g + 1) * 512)
                        p.op("dve", lambda e, g=g, gs=gs, Y=Y, SSG=SSG: e.scalar_tensor_tensor(out=Y[:, gs], in0=Y[:, gs], scalar=SSG[:, g:g + 1], in1=nwB[:, gs], op0=ALU.mult, op1=ALU.mult),
                             reads=[Y, SSG, nwB], writes=[Y])
                    btr = self.ps[7]
                    for a in range(8):
                        p.op("pe", lambda e, a=a, Y=Y: e.transpose(out=btr[:, a * 64:(a + 1) * 64], in_=Y[:, a * 128:(a + 1) * 128], identity=id64), reads=[Y, cs], writes=[btr])
                    p.op("act", lambda e, YT_=YT_: e.copy(out=YT_[:, :, :], in_=btr[:, :].rearrange("p (a l) -> p a l", a=8)), reads=[btr], writes=[YT_])
                    p.dma(self.YT[0:1024, c0:c0 + 64].rearrange("(a q) t -> q a t", q=128), YT_[:, :, :], reads=[YT_], writes=[(self.YT, ("ssd", c0))])
                    p.op("dve", lambda e, WV=WV, LT_=LT_, DT_=DT_: e.tensor_tensor(out=WV[:, :], in0=v3(LT_[:, :])[:, :, 63], in1=DT_[:, :], op=ALU.mult), reads=[LT_, DT_], writes=[WV])
                    p.op("dve", lambda e, XW=XW, xv_=xv_, WV=WV: e.tensor_tensor(out=v3(XW[:, :]), in0=xv_, in1=bcl(WV[:, :]), op=ALU.mult), reads=[XT_, WV], writes=[XW])
                    p.op("pool", lambda e, BTK=BTK, XT_=XT_: e.tensor_copy(out=BTK[:, :], in_=XT_[:, 1024:1280]), reads=[XT_], writes=[BTK])
                    p.op("pe", lambda e, AC=AC: e.matmul(bm[:, 256:272], lhsT=sel63, rhs=AC[:, :], start=True, stop=True), reads=[AC, cs], writes=[(bm, "dec")])
                    p.op("act", lambda e, DEC=DEC: e.activation(out=DEC[:, :], in_=bm[:, 256:272], func=AF.Exp), reads=[(bm, "dec")], writes=[DEC])
                    for g in range(2):
                        gs = slice(g * 512, (g + 1) * 512)
                        p.op("pe", lambda e, g=g, gs=gs, BTK=BTK, XW=XW: e.matmul(bd[g][:, :], lhsT=BTK[:, g * 128:(g + 1) * 128], rhs=XW[:, gs], start=True, stop=True), reads=[BTK, XW], writes=[bd[g]])
                        p.op("dve", lambda e, g=g, gs=gs, DEC=DEC: e.tensor_tensor(out=S32[:, gs].rearrange("p (h l) -> p h l", h=8), in0=S32[:, gs].rearrange("p (h l) -> p h l", h=8),
                                                                              in1=DEC[:, g * 8:(g + 1) * 8].unsqueeze(2).to_broadcast([128, 8, 64]), op=ALU.mult), reads=[S32, DEC], writes=[S32])
                        p.op("dve", lambda e, g=g, gs=gs: e.tensor_tensor(out=S32[:, gs], in0=S32[:, gs], in1=bd[g][:, :], op=ALU.add), reads=[S32, bd[g]], writes=[S32])
                    p.op("act", lambda e: e.copy(out=Sbf[:, :], in_=S32[:, :]), reads=[S32], writes=[Sbf])
                for a in range(8):
                    bank = self.ps[a // 4]
                    p.op("pe", lambda e, bank=bank, a=a: e.transpose(out=bank[:, (a % 4) * 128:(a % 4 + 1) * 128], in_=S32[:, a * 128:(a + 1) * 128], identity=self.ident[:, :]),
                         reads=[S32, self.ident], writes=[bank])
                for hf in range(2):
                    p.op("dve", lambda e, hf=hf: e.tensor_copy(out=stg[:, hf * 4:(hf + 1) * 4, :], in_=self.ps[hf][:, :].rearrange("p (a n) -> p a n", a=4)), reads=[self.ps[hf]], writes=[stg])
                p.dma(self.ssd_out[seq, :, :, :].rearrange("h q n -> (h q) n").rearrange("(a q) n -> q a n", q=128), stg[:, :, :], reads=[stg], writes=[(self.ssd_out, seq)])
            p.barrier()


W_NAMES = ["norm_w", "l0_w_in", "l0_conv_w", "l0_conv_b", "l0_rg_w_r", "l0_rg_b_r", "l0_rg_w_i", "l0_rg_b_i",
           "l0_rg_lambda", "l0_lq1", "l0_lk1", "l0_lq2", "l0_lk2", "l0_subln_w", "l0_w_out", "l1_w_in",
           "l1_conv_w", "l1_conv_b", "l1_dt_bias", "l1_a_log", "l1_d_skip", "l1_ssd_norm_w",
           "l1_hg_lower_bound", "l1_hg_norm_w", "l1_w_out", "ffn_w_gate", "ffn_w_up", "ffn_w_down"]


def make_consts():
    ident = np.eye(128, dtype=np.float32)
    mask = np.zeros((128, 4, 512), np.float32)
    for j in range(4):
        kc = (j * 128 + np.arange(128)) // 64
        qc = np.arange(512) // 64
        mask[:, j, :] = (kc[:, None] <= qc[None, :]).astype(np.float32)
    jj = np.arange(128)
    m16 = ((jj[:, None] // 16 == jj[None, :] // 16) & (jj[:, None] <= jj[None, :])).astype(np.float32)
    j6 = np.arange(64)
    tri = (j6[:, None] <= j6[None, :]).astype(np.float32)
    cs = np.zeros((64, 5 * 64 + 128), np.float32)
    cs[:, 0:64] = tri
    cs[:, 64:128] = -tri
    cs[:, 128:192] = 1.0
    cs[:, 192:256] = np.where(tri > 0, 0.0, -30000.0)
    cs[:, 256:320] = np.eye(64)
    cs[63, 320:448] = 1.0
    return {"c_ident": ident, "c_mask": mask.reshape(128, 2048).astype(ml_dtypes.bfloat16), "c_m16": m16, "c_ssd": cs}


def core_inputs(inp, c):
    f = lambda a: np.ascontiguousarray(np.asarray(a, dtype=np.float32))
    s0, s1 = NSAMP * c, NSAMP * (c + 1)
    LP = inp["x_prompt"].shape[1]
    m = {}
    m["x"] = f(np.concatenate([inp["x_prompt"][c].reshape(LP, D), inp["x_sample"][s0:s1].reshape(NSAMP * DSEQ, D)], 0))
    past = inp["cache_diff_k"].shape[1]
    m["ck"] = f(inp["cache_diff_k"][s0:s1].reshape(NSAMP, past, 1024))
    m["cv"] = f(inp["cache_diff_v"][s0:s1].reshape(NSAMP, past, 1024))
    m["st_rg_conv"] = f(inp["state_rglru_conv"][s0:s1])
    m["st_rg_h"] = f(inp["state_rglru_h"][s0:s1])
    m["st_ssd_conv"] = f(inp["state_ssd_conv"][s0:s1])
    m["st_ssd"] = f(inp["state_ssd"][s0:s1])
    m["st_hg"] = f(inp["state_hgrn"][s0:s1])
    for n in W_NAMES:
        m[n] = f(inp[n])
    m.update(make_consts())
    return m


_CACHE = {}


def get_builder(LP, PAST, debug=False):
    key = (LP, PAST, debug)
    if key not in _CACHE:
        b = Builder(LP, PAST, debug)
        b.build()
        _CACHE[key] = b
    return _CACHE[key]


def run_cores(inp, debug=False):
    LP = inp["x_prompt"].shape[1]
    PAST = inp["cache_diff_k"].shape[1]
    b = get_builder(LP, PAST, debug)
    maps = [core_inputs(inp, c % 2) for c in range(2)]
    import os
    ncores = int(os.environ.get("K_NCORES", "8"))
    zeros = {k: np.zeros_like(v) for k, v in maps[0].items()}
    in_maps = [maps[c] if c < 2 else zeros for c in range(ncores)]
    res = run_bass_kernel_spmd(b.nc, in_maps, core_ids=list(range(ncores)))
    return b, res.results


def kernel(**inp):
    b, r = run_cores(inp)
    LP = b.LP
    g = lambda name: [np.asarray(r[min(c, len(r) - 1)][name]) for c in range(2)]
    y, ko, vo = g("y"), g("k_out"), g("v_out")
    rgc, rgh, sc, so, ho = g("rg_conv_out"), g("rg_h_out"), g("ssd_conv_out"), g("ssd_out"), g("hg_out")
    P = lambda a, shp: np.stack([a[c][:LP] for c in range(2)], 0).reshape(shp).astype(np.float32)
    S = lambda a, shp: np.concatenate([a[c][LP:] for c in range(2)], 0).reshape(shp).astype(np.float32)
    P0 = lambda a: np.stack([a[c][0] for c in range(2)], 0).astype(np.float32)
    S0 = lambda a: np.concatenate([a[c][1:] for c in range(2)], 0).astype(np.float32)
    return (P(y, (2, LP, D)), S(y, (16, DSEQ, D)),
            P(ko, (2, LP, 8, 128)), P(vo, (2, LP, 8, 128)), P0(rgc), P0(rgh), P0(sc), P0(so), P0(ho),
            S(ko, (16, DSEQ, 8, 128)), S(vo, (16, DSEQ, 8, 128)), S0(rgc), S0(rgh), S0(sc), S0(so), S0(ho))
```

```python
import numpy as np
import ml_dtypes
from contextlib import ExitStack
import concourse.bass as bass
import concourse.mybir as mybir
from concourse.bass_utils import run_bass_kernel_spmd

F32 = mybir.dt.float32
BF16 = mybir.dt.bfloat16
ALU = mybir.AluOpType
AF = mybir.ActivationFunctionType
AX = mybir.AxisListType

D = 2048
TS = 512
NSAMP = 8
DSEQ = 64
EPS = 1e-6
D_FF = 5632
IN_EVEN = 5120
IN_ODD = 5648
ENGS = ("pe", "act", "dve", "pool", "sp")
EPOCH = 12000
NDMASEM = 40


class Res:
    __slots__ = ("w", "r")

    def __init__(self):
        self.w = None
        self.r = []


class Tile:
    def __init__(self, h, name=""):
        self.h = h
        self.name = name
        self.res = {None: Res()}

    def __getitem__(self, k):
        return self.h[k]

    def get(self, key):
        if key not in self.res:
            self.res[key] = Res()
        return self.res[key]


class Ins:
    __slots__ = ("eng", "fn", "deps", "is_dma", "sig", "sigidx", "dsem", "dval", "dprev", "pos")

    def __init__(self, eng, fn, is_dma):
        self.eng = eng
        self.fn = fn
        self.is_dma = is_dma
        self.deps = set()
        self.sig = False
        self.sigidx = -1
        self.dsem = -1
        self.dval = 0
        self.dprev = 0


def _norm_acc(a):
    if isinstance(a, Tile):
        return a, None
    return a


def _split_psum(reads, writes):
    r2, w2 = [], list(writes)
    for a in reads:
        t, k = _norm_acc(a)
        if getattr(t, "is_psum", False):
            w2.append(t)
        else:
            r2.append(a)
    w3 = []
    for a in w2:
        t, k = _norm_acc(a)
        w3.append(t if getattr(t, "is_psum", False) else a)
    return r2, w3


class Prog:
    def __init__(self, nc):
        self.nc = nc
        self.streams = {e: [] for e in ENGS}
        self.last = {e: None for e in ENGS}
        self.pending = {e: set() for e in ENGS}
        self.ndma = 0
        self.qdma = {}
        self.dma_last = [None] * NDMASEM
        self.dma_val = [0] * NDMASEM

    def _collect(self, ins, reads, writes):
        reads, writes = _split_psum(reads, writes)
        deps = ins.deps
        for a in reads:
            t, k = _norm_acc(a)
            rs = list(t.res.values()) if k is None else [t.get(k), t.res[None]]
            for r in rs:
                if r.w is not None:
                    deps.add(r.w)
        for a in writes:
            t, k = _norm_acc(a)
            rs = list(t.res.values()) if k is None else [t.get(k), t.res[None]]
            for r in rs:
                if r.w is not None:
                    if r.w.is_dma or r.w.eng != ins.eng or ins.is_dma:
                        deps.add(r.w)
                for x in r.r:
                    if x.is_dma or x.eng != ins.eng or ins.is_dma:
                        deps.add(x)
        for a in reads:
            t, k = _norm_acc(a)
            r = t.get(k)
            if not ins.is_dma:
                r.r = [x for x in r.r if x.is_dma or x.eng != ins.eng]
            r.r.append(ins)
        for a in writes:
            t, k = _norm_acc(a)
            if k is None:
                for r in t.res.values():
                    r.w = ins
                    r.r = []
            else:
                r = t.get(k)
                r.w = ins
                r.r = []
        deps.discard(ins)
        if self.pending[ins.eng]:
            deps.update(self.pending[ins.eng])
            self.pending[ins.eng] = set()

    _rec = None

    def start_record(self):
        self._rec = []

    def stop_record(self):
        r, self._rec = self._rec, None
        return r

    def replay(self, entries):
        for e in entries:
            if e[0] == "op":
                self.op(e[1], e[2], e[3], e[4])
            elif e[0] == "dma":
                self.dma(e[1], e[2], e[3], e[4], e[5], **e[6])

    @staticmethod
    def merge(a, b):
        a = [e for e in a if e[0] != "bar"]
        b = [e for e in b if e[0] != "bar"]
        out, i, j = [], 0, 0
        la, lb = max(len(a), 1), max(len(b), 1)
        while i < len(a) or j < len(b):
            if j >= len(b) or (i < len(a) and i * lb <= j * la):
                out.append(a[i]); i += 1
            else:
                out.append(b[j]); j += 1
        return out

    def op(self, eng, fn, reads=(), writes=()):
        if self._rec is not None:
            self._rec.append(("op", eng, fn, list(reads), list(writes)))
            return None
        ins = Ins(eng, fn, False)
        self._collect(ins, reads, writes)
        self.streams[eng].append(ins)
        self.last[eng] = ins
        return ins

    def dma(self, out_ap, in_ap, reads=(), writes=(), q="sp", **kw):
        if self._rec is not None:
            self._rec.append(("dma", out_ap, in_ap, list(reads), list(writes), q, kw))
            return None

        def fn(e, out_ap=out_ap, in_ap=in_ap, kw=kw):
            return e.dma_start(out=out_ap, in_=in_ap, **kw)
        ins = Ins(q, fn, True)
        self._collect(ins, reads, writes)
        lo, hi = (0, NDMASEM - 12) if q == "sp" else (NDMASEM - 12, NDMASEM)
        cnt = self.qdma.get(q, 0)
        self.qdma[q] = cnt + 1
        i = lo + cnt % (hi - lo)
        self.ndma += 1
        ins.dsem = i
        ins.dprev = self.dma_val[i]
        self.dma_val[i] += 16
        ins.dval = self.dma_val[i]
        self.dma_last[i] = ins
        self.streams[q].append(ins)
        self.last[q] = ins
        return ins

    def barrier(self):
        if self._rec is not None:
            self._rec.append(("bar",))
            return
        lasts = [x for x in self.last.values() if x is not None]
        dl = [x for x in self.dma_last if x is not None]
        for e in ENGS:
            self.pending[e].update(x for x in lasts if x.eng != e or x.is_dma)
            self.pending[e].update(dl)

    def emit(self, es):
        nc = self.nc
        for e in ENGS:
            for ins in self.streams[e]:
                for d in ins.deps:
                    if not d.is_dma:
                        d.sig = True
        nsig = {}
        for e in ENGS:
            c = 0
            for ins in self.streams[e]:
                if ins.sig and not ins.is_dma:
                    ins.sigidx = c
                    c += 1
            nsig[e] = c
        esems = {e: [es.enter_context(nc.semaphore(f"s_{e}_{j}")) for j in range(nsig[e] // EPOCH + 1)]
                 for e in ENGS}
        dsems = [es.enter_context(nc.semaphore(f"s_dma_{j}")) for j in range(NDMASEM)]
        block = es.enter_context(nc.Block())
        engobj = {"pe": block.tensor, "act": block.scalar, "dve": block.vector, "pool": block.gpsimd,
                  "sp": block.sync}

        def run(eng_name):
            def body(e):
                waited = {}
                for ins in self.streams[eng_name]:
                    waits = {}
                    for d in ins.deps:
                        if d.is_dma:
                            key, val, sem = ("d", d.dsem), d.dval, dsems[d.dsem]
                        else:
                            j = d.sigidx // EPOCH
                            key, val, sem = (d.eng, j), d.sigidx % EPOCH + 1, esems[d.eng][j]
                        if waits.get(key, (0, None))[0] < val:
                            waits[key] = (val, sem)
                    if ins.is_dma and ins.dprev > 0:
                        key = ("d", ins.dsem)
                        if waits.get(key, (0, None))[0] < ins.dprev:
                            waits[key] = (ins.dprev, dsems[ins.dsem])
                    for key, (val, sem) in waits.items():
                        if waited.get(key, 0) < val:
                            e.wait_ge(sem, val)
                            waited[key] = val
                    bi = ins.fn(e)
                    if ins.is_dma:
                        bi.then_inc(dsems[ins.dsem], 16)
                    elif ins.sig:
                        j = ins.sigidx // EPOCH
                        bi.then_inc(esems[eng_name][j], 1)
                if eng_name == "sp":
                    for i in range(NDMASEM):
                        if self.dma_val[i] > 0:
                            e.wait_ge(dsems[i], self.dma_val[i])
            return body

        for en in ENGS:
            engobj[en](run(en))


class Builder:
    def __init__(self, LP, PAST, debug=False):
        self.LP = LP
        self.PAST = PAST
        self.T = LP + NSAMP * DSEQ
        self.NT = self.T // TS
        self.debug = debug
        self.nc = bass.Bass("TRN2", target_bir_lowering=False)
        self.p = Prog(self.nc)
        self.es = ExitStack()
        self.dbg_names = []

    def din(self, name, shape, dt=F32):
        return Tile(self.nc.dram_tensor(name, list(shape), dt, kind="ExternalInput").ap(), name)

    def dout(self, name, shape, dt=F32):
        return Tile(self.nc.dram_tensor(name, list(shape), dt, kind="ExternalOutput").ap(), name)

    def dscr(self, name, shape, dt):
        if self.debug:
            self.dbg_names.append(name)
            return Tile(self.nc.dram_tensor(name, list(shape), dt, kind="ExternalOutput").ap(), name)
        return Tile(self.nc.dram_tensor(name, list(shape), dt).ap(), name)

    def sb(self, st, name, shape, dt):
        self._sbn = getattr(self, "_sbn", 0) + 1
        name = f"{name}_{self._sbn}"
        return Tile(st.enter_context(self.nc.sbuf_tensor(name, list(shape), dt)), name)

    def build(self):
        nc, p, T, NT, LP, PAST = self.nc, self.p, self.T, self.NT, self.LP, self.PAST
        es = self.es
        with es:
            self.x = self.din("x", [T, D])
            self.ck = self.din("ck", [NSAMP, PAST, 1024])
            self.cv = self.din("cv", [NSAMP, PAST, 1024])
            self.st_rg_conv = self.din("st_rg_conv", [NSAMP, 3, 1024])
            self.st_rg_h = self.din("st_rg_h", [NSAMP, 1024])
            self.st_ssd_conv = self.din("st_ssd_conv", [NSAMP, 3, 1536])
            self.st_ssd = self.din("st_ssd", [NSAMP, 16, 64, 128])
            self.st_hg = self.din("st_hg", [NSAMP, 8, 128, 128])
            self.norm_w = self.din("norm_w", [2, 4, D])
            self.w_in0 = self.din("l0_w_in", [D, IN_EVEN])
            self.l0_conv_w = self.din("l0_conv_w", [4, 1024])
            self.l0_conv_b = self.din("l0_conv_b", [1024])
            self.l0_w_r = self.din("l0_rg_w_r", [8, 128, 128])
            self.l0_b_r = self.din("l0_rg_b_r", [1024])
            self.l0_w_i = self.din("l0_rg_w_i", [8, 128, 128])
            self.l0_b_i = self.din("l0_rg_b_i", [1024])
            self.l0_lam = self.din("l0_rg_lambda", [1024])
            self.l0_lq1 = self.din("l0_lq1", [64])
            self.l0_lk1 = self.din("l0_lk1", [64])
            self.l0_lq2 = self.din("l0_lq2", [64])
            self.l0_lk2 = self.din("l0_lk2", [64])
            self.l0_subln = self.din("l0_subln_w", [128])
            self.w_out0 = self.din("l0_w_out", [D, D])
            self.w_in1 = self.din("l1_w_in", [D, IN_ODD])
            self.l1_conv_w = self.din("l1_conv_w", [4, 1536])
            self.l1_conv_b = self.din("l1_conv_b", [1536])
            self.l1_dt_bias = self.din("l1_dt_bias", [16])
            self.l1_a_log = self.din("l1_a_log", [16])
            self.l1_d_skip = self.din("l1_d_skip", [16])
            self.l1_ssd_nw = self.din("l1_ssd_norm_w", [1024])
            self.l1_hg_lb = self.din("l1_hg_lower_bound", [2, 1024])
            self.l1_hg_nw = self.din("l1_hg_norm_w", [128])
            self.w_out1 = self.din("l1_w_out", [D, D])
            self.wg = self.din("ffn_w_gate", [2, D, D_FF])
            self.wu = self.din("ffn_w_up", [2, D, D_FF])
            self.wd = self.din("ffn_w_down", [2, D_FF, D])
            self.c_ident = self.din("c_ident", [128, 128])
            self.c_mask = self.din("c_mask", [128, 4 * 512], BF16)
            self.c_m16 = self.din("c_m16", [128, 128])
            self.c_ssd = self.din("c_ssd", [64, 5 * 64 + 128])

            self.y = self.dout("y", [T, D])
            self.k_out = self.dout("k_out", [T, 1024])
            self.v_out = self.dout("v_out", [T, 1024])
            self.rg_conv_out = self.dout("rg_conv_out", [1 + NSAMP, 3, 1024])
            self.rg_h_out = self.dout("rg_h_out", [1 + NSAMP, 1024])
            self.ssd_conv_out = self.dout("ssd_conv_out", [1 + NSAMP, 3, 1536])
            self.ssd_out = self.dout("ssd_out", [1 + NSAMP, 16, 64, 128])
            self.hg_out = self.dout("hg_out", [1 + NSAMP, 8, 128, 128])

            self.ps = [Tile(es.enter_context(nc.psum_tensor(f"ps{i}", [128, 512], F32)), f"ps{i}")
                       for i in range(8)]
            for t_ in self.ps:
                t_.is_psum = True
            self.ident = self.sb(es, "ident", [128, 128], F32)
            p.dma(self.ident[:, :], self.c_ident[:, :], reads=[self.c_ident], writes=[self.ident])
            self.eps_t = self.sb(es, "eps_t", [128, 1], F32)
            p.op("dve", lambda e: e.memset(self.eps_t[:, :], EPS), writes=[self.eps_t])

            self.stage_convert_first()
            self.stage_inproj(0)
            with ExitStack() as st:
                p.start_record()
                self.stage_rglru(st)
                self.stage_convert_rest(st, engs=("pool",), q="pool")
                rb = p.stop_record()
                p.start_record()
                self.stage_cacheprep(st)
                self.stage_attn(st)
                ra = p.stop_record()
                p.replay(Prog.merge(ra, rb))
                p.barrier()
            self.stage_outffn(0)
            self.stage_inproj(1)
            self.stage_ssdconv()
            self.stage_ssd()
            self.stage_hgrn()
            self.stage_outffn(1)
            p.barrier()
            p.emit(es)
        return nc

    BUFW = 1536

    def conv_w(self, name, W, r0c0, K, groups, bufs, engs=("dve", "pool", "act"), q="sp"):
        p = self.p
        KC = K // 128
        outs = [self.dscr(f"{name}_g{gi}", [128, KC, w], BF16) for gi, (c0, w) in enumerate(groups)]
        batches, cur = [], []
        for gi, (c0, w) in enumerate(groups):
            if cur and (c0 + w - groups[cur[0]][0] > self.BUFW or c0 != groups[cur[-1]][0] + groups[cur[-1]][1]):
                batches.append(cur)
                cur = []
            cur.append(gi)
        batches.append(cur)
        for kc in range(KC):
            src = r0c0(kc)
            for bt in batches:
                self._cvn = getattr(self, "_cvn", 0) + 1
                f32t, bft = bufs[self._cvn % 2]
                lo = groups[bt[0]][0]
                hi = groups[bt[-1]][0] + groups[bt[-1]][1]
                n = hi - lo
                p.dma(f32t[:, 0:n], src[:, lo:hi], reads=[W], writes=[f32t], q=q)
                eng = engs[self._cvn % len(engs)]
                if eng == "act":
                    p.op("act", lambda e, a=bft[:, 0:n], b=f32t[:, 0:n]: e.copy(out=a, in_=b), reads=[f32t], writes=[bft])
                else:
                    p.op(eng, lambda e, a=bft[:, 0:n], b=f32t[:, 0:n]: e.tensor_copy(out=a, in_=b), reads=[f32t], writes=[bft])
                for gi in bt:
                    c0, w = groups[gi]
                    p.dma(outs[gi][:, kc, :], bft[:, c0 - lo:c0 - lo + w], reads=[bft], writes=[(outs[gi], kc)], q=q)
        return outs

    def cv_bufs(self, st):
        return [(self.sb(st, f"cvf{i}", [128, self.BUFW], F32), self.sb(st, f"cvb{i}", [128, self.BUFW], BF16)) for i in range(2)]

    def stage_convert_first(self):
        g512 = lambda n: [(i * 512, 512) for i in range(n // 512)]
        with ExitStack() as st:
            bufs = self.cv_bufs(st)
            self.Wt_in0 = self.conv_w("wt_in0", self.w_in0, lambda kc: self.w_in0[kc * 128:(kc + 1) * 128, :], D, g512(5120), bufs)
            self.p.barrier()

    def stage_convert_rest(self, st, engs=("dve", "pool"), q="sp"):
        g512 = lambda n: [(i * 512, 512) for i in range(n // 512)]
        bufs = self.cv_bufs(st)
        _cw = self.conv_w
        self.conv_w = lambda *a, **k: _cw(*a, q=q, **k)
        try:
            self._convert_rest_body(bufs, engs, g512)
        finally:
            self.conv_w = _cw
        self.p.barrier()

    def _convert_rest_body(self, bufs, engs, g512):
        self.Wt_out0 = self.conv_w("wt_out0", self.w_out0, lambda kc: self.w_out0[kc * 128:(kc + 1) * 128, :], D, g512(2048), bufs, engs)
        self.Wt_g, self.Wt_u, self.Wt_d = [None, None], [None, None], [None, None]
        g256 = [(i * 256, 256) for i in range(22)]
        for L in range(2):
            self.Wt_g[L] = self.conv_w(f"wt_g{L}", self.wg, lambda kc, L=L: self.wg[L, kc * 128:(kc + 1) * 128, :], D, g256, bufs, engs)
            self.Wt_u[L] = self.conv_w(f"wt_u{L}", self.wu, lambda kc, L=L: self.wu[L, kc * 128:(kc + 1) * 128, :], D, g256, bufs, engs)
            self.Wt_d[L] = self.conv_w(f"wt_d{L}", self.wd, lambda kc, L=L: self.wd[L, kc * 128:(kc + 1) * 128, :], D_FF, g512(2048), bufs, engs)
            if L == 0:
                g1 = g512(2560) + [(2560, 16)] + [(2576 + i * 512, 512) for i in range(6)]
                self.Wt_in1 = self.conv_w("wt_in1", self.w_in1, lambda kc: self.w_in1[kc * 128:(kc + 1) * 128, :], D, g1, bufs, engs)
                self.Wt_out1 = self.conv_w("wt_out1", self.w_out1, lambda kc: self.w_out1[kc * 128:(kc + 1) * 128, :], D, g512(2048), bufs, engs)

    def norm_hT(self, xt, wB, hT, t, tmp, h32, ss, rstd, tb, x_ap=None, x_key=None):
        p = self.p
        if x_ap is None:
            x_ap = xt[:, :]
        xr = xt if x_key is None else (xt, x_key)
        self.rms_stats(x_ap, xr, tmp, ss, rstd, D)
        p.op("dve", lambda e: e.scalar_tensor_tensor(out=h32[:, :], in0=x_ap, scalar=rstd[:, 0:1], in1=wB[:, :],
                                                     op0=ALU.mult, op1=ALU.mult),
             reads=[xr, rstd, wB], writes=[h32])
        for kq in range(4):
            bank = tb[kq % 2]
            for j in range(4):
                k = kq * 4 + j
                p.op("pe", lambda e, bank=bank, j=j, k=k: e.transpose(out=bank[:, j * 128:(j + 1) * 128],
                                                                     in_=h32[:, k * 128:(k + 1) * 128],
                                                                     identity=self.ident[:, :]),
                     reads=[h32, self.ident], writes=[bank])
            dst = hT[:, kq * 4:(kq + 1) * 4, t * 128:(t + 1) * 128]
            srcv = bank[:, :].rearrange("p (a b) -> p a b", a=4)
            if kq % 2 == 0:
                p.op("act", lambda e, dst=dst, srcv=srcv: e.copy(out=dst, in_=srcv), reads=[bank],
                     writes=[(hT, (kq, t))])
            else:
                p.op("dve", lambda e, dst=dst, srcv=srcv: e.tensor_copy(out=dst, in_=srcv), reads=[bank],
                     writes=[(hT, (kq, t))])

    def rms_stats(self, x_ap, x_tile, tmp, ss, rstd, n, width=None):
        p = self.p
        w = n if width is None else width
        p.op("dve", lambda e: e.memset(ss[:, 0:1], 0.0), writes=[ss])
        p.op("act", lambda e: e.activation(out=tmp[:, 0:w], in_=x_ap, func=AF.Square, accum_out=ss[:, 0:1]),
             reads=[x_tile, ss], writes=[tmp, ss])
        p.op("act", lambda e: e.activation(out=rstd[:, 0:1], in_=ss[:, 0:1], func=AF.Sqrt, scale=1.0 / n,
                                           bias=self.eps_t[:, 0:1]),
             reads=[ss, self.eps_t], writes=[rstd])
        p.op("dve", lambda e: e.reciprocal(out=rstd[:, 0:1], in_=rstd[:, 0:1]), reads=[rstd], writes=[rstd])

    def load_wB(self, st, name, L, i):
        wB = self.sb(st, name, [128, D], F32)
        self.p.dma(wB[:, :], self.norm_w[L, i:i + 1, :].to_broadcast([128, D]), reads=[self.norm_w], writes=[wB])
        return wB

    def stage_inproj(self, L):
        p, T, NT = self.p, self.T, self.NT
        src = self.x if L == 0 else self.X
        if L == 0:
            self.XA = self.dscr("XA", [1024, T], F32)
            self.GA = self.dscr("GA", [1024, T], F32)
            self.QT = self.dscr("QT", [1024, T], BF16)
            self.KT = self.dscr("KT", [1024, T], BF16)
            self.Vb = self.dscr("Vb", [T, 1024], BF16)
            Wt = self.Wt_in0
            A = [(Wt[0], 512, self.XA, 0, F32), (Wt[1], 512, self.XA, 512, F32),
                 (Wt[2], 512, self.GA, 0, F32), (Wt[3], 512, self.GA, 512, F32),
                 (Wt[4], 512, self.QT, 0, BF16), (Wt[5], 512, self.QT, 512, BF16),
                 (Wt[6], 512, self.KT, 0, BF16), (Wt[7], 512, self.KT, 512, BF16)]
            B = [(Wt[6], 512, [(self.k_out, 0, F32)]), (Wt[7], 512, [(self.k_out, 512, F32)]),
                 (Wt[8], 512, [(self.v_out, 0, F32), (self.Vb, 0, BF16)]),
                 (Wt[9], 512, [(self.v_out, 512, F32), (self.Vb, 512, BF16)])]
        else:
            self.XBC = self.dscr("XBC", [1536, T], F32)
            self.HQ = self.dscr("HQ", [1024, T], F32)
            self.HF = self.dscr("HF", [1024, T], F32)
            self.HI = self.dscr("HI", [1024, T], F32)
            self.Z = self.dscr("Z", [T, 1024], F32)
            self.DT = self.dscr("DT", [T, 16], F32)
            Wt = self.Wt_in1
            A = [(Wt[2], 512, self.XBC, 0, F32), (Wt[3], 512, self.XBC, 512, F32), (Wt[4], 512, self.XBC, 1024, F32),
                 (Wt[6], 512, self.HQ, 0, F32), (Wt[7], 512, self.HQ, 512, F32),
                 (Wt[8], 512, self.HF, 0, F32), (Wt[9], 512, self.HF, 512, F32),
                 (Wt[10], 512, self.HI, 0, F32), (Wt[11], 512, self.HI, 512, F32)]
            B = [(Wt[0], 512, [(self.Z, 0, F32)]), (Wt[1], 512, [(self.Z, 512, F32)]),
                 (Wt[5], 16, [(self.DT, 0, F32)])]
        with ExitStack() as st:
            wB = self.load_wB(st, "wB", L, 0)
            xts = [self.sb(st, f"xt{i}", [128, D], F32) for i in range(2)]
            tmp = self.sb(st, "tmp", [128, D], BF16)
            h32s = [self.sb(st, f"h32_{i}", [128, D], F32) for i in range(4)]
            ss = self.sb(st, "ss", [128, 1], F32)
            rstd = self.sb(st, "rstd", [128, 1], F32)
            hTs = [self.sb(st, f"hT{i}", [128, 16, TS], BF16) for i in range(2)]
            wts = [self.sb(st, f"wt{i}", [128, 16, 512], BF16) for i in range(3)]
            sgf = [self.sb(st, f"sgf{i}", [128, 512], F32) for i in range(3)]
            sgb = [self.sb(st, f"sgb{i}", [128, 512], BF16) for i in range(3)]
            accb = self.ps[2:8]
            cnt = {"si": 0, "bi": 0, "x": 0}
            jobs = [("A",) + a for a in A] + [("B",) + b for b in B]
            nj = len(jobs)
            loads = [(ti, j) for ti in range(NT) for j in range(nj)]
            nload = [0]

            def ensure_loaded(idx):
                while nload[0] <= min(idx + 2, len(loads) - 1):
                    k = nload[0]
                    W, width = jobs[loads[k][1]][1], jobs[loads[k][1]][2]
                    wt = wts[k % 3]
                    p.dma(wt[:, :, 0:width], W[:, :, :], reads=[W], writes=[wt])
                    nload[0] += 1
                return wts[idx % 3]

            def norm1(ti, t):
                xt = xts[cnt["x"] % 2]
                cnt["x"] += 1
                r0 = ti * TS + t * 128
                p.dma(xt[:, :], src[r0:r0 + 128, :], reads=[(src, ti)], writes=[xt])
                self.rms_stats(xt[:, :], xt, tmp, ss, rstd, D)
                h32 = h32s[t]
                p.op("dve", lambda e, h32=h32, xt=xt: e.scalar_tensor_tensor(out=h32[:, :], in0=xt[:, :], scalar=rstd[:, 0:1], in1=wB[:, :],
                                                                         op0=ALU.mult, op1=ALU.mult), reads=[xt, rstd, wB], writes=[h32])

            def norm2(ti, t):
                h32, hT = h32s[t], hTs[ti % 2]
                for kq in range(4):
                    bank = self.ps[kq % 2]
                    for j in range(4):
                        k = kq * 4 + j
                        p.op("pe", lambda e, bank=bank, j=j, k=k, h32=h32: e.transpose(out=bank[:, j * 128:(j + 1) * 128], in_=h32[:, k * 128:(k + 1) * 128],
                                                                                   identity=self.ident[:, :]), reads=[h32, self.ident], writes=[bank])
                    dstv = hT[:, kq * 4:(kq + 1) * 4, t * 128:(t + 1) * 128]
                    srcv = bank[:, :].rearrange("p (a b) -> p a b", a=4)
                    if kq % 2 == 0:
                        p.op("act", lambda e, dstv=dstv, srcv=srcv: e.copy(out=dstv, in_=srcv), reads=[bank], writes=[(hT, (kq, t))])
                    else:
                        p.op("dve", lambda e, dstv=dstv, srcv=srcv: e.tensor_copy(out=dstv, in_=srcv), reads=[bank], writes=[(hT, (kq, t))])

            for t in range(4):
                norm1(0, t)
            for t in range(4):
                norm2(0, t)
            for ti in range(NT):
                hT = hTs[ti % 2]
                for j, job in enumerate(jobs):
                    wt = ensure_loaded(ti * nj + j)
                    if ti + 1 < NT:
                        t1_ = j - (nj - 6)
                        if 0 <= t1_ < 4:
                            norm1(ti + 1, t1_)
                        t2_ = j - (nj - 5)
                        if 0 <= t2_ < 4:
                            norm2(ti + 1, t2_)
                    if job[0] == "A":
                        _, W, width, dst, row0, dt = job
                        for n in range(width // 128):
                            bank = accb[cnt["bi"] % 6]
                            cnt["bi"] += 1
                            for k in range(16):
                                p.op("pe", lambda e, bank=bank, wt=wt, hT=hT, k=k, n=n: e.matmul(
                                    bank[:, :], lhsT=wt[:, k, n * 128:(n + 1) * 128], rhs=hT[:, k, :],
                                    start=(k == 0), stop=(k == 15)), reads=[wt, hT], writes=[bank])
                            sg = (sgf if dt == F32 else sgb)[cnt["si"] % 3]
                            cnt["si"] += 1
                            self.evac(bank[:, :], sg[:, :], bank, sg, cnt["si"])
                            rr = row0 + n * 128
                            p.dma(dst[rr:rr + 128, ti * TS:(ti + 1) * TS], sg[:, :], reads=[sg], writes=[(dst, ti)])
                    else:
                        _, W, width, dsts = job
                        for t in range(4):
                            bank = accb[cnt["bi"] % 6]
                            cnt["bi"] += 1
                            for k in range(16):
                                p.op("pe", lambda e, bank=bank, wt=wt, hT=hT, k=k, t=t, width=width: e.matmul(
                                    bank[:, 0:width], lhsT=hT[:, k, t * 128:(t + 1) * 128], rhs=wt[:, k, 0:width],
                                    start=(k == 0), stop=(k == 15)), reads=[wt, hT], writes=[bank])
                            r0 = ti * TS + t * 128
                            sg0 = sgf[cnt["si"] % 3]
                            cnt["si"] += 1
                            self.evac(bank[:, 0:width], sg0[:, 0:width], bank, sg0, cnt["si"])
                            for (dst, c0, dt) in dsts:
                                if dt == F32:
                                    sg = sg0
                                else:
                                    sg = sgb[cnt["si"] % 3]
                                    cnt["si"] += 1
                                    p.op("pool", lambda e, a=sg[:, 0:width], b=sg0[:, 0:width]: e.tensor_copy(out=a, in_=b),
                                         reads=[sg0], writes=[sg])
                                p.dma(dst[r0:r0 + 128, c0:c0 + width], sg[:, 0:width], reads=[sg], writes=[(dst, ti)])
            p.barrier()

    def evac(self, src_ap, dst_ap, src_t, dst_t, i):
        if i % 2 == 0:
            self.p.op("act", lambda e: e.copy(out=dst_ap, in_=src_ap), reads=[src_t], writes=[dst_t])
        else:
            self.p.op("dve", lambda e: e.tensor_copy(out=dst_ap, in_=src_ap), reads=[src_t], writes=[dst_t])

    def stage_outffn(self, L):
        p, T, NT = self.p, self.T, self.NT
        src = self.x if L == 0 else self.X
        if L == 0:
            self.X = self.dscr("X", [T, D], F32)
        dst = self.X if L == 0 else self.y
        Wo = self.Wt_out0 if L == 0 else self.Wt_out1
        Wg, Wu, Wd = self.Wt_g[L], self.Wt_u[L], self.Wt_d[L]
        YT = self.YT
        with ExitStack() as st:
            wB = self.sb(st, "wBo", [128, D], F32)
            yT = self.sb(st, "yT", [128, 16, TS], BF16)
            wbufs = [self.sb(st, f"wb{i}", [128, 16, 512], BF16) for i in range(3)]
            m32 = self.sb(st, "m32", [128, 4, D], F32)
            d32 = self.sb(st, "d32", [128, 4, D], F32)
            xres = self.sb(st, "xres", [128, D], F32)
            actT = self.sb(st, "actT", [128, 44, TS], BF16)
            tmp = self.sb(st, "tmpo", [128, D], BF16)
            h32 = self.sb(st, "h32o", [128, D], F32)
            gsb = [self.sb(st, f"gsb{i}", [128, TS], F32) for i in range(2)]
            ss = self.sb(st, "sso", [128, 1], F32)
            rstd = self.sb(st, "rstdo", [128, 1], F32)
            wi = 0
            ei = 0
            gi_ = 0

            def load_wB(i):
                p.dma(wB[:, :], self.norm_w[L, i:i + 1, :].to_broadcast([128, D]), reads=[self.norm_w], writes=[wB])

            def postnorm_residual(buf, t, res_ap, res_reads, out_ap, out_writes):
                self.rms_stats(buf[:, t, :], (buf, t), tmp, ss, rstd, D)
                p.op("dve", lambda e: e.scalar_tensor_tensor(out=h32[:, :], in0=buf[:, t, :], scalar=rstd[:, 0:1],
                                                             in1=wB[:, :], op0=ALU.mult, op1=ALU.mult),
                     reads=[(buf, t), rstd, wB], writes=[h32])
                p.op("pool", lambda e: e.tensor_tensor(out=out_ap, in0=h32[:, :], in1=res_ap, op=ALU.add),
                     reads=[h32] + res_reads, writes=out_writes)

            for ti in range(NT):
                c0 = ti * TS
                p.dma(yT[:, :, :], YT[:, c0:c0 + TS].rearrange("(k q) t -> q k t", q=128), reads=[(YT, ti)], writes=[yT])
                for g in range(4):
                    wt = wbufs[wi % 3]
                    wi += 1
                    p.dma(wt[:, :, :], Wo[g][:, :, :], reads=[Wo[g]], writes=[wt])
                    for t in range(4):
                        bank = self.ps[(g % 2) * 4 + t]
                        for k in range(16):
                            p.op("pe", lambda e, bank=bank, wt=wt, k=k, t=t: e.matmul(
                                bank[:, :], lhsT=yT[:, k, t * 128:(t + 1) * 128], rhs=wt[:, k, :],
                                start=(k == 0), stop=(k == 15)), reads=[wt, yT], writes=[bank])
                        ei += 1
                        self.evac(bank[:, :], m32[:, t, g * 512:(g + 1) * 512], bank, (m32, t), ei)
                load_wB(1)
                for t in range(4):
                    r0 = c0 + t * 128
                    p.dma(xres[:, :], src[r0:r0 + 128, :], reads=[(src, ti)], writes=[xres])
                    postnorm_residual(m32, t, xres[:, :], [xres], m32[:, t, :], [(m32, t)])
                load_wB(2)
                for t in range(4):
                    self.norm_hT(m32, wB, yT, t, tmp, h32, ss, rstd, self.ps[6:8], x_ap=m32[:, t, :], x_key=t)
                hT = yT
                for j in range(22):
                    wt = wbufs[wi % 3]
                    wi += 1
                    p.dma(wt[:, :, 0:256], Wg[j][:, :, :], reads=[Wg[j]], writes=[(wt, 0)])
                    p.dma(wt[:, :, 256:512], Wu[j][:, :, :], reads=[Wu[j]], writes=[(wt, 1)])
                    for n in range(2):
                        f = j * 2 + n
                        bg = self.ps[(f % 3) * 2]
                        bu = self.ps[(f % 3) * 2 + 1]
                        for k in range(16):
                            p.op("pe", lambda e, bg=bg, wt=wt, k=k, n=n: e.matmul(
                                bg[:, :], lhsT=wt[:, k, n * 128:(n + 1) * 128], rhs=hT[:, k, :],
                                start=(k == 0), stop=(k == 15)), reads=[(wt, 0), hT], writes=[bg])
                        for k in range(16):
                            p.op("pe", lambda e, bu=bu, wt=wt, k=k, n=n: e.matmul(
                                bu[:, :], lhsT=wt[:, k, 256 + n * 128:256 + (n + 1) * 128], rhs=hT[:, k, :],
                                start=(k == 0), stop=(k == 15)), reads=[(wt, 1), hT], writes=[bu])
                        gs = gsb[gi_ % 2]
                        gi_ += 1
                        p.op("act", lambda e, gs=gs, bg=bg: e.activation(out=gs[:, :], in_=bg[:, :], func=AF.Silu),
                             reads=[bg], writes=[gs])
                        p.op("dve", lambda e, gs=gs, bu=bu, f=f: e.tensor_tensor(out=actT[:, f, :], in0=gs[:, :],
                                                                                in1=bu[:, :], op=ALU.mult),
                             reads=[gs, bu], writes=[(actT, f)])
                for g in range(4):
                    for part in range(4):
                        wt = wbufs[wi % 3]
                        wi += 1
                        p.dma(wt[:, 0:11, :], Wd[g][:, part * 11:(part + 1) * 11, :], reads=[Wd[g]], writes=[wt])
                        for fl in range(11):
                            f = part * 11 + fl
                            for t in range(4):
                                bank = self.ps[(g % 2) * 4 + t]
                                p.op("pe", lambda e, bank=bank, wt=wt, f=f, fl=fl, t=t: e.matmul(
                                    bank[:, :], lhsT=actT[:, f, t * 128:(t + 1) * 128], rhs=wt[:, fl, :],
                                    start=(f == 0), stop=(f == 43)), reads=[wt, (actT, f)], writes=[bank])
                    for t in range(4):
                        bank = self.ps[(g % 2) * 4 + t]
                        ei += 1
                        self.evac(bank[:, :], d32[:, t, g * 512:(g + 1) * 512], bank, (d32, t), ei)
                load_wB(3)
                for t in range(4):
                    r0 = c0 + t * 128
                    postnorm_residual(d32, t, m32[:, t, :], [(m32, t)], d32[:, t, :], [(d32, t)])
                    p.dma(dst[r0:r0 + 128, :], d32[:, t, :], reads=[(d32, t)], writes=[(dst, ti)])
            p.barrier()

    def small_T(self, dst_ap, src_ap, reads, writes):
        self.p.dma(dst_ap, src_ap, reads=reads, writes=writes, allow_slow_non_contiguous=True)

    def stage_rglru(self, st):
        p, T, NT, LP = self.p, self.T, self.NT, self.LP
        if not hasattr(self, "YT"):
            self.YT = self.dscr("YT", [D, T], BF16)
        if True:
            S = lambda n, shp, dt=F32: self.sb(st, n, shp, dt)
            cw = S("rg_cw", [128, 8, 4]); cb = S("rg_cb", [128, 8]); br = S("rg_br", [128, 8]); bi_ = S("rg_bi", [128, 8])
            lam = S("rg_lam", [128, 8]); cc = S("rg_c", [128, 8]); cc2 = S("rg_c2", [128, 8]); one = S("rg_one", [128, 1])
            wr32 = S("rg_wr32", [128, 8, 128]); wi32 = S("rg_wi32", [128, 8, 128])
            wrb = S("rg_wrb", [128, 8, 128], BF16); wib = S("rg_wib", [128, 8, 128], BF16)
            hst = S("rg_hst", [128, 8]); h0s = S("rg_h0s", [128, 8, NSAMP])
            for tap in range(4):
                self.small_T(cw[:, :, tap], self.l0_conv_w[tap, :].rearrange("(b c) -> c b", c=128), [self.l0_conv_w], [cw])
            for (dst, srcv) in ((cb, self.l0_conv_b), (br, self.l0_b_r), (bi_, self.l0_b_i), (lam, self.l0_lam)):
                self.small_T(dst[:, :], srcv[:].rearrange("(b c) -> c b", c=128), [srcv], [dst])
            p.dma(wr32[:, :, :], self.l0_w_r[:, :, :].rearrange("b i j -> i b j"), reads=[self.l0_w_r], writes=[wr32])
            p.dma(wi32[:, :, :], self.l0_w_i[:, :, :].rearrange("b i j -> i b j"), reads=[self.l0_w_i], writes=[wi32])
            for b in range(8):
                self.small_T(h0s[:, b, :], self.st_rg_h[:, b * 128:(b + 1) * 128].rearrange("s c -> c s"), [self.st_rg_h], [h0s])
            p.op("dve", lambda e: e.tensor_copy(out=wrb[:, :, :], in_=wr32[:, :, :]), reads=[wr32], writes=[wrb])
            p.op("dve", lambda e: e.tensor_copy(out=wib[:, :, :], in_=wi32[:, :, :]), reads=[wi32], writes=[wib])
            p.op("dve", lambda e: e.memset(one[:, :], 1.0), writes=[one])
            p.op("dve", lambda e: e.memset(hst[:, :], 0.0), writes=[hst])
            p.op("act", lambda e: e.activation(out=cc[:, :], in_=lam[:, :], func=AF.Exp, scale=-1.0), reads=[lam], writes=[cc])
            p.op("act", lambda e: e.activation(out=cc[:, :], in_=cc[:, :], func=AF.Ln, bias=one[:, 0:1]), reads=[cc, one], writes=[cc])
            p.op("dve", lambda e: e.tensor_scalar(out=cc[:, :], in0=cc[:, :], scalar1=-8.0, scalar2=None, op0=ALU.mult), reads=[cc], writes=[cc])
            p.op("dve", lambda e: e.tensor_scalar(out=cc2[:, :], in0=cc[:, :], scalar1=2.0, scalar2=None, op0=ALU.mult), reads=[cc], writes=[cc2])
            NB = 2
            xa = [S(f"rg_xa{i}", [128, 536]) for i in range(NB)]
            ga = [S(f"rg_ga{i}", [128, TS]) for i in range(NB)]
            xc = [S(f"rg_xc{i}", [128, TS]) for i in range(NB)]
            xcb = [S(f"rg_xcb{i}", [128, TS], BF16) for i in range(NB)]
            rr = [S(f"rg_r{i}", [128, TS]) for i in range(NB)]
            gg = [S(f"rg_g{i}", [128, TS]) for i in range(NB)]
            aa = [S(f"rg_a{i}", [128, TS]) for i in range(NB)]
            uu = [S(f"rg_u{i}", [128, TS]) for i in range(NB)]
            hh = [S(f"rg_h{i}", [128, TS]) for i in range(NB)]
            t1 = [S(f"rg_t1{i}", [128, TS]) for i in range(NB)]
            yo = [S(f"rg_yo{i}", [128, TS], BF16) for i in range(NB)]
            it = 0
            for ti in range(NT):
                samp = ti == NT - 1
                nseg, sl = (NSAMP, DSEQ) if samp else (1, TS)
                c0 = ti * TS
                for b in range(8):
                    i = it % NB
                    it += 1
                    X, G, XC, XCB, R_, GI, A_, U_, H_, T1, YO = xa[i], ga[i], xc[i], xcb[i], rr[i], gg[i], aa[i], uu[i], hh[i], t1[i], yo[i]
                    rows = slice(b * 128, (b + 1) * 128)
                    xv = X[:, 0:nseg * (3 + sl)].rearrange("p (s l) -> p s l", s=nseg)
                    if samp:
                        for sg in range(NSAMP):
                            self.small_T(xv[:, sg, 0:3], self.st_rg_conv[sg, :, rows].rearrange("t c -> c t"),
                                         [self.st_rg_conv], [X])
                    elif ti == 0:
                        p.op("dve", lambda e, xv=xv: e.memset(xv[:, :, 0:3], 0.0), writes=[X])
                    else:
                        p.dma(xv[:, 0, 0:3], self.XA[rows, c0 - 3:c0], reads=[(self.XA, ti - 1)], writes=[X])
                    p.dma(xv[:, :, 3:3 + sl], self.XA[rows, c0:c0 + TS].rearrange("p (s l) -> p s l", s=nseg),
                          reads=[(self.XA, ti)], writes=[X])
                    p.dma(G[:, :], self.GA[rows, c0:c0 + TS], reads=[(self.GA, ti)], writes=[G])
                    v3 = lambda tl: tl[:, :].rearrange("p (s l) -> p s l", s=nseg)
                    xc3 = v3(XC)
                    p.op("dve", lambda e, xc3=xc3, xv=xv, b=b, sl=sl: e.tensor_scalar(
                        out=xc3, in0=xv[:, :, 0:sl], scalar1=cw[:, b, 0:1], scalar2=cb[:, b:b + 1], op0=ALU.mult, op1=ALU.add),
                        reads=[X, cw, cb], writes=[XC])
                    for tap in range(1, 4):
                        p.op("dve", lambda e, xc3=xc3, xv=xv, b=b, sl=sl, tap=tap: e.scalar_tensor_tensor(
                            out=xc3, in0=xv[:, :, tap:tap + sl], scalar=cw[:, b, tap:tap + 1], in1=xc3, op0=ALU.mult, op1=ALU.add),
                            reads=[X, cw, XC], writes=[XC])
                    p.op("act", lambda e, XCB=XCB, XC=XC: e.copy(out=XCB[:, :], in_=XC[:, :]), reads=[XC], writes=[XCB])
                    bk_r = self.ps[3]
                    bk_i = self.ps[3]
                    p.op("pe", lambda e, bk_r=bk_r, XCB=XCB, b=b: e.matmul(bk_r[:, :], lhsT=wrb[:, b, :], rhs=XCB[:, :], start=True, stop=True),
                         reads=[wrb, XCB], writes=[bk_r])
                    p.op("act", lambda e, R_=R_, bk_r=bk_r, b=b: e.activation(out=R_[:, :], in_=bk_r[:, :], func=AF.Sigmoid, bias=br[:, b:b + 1]),
                         reads=[bk_r, br], writes=[R_])
                    p.op("pe", lambda e, bk_i=bk_i, XCB=XCB, b=b: e.matmul(bk_i[:, :], lhsT=wib[:, b, :], rhs=XCB[:, :], start=True, stop=True),
                         reads=[wib, XCB], writes=[bk_i])
                    p.op("act", lambda e, GI=GI, bk_i=bk_i, b=b: e.activation(out=GI[:, :], in_=bk_i[:, :], func=AF.Sigmoid, bias=bi_[:, b:b + 1]),
                         reads=[bk_i, bi_], writes=[GI])
                    p.op("act", lambda e, A_=A_, R_=R_, b=b: e.activation(out=A_[:, :], in_=R_[:, :], func=AF.Exp, scale=cc[:, b:b + 1]),
                         reads=[R_, cc], writes=[A_])
                    p.op("act", lambda e, T1=T1, R_=R_, b=b: e.activation(out=T1[:, :], in_=R_[:, :], func=AF.Exp, scale=cc2[:, b:b + 1]),
                         reads=[R_, cc2], writes=[T1])
                    p.op("act", lambda e, T1=T1: e.activation(out=T1[:, :], in_=T1[:, :], func=AF.Sqrt, scale=-1.0, bias=one[:, 0:1]),
                         reads=[T1, one], writes=[T1])
                    p.op("dve", lambda e, U_=U_, T1=T1, GI=GI: e.tensor_tensor(out=U_[:, :], in0=T1[:, :], in1=GI[:, :], op=ALU.mult),
                         reads=[T1, GI], writes=[U_])
                    p.op("dve", lambda e, U_=U_, XC=XC: e.tensor_tensor(out=U_[:, :], in0=U_[:, :], in1=XC[:, :], op=ALU.mult),
                         reads=[U_, XC], writes=[U_])
                    a3, u3, h3 = v3(A_), v3(U_), v3(H_)
                    for sg in range(nseg):
                        init = h0s[:, b, sg:sg + 1] if samp else hst[:, b:b + 1]
                        p.op("dve", lambda e, h3=h3, a3=a3, u3=u3, sg=sg, init=init: e.tensor_tensor_scan(
                            out=h3[:, sg, :], data0=a3[:, sg, :], data1=u3[:, sg, :], initial=init, op0=ALU.mult, op1=ALU.add),
                            reads=[A_, U_, hst, h0s], writes=[H_])
                    if not samp:
                        p.op("dve", lambda e, H_=H_, b=b: e.tensor_copy(out=hst[:, b:b + 1], in_=H_[:, TS - 1:TS]), reads=[H_], writes=[hst])
                    if samp:
                        self.small_T(self.rg_h_out[1:1 + NSAMP, rows].rearrange("s c -> c s"), h3[:, :, sl - 1],
                                     [H_], [(self.rg_h_out, ("s", b))])
                        for sg in range(NSAMP):
                            self.small_T(self.rg_conv_out[1 + sg, :, rows].rearrange("t c -> c t"), xv[:, sg, sl:sl + 3],
                                         [X], [(self.rg_conv_out, ("s", b, sg))])
                    elif ti == NT - 2:
                        self.small_T(self.rg_h_out[0:1, rows].rearrange("s c -> c s"), H_[:, TS - 1:TS],
                                     [H_], [(self.rg_h_out, ("p", b))])
                        self.small_T(self.rg_conv_out[0, :, rows].rearrange("t c -> c t"), X[:, TS:TS + 3],
                                     [X], [(self.rg_conv_out, ("p", b))])
                    p.op("act", lambda e, T1=T1, G=G: e.activation(out=T1[:, :], in_=G[:, :], func=AF.Square), reads=[G], writes=[T1])
                    p.op("dve", lambda e, T1=T1: e.tensor_scalar(out=T1[:, :], in0=T1[:, :], scalar1=0.044715, scalar2=1.0, op0=ALU.mult, op1=ALU.add),
                         reads=[T1], writes=[T1])
                    p.op("dve", lambda e, T1=T1, G=G: e.tensor_tensor(out=T1[:, :], in0=T1[:, :], in1=G[:, :], op=ALU.mult), reads=[T1, G], writes=[T1])
                    p.op("act", lambda e, T1=T1: e.activation(out=T1[:, :], in_=T1[:, :], func=AF.Sigmoid, scale=1.5957691216057308),
                         reads=[T1], writes=[T1])
                    p.op("dve", lambda e, T1=T1, G=G: e.tensor_tensor(out=T1[:, :], in0=T1[:, :], in1=G[:, :], op=ALU.mult), reads=[T1, G], writes=[T1])
                    p.op("dve", lambda e, T1=T1, H_=H_, YO=YO: e.tensor_tensor(out=YO[:, :], in0=T1[:, :], in1=H_[:, :], op=ALU.mult),
                         reads=[T1, H_], writes=[YO])
                    p.dma(self.YT[rows, c0:c0 + TS], YO[:, :], reads=[YO], writes=[(self.YT, ti)])
            p.barrier()

    def stage_cacheprep(self, st):
        p, PAST = self.p, self.PAST
        self.KcT = self.dscr("KcT", [NSAMP, 1024, PAST], BF16)
        self.Vcb = self.dscr("Vcb", [NSAMP, PAST, 1024], BF16)
        if True:
            kf = [self.sb(st, f"cp_kf{i}", [128, 1024], F32) for i in range(2)]
            vf = [self.sb(st, f"cp_vf{i}", [128, 1024], F32) for i in range(2)]
            vb = [self.sb(st, f"cp_vb{i}", [128, 1024], BF16) for i in range(2)]
            kt = [self.sb(st, f"cp_kt{i}", [128, 8, 128], BF16) for i in range(2)]
            it = 0
            for sg in range(NSAMP):
                for kb in range(PAST // 128):
                    i = it % 2
                    it += 1
                    r = slice(kb * 128, (kb + 1) * 128)
                    p.dma(kf[i][:, :], self.ck[sg, r, :], reads=[self.ck], writes=[kf[i]])
                    p.dma(vf[i][:, :], self.cv[sg, r, :], reads=[self.cv], writes=[vf[i]])
                    p.op("pool", lambda e, a=vb[i], b=vf[i]: e.tensor_copy(out=a[:, :], in_=b[:, :]), reads=[vf[i]], writes=[vb[i]])
                    p.dma(self.Vcb[sg, r, :], vb[i][:, :], reads=[vb[i]], writes=[(self.Vcb, sg)])
                    for hq in range(2):
                        bank = self.ps[hq]
                        for j in range(4):
                            h = hq * 4 + j
                            p.op("pe", lambda e, bank=bank, j=j, h=h, i=i: e.transpose(
                                out=bank[:, j * 128:(j + 1) * 128], in_=kf[i][:, h * 128:(h + 1) * 128], identity=self.ident[:, :]),
                                reads=[kf[i], self.ident], writes=[bank])
                        dstv = kt[i][:, hq * 4:(hq + 1) * 4, :]
                        srcv = bank[:, :].rearrange("p (a b) -> p a b", a=4)
                        if hq == 0:
                            p.op("act", lambda e, dstv=dstv, srcv=srcv: e.copy(out=dstv, in_=srcv), reads=[bank], writes=[(kt[i], hq)])
                        else:
                            p.op("dve", lambda e, dstv=dstv, srcv=srcv: e.tensor_copy(out=dstv, in_=srcv), reads=[bank], writes=[(kt[i], hq)])
                    p.dma(self.KcT[sg, :, r].rearrange("(h c) k -> c h k", c=128), kt[i][:, :, :], reads=[kt[i]], writes=[(self.KcT, sg)])
            p.barrier()

    def stage_attn(self, st):
        p, T, NT, LP, PAST = self.p, self.T, self.NT, self.LP, self.PAST
        if not hasattr(self, "YT"):
            self.YT = self.dscr("YT", [D, T], BF16)
        NKB = LP // 128
        NCB = PAST // 128
        lam_init = 0.8 - 0.6 * 1.0
        if True:
            S = lambda n, shp, dt=F32: self.sb(st, n, shp, dt)
            lq = S("at_lq", [128, 4, 64]); lsum = S("at_ls", [128, 2]); neglam = S("at_nl", [128, 1])
            for i, srcv in enumerate((self.l0_lq1, self.l0_lk1, self.l0_lq2, self.l0_lk2)):
                p.dma(lq[:, i, :], srcv[:].rearrange("(o d) -> o d", o=1).to_broadcast([128, 64]), reads=[srcv], writes=[lq])
            p.op("dve", lambda e: e.tensor_tensor(out=lq[:, 0, :], in0=lq[:, 0, :], in1=lq[:, 1, :], op=ALU.mult), reads=[lq], writes=[lq])
            p.op("dve", lambda e: e.tensor_tensor(out=lq[:, 2, :], in0=lq[:, 2, :], in1=lq[:, 3, :], op=ALU.mult), reads=[lq], writes=[lq])
            p.op("dve", lambda e: e.reduce_sum(out=lsum[:, 0:1], in_=lq[:, 0, :], axis=AX.X), reads=[lq], writes=[lsum])
            p.op("dve", lambda e: e.reduce_sum(out=lsum[:, 1:2], in_=lq[:, 2, :], axis=AX.X), reads=[lq], writes=[lsum])
            p.op("act", lambda e: e.activation(out=lsum[:, :], in_=lsum[:, :], func=AF.Exp), reads=[lsum], writes=[lsum])
            p.op("dve", lambda e: e.tensor_tensor(out=neglam[:, :], in0=lsum[:, 1:2], in1=lsum[:, 0:1], op=ALU.subtract), reads=[lsum], writes=[neglam])
            p.op("dve", lambda e: e.tensor_scalar(out=neglam[:, :], in0=neglam[:, :], scalar1=-lam_init, scalar2=None, op0=ALU.add), reads=[neglam], writes=[neglam])
            subw = S("at_subw", [128, 128])
            p.dma(subw[:, :], self.l0_subln[:].rearrange("(o d) -> o d", o=1).to_broadcast([128, 128]), reads=[self.l0_subln], writes=[subw])
            p.op("dve", lambda e: e.tensor_scalar(out=subw[:, :], in0=subw[:, :], scalar1=1.0 - lam_init, scalar2=None, op0=ALU.mult), reads=[subw], writes=[subw])
            maskb = S("at_mb", [128, 4, 512], BF16)
            p.dma(maskb[:, :, :], self.c_mask[:, :].rearrange("p (a b) -> p a b", a=4), reads=[self.c_mask], writes=[maskb])
            KTs = [S(f"at_kt{i}", [128, max(LP, PAST + DSEQ)], BF16) for i in range(2)]
            QTs = [S(f"at_qt{i}", [128, LP + NSAMP * DSEQ], BF16) for i in range(2)]
            Vs = [S(f"at_v{i}", [128, max(NKB, NCB + 1), 130], BF16) for i in range(2)]
            for i in range(2):
                p.op("pool", lambda e, i=i: e.memset(Vs[i][:, :, 128:130], 1.0), writes=[(Vs[i], "ones")])
            Pb = [S(f"at_p{i}", [128, 512], BF16) for i in range(4)]
            rc = S("at_rc", [128, 2]); tq = S("at_tq", [128, 128]); oq = S("at_oq", [128, 128]); junk = S("at_junk", [128, 128])
            ss = S("at_ss", [128, 1]); rstd = S("at_rstd", [128, 1])
            ybT = [S(f"at_ybT{i}", [128, 512], BF16) for i in range(2)]
            pi = [0]
            yi = [0]
            sbk = [0]

            def obank(m, sub):
                idx = m * 4 + sub
                return self.ps[4 + idx // 3], (idx % 3) * 129

            def attend(KT, QT, V, qc0, nq, blocks, h, ycol0, nsub, subq):
                nb = len(blocks)
                touched = set()
                last_for_sub = {}
                for bi_, (kc, nk, vs, mi, fs) in enumerate(blocks):
                    for sub in range(fs, nsub):
                        last_for_sub[sub] = bi_
                first_for_sub = {}
                for bi_, (kc, nk, vs, mi, fs) in enumerate(blocks):
                    for sub in range(fs, nsub):
                        first_for_sub.setdefault(sub, bi_)
                def emit_s(bi_):
                    kc, nk, vs, mi, fs = blocks[bi_]
                    Ps = []
                    for m in range(2):
                        bank = self.ps[sbk[0] % 3]
                        sbk[0] += 1
                        pr = slice(m * 64, (m + 1) * 64)
                        p.op("pe", lambda e, bank=bank, pr=pr, kc=kc, nk=nk: e.matmul(
                            bank[0:nk, 0:nq], lhsT=KT[pr, kc:kc + nk], rhs=QT[pr, qc0:qc0 + nq], start=True, stop=True),
                            reads=[KT, QT], writes=[bank])
                        P = Pb[pi[0] % 4]
                        pi[0] += 1
                        p.op("act", lambda e, P=P, bank=bank, nk=nk: e.activation(out=P[0:nk, 0:nq], in_=bank[0:nk, 0:nq], func=AF.Exp, scale=0.125),
                             reads=[bank], writes=[P])
                        if mi is not None:
                            eng = "dve"
                            p.op(eng, lambda e, P=P, mi=mi: e.tensor_tensor(out=P[:, :], in0=P[:, :], in1=maskb[:, mi, :], op=ALU.mult),
                                 reads=[P, maskb], writes=[P])
                        Ps.append(P)
                    return Ps

                def emit_pv(bi_, Ps):
                    kc, nk, vs, mi, fs = blocks[bi_]
                    for m in range(2):
                        for sub in range(fs, nsub):
                            ob, oc = obank(m, sub)
                            st_ = id(ob) not in touched
                            touched.add(id(ob))
                            assert (not st_) or bi_ == 0
                            p.op("pe", lambda e, ob=ob, oc=oc, P=Ps[m], sub=sub, vs=vs, nk=nk, st_=st_: e.matmul(
                                ob[0:subq, oc:oc + 129], lhsT=P[0:nk, sub * 128:sub * 128 + subq], rhs=V[0:nk, vs, 0:129],
                                start=st_, stop=False, skip_group_check=True),
                                reads=[Ps[m], V], writes=[(ob, oc)])

                cur = emit_s(0)
                for bi_ in range(nb):
                    nxt = emit_s(bi_ + 1) if bi_ + 1 < nb else None
                    emit_pv(bi_, cur)
                    cur = nxt
                yT = ybT[yi[0] % 2]
                yi[0] += 1
                for sub in range(nsub):
                    o1, c1 = obank(0, sub)
                    o2, c2 = obank(1, sub)
                    q = subq
                    p.op("dve", lambda e, o1=o1, c1=c1, q=q: e.reciprocal(out=rc[0:q, 0:1], in_=o1[0:q, c1 + 128:c1 + 129]), reads=[(o1, c1)], writes=[rc])
                    p.op("dve", lambda e, o2=o2, c2=c2, q=q: e.reciprocal(out=rc[0:q, 1:2], in_=o2[0:q, c2 + 128:c2 + 129]), reads=[(o2, c2)], writes=[rc])
                    p.op("dve", lambda e, q=q: e.tensor_tensor(out=rc[0:q, 1:2], in0=rc[0:q, 1:2], in1=neglam[0:q, :], op=ALU.mult), reads=[rc, neglam], writes=[rc])
                    p.op("dve", lambda e, o2=o2, c2=c2, q=q: e.tensor_scalar(out=tq[0:q, :], in0=o2[0:q, c2:c2 + 128], scalar1=rc[0:q, 1:2], scalar2=None, op0=ALU.mult),
                         reads=[(o2, c2), rc], writes=[tq])
                    p.op("dve", lambda e, o1=o1, c1=c1, q=q: e.scalar_tensor_tensor(out=oq[0:q, :], in0=o1[0:q, c1:c1 + 128], scalar=rc[0:q, 0:1], in1=tq[0:q, :],
                                                                                 op0=ALU.mult, op1=ALU.add), reads=[(o1, c1), rc, tq], writes=[oq])
                    p.op("dve", lambda e: e.memset(ss[:, 0:1], 0.0), writes=[ss])
                    p.op("act", lambda e, q=q: e.activation(out=junk[0:q, :], in_=oq[0:q, :], func=AF.Square, accum_out=ss[0:q, 0:1]), reads=[oq, ss], writes=[junk, ss])
                    p.op("act", lambda e, q=q: e.activation(out=rstd[0:q, 0:1], in_=ss[0:q, 0:1], func=AF.Sqrt, scale=1.0 / 128, bias=self.eps_t[0:q, 0:1]),
                         reads=[ss, self.eps_t], writes=[rstd])
                    p.op("dve", lambda e, q=q: e.reciprocal(out=rstd[0:q, 0:1], in_=rstd[0:q, 0:1]), reads=[rstd], writes=[rstd])
                    p.op("dve", lambda e, q=q: e.scalar_tensor_tensor(out=oq[0:q, :], in0=oq[0:q, :], scalar=rstd[0:q, 0:1], in1=subw[0:q, :], op0=ALU.mult, op1=ALU.mult),
                         reads=[oq, rstd, subw], writes=[oq])
                    tb = self.ps[7]
                    p.op("pe", lambda e, q=q, sub=sub: e.transpose(out=tb[:, sub * 128:sub * 128 + q], in_=oq[0:q, :], identity=self.ident[0:q, 0:q]),
                         reads=[oq, self.ident], writes=[tb])
                    p.op("act", lambda e, q=q, sub=sub, yT=yT: e.copy(out=yT[:, sub * 128:sub * 128 + q], in_=tb[:, sub * 128:sub * 128 + q]), reads=[tb], writes=[yT])
                nqt = (nsub - 1) * 128 + subq
                p.dma(self.YT[1024 + h * 128:1024 + (h + 1) * 128, ycol0:ycol0 + nqt], yT[:, 0:nqt], reads=[yT], writes=[(self.YT, ("att", h, ycol0))])

            for h in range(8):
                KT, QT, V = KTs[h % 2], QTs[h % 2], Vs[h % 2]
                rows = slice(h * 128, (h + 1) * 128)
                p.dma(KT[:, 0:LP], self.KT[rows, 0:LP], reads=[self.KT], writes=[KT])
                p.dma(QT[:, :], self.QT[rows, :], reads=[self.QT], writes=[QT])
                p.dma(V[:, 0:NKB, 0:128], self.Vb[0:LP, rows].rearrange("(kb q) d -> q kb d", q=128), reads=[self.Vb], writes=[(V, "d")])
                for qt in range(LP // 512):
                    blocks = []
                    for kb in range(4 * qt + 4):
                        j = kb - 4 * qt
                        blocks.append((kb * 128, 128, kb, (j if j >= 0 else None), max(j, 0)))
                    attend(KT, QT, V, qt * 512, 512, blocks, h, qt * 512, 4, 128)
            for h in range(8):
                rows = slice(h * 128, (h + 1) * 128)
                for sg in range(NSAMP):
                    i = (h * NSAMP + sg) % 2
                    KT, QT, V = KTs[i], QTs[i], Vs[i]
                    c0 = LP + sg * DSEQ
                    p.dma(KT[:, 0:PAST], self.KcT[sg, rows, :], reads=[(self.KcT, sg)], writes=[KT])
                    p.dma(KT[:, PAST:PAST + DSEQ], self.KT[rows, c0:c0 + DSEQ], reads=[self.KT], writes=[KT])
                    p.dma(QT[:, c0:c0 + DSEQ], self.QT[rows, c0:c0 + DSEQ], reads=[self.QT], writes=[QT])
                    p.dma(V[:, 0:NCB, 0:128], self.Vcb[sg, :, rows].rearrange("(kb q) d -> q kb d", q=128), reads=[(self.Vcb, sg)], writes=[(V, "d")])
                    p.dma(V[0:DSEQ, NCB, 0:128], self.Vb[c0:c0 + DSEQ, rows], reads=[self.Vb], writes=[(V, "d")])
                    blocks = [(kb * 128, 128, kb, None, 0) for kb in range(NCB)] + [(PAST, DSEQ, NCB, None, 0)]
                    attend(KT, QT, V, c0, DSEQ, blocks, h, c0, 1, DSEQ)
            p.barrier()

    def stage_hgrn(self):
        p, T, NT, LP = self.p, self.T, self.NT, self.LP
        with ExitStack() as st:
            S = lambda n, shp, dt=F32: self.sb(st, n, shp, dt)
            lb2 = S("hg_lb2", [128, 2, 8]); lbt = S("hg_lbt", [128, 8]); oml = S("hg_oml", [128, 8]); nw = S("hg_nw", [128, 1])
            ones = S("hg_ones", [128, 128]); cm = S("hg_cm", [128, TS]); m16f = S("hg_m16f", [128, 128]); m16 = S("hg_m16", [128, 128], BF16)
            for r in range(2):
                self.small_T(lb2[:, r, :], self.l1_hg_lb[r, :].rearrange("(h c) -> c h", c=128), [self.l1_hg_lb], [lb2])
            self.small_T(nw[:, :], self.l1_hg_nw[:].rearrange("(c o) -> c o", o=1), [self.l1_hg_nw], [nw])
            p.op("dve", lambda e: e.tensor_tensor(out=lbt[:, :], in0=lb2[:, 1, :], in1=lb2[:, 0, :], op=ALU.subtract), reads=[lb2], writes=[lbt])
            p.op("act", lambda e: e.activation(out=lbt[:, :], in_=lbt[:, :], func=AF.Sigmoid), reads=[lbt], writes=[lbt])
            p.op("dve", lambda e: e.tensor_scalar(out=oml[:, :], in0=lbt[:, :], scalar1=-1.0, scalar2=1.0, op0=ALU.mult, op1=ALU.add), reads=[lbt], writes=[oml])
            p.op("dve", lambda e: e.memset(ones[:, :], 1.0), writes=[ones])
            p.op("dve", lambda e: e.memset(cm[:, :], 1.0), writes=[cm])
            p.op("dve", lambda e: e.memset(cm[:, :].rearrange("p (c l) -> p c l", l=16)[:, :, 0:1], 0.0), writes=[cm])
            p.dma(m16f[:, :], self.c_m16[:, :], reads=[self.c_m16], writes=[m16f])
            p.op("dve", lambda e: e.tensor_copy(out=m16[:, :], in_=m16f[:, :]), reads=[m16f], writes=[m16])
            NB = 4
            mk = lambda nm, dt=F32: [S(f"hg_{nm}{i}", [128, TS], dt) for i in range(NB)]
            qf, ff, vf, gt, bt, ebt, t1, kk, kh = mk("q"), mk("f"), mk("v"), mk("g"), mk("b"), mk("eb"), mk("t1"), mk("kk"), mk("kh")
            qb, kb_ = mk("qb", BF16), mk("kb", BF16)
            attm = [[S(f"hg_am{hd}{i}", [128, 128], BF16) for i in range(2)] for hd in range(2)]
            itok = [[S(f"hg_it{hd}{i}", [128, 128], BF16) for i in range(2)] for hd in range(2)]
            khi = [[S(f"hg_khi{hd}{i}", [16, 256], BF16) for i in range(3)] for hd in range(2)]
            S32s = [S(f"hg_S32{hd}", [128, 128]) for hd in range(2)]
            Sbfs = [S(f"hg_Sbf{hd}", [128, 128], BF16) for hd in range(2)]
            osb = [S(f"hg_o{i}", [128, TS]) for i in range(2)]
            yb = [S(f"hg_y{i}", [128, TS], BF16) for i in range(2)]
            it = 0
            for hp in range(4):
                for ti in range(NT):
                    samp = ti == NT - 1
                    c0 = ti * TS
                    sets = []
                    for hd in range(2):
                        h = hp * 2 + hd
                        rows = slice(h * 128, (h + 1) * 128)
                        i = hd * 2 + (it % 2)
                        Q, F_, V_, G, B_, EB, T1, KK, KH, QB, KB = qf[i], ff[i], vf[i], gt[i], bt[i], ebt[i], t1[i], kk[i], kh[i], qb[i], kb_[i]
                        sets.append((Q, F_, V_, G, B_, EB, T1, KK, KH, QB, KB))
                        p.dma(Q[:, :], self.HQ[rows, c0:c0 + TS], reads=[(self.HQ, ti)], writes=[Q])
                        p.dma(F_[:, :], self.HF[rows, c0:c0 + TS], reads=[(self.HF, ti)], writes=[F_])
                        p.dma(V_[:, :], self.HI[rows, c0:c0 + TS], reads=[(self.HI, ti)], writes=[V_])
                        p.op("act", lambda e, G=G, F_=F_: e.activation(out=G[:, :], in_=F_[:, :], func=AF.Sigmoid), reads=[F_], writes=[G])
                        p.op("dve", lambda e, G=G, h=h: e.tensor_scalar(out=G[:, :], in0=G[:, :], scalar1=oml[:, h:h + 1], scalar2=lbt[:, h:h + 1], op0=ALU.mult, op1=ALU.add),
                             reads=[G, oml, lbt], writes=[G])
                        p.op("dve", lambda e, G=G, KK=KK: e.tensor_scalar(out=KK[:, :], in0=G[:, :], scalar1=-1.0, scalar2=1.0, op0=ALU.mult, op1=ALU.add), reads=[G], writes=[KK])
                        p.op("act", lambda e, G=G: e.activation(out=G[:, :], in_=G[:, :], func=AF.Ln), reads=[G], writes=[G])
                        p.op("dve", lambda e, G=G, B_=B_: e.tensor_tensor_scan(out=B_[:, :], data0=cm[:, :], data1=G[:, :], initial=0.0, op0=ALU.mult, op1=ALU.add),
                             reads=[G, cm], writes=[B_])
                        p.op("act", lambda e, Q=Q: e.activation(out=Q[:, :], in_=Q[:, :], func=AF.Silu), reads=[Q], writes=[Q])
                        p.op("act", lambda e, EB=EB, B_=B_: e.activation(out=EB[:, :], in_=B_[:, :], func=AF.Exp), reads=[B_], writes=[EB])
                        p.op("dve", lambda e, QB=QB, Q=Q, EB=EB: e.tensor_tensor(out=QB[:, :], in0=Q[:, :], in1=EB[:, :], op=ALU.mult), reads=[Q, EB], writes=[QB])
                        p.op("act", lambda e, T1=T1, B_=B_: e.activation(out=T1[:, :], in_=B_[:, :], func=AF.Exp, scale=-1.0), reads=[B_], writes=[T1])
                        p.op("dve", lambda e, KB=KB, KK=KK, T1=T1: e.tensor_tensor(out=KB[:, :], in0=KK[:, :], in1=T1[:, :], op=ALU.mult), reads=[KK, T1], writes=[KB])
                        b3 = B_[:, :].rearrange("p (c l) -> p c l", l=16)
                        t3 = T1[:, :].rearrange("p (c l) -> p c l", l=16)
                        p.op("dve", lambda e, b3=b3, t3=t3: e.tensor_tensor(out=t3, in0=b3[:, :, 15:16].to_broadcast([128, 32, 16]), in1=b3, op=ALU.subtract),
                             reads=[B_], writes=[T1])
                        p.op("act", lambda e, T1=T1: e.activation(out=T1[:, :], in_=T1[:, :], func=AF.Exp), reads=[T1], writes=[T1])
                        p.op("dve", lambda e, KH=KH, KK=KK, T1=T1: e.tensor_tensor(out=KH[:, :], in0=KK[:, :], in1=T1[:, :], op=ALU.mult), reads=[KK, T1], writes=[KH])
                    it += 1
                    kcnt = [0, 0]

                    def emit_T(hd, cg):
                        Q, F_, V_, G, B_, EB, T1, KK, KH, QB, KB = sets[hd]
                        cs = slice(cg * 16, (cg + 1) * 16)
                        bt_ = self.ps[4 + hd]
                        so = (cg % 2) * 256
                        kt = khi[hd][cg % 3]
                        p.op("pe", lambda e, bt_=bt_, KH=KH, cs=cs, so=so: e.transpose(out=bt_[0:16, so:so + 128], in_=KH[:, cs], identity=self.ident[:, :]),
                             reads=[KH, self.ident], writes=[(bt_, cg % 2)])
                        p.op("pe", lambda e, bt_=bt_, V_=V_, cs=cs, so=so: e.transpose(out=bt_[0:16, so + 128:so + 256], in_=V_[:, cs], identity=self.ident[:, :]),
                             reads=[V_, self.ident], writes=[(bt_, cg % 2)])
                        p.op("act", lambda e, kt=kt, bt_=bt_, so=so: e.copy(out=kt[:, :], in_=bt_[0:16, so:so + 256]), reads=[(bt_, cg % 2)], writes=[kt])

                    for blk in range(4):
                        bs = slice(blk * 128, (blk + 1) * 128)
                        for hd in range(2):
                            Q, F_, V_, G, B_, EB, T1, KK, KH, QB, KB = sets[hd]
                            ba, bo = self.ps[hd], self.ps[2 + hd]
                            am, itk = attm[hd][blk % 2], itok[hd][blk % 2]
                            p.op("pe", lambda e, ba=ba, KB=KB, QB=QB, bs=bs: e.matmul(ba[:, 0:128], lhsT=KB[:, bs], rhs=QB[:, bs], start=True, stop=True),
                                 reads=[KB, QB], writes=[(ba, 0)])
                            p.op("dve", lambda e, am=am, ba=ba: e.tensor_tensor(out=am[:, :], in0=ba[:, 0:128], in1=m16[:, :], op=ALU.mult), reads=[(ba, 0), m16], writes=[am])
                            p.op("pe", lambda e, ba=ba, V_=V_, bs=bs: e.transpose(out=ba[:, 128:256], in_=V_[:, bs], identity=self.ident[:, :]),
                                 reads=[V_, self.ident], writes=[(ba, 1)])
                            p.op("act", lambda e, itk=itk, ba=ba: e.copy(out=itk[:, :], in_=ba[:, 128:256]), reads=[(ba, 1)], writes=[itk])
                            p.op("pe", lambda e, bo=bo, itk=itk, am=am, bs=bs: e.matmul(bo[:, bs], lhsT=itk[:, :], rhs=am[:, :], start=True, stop=False, skip_group_check=True),
                                 reads=[itk, am], writes=[bo])
                            emit_T(hd, blk * 8)
                        for c in range(8):
                            cg = blk * 8 + c
                            cs = slice(cg * 16, (cg + 1) * 16)
                            for hd in range(2):
                                h = hp * 2 + hd
                                Q, F_, V_, G, B_, EB, T1, KK, KH, QB, KB = sets[hd]
                                bo, bu = self.ps[2 + hd], self.ps[6 + hd]
                                S32, Sbf = S32s[hd], Sbfs[hd]
                                if c + 1 < 8:
                                    emit_T(hd, cg + 1)
                                seq_start = (ti == 0 and cg == 0) if not samp else (cg % 4 == 0)
                                if seq_start:
                                    if samp:
                                        sg = cg // 4
                                        p.dma(S32[:, :], self.st_hg[sg, h, :, :], reads=[self.st_hg], writes=[S32])
                                    else:
                                        p.op("dve", lambda e, S32=S32: e.memset(S32[:, :], 0.0), writes=[S32])
                                    p.op("act", lambda e, S32=S32, Sbf=Sbf: e.copy(out=Sbf[:, :], in_=S32[:, :]), reads=[S32], writes=[Sbf])
                                p.op("pe", lambda e, bo=bo, QB=QB, cs=cs, Sbf=Sbf: e.matmul(bo[:, cs], lhsT=Sbf[:, :], rhs=QB[:, cs], start=False, stop=False, skip_group_check=True),
                                     reads=[Sbf, QB], writes=[bo])
                                kt = khi[hd][cg % 3]
                                p.op("pe", lambda e, bu=bu, kt=kt: e.matmul(bu[:, 0:128], lhsT=kt[0:16, 0:128], rhs=kt[0:16, 128:256], start=True, stop=True),
                                     reads=[kt], writes=[bu])
                                p.op("dve", lambda e, bu=bu, EB=EB, cg=cg, S32=S32: e.scalar_tensor_tensor(out=S32[:, :], in0=S32[:, :], scalar=EB[:, cg * 16 + 15:cg * 16 + 16], in1=bu[:, 0:128],
                                                                                                   op0=ALU.mult, op1=ALU.add), reads=[S32, EB, bu], writes=[S32])
                                p.op("act", lambda e, S32=S32, Sbf=Sbf: e.copy(out=Sbf[:, :], in_=S32[:, :]), reads=[S32], writes=[Sbf])
                                seq_end = (ti == NT - 2 and cg == 31) if not samp else (cg % 4 == 3)
                                if seq_end:
                                    seq = (1 + cg // 4) if samp else 0
                                    p.dma(self.hg_out[seq, h, :, :], S32[:, :], reads=[S32], writes=[(self.hg_out, (seq, h))])
                    for hd in range(2):
                        h = hp * 2 + hd
                        Q, F_, V_, G, B_, EB, T1, KK, KH, QB, KB = sets[hd]
                        bo, bn = self.ps[2 + hd], self.ps[hd]
                        O_, Y_ = osb[hd], yb[hd]
                        p.op("act", lambda e, O_=O_, bo=bo: e.copy(out=O_[:, :], in_=bo[:, :]), reads=[bo], writes=[O_])
                        p.op("act", lambda e, T1=T1, O_=O_: e.activation(out=T1[:, :], in_=O_[:, :], func=AF.Square), reads=[O_], writes=[T1])
                        p.op("pe", lambda e, bn=bn, T1=T1: e.matmul(bn[:, :], lhsT=ones[:, :], rhs=T1[:, :], start=True, stop=True), reads=[ones, T1], writes=[bn])
                        p.op("act", lambda e, T1=T1, bn=bn: e.activation(out=T1[:, :], in_=bn[:, :], func=AF.Sqrt, scale=1.0 / 128, bias=self.eps_t[:, 0:1]),
                             reads=[bn, self.eps_t], writes=[T1])
                        p.op("dve", lambda e, T1=T1: e.reciprocal(out=T1[:, :], in_=T1[:, :]), reads=[T1], writes=[T1])
                        p.op("dve", lambda e, O_=O_, T1=T1, Y_=Y_: e.scalar_tensor_tensor(out=Y_[:, :], in0=O_[:, :], scalar=nw[:, 0:1], in1=T1[:, :], op0=ALU.mult, op1=ALU.mult),
                             reads=[O_, T1, nw], writes=[Y_])
                        p.dma(self.YT[1024 + h * 128:1024 + (h + 1) * 128, c0:c0 + TS], Y_[:, :], reads=[Y_], writes=[(self.YT, ("hg", h, ti))])
            p.barrier()

    def stage_ssdconv(self):
        p, T, NT, LP = self.p, self.T, self.NT, self.LP
        self.XT = self.dscr("XT", [T, 1280], F32)
        self.BCt = self.dscr("BCt", [512, T], BF16)
        with ExitStack() as st:
            S = lambda n, shp, dt=F32: self.sb(st, n, shp, dt)
            cw = S("sc_cw", [128, 12, 4]); cb = S("sc_cb", [128, 12])
            for tap in range(4):
                self.small_T(cw[:, :, tap], self.l1_conv_w[tap, :].rearrange("(b c) -> c b", c=128), [self.l1_conv_w], [cw])
            self.small_T(cb[:, :], self.l1_conv_b[:].rearrange("(b c) -> c b", c=128), [self.l1_conv_b], [cb])
            xa = [S(f"sc_xa{i}", [128, 536]) for i in range(2)]
            xc = [S(f"sc_xc{i}", [128, TS]) for i in range(2)]
            xcb = [S(f"sc_xcb{i}", [128, TS], BF16) for i in range(2)]
            xtk = [S(f"sc_xt{i}", [128, 4, 128]) for i in range(2)]
            it = 0
            for ti in range(NT):
                samp = ti == NT - 1
                nseg, sl = (NSAMP, DSEQ) if samp else (1, TS)
                c0 = ti * TS
                for b in range(12):
                    i = it % 2
                    it += 1
                    X, XC, XCB, XTK = xa[i], xc[i], xcb[i], xtk[i]
                    rows = slice(b * 128, (b + 1) * 128)
                    xv = X[:, 0:nseg * (3 + sl)].rearrange("p (s l) -> p s l", s=nseg)
                    if samp:
                        for sg in range(NSAMP):
                            self.small_T(xv[:, sg, 0:3], self.st_ssd_conv[sg, :, rows].rearrange("t c -> c t"), [self.st_ssd_conv], [X])
                    elif ti == 0:
                        p.op("dve", lambda e, xv=xv: e.memset(xv[:, :, 0:3], 0.0), writes=[X])
                    else:
                        p.dma(xv[:, 0, 0:3], self.XBC[rows, c0 - 3:c0], reads=[(self.XBC, ti - 1)], writes=[X])
                    p.dma(xv[:, :, 3:3 + sl], self.XBC[rows, c0:c0 + TS].rearrange("p (s l) -> p s l", s=nseg), reads=[(self.XBC, ti)], writes=[X])
                    xc3 = XC[:, :].rearrange("p (s l) -> p s l", s=nseg)
                    p.op("dve", lambda e, xc3=xc3, xv=xv, b=b, sl=sl: e.tensor_scalar(out=xc3, in0=xv[:, :, 0:sl], scalar1=cw[:, b, 0:1], scalar2=cb[:, b:b + 1],
                                                                                 op0=ALU.mult, op1=ALU.add), reads=[X, cw, cb], writes=[XC])
                    for tap in range(1, 4):
                        p.op("dve", lambda e, xc3=xc3, xv=xv, b=b, sl=sl, tap=tap: e.scalar_tensor_tensor(
                            out=xc3, in0=xv[:, :, tap:tap + sl], scalar=cw[:, b, tap:tap + 1], in1=xc3, op0=ALU.mult, op1=ALU.add), reads=[X, cw, XC], writes=[XC])
                    p.op("act", lambda e, XC=XC: e.activation(out=XC[:, :], in_=XC[:, :], func=AF.Silu), reads=[XC], writes=[XC])
                    if samp:
                        for sg in range(NSAMP):
                            self.small_T(self.ssd_conv_out[1 + sg, :, rows].rearrange("t c -> c t"), xv[:, sg, sl:sl + 3], [X], [(self.ssd_conv_out, ("s", b, sg))])
                    elif ti == NT - 2:
                        self.small_T(self.ssd_conv_out[0, :, rows].rearrange("t c -> c t"), X[:, TS:TS + 3], [X], [(self.ssd_conv_out, ("p", b))])
                    if b >= 8:
                        p.op("pool", lambda e, XCB=XCB, XC=XC: e.tensor_copy(out=XCB[:, :], in_=XC[:, :]), reads=[XC], writes=[XCB])
                        p.dma(self.BCt[(b - 8) * 128:(b - 7) * 128, c0:c0 + TS], XCB[:, :], reads=[XCB], writes=[(self.BCt, ti)])
                    if b < 10:
                        bank = self.ps[it % 2]
                        for t in range(4):
                            p.op("pe", lambda e, bank=bank, XC=XC, t=t: e.transpose(out=bank[:, t * 128:(t + 1) * 128], in_=XC[:, t * 128:(t + 1) * 128], identity=self.ident[:, :]),
                                 reads=[XC, self.ident], writes=[bank])
                        p.op("act", lambda e, XTK=XTK, bank=bank: e.copy(out=XTK[:, :, :], in_=bank[:, :].rearrange("p (a b) -> p a b", a=4)), reads=[bank], writes=[XTK])
                        p.dma(self.XT[c0:c0 + TS, b * 128:(b + 1) * 128].rearrange("(t q) c -> q t c", q=128), XTK[:, :, :], reads=[XTK], writes=[(self.XT, ti)])
            p.barrier()

    def stage_ssd(self):
        p, T, NT, LP = self.p, self.T, self.NT, self.LP
        with ExitStack() as st:
            S = lambda n, shp, dt=F32: self.sb(st, n, shp, dt)
            cs = S("ss_cs", [64, 448]); dtb = S("ss_dtb", [64, 16]); aneg = S("ss_an", [64, 16]); dsk = S("ss_dsk", [64, 16]); nwB = S("ss_nw", [64, 1024])
            one = S("ss_one", [128, 1])
            p.dma(cs[:, :], self.c_ssd[:, :], reads=[self.c_ssd], writes=[cs])
            tri, ntri, ones64, mneg, id64, sel63 = cs[:, 0:64], cs[:, 64:128], cs[:, 128:192], cs[:, 192:256], cs[:, 256:320], cs[:, 320:448]
            bc16 = lambda t_: t_[:].rearrange("(o d) -> o d", o=1).to_broadcast([64, 16])
            p.dma(dtb[:, :], bc16(self.l1_dt_bias), reads=[self.l1_dt_bias], writes=[dtb])
            p.dma(aneg[:, :], bc16(self.l1_a_log), reads=[self.l1_a_log], writes=[aneg])
            p.dma(dsk[:, :], bc16(self.l1_d_skip), reads=[self.l1_d_skip], writes=[dsk])
            p.dma(nwB[:, :], self.l1_ssd_nw[:].rearrange("(o d) -> o d", o=1).to_broadcast([64, 1024]), reads=[self.l1_ssd_nw], writes=[nwB])
            p.op("act", lambda e: e.activation(out=aneg[:, :], in_=aneg[:, :], func=AF.Exp), reads=[aneg], writes=[aneg])
            p.op("dve", lambda e: e.tensor_scalar(out=aneg[:, :], in0=aneg[:, :], scalar1=-1.0, scalar2=None, op0=ALU.mult), reads=[aneg], writes=[aneg])
            p.op("dve", lambda e: e.memset(one[:, :], 1.0), writes=[one])
            S32 = S("ss_S32", [128, 1024]); Sbf = S("ss_Sbf", [128, 1024], BF16); stg = S("ss_stg", [128, 8, 128])
            NB = 2
            mk = lambda nm, shp, dt=F32: [S(f"ss_{nm}{i}", shp, dt) for i in range(NB)]
            xt, zt, dtr, bct = mk("xt", [64, 1280]), mk("zt", [64, 1024]), mk("dtr", [64, 16]), mk("bct", [128, 4, 64], BF16)
            dt_, dta, r1, r2, LT = mk("dt", [64, 16]), mk("dta", [64, 16]), mk("r1", [64, 1024]), mk("r2", [64, 1024]), mk("LT", [64, 1024])
            MT, xdt, xw, btk = mk("MT", [64, 1024], BF16), mk("xdt", [64, 1024], BF16), mk("xw", [64, 1024], BF16), mk("btk", [64, 256], BF16)
            ac, eac, wv, y32, tt = mk("ac", [64, 16]), mk("eac", [64, 16]), mk("wv", [64, 16]), mk("y32", [64, 1024]), mk("tt", [64, 1024])
            decB, ssg, yT = mk("decB", [128, 16]), mk("ssg", [64, 2]), mk("yT", [128, 8, 64], BF16)
            v3 = lambda ap: ap.rearrange("p (h l) -> p h l", h=16)
            bcl = lambda ap: ap.unsqueeze(2).to_broadcast([64, 16, 64])
            seqs = [(0, 0, LP // 64)] + [(1 + sg, LP + sg * DSEQ, 1) for sg in range(NSAMP)]
            it = 0
            for (seq, col0, nch) in seqs:
                if seq == 0:
                    p.op("dve", lambda e: e.memset(S32[:, :], 0.0), writes=[S32])
                else:
                    p.dma(stg[:, :, :], self.st_ssd[seq - 1, :, :, :].rearrange("h q n -> (h q) n").rearrange("(a q) n -> q a n", q=128), reads=[self.st_ssd], writes=[stg])
                    for a in range(8):
                        bank = self.ps[a // 4]
                        p.op("pe", lambda e, bank=bank, a=a: e.transpose(out=bank[:, (a % 4) * 128:(a % 4 + 1) * 128], in_=stg[:, a, :], identity=self.ident[:, :]),
                             reads=[stg, self.ident], writes=[bank])
                    for hf in range(2):
                        p.op("dve", lambda e, hf=hf: e.tensor_copy(out=S32[:, hf * 512:(hf + 1) * 512], in_=self.ps[hf][:, :]), reads=[self.ps[hf]], writes=[S32])
                p.op("act", lambda e: e.copy(out=Sbf[:, :], in_=S32[:, :]), reads=[S32], writes=[Sbf])
                for ch in range(nch):
                    i = it % NB
                    it += 1
                    c0 = col0 + ch * 64
                    ti = c0 // TS
                    XT_, ZT, DTR, BCT, DT_, DTA, R1, R2, LT_, MT_, XDT, XW, BTK, AC, EAC, WV, Y, TT, DEC, SSG, YT_ = (
                        xt[i], zt[i], dtr[i], bct[i], dt_[i], dta[i], r1[i], r2[i], LT[i], MT[i], xdt[i], xw[i], btk[i], ac[i], eac[i], wv[i], y32[i], tt[i], decB[i], ssg[i], yT[i])
                    p.dma(XT_[:, :], self.XT[c0:c0 + 64, :], reads=[(self.XT, ti)], writes=[XT_])
                    p.dma(ZT[:, :], self.Z[c0:c0 + 64, :], reads=[(self.Z, ti)], writes=[ZT])
                    p.dma(DTR[:, :], self.DT[c0:c0 + 64, :], reads=[(self.DT, ti)], writes=[DTR])
                    p.dma(BCT[:, :, :], self.BCt[:, c0:c0 + 64].rearrange("(a n) t -> n a t", n=128), reads=[(self.BCt, ti)], writes=[BCT])
                    p.op("dve", lambda e, DT_=DT_, DTR=DTR: e.tensor_tensor(out=DT_[:, :], in0=DTR[:, :], in1=dtb[:, :], op=ALU.add), reads=[DTR, dtb], writes=[DT_])
                    p.op("act", lambda e, DT_=DT_: e.activation(out=DT_[:, :], in_=DT_[:, :], func=AF.Exp), reads=[DT_], writes=[DT_])
                    p.op("act", lambda e, DT_=DT_: e.activation(out=DT_[:, :], in_=DT_[:, :], func=AF.Ln, bias=one[0:64, 0:1]), reads=[DT_, one], writes=[DT_])
                    p.op("dve", lambda e, DTA=DTA, DT_=DT_: e.tensor_tensor(out=DTA[:, :], in0=DT_[:, :], in1=aneg[:, :], op=ALU.mult), reads=[DT_, aneg], writes=[DTA])
                    p.op("dve", lambda e, R1=R1, DTA=DTA: e.tensor_tensor(out=v3(R1[:, :]), in0=bcl(DTA[:, :]), in1=tri.unsqueeze(1).to_broadcast([64, 16, 64]), op=ALU.mult),
                         reads=[DTA, cs], writes=[R1])
                    p.op("pool", lambda e, R2=R2, DTA=DTA: e.tensor_copy(out=v3(R2[:, :]), in_=bcl(DTA[:, :])), reads=[DTA], writes=[R2])
                    bd = [self.ps[0], self.ps[1]]
                    for hf in range(2):
                        hs = slice(hf * 512, (hf + 1) * 512)
                        p.op("pe", lambda e, hf=hf, hs=hs, R1=R1: e.matmul(bd[hf][0:64, :], lhsT=ones64, rhs=R1[:, hs], start=True, stop=False), reads=[R1, cs], writes=[bd[hf]])
                        p.op("pe", lambda e, hf=hf, hs=hs, R2=R2: e.matmul(bd[hf][0:64, :], lhsT=ntri, rhs=R2[:, hs], start=False, stop=True), reads=[R2, cs], writes=[bd[hf]])
                    bm = self.ps[6]
                    p.op("pe", lambda e, DTA=DTA: e.matmul(bm[0:64, 0:16], lhsT=tri, rhs=DTA[:, :], start=True, stop=True), reads=[DTA, cs], writes=[(bm, "ac")])
                    for hf in range(2):
                        hs = slice(hf * 512, (hf + 1) * 512)
                        p.op("dve", lambda e, hf=hf, hs=hs, LT_=LT_: e.tensor_tensor(out=LT_[:, hs].rearrange("p (h l) -> p h l", h=8), in0=bd[hf][0:64, :].rearrange("p (h l) -> p h l", h=8),
                                                                                 in1=mneg.unsqueeze(1).to_broadcast([64, 8, 64]), op=ALU.add), reads=[bd[hf], cs], writes=[LT_])
                    p.op("act", lambda e, LT_=LT_: e.activation(out=LT_[:, :], in_=LT_[:, :], func=AF.Exp), reads=[LT_], writes=[LT_])
                    p.op("act", lambda e, AC=AC: e.copy(out=AC[:, :], in_=bm[0:64, 0:16]), reads=[(bm, "ac")], writes=[AC])
                    p.op("act", lambda e, EAC=EAC, AC=AC: e.activation(out=EAC[:, :], in_=AC[:, :], func=AF.Exp), reads=[AC], writes=[EAC])
                    for g in range(2):
                        p.op("pe", lambda e, g=g, BCT=BCT: e.matmul(bm[0:64, 64 + g * 64:128 + g * 64], lhsT=BCT[:, g, :], rhs=BCT[:, 2 + g, :], start=True, stop=True),
                             reads=[BCT], writes=[(bm, "cb")])
                    for g in range(2):
                        gs = slice(g * 512, (g + 1) * 512)
                        p.op("dve", lambda e, g=g, gs=gs, MT_=MT_, LT_=LT_: e.tensor_tensor(out=MT_[:, gs].rearrange("p (h l) -> p h l", h=8), in0=LT_[:, gs].rearrange("p (h l) -> p h l", h=8),
                                                                                        in1=bm[0:64, 64 + g * 64:128 + g * 64].unsqueeze(1).to_broadcast([64, 8, 64]), op=ALU.mult),
                             reads=[LT_, (bm, "cb")], writes=[MT_])
                    xv_ = v3(XT_[:, 0:1024])
                    p.op("dve", lambda e, XDT=XDT, xv_=xv_, DT_=DT_: e.tensor_tensor(out=v3(XDT[:, :]), in0=xv_, in1=bcl(DT_[:, :]), op=ALU.mult), reads=[XT_, DT_], writes=[XDT])
                    byd = [self.ps[2], self.ps[3]]
                    for h in range(16):
                        p.op("pe", lambda e, h=h, MT_=MT_, XDT=XDT: e.matmul(byd[h // 8][0:64, (h % 8) * 64:(h % 8 + 1) * 64], lhsT=MT_[:, h * 64:(h + 1) * 64], rhs=XDT[:, h * 64:(h + 1) * 64],
                                                                            start=True, stop=True, skip_group_check=True), reads=[MT_, XDT], writes=[byd[h // 8]])
                    byo = [self.ps[4], self.ps[5]]
                    for g in range(2):
                        p.op("pe", lambda e, g=g, BCT=BCT: e.matmul(byo[g][0:64, :], lhsT=BCT[:, 2 + g, :], rhs=Sbf[:, g * 512:(g + 1) * 512], start=True, stop=True),
                             reads=[BCT, Sbf], writes=[byo[g]])
                    for g in range(2):
                        gs = slice(g * 512, (g + 1) * 512)
                        p.op("dve", lambda e, g=g, gs=gs, Y=Y, EAC=EAC: e.tensor_tensor(out=Y[:, gs].rearrange("p (h l) -> p h l", h=8), in0=byo[g][0:64, :].rearrange("p (h l) -> p h l", h=8),
                                                                                    in1=EAC[:, g * 8:(g + 1) * 8].unsqueeze(2).to_broadcast([64, 8, 64]), op=ALU.mult), reads=[byo[g], EAC], writes=[Y])
                        p.op("dve", lambda e, g=g, gs=gs, Y=Y: e.tensor_tensor(out=Y[:, gs], in0=Y[:, gs], in1=byd[g][0:64, :], op=ALU.add), reads=[Y, byd[g]], writes=[Y])
                    p.op("pool", lambda e, TT=TT, xv_=xv_: e.tensor_tensor(out=v3(TT[:, :]), in0=xv_, in1=bcl(dsk[:, :]), op=ALU.mult), reads=[XT_, dsk], writes=[TT])
                    p.op("dve", lambda e, Y=Y, TT=TT: e.tensor_tensor(out=Y[:, :], in0=Y[:, :], in1=TT[:, :], op=ALU.add), reads=[Y, TT], writes=[Y])
                    p.op("act", lambda e, ZT=ZT: e.activation(out=ZT[:, :], in_=ZT[:, :], func=AF.Silu), reads=[ZT], writes=[ZT])
                    p.op("dve", lambda e, Y=Y, ZT=ZT: e.tensor_tensor(out=Y[:, :], in0=Y[:, :], in1=ZT[:, :], op=ALU.mult), reads=[Y, ZT], writes=[Y])
                    p.op("dve", lambda e, SSG=SSG: e.memset(SSG[:, :], 0.0), writes=[SSG])
                    for g in range(2):
                        gs = slice(g * 512, (g + 1) * 512)
                        p.op("act", lambda e, g=g, gs=gs, TT=TT, Y=Y, SSG=SSG: e.activation(out=TT[:, gs], in_=Y[:, gs], func=AF.Square, accum_out=SSG[:, g:g + 1]), reads=[Y, SSG], writes=[TT, SSG])
                    p.op("act", lambda e, SSG=SSG: e.activation(out=SSG[:, :], in_=SSG[:, :], func=AF.Sqrt, scale=1.0 / 512, bias=self.eps_t[0:64, 0:1]), reads=[SSG, self.eps_t], writes=[SSG])
                    p.op("dve", lambda e, SSG=SSG: e.reciprocal(out=SSG[:, :], in_=SSG[:, :]), reads=[SSG], writes=[SSG])
                    for g in range(2):
                        gs = slice(g * 512, (g + 1) * 512)
                        p.op("dve", lambda e, g=g, gs=gs, Y=Y, SSG=SSG: e.scalar_tensor_tensor(out=Y[:, gs], in0=Y[:, gs], scalar=SSG[:, g:g + 1], in1=nwB[:, gs], op0=ALU.mult, op1=ALU.mult),
                             reads=[Y, SSG, nwB], writes=[Y])
                    btr = self.ps[7]
                    for a in range(8):
                        p.op("pe", lambda e, a=a, Y=Y: e.transpose(out=btr[:, a * 64:(a + 1) * 64], in_=Y[:, a * 128:(a + 1) * 128], identity=id64), reads=[Y, cs], writes=[btr])
                    p.op("act", lambda e, YT_=YT_: e.copy(out=YT_[:, :, :], in_=btr[:, :].rearrange("p (a l) -> p a l", a=8)), reads=[btr], writes=[YT_])
                    p.dma(self.YT[0:1024, c0:c0 + 64].rearrange("(a q) t -> q a t", q=128), YT_[:, :, :], reads=[YT_], writes=[(self.YT, ("ssd", c0))])
                    p.op("dve", lambda e, WV=WV, LT_=LT_, DT_=DT_: e.tensor_tensor(out=WV[:, :], in0=v3(LT_[:, :])[:, :, 63], in1=DT_[:, :], op=ALU.mult), reads=[LT_, DT_], writes=[WV])
                    p.op("dve", lambda e, XW=XW, xv_=xv_, WV=WV: e.tensor_tensor(out=v3(XW[:, :]), in0=xv_, in1=bcl(WV[:, :]), op=ALU.mult), reads=[XT_, WV], writes=[XW])
                    p.op("pool", lambda e, BTK=BTK, XT_=XT_: e.tensor_copy(out=BTK[:, :], in_=XT_[:, 1024:1280]), reads=[XT_], writes=[BTK])
                    p.op("pe", lambda e, AC=AC: e.matmul(bm[:, 256:272], lhsT=sel63, rhs=AC[:, :], start=True, stop=True), reads=[AC, cs], writes=[(bm, "dec")])
                    p.op("act", lambda e, DEC=DEC: e.activation(out=DEC[:, :], in_=bm[:, 256:272], func=AF.Exp), reads=[(bm, "dec")], writes=[DEC])
                    for g in range(2):
                        gs = slice(g * 512, (g + 1) * 512)
                        p.op("pe", lambda e, g=g, gs=gs, BTK=BTK, XW=XW: e.matmul(bd[g][:, :], lhsT=BTK[:, g * 128:(g + 1) * 128], rhs=XW[:, gs], start=True, stop=True), reads=[BTK, XW], writes=[bd[g]])
                        p.op("dve", lambda e, g=g, gs=gs, DEC=DEC: e.tensor_tensor(out=S32[:, gs].rearrange("p (h l) -> p h l", h=8), in0=S32[:, gs].rearrange("p (h l) -> p h l", h=8),
                                                                              in1=DEC[:, g * 8:(g + 1) * 8].unsqueeze(2).to_broadcast([128, 8, 64]), op=ALU.mult), reads=[S32, DEC], writes=[S32])
                        p.op("dve", lambda e, g=g, gs=gs: e.tensor_tensor(out=S32[:, gs], in0=S32[:, gs], in1=bd[g][:, :], op=ALU.add), reads=[S32, bd[g]], writes=[S32])
                    p.op("act", lambda e: e.copy(out=Sbf[:, :], in_=S32[:, :]), reads=[S32], writes=[Sbf])
                for a in range(8):
                    bank = self.ps[a // 4]
                    p.op("pe", lambda e, bank=bank, a=a: e.transpose(out=bank[:, (a % 4) * 128:(a % 4 + 1) * 128], in_=S32[:, a * 128:(a + 1) * 128], identity=self.ident[:, :]),
                         reads=[S32, self.ident], writes=[bank])
                for hf in range(2):
                    p.op("dve", lambda e, hf=hf: e.tensor_copy(out=stg[:, hf * 4:(hf + 1) * 4, :], in_=self.ps[hf][:, :].rearrange("p (a n) -> p a n", a=4)), reads=[self.ps[hf]], writes=[stg])
                p.dma(self.ssd_out[seq, :, :, :].rearrange("h q n -> (h q) n").rearrange("(a q) n -> q a n", q=128), stg[:, :, :], reads=[stg], writes=[(self.ssd_out, seq)])
            p.barrier()


W_NAMES = ["norm_w", "l0_w_in", "l0_conv_w", "l0_conv_b", "l0_rg_w_r", "l0_rg_b_r", "l0_rg_w_i", "l0_rg_b_i",
           "l0_rg_lambda", "l0_lq1", "l0_lk1", "l0_lq2", "l0_lk2", "l0_subln_w", "l0_w_out", "l1_w_in",
           "l1_conv_w", "l1_conv_b", "l1_dt_bias", "l1_a_log", "l1_d_skip", "l1_ssd_norm_w",
           "l1_hg_lower_bound", "l1_hg_norm_w", "l1_w_out", "ffn_w_gate", "ffn_w_up", "ffn_w_down"]


def make_consts():
    ident = np.eye(128, dtype=np.float32)
    mask = np.zeros((128, 4, 512), np.float32)
    for j in range(4):
        kc = (j * 128 + np.arange(128)) // 64
        qc = np.arange(512) // 64
        mask[:, j, :] = (kc[:, None] <= qc[None, :]).astype(np.float32)
    jj = np.arange(128)
    m16 = ((jj[:, None] // 16 == jj[None, :] // 16) & (jj[:, None] <= jj[None, :])).astype(np.float32)
    j6 = np.arange(64)
    tri = (j6[:, None] <= j6[None, :]).astype(np.float32)
    cs = np.zeros((64, 5 * 64 + 128), np.float32)
    cs[:, 0:64] = tri
    cs[:, 64:128] = -tri
    cs[:, 128:192] = 1.0
    cs[:, 192:256] = np.where(tri > 0, 0.0, -30000.0)
    cs[:, 256:320] = np.eye(64)
    cs[63, 320:448] = 1.0
    return {"c_ident": ident, "c_mask": mask.reshape(128, 2048).astype(ml_dtypes.bfloat16), "c_m16": m16, "c_ssd": cs}


def core_inputs(inp, c):
    f = lambda a: np.ascontiguousarray(np.asarray(a, dtype=np.float32))
    s0, s1 = NSAMP * c, NSAMP * (c + 1)
    LP = inp["x_prompt"].shape[1]
    m = {}
    m["x"] = f(np.concatenate([inp["x_prompt"][c].reshape(LP, D), inp["x_sample"][s0:s1].reshape(NSAMP * DSEQ, D)], 0))
    past = inp["cache_diff_k"].shape[1]
    m["ck"] = f(inp["cache_diff_k"][s0:s1].reshape(NSAMP, past, 1024))
    m["cv"] = f(inp["cache_diff_v"][s0:s1].reshape(NSAMP, past, 1024))
    m["st_rg_conv"] = f(inp["state_rglru_conv"][s0:s1])
    m["st_rg_h"] = f(inp["state_rglru_h"][s0:s1])
    m["st_ssd_conv"] = f(inp["state_ssd_conv"][s0:s1])
    m["st_ssd"] = f(inp["state_ssd"][s0:s1])
    m["st_hg"] = f(inp["state_hgrn"][s0:s1])
    for n in W_NAMES:
        m[n] = f(inp[n])
    m.update(make_consts())
    return m


_CACHE = {}


def get_builder(LP, PAST, debug=False):
    key = (LP, PAST, debug)
    if key not in _CACHE:
        b = Builder(LP, PAST, debug)
        b.build()
        _CACHE[key] = b
    return _CACHE[key]


def run_cores(inp, debug=False):
    LP = inp["x_prompt"].shape[1]
    PAST = inp["cache_diff_k"].shape[1]
    b = get_builder(LP, PAST, debug)
    maps = [core_inputs(inp, c % 2) for c in range(2)]
    import os
    ncores = int(os.environ.get("K_NCORES", "8"))
    zeros = {k: np.zeros_like(v) for k, v in maps[0].items()}
    in_maps = [maps[c] if c < 2 else zeros for c in range(ncores)]
    res = run_bass_kernel_spmd(b.nc, in_maps, core_ids=list(range(ncores)))
    return b, res.results


def kernel(**inp):
    b, r = run_cores(inp)
    LP = b.LP
    g = lambda name: [np.asarray(r[min(c, len(r) - 1)][name]) for c in range(2)]
    y, ko, vo = g("y"), g("k_out"), g("v_out")
    rgc, rgh, sc, so, ho = g("rg_conv_out"), g("rg_h_out"), g("ssd_conv_out"), g("ssd_out"), g("hg_out")
    P = lambda a, shp: np.stack([a[c][:LP] for c in range(2)], 0).reshape(shp).astype(np.float32)
    S = lambda a, shp: np.concatenate([a[c][LP:] for c in range(2)], 0).reshape(shp).astype(np.float32)
    P0 = lambda a: np.stack([a[c][0] for c in range(2)], 0).astype(np.float32)
    S0 = lambda a: np.concatenate([a[c][1:] for c in range(2)], 0).astype(np.float32)
    return (P(y, (2, LP, D)), S(y, (16, DSEQ, D)),
            P(ko, (2, LP, 8, 128)), P(vo, (2, LP, 8, 128)), P0(rgc), P0(rgh), P0(sc), P0(so), P0(ho),
            S(ko, (16, DSEQ, 8, 128)), S(vo, (16, DSEQ, 8, 128)), S0(rgc), S0(rgh), S0(sc), S0(so), S0(ho))
```
